# Optimizing a Trainium2 kernel written in Bass

```python
import jax, jax.numpy as jnp
from jax import lax
import numpy as np

D_MODEL = 2048
BATCH = 32
SEQ = 256
DEPTH = 4
DEC_BATCH = 4
DEC_SEQ = 1024
PAST_LEN = 256

GRID_W = 64
N_MIXERS = 3
N_ATTN = (DEPTH + 2) // 3
N_MLSTM = (DEPTH + 1) // 3
N_RWKV = DEPTH // 3
N_MOD = 9
D_FF = 5632
EPS = 1e-6

ATTN_HEADS = 16
ATTN_KV_HEADS = 4
ATTN_GROUP = ATTN_HEADS // ATTN_KV_HEADS
ATTN_HEAD_DIM = D_MODEL // ATTN_HEADS
ATTN_WINDOW = 128
ATTN_BLOCK = 128
ROPE_THETA = 10000.0
QKV_DIM = (ATTN_HEADS + 2 * ATTN_KV_HEADS) * ATTN_HEAD_DIM

MLSTM_HEADS = 8
MLSTM_DV = D_MODEL // MLSTM_HEADS
MLSTM_DK = MLSTM_DV // 2
MLSTM_CHUNK = 64
MLSTM_IN_DIM = 2 * MLSTM_HEADS * MLSTM_DK + 2 * D_MODEL

RWKV_HEAD = 64
RWKV_HEADS = D_MODEL // RWKV_HEAD
RWKV_LORA_W = 96
RWKV_LORA_A = 96
RWKV_LORA_G = 256
RWKV_GN_EPS = 64e-5

kernel_name = 'hybrid_dit_attn_mlstm_rwkv7_step'


def rmsnorm(x, g, eps=EPS):
    xf = x.astype(jnp.float32)
    y = xf * lax.rsqrt(jnp.mean(xf * xf, axis=-1, keepdims=True) + eps)
    return (y * g.astype(jnp.float32)).astype(x.dtype)


def modulate(x, g, shift, scale):
    return rmsnorm(x, g) * (1.0 + scale) + shift


def swiglu(h, w_in, w_out):
    gate, up = jnp.split(h @ w_in, 2, axis=-1)
    return (jax.nn.silu(gate) * up) @ w_out


def adaln(cond, mod_w, mod_b):
    m = jax.nn.silu(cond) @ mod_w + mod_b
    return m.reshape(cond.shape[0], N_MOD, D_MODEL)


def macaron_layer(x, mod, norm_g, ffn_w_in, ffn_w_out, mixer):
    m = lambda j: mod[:, None, j, :]
    x = x + 0.5 * m(2) * swiglu(modulate(x, norm_g[0], m(0), m(1)), ffn_w_in[0], ffn_w_out[0])
    y, st = mixer(modulate(x, norm_g[1], m(3), m(4)))
    x = x + m(5) * y
    x = x + 0.5 * m(8) * swiglu(modulate(x, norm_g[2], m(6), m(7)), ffn_w_in[1], ffn_w_out[1])
    return x, st


def _rotate(x, ang):
    f = ang.shape[-1]
    cos = jnp.cos(ang)[:, None, :].astype(x.dtype)
    sin = jnp.sin(ang)[:, None, :].astype(x.dtype)
    xa, xb = x[..., :f], x[..., f:]
    return jnp.concatenate([xa * cos - xb * sin, xb * cos + xa * sin], axis=-1)


def rope_2d(x):
    n_tok = x.shape[1]
    rows = n_tok // GRID_W
    row = jnp.repeat(jnp.arange(rows, dtype=jnp.float32), GRID_W)
    col = jnp.tile(jnp.arange(GRID_W, dtype=jnp.float32), rows)
    n_freq = ATTN_HEAD_DIM // 4
    inv_freq = ROPE_THETA ** (-jnp.arange(n_freq, dtype=jnp.float32) / n_freq)
    half = ATTN_HEAD_DIM // 2
    return jnp.concatenate([_rotate(x[..., :half], row[:, None] * inv_freq),
                            _rotate(x[..., half:], col[:, None] * inv_freq)], axis=-1)


def attn_qkv(h, w_qkv, q_norm, k_norm):
    b, t, _ = h.shape
    q, k, v = jnp.split(h @ w_qkv, [ATTN_HEADS * ATTN_HEAD_DIM, (ATTN_HEADS + ATTN_KV_HEADS) * ATTN_HEAD_DIM], axis=-1)
    q = rmsnorm(q.reshape(b, t, ATTN_HEADS, ATTN_HEAD_DIM), q_norm)
    k = rmsnorm(k.reshape(b, t, ATTN_KV_HEADS, ATTN_HEAD_DIM), k_norm)
    v = v.reshape(b, t, ATTN_KV_HEADS, ATTN_HEAD_DIM)
    return q, k, v


def sink_attend(q, k, v, valid, sink):
    s = jnp.einsum('bqhgd,bshd->bhgqs', q.astype(jnp.float32), k.astype(jnp.float32)) * ATTN_HEAD_DIM ** -0.5
    if valid is not None:
        s = jnp.where(valid, s, -jnp.inf)
    sk = jnp.broadcast_to(sink.astype(jnp.float32).reshape(1, ATTN_KV_HEADS, ATTN_GROUP, 1, 1), s.shape[:-1] + (1,))
    p = jax.nn.softmax(jnp.concatenate([s, sk], axis=-1), axis=-1)[..., :-1]
    return jnp.einsum('bhgqs,bshd->bqhgd', p.astype(v.dtype), v)


def attn_context(h, w_qkv, q_norm, k_norm, sink, w_o):
    b, t, _ = h.shape
    q, k, v = attn_qkv(h, w_qkv, q_norm, k_norm)
    nb = t // ATTN_BLOCK
    qb = q.reshape(b, nb, ATTN_BLOCK, ATTN_KV_HEADS, ATTN_GROUP, ATTN_HEAD_DIM).swapaxes(0, 1)
    o = lax.map(lambda qblk: sink_attend(qblk, k, v, None, sink), qb)
    o = o.swapaxes(0, 1).reshape(b, t, ATTN_HEADS * ATTN_HEAD_DIM)
    return o @ w_o, (k, v)


def attn_latent(h, ctx_k, ctx_v, w_qkv, q_norm, k_norm, sink, w_o):
    b, t, _ = h.shape
    q, k, v = attn_qkv(h, w_qkv, q_norm, k_norm)
    q, k = rope_2d(q), rope_2d(k)
    span = ATTN_BLOCK + 2 * ATTN_WINDOW
    pad = ((0, 0), (ATTN_WINDOW, ATTN_WINDOW), (0, 0), (0, 0))
    k_pad, v_pad = jnp.pad(k, pad), jnp.pad(v, pad)
    ctx_k = ctx_k.astype(k.dtype)
    ctx_v = ctx_v.astype(v.dtype)
    n_ctx = ctx_k.shape[1]
    q_idx = jnp.arange(ATTN_BLOCK)[:, None]
    k_idx = jnp.arange(span)[None, :]
    in_window = jnp.abs(k_idx - ATTN_WINDOW - q_idx) <= ATTN_WINDOW
    ctx_valid = jnp.ones((ATTN_BLOCK, n_ctx), dtype=bool)

    def block(args):
        qblk, blk = args
        start = blk * ATTN_BLOCK
        kb = lax.dynamic_slice_in_dim(k_pad, start, span, axis=1)
        vb = lax.dynamic_slice_in_dim(v_pad, start, span, axis=1)
        key_pos = start - ATTN_WINDOW + k_idx
        valid = in_window & (key_pos >= 0) & (key_pos < t)
        valid = jnp.concatenate([valid, ctx_valid], axis=-1)
        return sink_attend(qblk, jnp.concatenate([kb, ctx_k], axis=1), jnp.concatenate([vb, ctx_v], axis=1), valid, sink)

    nb = t // ATTN_BLOCK
    qb = q.reshape(b, nb, ATTN_BLOCK, ATTN_KV_HEADS, ATTN_GROUP, ATTN_HEAD_DIM).swapaxes(0, 1)
    o = lax.map(block, (qb, jnp.arange(nb)))
    o = o.swapaxes(0, 1).reshape(b, t, ATTN_HEADS * ATTN_HEAD_DIM)
    return o @ w_o, None


def mlstm_chunkwise(q, k, v, i_pre, f_pre, c0, n0, m0):
    b, nh, t, _ = q.shape
    L = MLSTM_CHUNK
    nc = t // L

    def chunks(x):
        return jnp.moveaxis(x.reshape(x.shape[:2] + (nc, L) + x.shape[3:]), 2, 0)

    causal = jnp.tril(jnp.ones((L, L), dtype=bool))

    def step(carry, inp):
        C, n, m = carry
        qc, kc, vc, ic, lfc = inp
        bcum = jnp.cumsum(lfc, axis=-1)
        d = jnp.where(causal, bcum[..., :, None] - bcum[..., None, :] + ic[..., None, :], -jnp.inf)
        inter = bcum + m[..., None]
        m_c = jnp.maximum(inter, jnp.max(d, axis=-1))
        s = jnp.einsum('bhtd,bhsd->bhts', qc, kc) * jnp.exp(d - m_c[..., None])
        a = jnp.exp(inter - m_c)
        num = jnp.einsum('bhts,bhsv->bhtv', s, vc) + a[..., None] * jnp.einsum('bhtd,bhdv->bhtv', qc, C)
        den = jnp.sum(s, axis=-1) + a * jnp.einsum('bhtd,bhd->bht', qc, n)
        hc = num / jnp.maximum(jnp.abs(den), jnp.exp(-m_c))[..., None]
        bl = bcum[..., -1]
        g = bl[..., None] - bcum + ic
        m_new = jnp.maximum(bl + m, jnp.max(g, axis=-1))
        decay = jnp.exp(bl + m - m_new)
        wg = jnp.exp(g - m_new[..., None])
        C_new = decay[..., None, None] * C + jnp.einsum('bhs,bhsd,bhsv->bhdv', wg, kc, vc)
        n_new = decay[..., None] * n + jnp.einsum('bhs,bhsd->bhd', wg, kc)
        return (C_new, n_new, m_new), hc

    lf = jax.nn.log_sigmoid(f_pre)
    (C, n, m), h = lax.scan(step, (c0, n0, m0), (chunks(q), chunks(k), chunks(v), chunks(i_pre), chunks(lf)))
    h = jnp.moveaxis(h, 0, 2).reshape(b, nh, t, -1)
    return h, C, n, m


def mlstm_mixer(h, state0, w_in, w_gate, b_gate, out_norm, w_o):
    b, t, _ = h.shape
    f32 = jnp.float32
    qk = MLSTM_HEADS * MLSTM_DK
    q, k, v, og = jnp.split(h @ w_in, [qk, 2 * qk, 2 * qk + D_MODEL], axis=-1)

    def heads(x, d):
        return x.reshape(b, t, MLSTM_HEADS, d).transpose(0, 2, 1, 3).astype(f32)

    q = heads(q, MLSTM_DK) * MLSTM_DK ** -0.5
    k = heads(k, MLSTM_DK)
    v = heads(v, MLSTM_DV)
    gates = (h @ w_gate + b_gate).astype(f32).reshape(b, t, 4, MLSTM_HEADS).transpose(2, 0, 3, 1)
    c0, n0, m0 = [s.astype(f32) for s in state0]
    h_f, cf, nf, mf = mlstm_chunkwise(q, k, v, gates[0], gates[1], c0[:, 0], n0[:, 0], m0[:, 0])
    flip = lambda x: jnp.flip(x, axis=2)
    h_b, cb, nbk, mb = mlstm_chunkwise(flip(q), flip(k), flip(v), flip(gates[2]), flip(gates[3]), c0[:, 1], n0[:, 1], m0[:, 1])
    hs = rmsnorm(h_f + flip(h_b), out_norm.reshape(MLSTM_HEADS, 1, MLSTM_DV))
    hs = hs.transpose(0, 2, 1, 3).reshape(b, t, D_MODEL)
    y = (jax.nn.sigmoid(og.astype(f32)) * hs).astype(h.dtype) @ w_o
    return y, (jnp.stack([cf, cb], axis=1), jnp.stack([nf, nbk], axis=1), jnp.stack([mf, mb], axis=1))


def centred_shift(x):
    prev = jnp.pad(x[:, :-1], ((0, 0), (1, 0), (0, 0)))
    nxt = jnp.pad(x[:, 1:], ((0, 0), (0, 1), (0, 0)))
    return 0.5 * (prev + nxt) - x


def rwkv_scan(s0, r, w, k, v, kk, a, reverse):
    def step(S, inp):
        rt, wt, kt, vt, kkt, at = inp
        sa = jnp.einsum('bhvk,bhk->bhv', S, kkt)
        S = S * wt[:, :, None, :] - sa[..., None] * (kkt * at)[:, :, None, :] + vt[..., None] * kt[:, :, None, :]
        return S, jnp.einsum('bhvk,bhk->bhv', S, rt)
    return lax.scan(step, s0, (r, w, k, v, kk, a), reverse=reverse)


def rwkv_mixer(h, s0, mu, w_rkv, w0, wA, wB, a0, aA, aB, gA, gB, k_k, k_a, r_k, ln_g, ln_b, w_o):
    b, t, d = h.shape
    f32 = jnp.float32
    xs = h[None] + centred_shift(h)[None] * mu[:, None, None, :]
    r, k, v = jnp.einsum('pbtd,pde->pbte', xs[:3], w_rkv)
    lw = jnp.einsum('zbtr,zrd->zbtd', jnp.tanh(jnp.einsum('btd,zdr->zbtr', xs[3], wA)), wB)
    w_par = (w0[:, None, None, :] + lw).astype(f32)
    decay = jnp.exp(-jnp.exp(-jax.nn.softplus(-w_par) - 0.5))
    a = jax.nn.sigmoid((a0[:, None, None, :] + jnp.einsum('zbtr,zrd->zbtd', jnp.einsum('btd,zdr->zbtr', xs[4], aA), aB)).astype(f32))
    g = jax.nn.sigmoid(xs[5] @ gA) @ gB
    r, k, v = r.astype(f32), k.astype(f32), v.astype(f32)
    kkh = (k * k_k.astype(f32)).reshape(b, t, RWKV_HEADS, RWKV_HEAD)
    kkh = kkh / jnp.maximum(jnp.linalg.norm(kkh, axis=-1, keepdims=True), 1e-12)
    k_dir = k[None] * (1.0 + (a - 1.0) * k_a.astype(f32))
    tm = lambda x: x.reshape(b, t, RWKV_HEADS, RWKV_HEAD).swapaxes(0, 1)
    s0 = s0.astype(f32)
    kk_t = kkh.swapaxes(0, 1)
    s_f, y_f = rwkv_scan(s0[:, 0], tm(r), tm(decay[0]), tm(k_dir[0]), tm(v), kk_t, tm(a[0]), False)
    s_b, y_b = rwkv_scan(s0[:, 1], tm(r), tm(decay[1]), tm(k_dir[1]), tm(v), kk_t, tm(a[1]), True)
    y = (y_f + y_b).swapaxes(0, 1)
    mean = jnp.mean(y, axis=-1, keepdims=True)
    var = jnp.mean(jnp.square(y - mean), axis=-1, keepdims=True)
    y = (y - mean) * lax.rsqrt(var + RWKV_GN_EPS) * ln_g.astype(f32).reshape(RWKV_HEADS, RWKV_HEAD) + ln_b.astype(f32).reshape(RWKV_HEADS, RWKV_HEAD)
    rh = r.reshape(b, t, RWKV_HEADS, RWKV_HEAD)
    kd = k_dir.reshape(2, b, t, RWKV_HEADS, RWKV_HEAD)
    bonus = jnp.sum(rh[None] * kd * r_k.astype(f32), axis=(0, -1))[..., None] * v.reshape(b, t, RWKV_HEADS, RWKV_HEAD)
    y = ((y + bonus).reshape(b, t, d) * g.astype(f32)).astype(h.dtype)
    return y @ w_o, jnp.stack([s_f, s_b], axis=1)


def setup_inputs(seed: int = 0) -> dict:
    key = jax.random.key(seed)
    ks = iter(jax.random.split(key, 48))
    D = D_MODEL

    def nrm(shape, scale=1.0):
        return jax.random.normal(next(ks), shape, jnp.float32) * scale

    inp = {}
    inp['x_prompt'] = nrm((BATCH, SEQ, D))
    inp['x_sample'] = nrm((DEC_BATCH, DEC_SEQ, D))
    inp['cache_k'] = nrm((DEC_BATCH, N_ATTN, PAST_LEN, ATTN_KV_HEADS, ATTN_HEAD_DIM))
    inp['cache_v'] = nrm((DEC_BATCH, N_ATTN, PAST_LEN, ATTN_KV_HEADS, ATTN_HEAD_DIM))
    inp['state_mlstm_C'] = nrm((DEC_BATCH, N_MLSTM, 2, MLSTM_HEADS, MLSTM_DK, MLSTM_DV), 0.1)
    inp['state_mlstm_n'] = nrm((DEC_BATCH, N_MLSTM, 2, MLSTM_HEADS, MLSTM_DK), 0.1)
    inp['state_mlstm_m'] = nrm((DEC_BATCH, N_MLSTM, 2, MLSTM_HEADS))
    inp['state_rwkv'] = nrm((DEC_BATCH, N_RWKV, 2, RWKV_HEADS, RWKV_HEAD, RWKV_HEAD), 0.1)
    inp['c'] = nrm((DEC_BATCH, D))
    inp['c_ctx'] = nrm((D,))
    inp['mod_w'] = nrm((DEPTH, D, N_MOD * D), 0.5 * D ** -0.5)
    inp['mod_b'] = nrm((DEPTH, N_MOD * D), 0.02)
    inp['norm_g'] = 1.0 + nrm((DEPTH, 3, D), 0.02)
    inp['ffn_w_in'] = nrm((DEPTH, 2, D, 2 * D_FF), D ** -0.5)
    inp['ffn_w_out'] = nrm((DEPTH, 2, D_FF, D), D_FF ** -0.5)
    inp['attn_w_qkv'] = nrm((N_ATTN, D, QKV_DIM), D ** -0.5)
    inp['attn_q_norm'] = 1.0 + nrm((N_ATTN, ATTN_HEAD_DIM), 0.02)
    inp['attn_k_norm'] = 1.0 + nrm((N_ATTN, ATTN_HEAD_DIM), 0.02)
    inp['attn_sink'] = nrm((N_ATTN, ATTN_HEADS), 0.5)
    inp['attn_w_o'] = nrm((N_ATTN, ATTN_HEADS * ATTN_HEAD_DIM, D), (ATTN_HEADS * ATTN_HEAD_DIM) ** -0.5)
    inp['mlstm_w_in'] = nrm((N_MLSTM, D, MLSTM_IN_DIM), D ** -0.5)
    inp['mlstm_w_gate'] = nrm((N_MLSTM, D, 4 * MLSTM_HEADS), D ** -0.5)
    gate_base = jnp.tile(jnp.array([0.0, 3.0, 0.0, 3.0], jnp.float32)[:, None], (1, MLSTM_HEADS)).reshape(-1)
    inp['mlstm_b_gate'] = gate_base[None, :] + nrm((N_MLSTM, 4 * MLSTM_HEADS), 0.1)
    inp['mlstm_out_norm'] = 1.0 + nrm((N_MLSTM, D), 0.02)
    inp['mlstm_w_o'] = nrm((N_MLSTM, D, D), D ** -0.5)
    inp['rwkv_mu'] = jax.random.uniform(next(ks), (N_RWKV, 6, D), jnp.float32)
    inp['rwkv_w_rkv'] = nrm((N_RWKV, 3, D, D), D ** -0.5)
    inp['rwkv_w0'] = nrm((N_RWKV, 2, D), 0.5)
    inp['rwkv_wA'] = nrm((N_RWKV, 2, D, RWKV_LORA_W), D ** -0.5)
    inp['rwkv_wB'] = nrm((N_RWKV, 2, RWKV_LORA_W, D), 0.5 * RWKV_LORA_W ** -0.5)
    inp['rwkv_a0'] = nrm((N_RWKV, 2, D), 0.5)
    inp['rwkv_aA'] = nrm((N_RWKV, 2, D, RWKV_LORA_A), D ** -0.5)
    inp['rwkv_aB'] = nrm((N_RWKV, 2, RWKV_LORA_A, D), 0.5 * RWKV_LORA_A ** -0.5)
    inp['rwkv_gA'] = nrm((N_RWKV, D, RWKV_LORA_G), D ** -0.5)
    inp['rwkv_gB'] = nrm((N_RWKV, RWKV_LORA_G, D), RWKV_LORA_G ** -0.5)
    inp['rwkv_k_k'] = 0.85 + nrm((N_RWKV, D), 0.05)
    inp['rwkv_k_a'] = 1.0 + nrm((N_RWKV, D), 0.05)
    inp['rwkv_r_k'] = nrm((N_RWKV, RWKV_HEADS, RWKV_HEAD), 0.1)
    inp['rwkv_ln_g'] = 1.0 + nrm((N_RWKV, D), 0.02)
    inp['rwkv_ln_b'] = nrm((N_RWKV, D), 0.02)
    inp['rwkv_w_o'] = nrm((N_RWKV, D, D), D ** -0.5)
    return inp


def reference(x_prompt, x_sample, cache_k, cache_v, state_mlstm_C, state_mlstm_n, state_mlstm_m, state_rwkv,
              c, c_ctx, mod_w, mod_b, norm_g, ffn_w_in, ffn_w_out,
              attn_w_qkv, attn_q_norm, attn_k_norm, attn_sink, attn_w_o,
              mlstm_w_in, mlstm_w_gate, mlstm_b_gate, mlstm_out_norm, mlstm_w_o,
              rwkv_mu, rwkv_w_rkv, rwkv_w0, rwkv_wA, rwkv_wB, rwkv_a0, rwkv_aA, rwkv_aB,
              rwkv_gA, rwkv_gB, rwkv_k_k, rwkv_k_a, rwkv_r_k, rwkv_ln_g, rwkv_ln_b, rwkv_w_o):
    xc = x_prompt
    xl = x_sample
    b_ctx = x_prompt.shape[0]
    new_k, new_v, new_C, new_n, new_m, new_S = [], [], [], [], [], []
    for i in range(DEPTH):
        kind, slot = i % N_MIXERS, i // N_MIXERS
        mod_ctx = adaln(c_ctx[None, :], mod_w[i], mod_b[i])
        mod_lat = adaln(c, mod_w[i], mod_b[i])
        lp = (norm_g[i], ffn_w_in[i], ffn_w_out[i])
        if kind == 0:
            ap = (attn_w_qkv[slot], attn_q_norm[slot], attn_k_norm[slot], attn_sink[slot], attn_w_o[slot])
            xc, (k_ctx, v_ctx) = macaron_layer(xc, mod_ctx, *lp, lambda h: attn_context(h, *ap))
            xl, _ = macaron_layer(xl, mod_lat, *lp, lambda h: attn_latent(h, cache_k[:, slot], cache_v[:, slot], *ap))
            new_k.append(k_ctx)
            new_v.append(v_ctx)
        elif kind == 1:
            mp = (mlstm_w_in[slot], mlstm_w_gate[slot], mlstm_b_gate[slot], mlstm_out_norm[slot], mlstm_w_o[slot])
            zero_state = (jnp.zeros((b_ctx, 2, MLSTM_HEADS, MLSTM_DK, MLSTM_DV), jnp.float32),
                          jnp.zeros((b_ctx, 2, MLSTM_HEADS, MLSTM_DK), jnp.float32),
                          jnp.zeros((b_ctx, 2, MLSTM_HEADS), jnp.float32))
            lat_state = (state_mlstm_C[:, slot], state_mlstm_n[:, slot], state_mlstm_m[:, slot])
            xc, (C_ctx, n_ctx, m_ctx) = macaron_layer(xc, mod_ctx, *lp, lambda h: mlstm_mixer(h, zero_state, *mp))
            xl, _ = macaron_layer(xl, mod_lat, *lp, lambda h: mlstm_mixer(h, lat_state, *mp))
            new_C.append(C_ctx)
            new_n.append(n_ctx)
            new_m.append(m_ctx)
        else:
            rp = (rwkv_mu[slot], rwkv_w_rkv[slot], rwkv_w0[slot], rwkv_wA[slot], rwkv_wB[slot], rwkv_a0[slot],
                  rwkv_aA[slot], rwkv_aB[slot], rwkv_gA[slot], rwkv_gB[slot], rwkv_k_k[slot], rwkv_k_a[slot],
                  rwkv_r_k[slot], rwkv_ln_g[slot], rwkv_ln_b[slot], rwkv_w_o[slot])
            zero_s = jnp.zeros((b_ctx, 2, RWKV_HEADS, RWKV_HEAD, RWKV_HEAD), jnp.float32)
            xc, S_ctx = macaron_layer(xc, mod_ctx, *lp, lambda h: rwkv_mixer(h, zero_s, *rp))
            xl, _ = macaron_layer(xl, mod_lat, *lp, lambda h: rwkv_mixer(h, state_rwkv[:, slot], *rp))
            new_S.append(S_ctx)
    new_cache_k = jnp.stack(new_k, axis=1)
    new_cache_v = jnp.stack(new_v, axis=1)
    new_state_mlstm_C = jnp.stack(new_C, axis=1)
    new_state_mlstm_n = jnp.stack(new_n, axis=1)
    new_state_mlstm_m = jnp.stack(new_m, axis=1)
    new_state_rwkv = jnp.stack(new_S, axis=1)
    return (xc, xl, new_cache_k, new_cache_v, new_state_mlstm_C, new_state_mlstm_n, new_state_mlstm_m, new_state_rwkv)
```

```python
import contextlib
import numpy as np
import concourse.bass as bass
import concourse.mybir as mybir
from concourse.bass_utils import run_bass_kernel_spmd

F32 = mybir.dt.float32
BF16 = mybir.dt.bfloat16
AF = mybir.ActivationFunctionType
ALU = mybir.AluOpType
AX = mybir.AxisListType

D = 2048
KC = 16
NTOK = 1536
NSEG = 6
SEG = 256
NGRP = 3
TT = 512
DEPTH = 4
DFF = 5632
NMOD = 9
EPS = 1e-6
NCORES = 8
NAP = 147 + 2 * NTOK
NMP = 801 + D
NRP = 13 * KC + 64 + 1 + 128


_UNIQ = [0]


def _sbuf(nc, name, shape, dt):
    _UNIQ[0] += 1
    return nc.sbuf_tensor(f"{name}_u{_UNIQ[0]}", shape, dt)


class Buf:
    __slots__ = ("w", "r", "name", "excl", "mw")

    def __init__(self, name="", excl=False):
        self.w = None
        self.r = {}
        self.mw = {}
        self.name = name
        self.excl = excl


class Eng:
    def __init__(self, k, name, handle, is_pe=False):
        self.k = k
        self.name = name
        self.h = handle
        self.is_pe = is_pe
        self.sem = None
        self.count = 0
        self.waited = {}
        self.nsem = 0

    def cur_sem(self):
        if self.sem is None or self.count >= 30000:
            self.sem = self.k.new_sem(f"{self.name}{self.nsem}")
            self.nsem += 1
            self.count = 0
        return self.sem


class K:
    def __init__(self, nc, es):
        self.nc = nc
        self.es = es
        self.semcount = 0
        self.pe = Eng(self, "pe", nc.tensor, is_pe=True)
        self.act = Eng(self, "act", nc.scalar)
        self.dve = Eng(self, "dve", nc.vector)
        self.pool = Eng(self, "pool", nc.gpsimd)
        self.sp = Eng(self, "sp", nc.sync)
        self.dma_sems = []
        self.dma_rr = 0
        self.sync_same_engine = True
        self.n_inst = 0

    def new_sem(self, name):
        self.semcount += 1
        return self.es.enter_context(self.nc.semaphore(f"s_{name}_{self.semcount}"))

    def sb(self, name, shape, dt):
        return self.es.enter_context(_sbuf(self.nc, name, list(shape), dt))

    def _wait_deps(self, eng, reads, writes, appends=()):
        deps = {}
        for b in appends:
            if b.w is not None:
                s, v = b.w
                if deps.get(s, 0) < v:
                    deps[s] = v
            for s, v in b.r.items():
                if deps.get(s, 0) < v:
                    deps[s] = v
        for b in reads:
            if b.w is not None:
                s, v = b.w
                if deps.get(s, 0) < v:
                    deps[s] = v
            for s, v in b.mw.items():
                if deps.get(s, 0) < v:
                    deps[s] = v
            if b.excl:
                for s, v in b.r.items():
                    if deps.get(s, 0) < v:
                        deps[s] = v
        for b in writes:
            if b.w is not None:
                s, v = b.w
                if deps.get(s, 0) < v:
                    deps[s] = v
            for s, v in b.r.items():
                if deps.get(s, 0) < v:
                    deps[s] = v
            for s, v in b.mw.items():
                if deps.get(s, 0) < v:
                    deps[s] = v
        for s, v in deps.items():
            if eng.is_pe and s is eng.sem:
                continue
            if (not self.sync_same_engine) and s is eng.sem:
                continue
            if eng.waited.get(s, 0) < v:
                eng.h.wait_ge(s, v)
                eng.waited[s] = v

    def _mark(self, tok, reads, writes, appends=()):
        s, v = tok
        for b in reads:
            if b.r.get(s, 0) < v:
                b.r[s] = v
        for b in writes:
            b.w = tok
            b.r = {}
            b.mw = {}
        for b in appends:
            if b.mw.get(s, 0) < v:
                b.mw[s] = v

    def op(self, eng, fn, reads=(), writes=()):
        self._wait_deps(eng, reads, writes)
        sem = eng.cur_sem()
        ins = fn()
        ins.then_inc(sem, 1)
        eng.count += 1
        self.n_inst += 1
        self._mark((sem, eng.count), reads, writes)

    def dma(self, eng, out, in_, reads=(), writes=(), appends=(), **kw):
        if len(self.dma_sems) < 24:
            self.dma_sems.append([self.new_sem(f"dma{len(self.dma_sems)}"), 0])
            ent = self.dma_sems[-1]
        else:
            ent = self.dma_sems[self.dma_rr % len(self.dma_sems)]
            self.dma_rr += 1
        sem, cnt = ent
        if cnt >= 1800:
            ent[0] = sem = self.new_sem("dmax")
            ent[1] = cnt = 0
        self._wait_deps(eng, reads, writes, appends)
        if cnt > 0 and eng.waited.get(sem, 0) < cnt * 16:
            eng.h.wait_ge(sem, cnt * 16)
            eng.waited[sem] = cnt * 16
        eng.h.dma_start(out=out, in_=in_, **kw).then_inc(sem, 16)
        ent[1] = cnt + 1
        self.n_inst += 1
        self._mark((sem, (cnt + 1) * 16), reads, writes, appends)

    def wait_all(self, eng, bufs):
        self._wait_deps(eng, bufs, ())


def pack_layout():
    lay = {}
    off = 0

    def add(name, width):
        nonlocal off
        lay[name] = (off, width)
        off += width

    add("cond", KC * NGRP)
    add("modb", DEPTH * NMOD * KC)
    add("normg", DEPTH * 3 * KC)
    lay["_total"] = off
    return lay


def core_segments(core):
    if core < 4:
        return [("lat", core)] * 4 + [("ctx", 2 * core), ("ctx", 2 * core + 1)]
    base = 8 + (core - 4) * 6
    return [("ctx", base + j) for j in range(6)]


def fm(vec):
    v = np.asarray(vec, np.float32)
    lead = v.shape[:-1]
    v = v.reshape(lead + (KC, 128))
    v = np.moveaxis(v, -1, 0)
    return np.ascontiguousarray(v)


def host_pack(core, inp):
    lay = pack_layout()
    P = np.zeros((128, lay["_total"]), np.float32)

    def put(name, arr):
        o, w = lay[name]
        a = np.asarray(arr, np.float32).reshape(128, -1)
        assert a.shape[1] == w, (name, a.shape, w)
        P[:, o:o + w] = a

    segs = core_segments(core)
    conds = []
    for g in range(NGRP):
        kind, idx = segs[2 * g]
        conds.append(inp["c"][idx] if kind == "lat" else inp["c_ctx"])
    cond = fm(np.stack(conds, 0))
    put("cond", np.transpose(cond, (0, 2, 1)))
    put("modb", fm(inp["mod_b"].reshape(DEPTH, NMOD, D)))
    put("normg", fm(inp["norm_g"]))
    return P


class Prog:
    def __init__(self, cfg):
        self.cfg = cfg
        self.lay = pack_layout()

    def build(self):
        cfg = self.cfg
        nc = bass.Bass("TRN2", target_bir_lowering=False)
        self.nc = nc
        es = contextlib.ExitStack()
        with es:
            k = K(nc, es)
            self.k = k
            self.declare_io()
            self.alloc_common()
            self.load_common()
            self.adaln_all()
            for l in cfg["layers"]:
                self.layer(l)
            self.finish()
        return nc

    def declare_io(self):
        nc = self.nc
        NL = len(self.cfg["layers"])
        self.lidx = {l: i for i, l in enumerate(self.cfg["layers"])}
        self.in_shapes = {
            "xT": [D, NTOK], "pack": [128, self.lay["_total"]], "mod_w": [NL, D, NMOD * D],
            "ffn_w_in": [NL, 2, D, 2 * DFF], "ffn_w_out": [NL, 2, DFF, D],
            "attn_w_qkv": [2, D, 3072], "attn_w_o": [2, D, D], "apack": [2, 128, NAP],
            "mlstm_w_in": [1, D, 6144], "mlstm_w_gate": [1, D, 32], "mlstm_w_o": [1, D, D], "mpack": [1, 128, NMP],
            "ml_initC": [2, 128, 8 * 257], "ml_initm": [2, 128, 8],
            "rpack": [128, NRP], "rmask": [128, 2 * NTOK], "rln": [128, 2 * D], "rk_initS": [2, 128, 1024],
            "rwkv_w_rkv": [1, 3, D, D], "rwkv_wA": [1, 2, D, 96], "rwkv_wB": [1, 2, 96, D], "rwkv_aA": [1, 2, D, 96],
            "rwkv_aB": [1, 2, 96, D], "rwkv_gA": [1, D, 256], "rwkv_gB": [1, 256, D], "rwkv_w_o": [1, D, D],
            "amask": [128, 14 * 128], "permT": [128, 128], "cache_k": [2, 256, 512], "cache_v": [2, 256, 512],
        }
        self.out_shapes = {"rk_stS": [6, 2, 128, 1024], "ml_stC": [6, 2, 128, 8 * 257], "ml_stm": [6, 2, 128, 8], "yT": [D, NTOK], "newk": [2, NTOK, 512], "newv": [2, NTOK, 512]}
        self.inputs = {}
        self.outputs = {}

    def I(self, name):
        if name not in self.inputs:
            self.inputs[name] = self.nc.dram_tensor(name, list(self.in_shapes[name]), F32, kind="ExternalInput").ap()
        return self.inputs[name]

    def dbg_out(self, name, shape):
        if name not in self.outputs:
            self.outputs[name] = self.nc.dram_tensor(name, list(shape), BF16, kind="ExternalOutput").ap()
        return self.outputs[name]

    def O(self, name):
        if name not in self.outputs:
            self.outputs[name] = self.nc.dram_tensor(name, list(self.out_shapes[name]), F32, kind="ExternalOutput").ap()
        return self.outputs[name]

    def alloc_common(self):
        k = self.k
        self.x_sb = k.sb("x_sb", [128, KC, NTOK], F32)
        self.xb = [[Buf(f"x{c}_{t}") for t in range(NGRP)] for c in range(KC)]
        self.pack = None
        self.pack_b = Buf("pack")
        self.modT = k.sb("modT", [128, DEPTH, NMOD, KC, NGRP], F32)
        self.modT_b = Buf("modT")
        self.ones_bf = k.sb("ones_bf", [128, 128], BF16)
        self.ones_b = Buf("ones")
        self.ps = [self.k.es.enter_context(self.nc.psum_tensor(f"ps{i}", [128, 512], F32)) for i in range(8)]
        self.psb = [Buf(f"ps{i}", excl=True) for i in range(8)]
        self.ps_rr = 0
        self.ps_reserved = set()

    def next_ps(self):
        while True:
            i = self.ps_rr % 8
            self.ps_rr += 1
            if i not in self.ps_reserved:
                return self.ps[i], self.psb[i]

    def reserve_ps(self):
        pt, pb = self.next_ps()
        i = self.ps.index(pt)
        self.ps_reserved.add(i)
        return pt, pb

    def release_ps(self, pt):
        self.ps_reserved.discard(self.ps.index(pt))

    def pk(self, name):
        o, w = self.lay[name]
        return self.pack[:, o:o + w]

    def load_common(self):
        k = self.k
        nc = self.nc
        xv = self.I("xT").rearrange("(c p) t -> p c t", p=128)
        for c in range(KC):
            k.dma(k.sp, self.x_sb[:, c, :], xv[:, c, :], writes=self.xb[c])
        k.op(k.dve, lambda: nc.vector.memset(self.ones_bf[:], 1.0), writes=[self.ones_b])

    def adaln_all(self):
        k = self.k
        nc = self.nc
        cfg = self.cfg
        with contextlib.ExitStack() as es2:
            self.pack = es2.enter_context(_sbuf(nc, "pack_sb", [128, self.lay["_total"]], F32))
            k.dma(k.sp, self.pack[:], self.I("pack")[:, :], writes=[self.pack_b])
            sc = es2.enter_context(_sbuf(nc, "ada_sc", [128, KC, NGRP], BF16))
            sc_b = Buf("sc")
            wts = [es2.enter_context(_sbuf(nc, f"ada_w{i}", [128, KC, 512], BF16)) for i in range(3)]
            wts_b = [Buf(f"adaw{i}") for i in range(3)]
            condv = self.pk("cond").rearrange("p (c g) -> p c g", g=NGRP)
            k.op(k.act, lambda: nc.scalar.activation(out=sc[:], in_=condv, func=AF.Silu),
                 reads=[self.pack_b], writes=[sc_b])
            modb = self.pk("modb").rearrange("p (l j c) -> p l j c", l=DEPTH, j=NMOD)
            NCT = NMOD * D // 512
            jobs = [(l, ct) for l in cfg["layers"] for ct in range(NCT)]

            def load(i):
                l, ct = jobs[i]
                src = self.I("mod_w")[self.lidx[l]].rearrange("(kc p) n -> p kc n", p=128)[:, :, ct * 512:(ct + 1) * 512]
                k.dma(k.pool, wts[i % 3][:], src, writes=[wts_b[i % 3]])

            for i in range(min(2, len(jobs))):
                load(i)
            for i, (l, ct) in enumerate(jobs):
                if i + 2 < len(jobs):
                    load(i + 2)
                w = wts[i % 3]
                wb = wts_b[i % 3]
                pst, psb = self.next_ps()
                for q in range(4):
                    for kc in range(KC):
                        k.op(k.pe, lambda q=q, kc=kc: nc.tensor.matmul(
                            pst[:, q * NGRP:(q + 1) * NGRP], w[:, kc, q * 128:(q + 1) * 128], sc[:, kc, :],
                            start=(kc == 0), stop=(kc == KC - 1)),
                            reads=[wb, sc_b], writes=[psb])
                j = (ct * 4) // KC
                c0 = (ct * 4) % KC
                k.op(k.dve, lambda l=l, j=j, c0=c0: nc.vector.tensor_tensor(
                    out=self.modT[:, l, j, c0:c0 + 4, :],
                    in0=pst[:, 0:4 * NGRP].rearrange("p (q g) -> p q g", g=NGRP),
                    in1=modb[:, l, j, c0:c0 + 4].unsqueeze(2).to_broadcast([128, 4, NGRP]),
                    op=ALU.add),
                    reads=[psb, self.pack_b], writes=[self.modT_b])
            ng = self.pk("normg").rearrange("p (l s c) -> p l s c", l=DEPTH, s=3)
            for l in cfg["layers"]:
                for s in range(3):
                    k.op(k.dve, lambda l=l, s=s: nc.vector.scalar_tensor_tensor(
                        out=self.modT[:, l, 3 * s + 1, :, :], in0=self.modT[:, l, 3 * s + 1, :, :], scalar=1.0,
                        in1=ng[:, l, s, :].unsqueeze(2).to_broadcast([128, KC, NGRP]),
                        op0=ALU.add, op1=ALU.mult),
                        reads=[self.pack_b, self.modT_b], writes=[self.modT_b])
                    if s != 1:
                        k.op(k.dve, lambda l=l, s=s: nc.vector.tensor_scalar(
                            out=self.modT[:, l, 3 * s + 2, :, :], in0=self.modT[:, l, 3 * s + 2, :, :],
                            scalar1=0.5, scalar2=None, op0=ALU.mult),
                            reads=[self.modT_b], writes=[self.modT_b])
            self.phase_barrier()

    def compute_h(self, l, s, h, hb, sq, sqb):
        k = self.k
        nc = self.nc
        for t in range(NGRP):
            ts = slice(t * TT, (t + 1) * TT)
            pst, psb = self.next_ps()
            for c in range(KC):
                i = c % 2
                k.op(k.act, lambda c=c, i=i: nc.scalar.activation(out=sq[i], in_=self.x_sb[:, c, ts], func=AF.Square),
                     reads=[self.xb[c][t]], writes=[sqb[i]])
                k.op(k.pe, lambda c=c, i=i: nc.tensor.matmul(pst[:], self.ones_bf[:], sq[i],
                                                             start=(c == 0), stop=(c == KC - 1)),
                     reads=[sqb[i], self.ones_b], writes=[psb])
            k.op(k.dve, lambda: nc.vector.tensor_scalar(out=pst[:], in0=pst[:], scalar1=1.0 / D, scalar2=EPS,
                                                        op0=ALU.mult, op1=ALU.add),
                 reads=[psb], writes=[psb])
            k.op(k.act, lambda: nc.scalar.activation(out=pst[:], in_=pst[:], func=AF.Sqrt), reads=[psb], writes=[psb])
            k.op(k.dve, lambda: nc.vector.reciprocal(out=pst[:], in_=pst[:]), reads=[psb], writes=[psb])
            for c in range(KC):
                pt, ptb = self.next_ps()
                if pt is pst:
                    pt, ptb = self.next_ps()
                k.op(k.dve, lambda c=c, pt=pt: nc.vector.tensor_tensor(out=pt[:], in0=self.x_sb[:, c, ts], in1=pst[:],
                                                                       op=ALU.mult),
                     reads=[self.xb[c][t], psb], writes=[ptb])
                k.op(k.act, lambda c=c, pt=pt: nc.scalar.activation(
                    out=h[:, c, ts], in_=pt[:], func=AF.Identity,
                    scale=self.modT[:, l, 3 * s + 1, c, t:t + 1], bias=self.modT[:, l, 3 * s, c, t:t + 1]),
                    reads=[ptb, self.modT_b], writes=[hb[t]])

    def ffn(self, l, j):
        k = self.k
        nc = self.nc
        s = 0 if j == 0 else 2
        FG = 256
        NFG = DFF // FG
        with contextlib.ExitStack() as es2:
            sbt = lambda name, shape, dt: es2.enter_context(_sbuf(nc, name, list(shape), dt))
            h = sbt("ffn_h", [128, KC, NTOK], BF16)
            hb = [Buf(f"h{t}") for t in range(NGRP)]
            tmpn = [sbt(f"ffn_tmpn{i}", [128, TT], BF16) for i in range(2)]
            tmpnb = [Buf() for _ in range(2)]
            wg = [sbt(f"ffn_wg{i}", [128, KC, FG], BF16) for i in range(2)]
            wu = [sbt(f"ffn_wu{i}", [128, KC, FG], BF16) for i in range(2)]
            wo = [sbt(f"ffn_wo{i}", [128, FG // 128, D], BF16) for i in range(2)]
            wgb = [Buf() for _ in range(2)]
            wub = [Buf() for _ in range(2)]
            wob = [Buf() for _ in range(2)]
            gsb = [sbt(f"ffn_g{i}", [128, FG // 128, TT], BF16) for i in range(2)]
            gsbb = [Buf() for _ in range(2)]
            sq = [gsb[i][:, 0, :] for i in range(2)]
            sqb = gsbb
            self.phase_barrier()

            w_in_v = self.I("ffn_w_in")[self.lidx[l], j].rearrange("(kc p) n -> p kc n", p=128)
            w_out_v = self.I("ffn_w_out")[self.lidx[l], j].rearrange("(fc p) n -> p fc n", p=128)

            def load(fg):
                i = fg % 2
                k.dma(k.pool, wg[i][:], w_in_v[:, :, fg * FG:(fg + 1) * FG], writes=[wgb[i]])
                k.dma(k.pool, wu[i][:], w_in_v[:, :, DFF + fg * FG:DFF + (fg + 1) * FG], writes=[wub[i]])
                k.dma(k.pool, wo[i][:], w_out_v[:, fg * (FG // 128):(fg + 1) * (FG // 128), :], writes=[wob[i]])

            load(0)
            self.compute_h(l, s, h, hb, sq, sqb)
            it = 0
            for fg in range(NFG):
                if fg + 1 < NFG:
                    load(fg + 1)
                i = fg % 2
                for t in range(NGRP):
                    ts = slice(t * TT, (t + 1) * TT)
                    gi = it % 2
                    it += 1
                    for hf in range(FG // 128):
                        pg, pgb = self.next_ps()
                        pu, pub = self.next_ps()
                        for kc in range(KC):
                            k.op(k.pe, lambda kc=kc, hf=hf, pg=pg: nc.tensor.matmul(
                                pg[:], wg[i][:, kc, hf * 128:(hf + 1) * 128], h[:, kc, ts],
                                start=(kc == 0), stop=(kc == KC - 1)),
                                reads=[wgb[i], hb[t]], writes=[pgb])
                        for kc in range(KC):
                            k.op(k.pe, lambda kc=kc, hf=hf, pu=pu: nc.tensor.matmul(
                                pu[:], wu[i][:, kc, hf * 128:(hf + 1) * 128], h[:, kc, ts],
                                start=(kc == 0), stop=(kc == KC - 1)),
                                reads=[wub[i], hb[t]], writes=[pub])
                        si = hf % 2
                        k.op(k.act, lambda pg=pg, si=si: nc.scalar.activation(out=tmpn[si][:], in_=pg[:], func=AF.Silu),
                             reads=[pgb], writes=[tmpnb[si]])
                        k.op(k.dve, lambda pu=pu, si=si, hf=hf, gi=gi: nc.vector.tensor_tensor(
                            out=gsb[gi][:, hf, :], in0=tmpn[si][:], in1=pu[:], op=ALU.mult),
                            reads=[tmpnb[si], pub], writes=[gsbb[gi]])
                    for dc in range(KC):
                        py, pyb = self.next_ps()
                        for hf in range(FG // 128):
                            k.op(k.pe, lambda hf=hf, dc=dc, py=py, gi=gi: nc.tensor.matmul(
                                py[:], wo[i][:, hf, dc * 128:(dc + 1) * 128], gsb[gi][:, hf, :],
                                start=(hf == 0), stop=(hf == FG // 128 - 1)),
                                reads=[wob[i], gsbb[gi]], writes=[pyb])
                        k.op(k.dve, lambda dc=dc, py=py, t=t, ts=ts: nc.vector.scalar_tensor_tensor(
                            out=self.x_sb[:, dc, ts], in0=py[:], scalar=self.modT[:, l, 3 * s + 2, dc, t:t + 1],
                            in1=self.x_sb[:, dc, ts], op0=ALU.mult, op1=ALU.add),
                            reads=[pyb, self.modT_b, self.xb[dc][t]], writes=[self.xb[dc][t]])
            self.phase_barrier()

    def phase_barrier(self):
        k = self.k
        toks = []
        for e in (k.pe, k.act, k.dve, k.pool, k.sp):
            if e.sem is not None and e.count > 0:
                toks.append((e.sem, e.count))
        for ent in k.dma_sems:
            if ent[1] > 0:
                toks.append((ent[0], ent[1] * 16))
        for e in (k.pe, k.act, k.dve, k.pool, k.sp):
            for s, v in toks:
                if s is e.sem:
                    continue
                if e.waited.get(s, 0) < v:
                    e.h.wait_ge(s, v)
                    e.waited[s] = v


    def linear_fm(self, h, hb, wview, col0, ncols, evac, tag):
        k, nc = self.k, self.nc
        TC = 256
        with contextlib.ExitStack() as es2:
            wt = [es2.enter_context(_sbuf(nc, f"lf_{tag}_w{i}", [128, KC, TC], BF16)) for i in range(2)]
            wtb = [Buf() for _ in range(2)]
            ntile = ncols // TC

            def load(ti):
                k.dma(k.pool, wt[ti % 2][:], wview[:, :, col0 + ti * TC:col0 + (ti + 1) * TC], writes=[wtb[ti % 2]])

            load(0)
            for ti in range(ntile):
                if ti + 1 < ntile:
                    load(ti + 1)
                w, wb = wt[ti % 2], wtb[ti % 2]
                for sub in range(TC // 128):
                    oc = ti * (TC // 128) + sub
                    for t in range(NGRP):
                        ts = slice(t * TT, (t + 1) * TT)
                        ps, psb = self.next_ps()
                        for kc in range(KC):
                            k.op(k.pe, lambda kc=kc: nc.tensor.matmul(ps[:], w[:, kc, sub * 128:(sub + 1) * 128], h[:, kc, ts],
                                                                      start=(kc == 0), stop=(kc == KC - 1)),
                                 reads=[wb, hb[t]], writes=[psb])
                        evac(oc, t, ps, psb)
            self.scope_end(wtb)

    def linear_tm(self, h, hb, wview, col0, ncols, evac, tag):
        k, nc = self.k, self.nc
        TC = 512
        with contextlib.ExitStack() as es2:
            wt = [es2.enter_context(_sbuf(nc, f"lt_{tag}_w{i}", [128, KC, TC], BF16)) for i in range(2)]
            wtb = [Buf() for _ in range(2)]
            ntile = ncols // TC

            def load(ti):
                k.dma(k.pool, wt[ti % 2][:], wview[:, :, col0 + ti * TC:col0 + (ti + 1) * TC], writes=[wtb[ti % 2]])

            load(0)
            for ti in range(ntile):
                if ti + 1 < ntile:
                    load(ti + 1)
                w, wb = wt[ti % 2], wtb[ti % 2]
                for blk in range(NTOK // 128):
                    ps, psb = self.next_ps()
                    for kc in range(KC):
                        k.op(k.pe, lambda kc=kc: nc.tensor.matmul(ps[:], h[:, kc, blk * 128:(blk + 1) * 128], w[:, kc, :],
                                                                  start=(kc == 0), stop=(kc == KC - 1)),
                             reads=[wb, hb[blk // 4]], writes=[psb])
                    evac(blk, ti, ps, psb)
            self.scope_end(wtb)

    def scope_end(self, bufs):
        k = self.k
        for e in (k.pe, k.act, k.dve, k.pool, k.sp):
            k._wait_deps(e, (), bufs)

    def attn(self, l):
        k = self.k
        nc = self.nc
        slot = l // 3
        NH, NKV = 16, 4
        qs = nc.dram_tensor(f"qs{l}", [NH, 128, NTOK], BF16).ap()
        ks = nc.dram_tensor(f"ks{l}", [NKV, 128, NTOK], BF16).ap()
        vs = nc.dram_tensor(f"vs{l}", [12, 128, 512], BF16).ap()
        qs_b, ks_b, vs_b = Buf("qs"), Buf("ks"), Buf("vs")
        outb = Buf("attn_out")
        with contextlib.ExitStack() as es1:
            sbt1 = lambda name, shape, dt: es1.enter_context(_sbuf(nc, name, list(shape), dt))
            self.phase_barrier()
            apk = sbt1("apack_sb", [128, NAP], F32)
            apk_b = Buf("apk")
            k.dma(k.sp, apk[:], self.I("apack")[slot], writes=[apk_b])
            qng = apk[:, 0:1]
            kng = apk[:, 1:2]
            sink = apk[:, 2:18]
            cflag = apk[:, 18:19]
            ident = apk[:, 19:147]
            cos = apk[:, 147:147 + NTOK]
            sin = apk[:, 147 + NTOK:147 + 2 * NTOK]
            with contextlib.ExitStack() as es2:
                sbt = lambda name, shape, dt: es2.enter_context(_sbuf(nc, name, list(shape), dt))
                h = sbt("at_h", [128, KC, NTOK], BF16)
                hb = [Buf(f"h{t}") for t in range(NGRP)]
                sq = [sbt(f"at_sq{i}", [128, TT], BF16) for i in range(2)]
                sqb = [Buf() for _ in range(2)]
                PT = sbt("at_PT", [128, 128], BF16)
                PT_b = Buf()
                k.dma(k.pool, PT[:], self.I("permT")[:, :], writes=[PT_b])
                wq = [sbt(f"at_wq{i}", [128, KC, 256], BF16) for i in range(2)]
                wqb = [Buf() for _ in range(2)]
                rawg = [sbt(f"at_rawg{i}", [128, TT], F32) for i in range(2)]
                rawgb = [Buf() for _ in range(2)]
                qnf = [sbt(f"at_qnf{i}", [128, TT], F32) for i in range(2)]
                qnfb = [Buf() for _ in range(2)]
                qnb = [sbt(f"at_qnb{i}", [128, TT], BF16) for i in range(2)]
                qnbb = [Buf() for _ in range(2)]
                t1 = [sbt(f"at_t1{i}", [128, TT], F32) for i in range(2)]
                t1b = [Buf() for _ in range(2)]
                qrb = [sbt(f"at_qrb{i}", [128, TT], BF16) for i in range(2)]
                qrbb = [Buf() for _ in range(2)]
                kout = [sbt(f"at_kout{i}", [128, 4, 128], F32) for i in range(2)]
                koutb = [Buf() for _ in range(2)]
                vf = [sbt(f"at_vf{i}", [128, 256], F32) for i in range(2)]
                vfb = [Buf() for _ in range(2)]
                vb = [sbt(f"at_vb{i}", [128, 256], BF16) for i in range(2)]
                vbb = [Buf() for _ in range(2)]
                wv_ = self.I("attn_w_qkv")[slot].rearrange("(kc p) n -> p kc n", p=128)

                def load(ct):
                    k.dma(k.pool, wq[ct % 2][:], wv_[:, :, ct * 256:(ct + 1) * 256], writes=[wqb[ct % 2]])

                load(0)
                self.compute_h(l, 1, h, hb, [t_[:] for t_ in sq], sqb)
                it = 0
                ct_list = self.cfg.get("ct_list", list(range(12)))
                for ct in ct_list:
                    if ct + 1 < 12 and (ct + 1) in ct_list:
                        load(ct + 1)
                    w = wq[ct % 2]
                    wb = wqb[ct % 2]
                    if ct < 10:
                        for hh in range(2):
                            head = ct * 2 + hh
                            is_k = head >= 16
                            gain = kng if is_k else qng
                            for t in range(NGRP):
                                ts = slice(t * TT, (t + 1) * TT)
                                i = it % 2
                                it += 1
                                praw, prawb = self.next_ps()
                                for kc in range(KC):
                                    k.op(k.pe, lambda kc=kc: nc.tensor.matmul(
                                        praw[:], w[:, kc, hh * 128:(hh + 1) * 128], h[:, kc, ts],
                                        start=(kc == 0), stop=(kc == KC - 1)), reads=[wb, hb[t]], writes=[prawb])
                                k.op(k.act, lambda: nc.scalar.activation(out=sq[i][:], in_=praw[:], func=AF.Square),
                                     reads=[prawb], writes=[sqb[i]])
                                k.op(k.act, lambda: nc.scalar.activation(out=rawg[i][:], in_=praw[:], func=AF.Identity, scale=gain),
                                     reads=[prawb, apk_b], writes=[rawgb[i]])
                                pss, pssb = self.next_ps()
                                k.op(k.pe, lambda: nc.tensor.matmul(pss[:], self.ones_bf[:], sq[i][:], start=True, stop=True),
                                     reads=[sqb[i], self.ones_b], writes=[pssb])
                                k.op(k.dve, lambda: nc.vector.tensor_scalar(out=pss[:], in0=pss[:], scalar1=1.0 / 128, scalar2=EPS,
                                                                            op0=ALU.mult, op1=ALU.add), reads=[pssb], writes=[pssb])
                                k.op(k.act, lambda: nc.scalar.activation(out=pss[:], in_=pss[:], func=AF.Sqrt), reads=[pssb], writes=[pssb])
                                k.op(k.dve, lambda: nc.vector.reciprocal(out=pss[:], in_=pss[:]), reads=[pssb], writes=[pssb])
                                k.op(k.dve, lambda: nc.vector.tensor_tensor(out=qnf[i][:], in0=rawg[i][:], in1=pss[:], op=ALU.mult),
                                     reads=[rawgb[i], pssb], writes=[qnfb[i]])
                                k.op(k.act, lambda: nc.scalar.copy(out=qnb[i][:], in_=qnf[i][:]), reads=[qnfb[i]], writes=[qnbb[i]])
                                pp, ppb = self.next_ps()
                                k.op(k.pe, lambda: nc.tensor.matmul(pp[:], PT[:], qnb[i][:], start=True, stop=True),
                                     reads=[PT_b, qnbb[i]], writes=[ppb])
                                k.op(k.dve, lambda: nc.vector.tensor_tensor(out=t1[i][:], in0=qnf[i][:], in1=cos[:, ts], op=ALU.mult),
                                     reads=[qnfb[i], apk_b], writes=[t1b[i]])
                                k.op(k.dve, lambda: nc.vector.tensor_tensor(out=qnf[i][:], in0=pp[:], in1=sin[:, ts], op=ALU.mult),
                                     reads=[ppb, apk_b, qnbb[i]], writes=[qnfb[i]])
                                if not is_k:
                                    k.op(k.dve, lambda: nc.vector.tensor_tensor(out=qrb[i][:], in0=t1[i][:], in1=qnf[i][:], op=ALU.add),
                                         reads=[t1b[i], qnfb[i]], writes=[qrbb[i]])
                                    if not self.cfg.get("no_scratch"):
                                        k.dma(k.sp, qs[head, :, ts], qrb[i][:], reads=[qrbb[i]], writes=[qs_b])
                                else:
                                    kvh = head - 16
                                    k.op(k.dve, lambda: nc.vector.tensor_tensor(out=t1[i][:], in0=t1[i][:], in1=qnf[i][:], op=ALU.add),
                                         reads=[t1b[i], qnfb[i]], writes=[t1b[i]])
                                    k.op(k.act, lambda: nc.scalar.copy(out=qrb[i][:], in_=t1[i][:]), reads=[t1b[i]], writes=[qrbb[i]])
                                    if not self.cfg.get("no_scratch"):
                                        k.dma(k.sp, ks[kvh, :, ts], qrb[i][:], reads=[qrbb[i]], writes=[ks_b])
                                    if self.cfg.get("no_tr"):
                                        continue
                                    ptr, ptrb = self.next_ps()
                                    for b4 in range(4):
                                        k.op(k.pe, lambda b4=b4: nc.tensor.transpose(
                                            ptr[:, b4 * 128:(b4 + 1) * 128], t1[i][:, b4 * 128:(b4 + 1) * 128], ident),
                                            reads=[t1b[i], apk_b], writes=[ptrb])
                                    k.op(k.act, lambda: nc.scalar.copy(out=kout[i][:].rearrange("p b d -> p (b d)"), in_=ptr[:]),
                                         reads=[ptrb], writes=[koutb[i]])
                                    dst = self.O("newk")[slot, t * TT:(t + 1) * TT, kvh * 128:(kvh + 1) * 128].rearrange(
                                        "(b p) d -> p b d", p=128)
                                    k.dma(k.sp, dst, kout[i][:], reads=[koutb[i]], writes=[outb])
                    else:
                        c0 = (ct - 10) * 256
                        for blk in range(12):
                            i = it % 2
                            it += 1
                            pv, pvb = self.next_ps()
                            for kc in range(KC):
                                k.op(k.pe, lambda kc=kc: nc.tensor.matmul(
                                    pv[:, 0:256], h[:, kc, blk * 128:(blk + 1) * 128], w[:, kc, :],
                                    start=(kc == 0), stop=(kc == KC - 1)), reads=[wb, hb[blk // 4]], writes=[pvb])
                            k.op(k.act, lambda: nc.scalar.copy(out=vf[i][:], in_=pv[:, 0:256]), reads=[pvb], writes=[vfb[i]])
                            k.op(k.dve, lambda: nc.vector.tensor_copy(out=vb[i][:], in_=pv[:, 0:256]), reads=[pvb], writes=[vbb[i]])
                            if not self.cfg.get("no_newv"):
                                k.dma(k.sp, self.O("newv")[slot, blk * 128:(blk + 1) * 128, c0:c0 + 256], vf[i][:],
                                      reads=[vfb[i]], writes=[outb])
                            if not self.cfg.get("no_scratch"):
                                k.dma(k.sp, vs[blk, :, c0:c0 + 256], vb[i][:], reads=[vbb[i]], writes=[vs_b])
                self.phase_barrier()
            if self.cfg.get("attn_phase", "AB") == "A":
                return
            with contextlib.ExitStack() as es2:
                sbt = lambda name, shape, dt: es2.enter_context(_sbuf(nc, name, list(shape), dt))
                masks = sbt("at_masks", [128, 14, 128], BF16)
                masks_b = Buf()
                k.dma(k.pool, masks[:], self.I("amask").rearrange("p (m q) -> p m q", q=128), writes=[masks_b])
                ckf = sbt("at_ckf", [128, 2, 512], F32)
                ckf_b = Buf()
                k.dma(k.sp, ckf[:], self.I("cache_k")[slot].rearrange("(b p) d -> p b d", p=128), writes=[ckf_b])
                vc = sbt("at_vc", [128, 2, 512], BF16)
                vc_b = Buf()
                k.dma(k.pool, vc[:], self.I("cache_v")[slot].rearrange("(b p) d -> p b d", p=128), writes=[vc_b])
                kTc = sbt("at_kTc", [128, 4, 256], BF16)
                kTc_b = Buf()
                for cb in range(2):
                    ptr, ptrb = self.next_ps()
                    for kvh in range(4):
                        k.op(k.pe, lambda kvh=kvh: nc.tensor.transpose(
                            ptr[:, kvh * 128:(kvh + 1) * 128], ckf[:, cb, kvh * 128:(kvh + 1) * 128], ident),
                            reads=[ckf_b, apk_b], writes=[ptrb])
                    k.op(k.act, lambda: nc.scalar.copy(out=kTc[:, :, cb * 128:(cb + 1) * 128],
                                                       in_=ptr[:].rearrange("p (h s) -> p h s", s=128)),
                         reads=[ptrb], writes=[kTc_b])
                esink = sbt("at_esink", [128, 16], F32)
                esink_b = Buf()
                k.op(k.act, lambda: nc.scalar.activation(out=esink[:], in_=sink, func=AF.Exp), reads=[apk_b], writes=[esink_b])
                qg = [sbt(f"at_qg{i}", [128, 4, NTOK], BF16) for i in range(2)]
                kg = [sbt(f"at_kg{i}", [128, NTOK], BF16) for i in range(2)]
                vg = [sbt(f"at_vg{i}", [128, 12, 128], BF16) for i in range(2)]
                wo1 = sbt("at_wo", [128, 4, D], BF16)
                wo = [wo1, wo1]
                qgb = [Buf() for _ in range(2)]
                kgb = [Buf() for _ in range(2)]
                vgb = [Buf() for _ in range(2)]
                wob1 = Buf()
                wob = [wob1, wob1]
                og = sbt("at_og", [128, 4, NTOK], BF16)
                ogb = [Buf() for _ in range(NGRP)]
                ptile = [sbt(f"at_pt{i}", [128, 512], BF16) for i in range(3)]
                ptileb = [Buf() for _ in range(3)]
                dtmp = sbt("at_dtmp", [128, 512], F32)
                dtmp_b = Buf()
                wo_v = self.I("attn_w_o")[slot].rearrange("(hh p) n -> p hh n", p=128)

                def loadg(g):
                    i = g % 2
                    k.dma(k.sp, qg[i][:], qs[g * 4:(g + 1) * 4].rearrange("h p t -> p h t"), reads=[qs_b], writes=[qgb[i]])
                    k.dma(k.sp, kg[i][:], ks[g], reads=[ks_b], writes=[kgb[i]])
                    k.dma(k.sp, vg[i][:], vs[:, :, g * 128:(g + 1) * 128].rearrange("b p d -> p b d"), reads=[vs_b], writes=[vgb[i]])

                def loadwo(g):
                    k.dma(k.pool, wo1[:], wo_v[:, g * 4:(g + 1) * 4, :], writes=[wob1])

                loadg(0)
                scale = 128.0 ** -0.5
                pit = 0
                for g in range(4):
                    if g + 1 < 4:
                        loadg(g + 1)
                    loadwo(g)
                    i = g % 2
                    for qb in range(12):
                        kbs = []
                        if qb < 8:
                            for kb in (qb - 1, qb, qb + 1):
                                if 0 <= kb < 8:
                                    if kb == qb:
                                        midx = None
                                    elif kb == qb - 1:
                                        midx = 2 * (qb - 1)
                                    else:
                                        midx = 2 * qb + 1
                                    kbs.append((kg[i][:, kb * 128:(kb + 1) * 128], kgb[i], vg[i][:, kb, :], vgb[i], midx, False))
                            for cb in range(2):
                                kbs.append((kTc[:, g, cb * 128:(cb + 1) * 128], kTc_b, vc[:, cb, g * 128:(g + 1) * 128], vc_b, None, True))
                        else:
                            sb0 = 8 + 2 * ((qb - 8) // 2)
                            for kb in (sb0, sb0 + 1):
                                kbs.append((kg[i][:, kb * 128:(kb + 1) * 128], kgb[i], vg[i][:, kb, :], vgb[i], None, False))
                        pO, pOb = self.reserve_ps()
                        pD, pDb = self.reserve_ps()
                        for j, (kap, kbuf, vap, vbuf, midx, is_c) in enumerate(kbs):
                            pS, pSb = self.next_ps()
                            k.op(k.pe, lambda: nc.tensor.matmul(pS[:], kap, qg[i][:, :, qb * 128:(qb + 1) * 128], start=True, stop=True),
                                 reads=[kbuf, qgb[i]], writes=[pSb])
                            pi = pit % 3
                            pit += 1
                            k.op(k.act, lambda: nc.scalar.activation(out=ptile[pi][:], in_=pS[:], func=AF.Exp, scale=scale),
                                 reads=[pSb], writes=[ptileb[pi]])
                            if midx is not None:
                                k.op(k.dve, lambda: nc.vector.tensor_tensor(
                                    out=ptile[pi][:].rearrange("p (h q) -> p h q", q=128),
                                    in0=ptile[pi][:].rearrange("p (h q) -> p h q", q=128),
                                    in1=masks[:, midx, :].unsqueeze(1).to_broadcast([128, 4, 128]), op=ALU.mult),
                                    reads=[ptileb[pi], masks_b], writes=[ptileb[pi]])
                            if is_c:
                                k.op(k.dve, lambda: nc.vector.tensor_scalar(out=ptile[pi][:], in0=ptile[pi][:], scalar1=cflag, scalar2=None,
                                                                            op0=ALU.mult), reads=[ptileb[pi], apk_b], writes=[ptileb[pi]])
                            k.op(k.pe, lambda: nc.tensor.matmul(pO[:], vap, ptile[pi][:], start=(j == 0), stop=(j == len(kbs) - 1)),
                                 reads=[vbuf, ptileb[pi]], writes=[pOb])
                            k.op(k.pe, lambda: nc.tensor.matmul(pD[:], self.ones_bf[:], ptile[pi][:], start=(j == 0), stop=(j == len(kbs) - 1)),
                                 reads=[self.ones_b, ptileb[pi]], writes=[pDb])
                        k.op(k.dve, lambda: nc.vector.tensor_tensor(
                            out=dtmp[:].rearrange("p (h q) -> p h q", q=128), in0=pD[:].rearrange("p (h q) -> p h q", q=128),
                            in1=esink[:, g * 4:(g + 1) * 4].unsqueeze(2).to_broadcast([128, 4, 128]), op=ALU.add),
                            reads=[pDb, esink_b], writes=[dtmp_b])
                        k.op(k.dve, lambda: nc.vector.reciprocal(out=dtmp[:], in_=dtmp[:]), reads=[dtmp_b], writes=[dtmp_b])
                        k.op(k.dve, lambda: nc.vector.tensor_tensor(
                            out=og[:, :, qb * 128:(qb + 1) * 128], in0=pO[:].rearrange("p (h q) -> p h q", q=128),
                            in1=dtmp[:].rearrange("p (h q) -> p h q", q=128), op=ALU.mult),
                            reads=[pOb, dtmp_b], writes=[ogb[qb // 4]])
                        self.release_ps(pO)
                        self.release_ps(pD)
                    if self.cfg.get("dbg_attn"):
                            dbo = self.dbg_out("dbg_o", [16, 128, NTOK])
                            k.dma(k.sp, dbo[g * 4:(g + 1) * 4].rearrange("h p t -> p h t"), og[:], reads=ogb, writes=[outb])
                            dbq = self.dbg_out("dbg_q", [16, 128, NTOK])
                            k.dma(k.sp, dbq[g * 4:(g + 1) * 4].rearrange("h p t -> p h t"), qg[i][:], reads=[qgb[i]], writes=[outb])
                    for t in range(NGRP):
                        ts = slice(t * TT, (t + 1) * TT)
                        for dc in range(KC):
                            py, pyb = self.next_ps()
                            for hh in range(4):
                                k.op(k.pe, lambda hh=hh: nc.tensor.matmul(py[:], wo[i][:, hh, dc * 128:(dc + 1) * 128], og[:, hh, ts],
                                                                          start=(hh == 0), stop=(hh == 3)),
                                     reads=[wob[i], ogb[t]], writes=[pyb])
                            k.op(k.dve, lambda: nc.vector.scalar_tensor_tensor(
                                out=self.x_sb[:, dc, ts], in0=py[:], scalar=self.modT[:, l, 5, dc, t:t + 1],
                                in1=self.x_sb[:, dc, ts], op0=ALU.mult, op1=ALU.add),
                                reads=[pyb, self.modT_b, self.xb[dc][t]], writes=[self.xb[dc][t]])
                self.phase_barrier()


    def mlstm(self, l):
        k, nc = self.k, self.nc
        slot = l // 3
        NHm, DKm, DVm, NB = 8, 128, 256, NTOK // 128
        DV1 = DVm + 1
        qT_s = nc.dram_tensor(f"ml_qT{l}", [NHm, 128, NTOK], BF16).ap()
        kT_s = nc.dram_tensor(f"ml_kT{l}", [NHm, 128, NTOK], BF16).ap()
        ktm_s = nc.dram_tensor(f"ml_ktm{l}", [NB, 128, NHm * DKm], BF16).ap()
        vtm_s = nc.dram_tensor(f"ml_vtm{l}", [NB, 128, D], BF16).ap()
        og_s = nc.dram_tensor(f"ml_og{l}", [NB, 128, D], BF16).ap()
        hf_s = nc.dram_tensor(f"ml_hf{l}", [NB, 128, D], F32).ap()
        hsT_s = nc.dram_tensor(f"ml_hsT{l}", [KC, 128, NTOK], BF16).ap()
        scr_b = Buf("ml_scr")
        hf_b = Buf("ml_hf")
        hsT_b = Buf("ml_hsT")
        outb = Buf("ml_out")
        w_in_v = self.I("mlstm_w_in")[slot].rearrange("(kc p) n -> p kc n", p=128)
        with contextlib.ExitStack() as es1:
            sbt1 = lambda name, shape, dt: es1.enter_context(_sbuf(nc, name, list(shape), dt))
            self.phase_barrier()
            mpk = sbt1("ml_mpk", [128, NMP], F32)
            mpk_b = Buf()
            k.dma(k.sp, mpk[:], self.I("mpack")[slot], writes=[mpk_b])
            ident = mpk[:, 0:128]
            onesf = mpk[:, 128:256]
            V1_01, V2_01 = mpk[:, 256:384], mpk[:, 384:512]
            V1b, V2b = mpk[:, 512:640], mpk[:, 640:768]
            bgate = mpk[:, 768:800]
            flag = mpk[:, 800:801]
            onorm = mpk[:, 801:801 + D]
            gates = sbt1("ml_gates", [128, NB, 32], F32)
            gates_b = Buf()
            with contextlib.ExitStack() as es2:
                sbt = lambda name, shape, dt: es2.enter_context(_sbuf(nc, name, list(shape), dt))
                h = sbt("ml_h", [128, KC, NTOK], BF16)
                hb = [Buf(f"h{t}") for t in range(NGRP)]
                sq = [sbt(f"ml_sq{i}", [128, TT], BF16) for i in range(2)]
                sqb = [Buf() for _ in range(2)]
                ev = [sbt(f"ml_ev{i}", [128, TT], BF16) for i in range(3)]
                evb = [Buf() for _ in range(3)]
                wg = sbt("ml_wg", [128, KC, 32], BF16)
                wg_b = Buf()
                k.dma(k.pool, wg[:], self.I("mlstm_w_gate")[slot].rearrange("(kc p) n -> p kc n", p=128), writes=[wg_b])
                self.compute_h(l, 1, h, hb, [t_[:] for t_ in sq], sqb)
                cnt = [0]

                def nxt():
                    cnt[0] += 1
                    return cnt[0] % 3

                def ev_q(oc, t, ps, psb):
                    i = nxt()
                    k.op(k.act, lambda: nc.scalar.activation(out=ev[i][:], in_=ps[:], func=AF.Identity, scale=float(DKm) ** -0.5),
                         reads=[psb], writes=[evb[i]])
                    k.dma(k.sp, qT_s[oc, :, t * TT:(t + 1) * TT], ev[i][:], reads=[evb[i]], writes=[scr_b])

                def ev_k(oc, t, ps, psb):
                    i = nxt()
                    k.op(k.act, lambda: nc.scalar.copy(out=ev[i][:], in_=ps[:]), reads=[psb], writes=[evb[i]])
                    k.dma(k.sp, kT_s[oc, :, t * TT:(t + 1) * TT], ev[i][:], reads=[evb[i]], writes=[scr_b])

                def mk_tm(dst, func):
                    def f(blk, ci, ps, psb):
                        i = nxt()
                        if func is None:
                            k.op(k.dve, lambda: nc.vector.tensor_copy(out=ev[i][:], in_=ps[:]), reads=[psb], writes=[evb[i]])
                        else:
                            k.op(k.act, lambda: nc.scalar.activation(out=ev[i][:], in_=ps[:], func=func), reads=[psb], writes=[evb[i]])
                        k.dma(k.sp, dst[blk, :, ci * 512:(ci + 1) * 512], ev[i][:], reads=[evb[i]], writes=[scr_b])
                    return f

                self.linear_fm(h, hb, w_in_v, 0, 1024, ev_q, "q")
                self.linear_fm(h, hb, w_in_v, 1024, 1024, ev_k, "k")
                self.linear_tm(h, hb, w_in_v, 1024, 1024, mk_tm(ktm_s, None), "kt")
                self.linear_tm(h, hb, w_in_v, 2048, 2048, mk_tm(vtm_s, None), "v")
                self.linear_tm(h, hb, w_in_v, 4096, 2048, mk_tm(og_s, AF.Sigmoid), "og")
                for blk in range(NB):
                    ps, psb = self.next_ps()
                    for kc in range(KC):
                        k.op(k.pe, lambda kc=kc: nc.tensor.matmul(ps[:, 0:32], h[:, kc, blk * 128:(blk + 1) * 128], wg[:, kc, :],
                                                                  start=(kc == 0), stop=(kc == KC - 1)),
                             reads=[wg_b, hb[blk // 4]], writes=[psb])
                    k.op(k.dve, lambda: nc.vector.tensor_tensor(out=gates[:, blk, :], in0=ps[:, 0:32], in1=bgate, op=ALU.add),
                         reads=[psb, mpk_b], writes=[gates_b])
                self.phase_barrier()
            with contextlib.ExitStack() as es2:
                sbt = lambda name, shape, dt: es2.enter_context(_sbuf(nc, name, list(shape), dt))
                C = sbt("ml_C", [128, NHm, DV1], F32)
                Cb = sbt("ml_Cb", [128, NHm, DV1], BF16)
                mst = sbt("ml_mst", [128, NHm], F32)
                C_b, Cb_b, mst_b = Buf(), Buf(), Buf()
                qc = [sbt(f"ml_qc{i}", [128, NHm, 128], BF16) for i in range(2)]
                kc_ = [sbt(f"ml_kc{i}", [128, NHm, 128], BF16) for i in range(2)]
                ktc = [sbt(f"ml_ktc{i}", [128, NHm, 128], BF16) for i in range(2)]
                vx = [sbt(f"ml_vx{i}", [128, NHm, DV1], BF16) for i in range(2)]
                ldb = [[Buf() for _ in range(4)] for _ in range(2)]
                for i in range(2):
                    k.op(k.dve, lambda i=i: nc.vector.memset(vx[i][:, :, DVm:DV1], 1.0), writes=[ldb[i][3]])
                sm = {n_: sbt(f"ml_s_{n_}", [128, NHm], F32) for n_ in
                      ("e", "lsp", "b", "g", "gmax", "cm", "mx", "a", "em", "nmx", "mx2", "wgt", "dec", "tmp")}
                smb = {n_: Buf() for n_ in sm}
                diag = sbt("ml_diag", [128, 4, 128], F32)
                diag_b = Buf()
                dmat = sbt("ml_dmat", [128, 4, 128], F32)
                dmat_b = Buf()
                DmT = sbt("ml_DmT", [128, NHm, 128], F32)
                DmT_b = Buf()
                PT8 = sbt("ml_PT8", [128, NHm, 128], BF16)
                PT8_b = Buf()
                kw = sbt("ml_kw", [128, NHm, 128], BF16)
                kw_b = Buf()
                tmpn = [sbt(f"ml_tmpn{i}", [128, DV1], F32) for i in range(2)]
                tmpn_b = [Buf() for _ in range(2)]
                num = [sbt(f"ml_num{i}", [128, DV1], F32) for i in range(2)]
                num_b = [Buf() for _ in range(2)]
                dn = [sbt(f"ml_dn{i}", [128, 1], F32) for i in range(2)]
                dn_b = [Buf() for _ in range(2)]
                hout = sbt("ml_hout", [128, NHm, DVm], F32)
                hout_b = Buf()
                hfl = sbt("ml_hfl", [128, NHm, DVm], F32)
                hfl_b = Buf()
                ogl = sbt("ml_ogl", [128, D], BF16)
                ogl_b = Buf()
                ssum = sbt("ml_ssum", [128, NHm], F32)
                ssum_b = Buf()
                hsT = sbt("ml_hsTt", [128, KC, 128], BF16)
                hsT_tb = Buf()

                def load_chunk(blk, i):
                    sl = slice(blk * 128, (blk + 1) * 128)
                    k.dma(k.sp, qc[i][:], qT_s[:, :, sl].rearrange("h p t -> p h t"), reads=[scr_b], writes=[ldb[i][0]])
                    k.dma(k.sp, kc_[i][:], kT_s[:, :, sl].rearrange("h p t -> p h t"), reads=[scr_b], writes=[ldb[i][1]])
                    k.dma(k.sp, ktc[i][:], ktm_s[blk].rearrange("p (h d) -> p h d", d=128), reads=[scr_b], writes=[ldb[i][2]])
                    k.dma(k.sp, vx[i][:, :, 0:DVm], vtm_s[blk].rearrange("p (h d) -> p h d", d=DVm), reads=[scr_b], writes=[ldb[i][3]])

                def small(eng_fn, out_n, reads_n, extra_reads=()):
                    k.op(k.dve, eng_fn, reads=[smb[n_] for n_ in reads_n] + list(extra_reads), writes=[smb[out_n]])

                for dirn in range(2):
                    order = list(range(NB)) if dirn == 0 else list(range(NB - 1, -1, -1))
                    gi0 = dirn * 16
                    tri01 = V1_01 if dirn == 0 else V2_01
                    mb_st = V1b if dirn == 0 else V2b
                    mb_ts = V2b if dirn == 0 else V1b
                    load_chunk(order[0], 0)
                    for oi, blk in enumerate(order):
                        i = oi % 2
                        if oi + 1 < NB:
                            load_chunk(order[oi + 1], (oi + 1) % 2)
                        seg = blk // 2
                        first_in_seg = (blk % 2 == 0) if dirn == 0 else (blk % 2 == 1)
                        last_in_seg = not first_in_seg
                        if first_in_seg:
                            if seg >= 4:
                                k.op(k.dve, lambda: nc.vector.memset(C[:], 0.0), writes=[C_b])
                                k.op(k.dve, lambda: nc.vector.memset(mst[:], 0.0), writes=[mst_b])
                                k.op(k.dve, lambda: nc.vector.memset(Cb[:], 0.0), writes=[Cb_b])
                            elif (seg == 0 and dirn == 0) or (seg == 3 and dirn == 1):
                                k.dma(k.sp, C[:].rearrange("p h d -> p (h d)"), self.I("ml_initC")[dirn], writes=[C_b])
                                k.dma(k.sp, mst[:], self.I("ml_initm")[dirn], writes=[mst_b])
                                k.op(k.act, lambda: nc.scalar.copy(out=Cb[:], in_=C[:]), reads=[C_b], writes=[Cb_b])
                            else:
                                k.op(k.dve, lambda: nc.vector.tensor_scalar(out=C[:], in0=C[:], scalar1=flag, scalar2=None, op0=ALU.mult),
                                     reads=[C_b, mpk_b], writes=[C_b])
                                k.op(k.dve, lambda: nc.vector.tensor_scalar(out=mst[:], in0=mst[:], scalar1=flag, scalar2=None, op0=ALU.mult),
                                     reads=[mst_b, mpk_b], writes=[mst_b])
                                k.op(k.act, lambda: nc.scalar.copy(out=Cb[:], in_=C[:]), reads=[C_b], writes=[Cb_b])
                        gi = gates[:, blk, gi0:gi0 + 8]
                        gf = gates[:, blk, gi0 + 8:gi0 + 16]
                        k.op(k.act, lambda: nc.scalar.activation(out=sm["e"][:], in_=gf, func=AF.Exp, scale=-1.0),
                             reads=[gates_b], writes=[smb["e"]])
                        k.op(k.act, lambda: nc.scalar.activation(out=sm["lsp"][:], in_=sm["e"][:], func=AF.Ln, bias=1.0),
                             reads=[smb["e"]], writes=[smb["lsp"]])
                        pb, pbb = self.next_ps()
                        k.op(k.pe, lambda: nc.tensor.matmul(pb[:, 0:8], tri01, sm["lsp"][:], start=True, stop=True),
                             reads=[mpk_b, smb["lsp"]], writes=[pbb])
                        k.op(k.dve, lambda: nc.vector.tensor_tensor(out=sm["g"][:], in0=pb[:, 0:8], in1=gi, op=ALU.add),
                             reads=[pbb, gates_b], writes=[smb["g"]])
                        for hh in range(2):
                            hs_ = slice(hh * 4, (hh + 1) * 4)
                            k.op(k.dve, lambda: nc.vector.tensor_tensor(
                                out=diag[:], in0=ident.unsqueeze(1).to_broadcast([128, 4, 128]),
                                in1=sm["g"][:, hs_].unsqueeze(2).to_broadcast([128, 4, 128]), op=ALU.mult),
                                reads=[mpk_b, smb["g"]], writes=[diag_b])
                            pg, pgb = self.next_ps()
                            k.op(k.pe, lambda: nc.tensor.matmul(pg[:], onesf, diag[:].rearrange("p h s -> p (h s)"), start=True, stop=True),
                                 reads=[mpk_b, diag_b], writes=[pgb])
                            pg3 = pg[:].rearrange("p (h s) -> p h s", s=128)
                            k.op(k.dve, lambda: nc.vector.tensor_reduce(out=sm["gmax"][:, hs_], in_=pg3, axis=AX.X, op=ALU.max),
                                 reads=[pgb], writes=[smb["gmax"]])
                            k.op(k.dve, lambda: nc.vector.tensor_tensor(out=dmat[:], in0=pg3, in1=mb_ts.unsqueeze(1).to_broadcast([128, 4, 128]),
                                                                        op=ALU.add), reads=[pgb, mpk_b], writes=[dmat_b])
                            k.op(k.dve, lambda: nc.vector.tensor_reduce(out=sm["cm"][:, hs_], in_=dmat[:], axis=AX.X, op=ALU.max),
                                 reads=[dmat_b], writes=[smb["cm"]])
                        small(lambda: nc.vector.tensor_tensor(out=sm["mx"][:], in0=sm["cm"][:], in1=mst[:], op=ALU.max), "mx", ["cm"], [mst_b])
                        small(lambda: nc.vector.tensor_tensor(out=sm["tmp"][:], in0=mst[:], in1=sm["mx"][:], op=ALU.subtract), "tmp", ["mx"], [mst_b])
                        k.op(k.act, lambda: nc.scalar.activation(out=sm["a"][:], in_=sm["tmp"][:], func=AF.Exp), reads=[smb["tmp"]], writes=[smb["a"]])
                        small(lambda: nc.vector.tensor_tensor(out=sm["b"][:], in0=pb[:, 0:8], in1=sm["mx"][:], op=ALU.subtract), "b", ["mx"], [pbb])
                        k.op(k.act, lambda: nc.scalar.activation(out=sm["em"][:], in_=sm["b"][:], func=AF.Exp), reads=[smb["b"]], writes=[smb["em"]])
                        small(lambda: nc.vector.tensor_scalar(out=sm["nmx"][:], in0=sm["mx"][:], scalar1=-1.0, scalar2=None, op0=ALU.mult), "nmx", ["mx"])
                        small(lambda: nc.vector.tensor_tensor(out=sm["mx2"][:], in0=sm["gmax"][:], in1=mst[:], op=ALU.max), "mx2", ["gmax"], [mst_b])
                        small(lambda: nc.vector.tensor_tensor(out=sm["tmp"][:], in0=sm["g"][:], in1=sm["mx2"][:], op=ALU.subtract), "tmp", ["g", "mx2", "a"])
                        k.op(k.act, lambda: nc.scalar.activation(out=sm["wgt"][:], in_=sm["tmp"][:], func=AF.Exp), reads=[smb["tmp"]], writes=[smb["wgt"]])
                        small(lambda: nc.vector.tensor_tensor(out=sm["e"][:], in0=mst[:], in1=sm["mx2"][:], op=ALU.subtract), "e", ["mx2", "lsp"], [mst_b])
                        k.op(k.act, lambda: nc.scalar.activation(out=sm["dec"][:], in_=sm["e"][:], func=AF.Exp), reads=[smb["e"]], writes=[smb["dec"]])
                        for hh in range(2):
                            hs_ = slice(hh * 4, (hh + 1) * 4)
                            k.op(k.dve, lambda: nc.vector.tensor_tensor(
                                out=diag[:], in0=ident.unsqueeze(1).to_broadcast([128, 4, 128]),
                                in1=sm["nmx"][:, hs_].unsqueeze(2).to_broadcast([128, 4, 128]), op=ALU.mult),
                                reads=[mpk_b, smb["nmx"]], writes=[diag_b])
                            pu, pub = self.next_ps()
                            k.op(k.pe, lambda: nc.tensor.matmul(pu[:], onesf, diag[:].rearrange("p h s -> p (h s)"), start=True, stop=True),
                                 reads=[mpk_b, diag_b], writes=[pub])
                            pu3 = pu[:].rearrange("p (h t) -> p h t", t=128)
                            k.op(k.dve, lambda: nc.vector.tensor_tensor(out=dmat[:], in0=pu3,
                                                                        in1=sm["g"][:, hs_].unsqueeze(2).to_broadcast([128, 4, 128]), op=ALU.add),
                                 reads=[pub, smb["g"]], writes=[dmat_b])
                            k.op(k.dve, lambda: nc.vector.tensor_tensor(out=dmat[:], in0=dmat[:],
                                                                        in1=mb_st.unsqueeze(1).to_broadcast([128, 4, 128]), op=ALU.add),
                                 reads=[dmat_b, mpk_b], writes=[dmat_b])
                            k.op(k.act, lambda: nc.scalar.activation(out=DmT[:, hs_, :], in_=dmat[:], func=AF.Exp),
                                 reads=[dmat_b], writes=[DmT_b])
                            pss, pssb = self.next_ps()
                            for h4 in range(4):
                                hd_ = hh * 4 + h4
                                k.op(k.pe, lambda: nc.tensor.matmul(pss[:, h4 * 128:(h4 + 1) * 128], kc_[i][:, hd_, :], qc[i][:, hd_, :],
                                                                    start=True, stop=True),
                                     reads=[ldb[i][0], ldb[i][1]], writes=[pssb])
                            k.op(k.dve, lambda: nc.vector.tensor_tensor(out=PT8[:, hs_, :], in0=pss[:].rearrange("p (h t) -> p h t", t=128),
                                                                        in1=DmT[:, hs_, :], op=ALU.mult),
                                 reads=[pssb, DmT_b], writes=[PT8_b])
                        k.op(k.dve, lambda: nc.vector.tensor_tensor(out=kw[:], in0=ktc[i][:],
                                                                    in1=sm["wgt"][:].unsqueeze(2).to_broadcast([128, NHm, 128]), op=ALU.mult),
                             reads=[ldb[i][2], smb["wgt"]], writes=[kw_b])
                        for hd_ in range(NHm):
                            j = hd_ % 2
                            p1, p1b = self.next_ps()
                            k.op(k.pe, lambda: nc.tensor.matmul(p1[:, 0:DV1], PT8[:, hd_, :], vx[i][:, hd_, :], start=True, stop=True),
                                 reads=[PT8_b, ldb[i][3]], writes=[p1b])
                            p2, p2b = self.next_ps()
                            k.op(k.pe, lambda: nc.tensor.matmul(p2[:, 0:DV1], qc[i][:, hd_, :], Cb[:, hd_, :], start=True, stop=True),
                                 reads=[ldb[i][0], Cb_b], writes=[p2b])
                            k.op(k.act, lambda: nc.scalar.activation(out=tmpn[j][:], in_=p2[:, 0:DV1], func=AF.Identity,
                                                                     scale=sm["a"][:, hd_:hd_ + 1]),
                                 reads=[p2b, smb["a"]], writes=[tmpn_b[j]])
                            k.op(k.dve, lambda: nc.vector.tensor_tensor(out=num[j][:], in0=tmpn[j][:], in1=p1[:, 0:DV1], op=ALU.add),
                                 reads=[tmpn_b[j], p1b], writes=[num_b[j]])
                            k.op(k.dve, lambda: nc.vector.tensor_scalar(out=dn[j][:], in0=num[j][:, DVm:DV1], scalar1=-1.0,
                                                                        scalar2=None, op0=ALU.mult),
                                 reads=[num_b[j]], writes=[dn_b[j]])
                            k.op(k.dve, lambda: nc.vector.tensor_tensor(out=dn[j][:], in0=dn[j][:], in1=num[j][:, DVm:DV1], op=ALU.max),
                                 reads=[num_b[j], dn_b[j]], writes=[dn_b[j]])
                            k.op(k.dve, lambda: nc.vector.tensor_tensor(out=dn[j][:], in0=dn[j][:], in1=sm["em"][:, hd_:hd_ + 1], op=ALU.max),
                                 reads=[dn_b[j], smb["em"]], writes=[dn_b[j]])
                            k.op(k.dve, lambda: nc.vector.reciprocal(out=dn[j][:], in_=dn[j][:]), reads=[dn_b[j]], writes=[dn_b[j]])
                            k.op(k.dve, lambda: nc.vector.tensor_scalar(out=hout[:, hd_, :], in0=num[j][:, 0:DVm], scalar1=dn[j][:, 0:1],
                                                                        scalar2=None, op0=ALU.mult),
                                 reads=[num_b[j], dn_b[j]], writes=[hout_b])
                            p3, p3b = self.next_ps()
                            k.op(k.pe, lambda: nc.tensor.matmul(p3[:, 0:DV1], kw[:, hd_, :], vx[i][:, hd_, :], start=True, stop=True),
                                 reads=[kw_b, ldb[i][3]], writes=[p3b])
                            k.op(k.dve, lambda: nc.vector.scalar_tensor_tensor(out=C[:, hd_, :], in0=C[:, hd_, :], scalar=sm["dec"][:, hd_:hd_ + 1],
                                                                               in1=p3[:, 0:DV1], op0=ALU.mult, op1=ALU.add),
                                 reads=[C_b, smb["dec"], p3b], writes=[C_b])
                        k.op(k.act, lambda: nc.scalar.copy(out=Cb[:], in_=C[:]), reads=[C_b], writes=[Cb_b])
                        pbl, pblb = self.next_ps()
                        k.op(k.pe, lambda: nc.tensor.matmul(pbl[:, 0:8], onesf, sm["lsp"][:], start=True, stop=True),
                             reads=[mpk_b, smb["lsp"]], writes=[pblb])
                        k.op(k.dve, lambda: nc.vector.tensor_tensor(out=mst[:], in0=sm["mx2"][:], in1=pbl[:, 0:8], op=ALU.subtract),
                             reads=[smb["mx2"], pblb, smb["dec"], smb["a"], smb["mx"]], writes=[mst_b])
                        if last_in_seg:
                            k.dma(k.sp, self.O("ml_stC")[seg, dirn], C[:].rearrange("p h d -> p (h d)"), reads=[C_b], writes=[outb])
                            k.dma(k.sp, self.O("ml_stm")[seg, dirn], mst[:], reads=[mst_b], writes=[outb])
                        if dirn == 0:
                            k.dma(k.sp, hf_s[blk].rearrange("p (h d) -> p h d", d=DVm), hout[:], reads=[hout_b], writes=[hf_b])
                        else:
                            k.dma(k.sp, hfl[:], hf_s[blk].rearrange("p (h d) -> p h d", d=DVm), reads=[hf_b], writes=[hfl_b])
                            k.dma(k.sp, ogl[:], og_s[blk], reads=[scr_b], writes=[ogl_b])
                            k.op(k.dve, lambda: nc.vector.tensor_tensor(out=hout[:], in0=hout[:], in1=hfl[:], op=ALU.add),
                                 reads=[hout_b, hfl_b], writes=[hout_b])
                            k.op(k.act, lambda: nc.scalar.activation(out=hfl[:], in_=hout[:], func=AF.Square), reads=[hout_b], writes=[hfl_b])
                            k.op(k.dve, lambda: nc.vector.tensor_reduce(out=ssum[:], in_=hfl[:], axis=AX.X, op=ALU.add),
                                 reads=[hfl_b], writes=[ssum_b])
                            k.op(k.dve, lambda: nc.vector.tensor_scalar(out=ssum[:], in0=ssum[:], scalar1=1.0 / DVm, scalar2=EPS,
                                                                        op0=ALU.mult, op1=ALU.add), reads=[ssum_b], writes=[ssum_b])
                            k.op(k.act, lambda: nc.scalar.activation(out=ssum[:], in_=ssum[:], func=AF.Sqrt), reads=[ssum_b], writes=[ssum_b])
                            k.op(k.dve, lambda: nc.vector.reciprocal(out=ssum[:], in_=ssum[:]), reads=[ssum_b], writes=[ssum_b])
                            k.op(k.dve, lambda: nc.vector.tensor_tensor(out=hout[:], in0=hout[:],
                                                                        in1=ssum[:].unsqueeze(2).to_broadcast([128, NHm, DVm]), op=ALU.mult),
                                 reads=[hout_b, ssum_b], writes=[hout_b])
                            hflat = hout[:].rearrange("p h d -> p (h d)")
                            k.op(k.dve, lambda: nc.vector.tensor_tensor(out=hflat, in0=hflat, in1=onorm, op=ALU.mult),
                                 reads=[hout_b, mpk_b], writes=[hout_b])
                            k.op(k.dve, lambda: nc.vector.tensor_tensor(out=hflat, in0=hflat, in1=ogl[:], op=ALU.mult),
                                 reads=[hout_b, ogl_b], writes=[hout_b])
                            for q4 in range(4):
                                ptr, ptrb = self.next_ps()
                                for c4 in range(4):
                                    c = q4 * 4 + c4
                                    k.op(k.pe, lambda: nc.tensor.transpose(ptr[:, c4 * 128:(c4 + 1) * 128], hflat[:, c * 128:(c + 1) * 128], ident),
                                         reads=[hout_b, mpk_b], writes=[ptrb])
                                k.op(k.act, lambda: nc.scalar.copy(out=hsT[:, q4 * 4:(q4 + 1) * 4, :], in_=ptr[:].rearrange("p (c t) -> p c t", t=128)),
                                     reads=[ptrb], writes=[hsT_tb])
                            k.dma(k.sp, hsT_s[:, :, blk * 128:(blk + 1) * 128].rearrange("c p t -> p c t"), hsT[:], reads=[hsT_tb], writes=[hsT_b])
                self.phase_barrier()
            with contextlib.ExitStack() as es2:
                sbt = lambda name, shape, dt: es2.enter_context(_sbuf(nc, name, list(shape), dt))
                h2 = sbt("ml_h2", [128, KC, NTOK], BF16)
                h2b = [Buf() for _ in range(NGRP)]
                for t in range(NGRP):
                    k.dma(k.sp, h2[:, :, t * TT:(t + 1) * TT], hsT_s[:, :, t * TT:(t + 1) * TT].rearrange("c p t -> p c t"),
                          reads=[hsT_b], writes=[h2b[t]])
                wo_v = self.I("mlstm_w_o")[slot].rearrange("(kc p) n -> p kc n", p=128)

                def ev_o(oc, t, ps, psb):
                    ts = slice(t * TT, (t + 1) * TT)
                    k.op(k.dve, lambda: nc.vector.scalar_tensor_tensor(
                        out=self.x_sb[:, oc, ts], in0=ps[:], scalar=self.modT[:, l, 5, oc, t:t + 1],
                        in1=self.x_sb[:, oc, ts], op0=ALU.mult, op1=ALU.add),
                        reads=[psb, self.modT_b, self.xb[oc][t]], writes=[self.xb[oc][t]])

                self.linear_fm(h2, h2b, wo_v, 0, D, ev_o, "o")
                self.phase_barrier()


    def rwkv(self, l):
        k, nc = self.k, self.nc
        slot = l // 3
        TB, SB = 32, 4
        KZ = [nc.dram_tensor(f"rk_KZ{z}_{l}", [128, 5, KC, NTOK], F32).ap() for z in range(2)]
        VV = nc.dram_tensor(f"rk_VV{l}", [128, KC, NTOK], F32).ap()
        GG = nc.dram_tensor(f"rk_GG{l}", [128, KC, NTOK], F32).ap()
        VB = nc.dram_tensor(f"rk_VB{l}", [128, KC, NTOK], F32).ap()
        RAWK = nc.dram_tensor(f"rk_RAWK{l}", [128, KC, NTOK], F32).ap()
        AZ = [nc.dram_tensor(f"rk_AZ{z}_{l}", [128, KC, NTOK], F32).ap() for z in range(2)]
        YT = [nc.dram_tensor(f"rk_YT{z}_{l}", [NTOK, D], BF16).ap() for z in range(2)]
        hsT_s = nc.dram_tensor(f"rk_hsT{l}", [KC, 128, NTOK], BF16).ap()
        scr_b, yt_b, hsT_b, outb = Buf("rk_scr"), Buf("rk_yt"), Buf("rk_hsT"), Buf("rk_out")
        NFM = 13 * KC
        with contextlib.ExitStack() as es1:
            sbt1 = lambda name, shape, dt: es1.enter_context(_sbuf(nc, name, list(shape), dt))
            self.phase_barrier()
            rpk = sbt1("rk_rpk", [128, NRP], F32)
            rpk_b = Buf()
            k.dma(k.sp, rpk[:], self.I("rpack")[:, :], writes=[rpk_b])
            fmv = rpk[:, 0:NFM].rearrange("p (a c) -> p a c", c=KC)
            I2 = rpk[:, NFM:NFM + 64]
            flag = rpk[:, NFM + 64:NFM + 65]
            ident = rpk[:, NFM + 65:NFM + 193]
            BO = sbt1("rk_BO", [128, 128], BF16)
            hsel = sbt1("rk_hsel", [128, 2], BF16)
            cst_b = Buf()
            k.op(k.dve, lambda: nc.vector.memset(BO[:], 0.0), writes=[cst_b])
            k.op(k.dve, lambda: nc.vector.memset(BO[0:64, 0:64], 1.0), writes=[cst_b])
            k.op(k.dve, lambda: nc.vector.memset(BO[64:128, 64:128], 1.0), writes=[cst_b])
            k.op(k.dve, lambda: nc.vector.memset(hsel[:], 0.0), writes=[cst_b])
            k.op(k.dve, lambda: nc.vector.memset(hsel[0:64, 0:1], 1.0), writes=[cst_b])
            k.op(k.dve, lambda: nc.vector.memset(hsel[64:128, 1:2], 1.0), writes=[cst_b])
            with contextlib.ExitStack() as es2:
                sbt = lambda name, shape, dt: es2.enter_context(_sbuf(nc, name, list(shape), dt))
                hp_ = sbt("rk_h", [128, KC, NTOK + 2], BF16)
                hb = [Buf(f"h{t}") for t in range(NGRP)]
                k.op(k.dve, lambda: nc.vector.memset(hp_[:, :, 0:1], 0.0), writes=[hb[0]])
                k.op(k.dve, lambda: nc.vector.memset(hp_[:, :, NTOK + 1:NTOK + 2], 0.0), writes=[hb[2]])

                class Shift:
                    def __getitem__(self_, idx):
                        p, c, sl = idx
                        return hp_[p, c, slice(sl.start + 1, sl.stop + 1)]
                sq = [sbt(f"rk_sq{i}", [128, TT], BF16) for i in range(2)]
                sqb = [Buf() for _ in range(2)]
                self.compute_h(l, 1, Shift(), hb, [t_[:] for t_ in sq], sqb)
                msk = sbt("rk_msk", [128, 2, NTOK], BF16)
                msk_b = Buf()
                k.dma(k.pool, msk[:], self.I("rmask").rearrange("p (a t) -> p a t", a=2), writes=[msk_b])
                omm = sbt("rk_omm", [128, 6, KC], F32)
                hmu = sbt("rk_hmu", [128, 6, KC], F32)
                omm_b = Buf()
                k.op(k.dve, lambda: nc.vector.tensor_scalar(out=omm[:], in0=fmv[:, 0:6, :], scalar1=-1.0, scalar2=1.0, op0=ALU.mult, op1=ALU.add),
                     reads=[rpk_b], writes=[omm_b])
                k.op(k.dve, lambda: nc.vector.tensor_scalar(out=hmu[:], in0=fmv[:, 0:6, :], scalar1=0.5, scalar2=None, op0=ALU.mult),
                     reads=[rpk_b], writes=[omm_b])
                xs = sbt("rk_xs", [128, KC, TT], BF16)
                xs_b = Buf()
                tA = [sbt(f"rk_tA{i}", [128, TT], F32) for i in range(3)]
                tA_b = [Buf() for _ in range(3)]
                ev = [sbt(f"rk_ev{i}", [128, TT], F32) for i in range(3)]
                ev_b = [Buf() for _ in range(3)]
                mid = sbt("rk_mid", [128, 2, TT], BF16)
                mid_b = Buf()
                cnt = [0]

                def nxt():
                    cnt[0] += 1
                    return cnt[0] % 3

                def build_xs(p, t):
                    t0 = t * TT
                    for c in range(KC):
                        k.op(k.dve, lambda: nc.vector.tensor_tensor(out=tA[0][:], in0=hp_[:, c, t0:t0 + TT], in1=msk[:, 0, t0:t0 + TT], op=ALU.mult),
                             reads=[hb[t], hb[max(t - 1, 0)], msk_b], writes=[tA_b[0]])
                        k.op(k.dve, lambda: nc.vector.tensor_tensor(out=tA[1][:], in0=hp_[:, c, t0 + 2:t0 + TT + 2], in1=msk[:, 1, t0:t0 + TT], op=ALU.mult),
                             reads=[hb[t], hb[min(t + 1, 2)], msk_b], writes=[tA_b[1]])
                        k.op(k.dve, lambda: nc.vector.tensor_tensor(out=tA[0][:], in0=tA[0][:], in1=tA[1][:], op=ALU.add),
                             reads=[tA_b[0], tA_b[1]], writes=[tA_b[0]])
                        k.op(k.act, lambda: nc.scalar.activation(out=tA[2][:], in_=hp_[:, c, t0 + 1:t0 + TT + 1], func=AF.Identity, scale=omm[:, p, c:c + 1]),
                             reads=[hb[t], omm_b], writes=[tA_b[2]])
                        k.op(k.dve, lambda: nc.vector.scalar_tensor_tensor(out=xs[:, c, :], in0=tA[0][:], scalar=hmu[:, p, c:c + 1], in1=tA[2][:],
                                                                           op0=ALU.mult, op1=ALU.add),
                             reads=[tA_b[0], tA_b[2], omm_b], writes=[xs_b])

                def proj_tile(wview, col0, ncols, evac, tag):
                    TC = 256
                    with contextlib.ExitStack() as es3:
                        wt = [es3.enter_context(_sbuf(nc, f"rkw_{tag}{i}", [128, KC, TC], BF16)) for i in range(2)]
                        wtb = [Buf() for _ in range(2)]
                        ntile = ncols // TC

                        def load(ti):
                            k.dma(k.pool, wt[ti % 2][:], wview[:, :, col0 + ti * TC:col0 + (ti + 1) * TC], writes=[wtb[ti % 2]])
                        load(0)
                        for ti in range(ntile):
                            if ti + 1 < ntile:
                                load(ti + 1)
                            for sub in range(TC // 128):
                                oc = ti * (TC // 128) + sub
                                ps, psb = self.next_ps()
                                for kc in range(KC):
                                    k.op(k.pe, lambda kc=kc: nc.tensor.matmul(ps[:], wt[ti % 2][:, kc, sub * 128:(sub + 1) * 128], xs[:, kc, :],
                                                                              start=(kc == 0), stop=(kc == KC - 1)),
                                         reads=[wtb[ti % 2], xs_b], writes=[psb])
                                evac(oc, ps, psb)
                        self.scope_end(wtb)

                for p in range(3):
                    wv_ = self.I("rwkv_w_rkv")[slot, p].rearrange("(kc p) n -> p kc n", p=128)
                    for t in range(NGRP):
                        ts = slice(t * TT, (t + 1) * TT)
                        build_xs(p, t)

                        def ev_rkv(oc, ps, psb, p=p, ts=ts):
                            i = nxt()
                            k.op(k.act, lambda: nc.scalar.copy(out=ev[i][:], in_=ps[:]), reads=[psb], writes=[ev_b[i]])
                            if p == 0:
                                k.dma(k.sp, KZ[0][:, 4, oc, ts], ev[i][:], reads=[ev_b[i]], appends=[scr_b])
                                k.dma(k.sp, KZ[1][:, 4, oc, ts], ev[i][:], reads=[ev_b[i]], appends=[scr_b])
                            elif p == 1:
                                k.dma(k.sp, RAWK[:, oc, ts], ev[i][:], reads=[ev_b[i]], appends=[scr_b])
                            else:
                                k.dma(k.sp, VV[:, oc, ts], ev[i][:], reads=[ev_b[i]], appends=[scr_b])
                        proj_tile(wv_, 0, D, ev_rkv, f"p{p}")
                for p in (3, 4, 5):
                    with contextlib.ExitStack() as es3:
                        if p < 5:
                            nmA, nmB, R_ = ("rwkv_wA", "rwkv_wB", 96) if p == 3 else ("rwkv_aA", "rwkv_aB", 96)
                            dn_w = [es3.enter_context(_sbuf(nc, f"rk_lA{p}{z}", [128, KC, R_], BF16)) for z in range(2)]
                            up_w = [es3.enter_context(_sbuf(nc, f"rk_lB{p}{z}", [128, 1, D], BF16)) for z in range(2)]
                            lw_b = Buf()
                            for z in range(2):
                                k.dma(k.pool, dn_w[z][:], self.I(nmA)[slot, z].rearrange("(kc p) r -> p kc r", p=128), writes=[lw_b])
                                k.dma(k.pool, up_w[z][0:R_, 0, :], self.I(nmB)[slot, z], writes=[lw_b])
                            nz, nch = 2, 1
                        else:
                            R_ = 256
                            dn_w = [es3.enter_context(_sbuf(nc, "rk_lA5", [128, KC, R_], BF16))]
                            up_w = [es3.enter_context(_sbuf(nc, "rk_lB5", [128, 2, D], BF16))]
                            lw_b = Buf()
                            k.dma(k.pool, dn_w[0][:], self.I("rwkv_gA")[slot].rearrange("(kc p) r -> p kc r", p=128), writes=[lw_b])
                            k.dma(k.pool, up_w[0][:], self.I("rwkv_gB")[slot].rearrange("(c p) n -> p c n", p=128), writes=[lw_b])
                            nz, nch = 1, 2
                        for t in range(NGRP):
                            ts = slice(t * TT, (t + 1) * TT)
                            build_xs(p, t)
                            for z in range(nz):
                                rows = 96 if p < 5 else 128
                                for ch in range(nch):
                                    pd, pdb = self.next_ps()
                                    for kc in range(KC):
                                        k.op(k.pe, lambda kc=kc: nc.tensor.matmul(pd[0:rows, :], dn_w[z][:, kc, ch * 128:ch * 128 + rows], xs[:, kc, :],
                                                                                  start=(kc == 0), stop=(kc == KC - 1)),
                                             reads=[lw_b, xs_b], writes=[pdb])
                                    fn = AF.Tanh if p == 3 else (AF.Identity if p == 4 else AF.Sigmoid)
                                    k.op(k.act, lambda: nc.scalar.activation(out=mid[0:rows, ch, :], in_=pd[0:rows, :], func=fn),
                                         reads=[pdb], writes=[mid_b])
                                for oc in range(KC):
                                    pu, pub = self.next_ps()
                                    for ch in range(nch):
                                        k.op(k.pe, lambda ch=ch: nc.tensor.matmul(pu[:], up_w[z][0:rows, ch, oc * 128:(oc + 1) * 128], mid[0:rows, ch, :],
                                                                                  start=(ch == 0), stop=(ch == nch - 1)),
                                             reads=[lw_b, mid_b], writes=[pub])
                                    i = nxt()
                                    if p == 3:
                                        k.op(k.act, lambda: nc.scalar.activation(out=ev[i][:], in_=pu[:], func=AF.Sigmoid, bias=fmv[:, 6 + z, oc:oc + 1]),
                                             reads=[pub, rpk_b], writes=[ev_b[i]])
                                        k.op(k.act, lambda: nc.scalar.activation(out=ev[i][:], in_=ev[i][:], func=AF.Exp, scale=-float(np.exp(-0.5))),
                                             reads=[ev_b[i]], writes=[ev_b[i]])
                                        k.dma(k.sp, KZ[z][:, 1, oc, ts], ev[i][:], reads=[ev_b[i]], appends=[scr_b])
                                    elif p == 4:
                                        k.op(k.act, lambda: nc.scalar.activation(out=ev[i][:], in_=pu[:], func=AF.Sigmoid, bias=fmv[:, 8 + z, oc:oc + 1]),
                                             reads=[pub, rpk_b], writes=[ev_b[i]])
                                        k.dma(k.sp, AZ[z][:, oc, ts], ev[i][:], reads=[ev_b[i]], appends=[scr_b])
                                    else:
                                        k.op(k.act, lambda: nc.scalar.copy(out=ev[i][:], in_=pu[:]), reads=[pub], writes=[ev_b[i]])
                                        k.dma(k.sp, GG[:, oc, ts], ev[i][:], reads=[ev_b[i]], appends=[scr_b])
                        self.scope_end([lw_b])
                self.phase_barrier()
            with contextlib.ExitStack() as es2:
                sbt = lambda name, shape, dt: es2.enter_context(_sbuf(nc, name, list(shape), dt))
                names = ("k", "r", "v", "a0", "a1", "kq", "kk", "t", "kd0", "kd1", "b0", "b1", "vb")
                T2 = {n_: [sbt(f"rk2_{n_}{i}", [128, TT], F32) for i in range(2)] for n_ in names}
                T2b = {n_: [Buf() for _ in range(2)] for n_ in names}
                sqk = [sbt(f"rk2_sq{i}", [128, TT], BF16) for i in range(2)]
                sqk_b = [Buf() for _ in range(2)]
                it = 0
                for t in range(NGRP):
                    ts = slice(t * TT, (t + 1) * TT)
                    for c in range(KC):
                        i = it % 2
                        it += 1
                        X = {n_: T2[n_][i] for n_ in names}
                        B = {n_: T2b[n_][i] for n_ in names}
                        k.dma(k.sp, X["k"][:], RAWK[:, c, ts], reads=[scr_b], writes=[B["k"]])
                        k.dma(k.sp, X["r"][:], KZ[0][:, 4, c, ts], reads=[scr_b], writes=[B["r"]])
                        k.dma(k.sp, X["v"][:], VV[:, c, ts], reads=[scr_b], writes=[B["v"]])
                        k.dma(k.sp, X["a0"][:], AZ[0][:, c, ts], reads=[scr_b], writes=[B["a0"]])
                        k.dma(k.sp, X["a1"][:], AZ[1][:, c, ts], reads=[scr_b], writes=[B["a1"]])
                        k.op(k.act, lambda: nc.scalar.activation(out=X["kq"][:], in_=X["k"][:], func=AF.Identity, scale=fmv[:, 10, c:c + 1]),
                             reads=[B["k"], rpk_b], writes=[B["kq"]])
                        k.op(k.act, lambda: nc.scalar.activation(out=sqk[i][:], in_=X["kq"][:], func=AF.Square), reads=[B["kq"]], writes=[sqk_b[i]])
                        pn, pnb = self.next_ps()
                        k.op(k.pe, lambda: nc.tensor.matmul(pn[:], BO[:], sqk[i][:], start=True, stop=True), reads=[cst_b, sqk_b[i]], writes=[pnb])
                        k.op(k.act, lambda: nc.scalar.activation(out=pn[:], in_=pn[:], func=AF.Sqrt), reads=[pnb], writes=[pnb])
                        k.op(k.dve, lambda: nc.vector.tensor_scalar(out=pn[:], in0=pn[:], scalar1=1e-12, scalar2=None, op0=ALU.max), reads=[pnb], writes=[pnb])
                        k.op(k.dve, lambda: nc.vector.reciprocal(out=pn[:], in_=pn[:]), reads=[pnb], writes=[pnb])
                        k.op(k.dve, lambda: nc.vector.tensor_tensor(out=X["kk"][:], in0=X["kq"][:], in1=pn[:], op=ALU.mult),
                             reads=[B["kq"], pnb], writes=[B["kk"]])
                        k.dma(k.sp, KZ[0][:, 0, c, ts], X["kk"][:], reads=[B["kk"]], appends=[scr_b])
                        k.dma(k.sp, KZ[1][:, 0, c, ts], X["kk"][:], reads=[B["kk"]], appends=[scr_b])
                        for z in range(2):
                            az, kd, bz = X[f"a{z}"], X[f"kd{z}"], X[f"b{z}"]
                            k.op(k.dve, lambda: nc.vector.tensor_scalar(out=X["t"][:], in0=az[:], scalar1=-1.0, scalar2=fmv[:, 11, c:c + 1],
                                                                        op0=ALU.add, op1=ALU.mult), reads=[B[f"a{z}"], rpk_b], writes=[B["t"]])
                            k.op(k.dve, lambda: nc.vector.scalar_tensor_tensor(out=kd[:], in0=X["t"][:], scalar=1.0, in1=X["k"][:], op0=ALU.add, op1=ALU.mult),
                                 reads=[B["t"], B["k"]], writes=[B[f"kd{z}"]])
                            k.op(k.dve, lambda: nc.vector.tensor_tensor(out=bz[:], in0=X["kk"][:], in1=az[:], op=ALU.mult),
                                 reads=[B["kk"], B[f"a{z}"]], writes=[B[f"b{z}"]])
                            k.dma(k.sp, KZ[z][:, 3, c, ts], kd[:], reads=[B[f"kd{z}"]], appends=[scr_b])
                            k.dma(k.sp, KZ[z][:, 2, c, ts], bz[:], reads=[B[f"b{z}"]], appends=[scr_b])
                        k.op(k.dve, lambda: nc.vector.tensor_tensor(out=X["t"][:], in0=X["kd0"][:], in1=X["kd1"][:], op=ALU.add),
                             reads=[B["kd0"], B["kd1"]], writes=[B["t"]])
                        k.op(k.dve, lambda: nc.vector.scalar_tensor_tensor(out=sqk[i][:], in0=X["t"][:], scalar=fmv[:, 12, c:c + 1], in1=X["r"][:],
                                                                           op0=ALU.mult, op1=ALU.mult),
                             reads=[B["t"], B["r"], rpk_b], writes=[sqk_b[i]])
                        pbn, pbnb = self.next_ps()
                        k.op(k.pe, lambda: nc.tensor.matmul(pbn[:], BO[:], sqk[i][:], start=True, stop=True), reads=[cst_b, sqk_b[i]], writes=[pbnb])
                        k.op(k.dve, lambda: nc.vector.tensor_tensor(out=X["vb"][:], in0=X["v"][:], in1=pbn[:], op=ALU.mult),
                             reads=[B["v"], pbnb], writes=[B["vb"]])
                        k.dma(k.sp, VB[:, c, ts], X["vb"][:], reads=[B["vb"]], appends=[scr_b])
                self.phase_barrier()
            with contextlib.ExitStack() as es2:
                sbt = lambda name, shape, dt: es2.enter_context(_sbuf(nc, name, list(shape), dt))
                CH = []
                for z in range(2):
                    ch = dict(z=z)
                    ch["S"] = sbt(f"rks_S{z}", [128, KC, 64], F32)
                    ch["tA"] = sbt(f"rks_tA{z}", [128, KC, 64], BF16)
                    ch["vd"] = sbt(f"rks_vd{z}", [128, KC, 64], BF16)
                    ch["tF"] = sbt(f"rks_tF{z}", [128, KC, 64], F32)
                    ch["kb"] = [sbt(f"rks_kb{z}{i}", [128, 5, KC, TB], F32) for i in range(2)]
                    ch["vb"] = [sbt(f"rks_vb{z}{i}", [128, KC, TB], F32) for i in range(2)]
                    ch["ys"] = sbt(f"rks_ys{z}", [2, SB, KC * 64], BF16)
                    for n_ in ("S", "tA", "vd", "tF", "ys"):
                        ch[n_ + "_b"] = Buf()
                    ch["kb_b"] = [Buf() for _ in range(2)]
                    ch["vb_b"] = [Buf() for _ in range(2)]
                    CH.append(ch)
                e_t1 = k.pool if self.cfg.get("rk_pool", True) else k.dve
                veng = lambda e: (nc.gpsimd if e is k.pool else nc.vector)

                def load_blk(ch, t0, i):
                    z = ch["z"]
                    k.dma(k.sp, ch["kb"][i][:], KZ[z][:, :, :, t0:t0 + TB], reads=[scr_b], writes=[ch["kb_b"][i]])
                    k.dma(k.sp, ch["vb"][i][:], VV[:, :, t0:t0 + TB], reads=[scr_b], writes=[ch["vb_b"][i]])

                def step(ch, i, j, t, sidx):
                    S, Sb = ch["S"], ch["S_b"]
                    kb, kbb = ch["kb"][i], ch["kb_b"][i]
                    col = lambda a: kb[:, a, :, j:j + 1].to_broadcast([128, KC, 64])
                    vcol = ch["vb"][i][:, :, j:j + 1].to_broadcast([128, KC, 64])
                    k.op(e_t1, lambda: veng(e_t1).tensor_tensor(out=ch["tA"][:], in0=S[:], in1=col(0), op=ALU.mult),
                         reads=[Sb, kbb], writes=[ch["tA_b"]])
                    k.op(k.pool, lambda: nc.gpsimd.tensor_tensor(out=ch["vd"][:], in0=I2.unsqueeze(1).to_broadcast([128, KC, 64]), in1=vcol, op=ALU.mult),
                         reads=[rpk_b, ch["vb_b"][i]], writes=[ch["vd_b"]])
                    tAf = ch["tA"][:].rearrange("p g v -> p (g v)")
                    vdf = ch["vd"][:].rearrange("p g v -> p (g v)")
                    psa, psv = [], []
                    for q in range(2):
                        p_, pb_ = self.next_ps()
                        k.op(k.pe, lambda: nc.tensor.matmul(p_[:], BO[:], tAf[:, q * 512:(q + 1) * 512], start=True, stop=True),
                             reads=[cst_b, ch["tA_b"]], writes=[pb_])
                        psa.append((p_, pb_))
                    for q in range(2):
                        p_, pb_ = self.next_ps()
                        k.op(k.pe, lambda: nc.tensor.matmul(p_[:], BO[:], vdf[:, q * 512:(q + 1) * 512], start=True, stop=True),
                             reads=[cst_b, ch["vd_b"]], writes=[pb_])
                        psv.append((p_, pb_))
                    k.op(k.dve, lambda: nc.vector.tensor_tensor(out=S[:], in0=S[:], in1=col(1), op=ALU.mult), reads=[Sb, kbb], writes=[Sb])
                    for q in range(2):
                        p_, pb_ = psa[q]
                        gs = slice(q * 8, (q + 1) * 8)
                        k.op(k.dve, lambda: nc.vector.tensor_tensor(out=ch["tF"][:, gs, :], in0=p_[:].rearrange("p (g v) -> p g v", v=64),
                                                                    in1=kb[:, 2, gs, j:j + 1].to_broadcast([128, 8, 64]), op=ALU.mult),
                             reads=[pb_, kbb], writes=[ch["tF_b"]])
                    k.op(k.dve, lambda: nc.vector.tensor_tensor(out=S[:], in0=S[:], in1=ch["tF"][:], op=ALU.subtract),
                         reads=[Sb, ch["tF_b"]], writes=[Sb])
                    for q in range(2):
                        p_, pb_ = psv[q]
                        gs = slice(q * 8, (q + 1) * 8)
                        k.op(k.dve, lambda: nc.vector.tensor_tensor(out=ch["tF"][:, gs, :], in0=p_[:].rearrange("p (g v) -> p g v", v=64),
                                                                    in1=kb[:, 3, gs, j:j + 1].to_broadcast([128, 8, 64]), op=ALU.mult),
                             reads=[pb_, kbb], writes=[ch["tF_b"]])
                    k.op(k.dve, lambda: nc.vector.tensor_tensor(out=S[:], in0=S[:], in1=ch["tF"][:], op=ALU.add),
                         reads=[Sb, ch["tF_b"]], writes=[Sb])
                    k.op(e_t1, lambda: veng(e_t1).tensor_tensor(out=ch["tA"][:], in0=S[:], in1=col(4), op=ALU.mult),
                         reads=[Sb, kbb], writes=[ch["tA_b"]])
                    for q in range(2):
                        p_, pb_ = self.next_ps()
                        k.op(k.pe, lambda: nc.tensor.matmul(p_[0:2, :], hsel[:], tAf[:, q * 512:(q + 1) * 512], start=True, stop=True),
                             reads=[cst_b, ch["tA_b"]], writes=[pb_])
                        k.op(k.act, lambda: nc.scalar.copy(out=ch["ys"][:, sidx, q * 512:(q + 1) * 512], in_=p_[0:2, :]),
                             reads=[pb_], writes=[ch["ys_b"]])

                def run_pass(T0, T1, is_L):
                    nblk = (T1 - T0) // TB
                    for ch in CH:
                        z = ch["z"]
                        first_t0 = T0 if z == 0 else T1 - TB
                        load_blk(ch, first_t0, 0)
                    for bi in range(nblk):
                        for ch in CH:
                            z = ch["z"]
                            if bi + 1 < nblk:
                                nt0 = T0 + (bi + 1) * TB if z == 0 else T1 - (bi + 2) * TB
                                load_blk(ch, nt0, (bi + 1) % 2)
                        for jj in range(TB):
                            for ch in CH:
                                z = ch["z"]
                                t0 = T0 + bi * TB if z == 0 else T1 - (bi + 1) * TB
                                j = jj if z == 0 else TB - 1 - jj
                                t = t0 + j
                                seg = t // SEG
                                at_start = (t % SEG == 0) if z == 0 else (t % SEG == SEG - 1)
                                at_end = (t % SEG == SEG - 1) if z == 0 else (t % SEG == 0)
                                S, Sb = ch["S"], ch["S_b"]
                                if at_start:
                                    chain_start = (t == T0) if z == 0 else (t == T1 - 1)
                                    if not is_L:
                                        k.op(k.dve, lambda: nc.vector.memset(S[:], 0.0), writes=[Sb])
                                    elif chain_start:
                                        k.dma(k.sp, S[:].rearrange("p g v -> p (g v)"), self.I("rk_initS")[z], writes=[Sb])
                                    else:
                                        k.op(k.dve, lambda: nc.vector.tensor_scalar(out=S[:], in0=S[:], scalar1=flag, scalar2=None, op0=ALU.mult),
                                             reads=[Sb, rpk_b], writes=[Sb])
                                sidx = (jj % SB) if z == 0 else (SB - 1 - (jj % SB))
                                step(ch, bi % 2, j, t, sidx)
                                if jj % SB == SB - 1:
                                    tlo = t - (SB - 1) if z == 0 else t
                                    dst = YT[z][tlo:tlo + SB, :].rearrange("t (g hp v) -> hp t g v", hp=2, v=64)
                                    k.dma(k.sp, dst, ch["ys"][:].rearrange("p s (g v) -> p s g v", v=64), reads=[ch["ys_b"]], appends=[yt_b])
                                if at_end:
                                    k.dma(k.sp, self.O("rk_stS")[seg, z], S[:].rearrange("p g v -> p (g v)"), reads=[Sb], writes=[outb])

                run_pass(0, 1024, True)
                run_pass(1024, 1280, False)
                run_pass(1280, 1536, False)
                self.phase_barrier()
            with contextlib.ExitStack() as es2:
                sbt = lambda name, shape, dt: es2.enter_context(_sbuf(nc, name, list(shape), dt))
                rln = sbt("rk_rln", [128, 2, D], F32)
                rln_b = Buf()
                k.dma(k.sp, rln[:], self.I("rln").rearrange("p (a d) -> p a d", a=2), writes=[rln_b])
                yf = [sbt(f"rkc_yf{i}", [128, D], BF16) for i in range(2)]
                yb = [sbt(f"rkc_yb{i}", [128, D], BF16) for i in range(2)]
                yfb = [Buf() for _ in range(2)]
                ybb = [Buf() for _ in range(2)]
                ysum = sbt("rkc_ysum", [128, 32, 64], F32)
                ysq = sbt("rkc_ysq", [128, 32, 64], F32)
                ysum_b, ysq_b = Buf(), Buf()
                st1 = sbt("rkc_st1", [128, 32], F32)
                st2 = sbt("rkc_st2", [128, 32], F32)
                st1_b, st2_b = Buf(), Buf()
                yT = sbt("rkc_yT", [128, KC, 128], F32)
                yT_b = Buf()
                gl = [sbt(f"rkc_gl{i}", [128, KC, 128], F32) for i in range(2)]
                vl = [sbt(f"rkc_vl{i}", [128, KC, 128], F32) for i in range(2)]
                glb = [Buf() for _ in range(2)]
                vlb = [Buf() for _ in range(2)]
                zT = sbt("rkc_zT", [128, KC, 128], BF16)
                zT_b = Buf()

                def loadc(blk):
                    i = blk % 2
                    sl = slice(blk * 128, (blk + 1) * 128)
                    k.dma(k.sp, yf[i][:], YT[0][sl, :], reads=[yt_b], writes=[yfb[i]])
                    k.dma(k.sp, yb[i][:], YT[1][sl, :], reads=[yt_b], writes=[ybb[i]])
                    k.dma(k.sp, gl[i][:], GG[:, :, sl], reads=[scr_b], writes=[glb[i]])
                    k.dma(k.sp, vl[i][:], VB[:, :, sl], reads=[scr_b], writes=[vlb[i]])

                loadc(0)
                NBk = NTOK // 128
                for blk in range(NBk):
                    i = blk % 2
                    if blk + 1 < NBk:
                        loadc(blk + 1)
                    ysf = ysum[:].rearrange("p h v -> p (h v)")
                    k.op(k.dve, lambda: nc.vector.tensor_tensor(out=ysf, in0=yf[i][:], in1=yb[i][:], op=ALU.add),
                         reads=[yfb[i], ybb[i]], writes=[ysum_b])
                    k.op(k.dve, lambda: nc.vector.tensor_reduce(out=st1[:], in_=ysum[:], axis=AX.X, op=ALU.add), reads=[ysum_b], writes=[st1_b])
                    k.op(k.dve, lambda: nc.vector.tensor_scalar(out=st1[:], in0=st1[:], scalar1=1.0 / 64, scalar2=None, op0=ALU.mult),
                         reads=[st1_b], writes=[st1_b])
                    k.op(k.dve, lambda: nc.vector.tensor_tensor(out=ysum[:], in0=ysum[:], in1=st1[:].unsqueeze(2).to_broadcast([128, 32, 64]),
                                                                op=ALU.subtract), reads=[ysum_b, st1_b], writes=[ysum_b])
                    k.op(k.act, lambda: nc.scalar.activation(out=ysq[:], in_=ysum[:], func=AF.Square), reads=[ysum_b], writes=[ysq_b])
                    k.op(k.dve, lambda: nc.vector.tensor_reduce(out=st2[:], in_=ysq[:], axis=AX.X, op=ALU.add), reads=[ysq_b], writes=[st2_b])
                    k.op(k.dve, lambda: nc.vector.tensor_scalar(out=st2[:], in0=st2[:], scalar1=1.0 / 64, scalar2=64e-5, op0=ALU.mult, op1=ALU.add),
                         reads=[st2_b], writes=[st2_b])
                    k.op(k.act, lambda: nc.scalar.activation(out=st2[:], in_=st2[:], func=AF.Sqrt), reads=[st2_b], writes=[st2_b])
                    k.op(k.dve, lambda: nc.vector.reciprocal(out=st2[:], in_=st2[:]), reads=[st2_b], writes=[st2_b])
                    k.op(k.dve, lambda: nc.vector.tensor_tensor(out=ysum[:], in0=ysum[:], in1=st2[:].unsqueeze(2).to_broadcast([128, 32, 64]),
                                                                op=ALU.mult), reads=[ysum_b, st2_b], writes=[ysum_b])
                    k.op(k.dve, lambda: nc.vector.tensor_tensor(out=ysf, in0=ysf, in1=rln[:, 0, :], op=ALU.mult), reads=[ysum_b, rln_b], writes=[ysum_b])
                    k.op(k.dve, lambda: nc.vector.tensor_tensor(out=ysf, in0=ysf, in1=rln[:, 1, :], op=ALU.add), reads=[ysum_b, rln_b], writes=[ysum_b])
                    for q4 in range(4):
                        ptr, ptrb = self.next_ps()
                        for c4 in range(4):
                            c = q4 * 4 + c4
                            k.op(k.pe, lambda: nc.tensor.transpose(ptr[:, c4 * 128:(c4 + 1) * 128], ysf[:, c * 128:(c + 1) * 128], ident),
                                 reads=[ysum_b, rpk_b], writes=[ptrb])
                        k.op(k.act, lambda: nc.scalar.copy(out=yT[:, q4 * 4:(q4 + 1) * 4, :], in_=ptr[:].rearrange("p (c t) -> p c t", t=128)),
                             reads=[ptrb], writes=[yT_b])
                    k.op(k.dve, lambda: nc.vector.tensor_tensor(out=yT[:], in0=yT[:], in1=vl[i][:], op=ALU.add), reads=[yT_b, vlb[i]], writes=[yT_b])
                    k.op(k.dve, lambda: nc.vector.tensor_tensor(out=zT[:], in0=yT[:], in1=gl[i][:], op=ALU.mult), reads=[yT_b, glb[i]], writes=[zT_b])
                    k.dma(k.sp, hsT_s[:, :, blk * 128:(blk + 1) * 128].rearrange("c p t -> p c t"), zT[:], reads=[zT_b], appends=[hsT_b])
                self.phase_barrier()
            with contextlib.ExitStack() as es2:
                sbt = lambda name, shape, dt: es2.enter_context(_sbuf(nc, name, list(shape), dt))
                h2 = sbt("rk_h2", [128, KC, NTOK], BF16)
                h2b = [Buf() for _ in range(NGRP)]
                for t in range(NGRP):
                    k.dma(k.sp, h2[:, :, t * TT:(t + 1) * TT], hsT_s[:, :, t * TT:(t + 1) * TT].rearrange("c p t -> p c t"),
                          reads=[hsT_b], writes=[h2b[t]])
                wo_v = self.I("rwkv_w_o")[slot].rearrange("(kc p) n -> p kc n", p=128)

                def ev_o(oc, t, ps, psb):
                    ts = slice(t * TT, (t + 1) * TT)
                    k.op(k.dve, lambda: nc.vector.scalar_tensor_tensor(
                        out=self.x_sb[:, oc, ts], in0=ps[:], scalar=self.modT[:, l, 5, oc, t:t + 1],
                        in1=self.x_sb[:, oc, ts], op0=ALU.mult, op1=ALU.add),
                        reads=[psb, self.modT_b, self.xb[oc][t]], writes=[self.xb[oc][t]])

                self.linear_fm(h2, h2b, wo_v, 0, D, ev_o, "ro")
                self.phase_barrier()

    def layer(self, l):
        cfg = self.cfg
        ph = cfg.get("phases", ("ffn1", "mix", "ffn2"))
        if "ffn1" in ph:
            self.ffn(l, 0)
        if "mix" in ph:
            if l % 3 == 0:
                self.attn(l)
            elif l % 3 == 1:
                self.mlstm(l)
            else:
                self.rwkv(l)
        if "ffn2" in ph:
            self.ffn(l, 1)

    def finish(self):
        k = self.k
        yv = self.O("yT").rearrange("(c p) t -> p c t", p=128)
        outb = Buf("out")
        for c in range(KC):
            k.dma(k.sp, yv[:, c, :], self.x_sb[:, c, :], reads=self.xb[c], writes=[outb])
        self.phase_barrier()


def build_program(cfg):
    p = Prog(cfg)
    nc = p.build()
    nc._prog = p
    return nc


def filter_maps(nc, maps):
    names = set(nc._prog.inputs.keys())
    return [{k_: v for k_, v in m.items() if k_ in names} for m in maps]


def rope_tables(core):
    cos = np.ones((128, NTOK), np.float32)
    sin = np.zeros((128, NTOK), np.float32)
    if core < 4:
        t = np.arange(1024)
        row = (t // 64).astype(np.float32)
        col = (t % 64).astype(np.float32)
        inv = (10000.0 ** (-np.arange(32, dtype=np.float32) / 32)).astype(np.float32)
        for d in range(128):
            pos = row if d < 64 else col
            ang = pos * inv[d % 32]
            cos[d, :1024] = np.cos(ang)
            sin[d, :1024] = np.sin(ang)
    return cos, sin


def perm_T():
    PT = np.zeros((128, 128), np.float32)
    for m in range(128):
        if (m % 64) < 32:
            PT[m + 32, m] = -1.0
        else:
            PT[m - 32, m] = 1.0
    return PT


def attn_masks(core):
    M = np.zeros((128, 14, 128), np.float32)
    iq = np.arange(128)[None, :]
    is_ = np.arange(128)[:, None]
    for qb in range(8):
        if qb >= 1:
            if core < 4:
                M[:, 2 * (qb - 1), :] = (iq <= is_)
            else:
                M[:, 2 * (qb - 1), :] = 1.0 if (qb // 2 == (qb - 1) // 2) else 0.0
        if qb <= 6:
            if core < 4:
                M[:, 2 * qb + 1, :] = (is_ <= iq)
            else:
                M[:, 2 * qb + 1, :] = 1.0 if (qb // 2 == (qb + 1) // 2) else 0.0
    return M.reshape(128, 14 * 128)


def attn_pack(core, inp):
    A = np.zeros((2, 128, NAP), np.float32)
    cos, sin = rope_tables(core)
    for slot in range(2):
        A[slot, :, 0] = inp["attn_q_norm"][slot]
        A[slot, :, 1] = inp["attn_k_norm"][slot]
        A[slot, :, 2:18] = inp["attn_sink"][slot][None, :]
        A[slot, :, 18] = 1.0 if core < 4 else 0.0
        A[slot, :, 19:147] = np.eye(128, dtype=np.float32)
        A[slot, :, 147:147 + NTOK] = cos
        A[slot, :, 147 + NTOK:] = sin
    return A


def mlstm_pack(core, inp):
    M = np.zeros((1, 128, NMP), np.float32)
    p = np.arange(128)[:, None]
    f = np.arange(128)[None, :]
    M[0, :, 0:128] = np.eye(128, dtype=np.float32)
    M[0, :, 128:256] = 1.0
    M[0, :, 256:384] = (p <= f)
    M[0, :, 384:512] = (p >= f)
    M[0, :, 512:640] = np.where(p <= f, 0.0, -1e30)
    M[0, :, 640:768] = np.where(p >= f, 0.0, -1e30)
    M[0, :, 768:800] = inp["mlstm_b_gate"][0][None, :]
    M[0, :, 800] = 1.0 if core < 4 else 0.0
    M[0, :, 801:801 + D] = inp["mlstm_out_norm"][0][None, :]
    return M


def mlstm_init(core, inp):
    C0 = np.zeros((2, 128, 8, 257), np.float32)
    m0 = np.zeros((2, 128, 8), np.float32)
    if core < 4:
        C = inp["state_mlstm_C"][core, 0]
        n = inp["state_mlstm_n"][core, 0]
        m = inp["state_mlstm_m"][core, 0]
        C0[:, :, :, :256] = np.transpose(C, (0, 2, 1, 3))
        C0[:, :, :, 256] = np.transpose(n, (0, 2, 1))
        m0[:] = m[:, None, :]
    return C0.reshape(2, 128, 8 * 257), m0


def rwkv_host(core, inp):
    segs = core_segments(core)
    P = np.zeros((128, NRP), np.float32)
    NFM = 13 * KC
    vecs = [inp["rwkv_mu"][0][i] for i in range(6)] + [inp["rwkv_w0"][0][0], inp["rwkv_w0"][0][1], inp["rwkv_a0"][0][0], inp["rwkv_a0"][0][1],
                                                       inp["rwkv_k_k"][0], inp["rwkv_k_a"][0], inp["rwkv_r_k"][0].reshape(-1)]
    P[:, 0:NFM] = fm(np.stack(vecs, 0)).reshape(128, NFM)
    I2 = np.zeros((128, 64), np.float32)
    I2[np.arange(128), np.arange(128) % 64] = 1.0
    P[:, NFM:NFM + 64] = I2
    P[:, NFM + 64] = 1.0 if core < 4 else 0.0
    P[:, NFM + 65:NFM + 193] = np.eye(128, dtype=np.float32)
    seq_id = []
    for g in range(NSEG):
        kind, idx = segs[g]
        seq_id += [(0 if kind == "lat" else 1, idx)] * SEG
    pm = np.zeros(NTOK, np.float32)
    nm = np.zeros(NTOK, np.float32)
    for t in range(NTOK):
        if t > 0 and seq_id[t - 1] == seq_id[t]:
            pm[t] = 1.0
        if t < NTOK - 1 and seq_id[t + 1] == seq_id[t]:
            nm[t] = 1.0
    rmask = np.ascontiguousarray(np.broadcast_to(np.concatenate([pm, nm])[None, :], (128, 2 * NTOK)))
    rln = np.ascontiguousarray(np.broadcast_to(np.concatenate([inp["rwkv_ln_g"][0], inp["rwkv_ln_b"][0]])[None, :], (128, 2 * D)))
    initS = np.zeros((2, 128, 1024), np.float32)
    if core < 4:
        S0 = inp["state_rwkv"][core, 0]
        S0 = S0.reshape(2, 16, 2, 64, 64)
        initS = np.ascontiguousarray(np.transpose(S0, (0, 2, 4, 1, 3))).reshape(2, 128, 1024)
    return P, rmask, rln, initS


def make_in_maps(inp, cfg, cores=None):
    maps = []
    layers = list(cfg["layers"])
    ph = cfg.get("phases", ("ffn1", "mix", "ffn2"))
    full = (layers == [0, 1, 2, 3])
    mod_w_sel = inp["mod_w"] if full else np.ascontiguousarray(inp["mod_w"][layers])
    has_ffn = ("ffn1" in ph or "ffn2" in ph)
    if has_ffn:
        ffn_in_sel = inp["ffn_w_in"] if full else np.ascontiguousarray(inp["ffn_w_in"][layers])
        ffn_out_sel = inp["ffn_w_out"] if full else np.ascontiguousarray(inp["ffn_w_out"][layers])
    for core in (range(NCORES) if cores is None else cores):
        segs = core_segments(core)
        rows = []
        for g in range(NSEG):
            kind, idx = segs[g]
            if kind == "lat":
                rows.append(inp["x_sample"][idx, g * SEG:(g + 1) * SEG])
            else:
                rows.append(inp["x_prompt"][idx])
        xs = np.concatenate(rows, 0)
        m = {
            "xT": np.ascontiguousarray(xs.T),
            "pack": host_pack(core, inp),
            "mod_w": mod_w_sel,
            "attn_w_qkv": inp["attn_w_qkv"],
            "attn_w_o": inp["attn_w_o"],
            "apack": attn_pack(core, inp),
            "amask": attn_masks(core),
            "permT": perm_T(),
            "cache_k": (inp["cache_k"][core].reshape(2, 256, 512) if core < 4 else np.zeros((2, 256, 512), np.float32)),
            "cache_v": (inp["cache_v"][core].reshape(2, 256, 512) if core < 4 else np.zeros((2, 256, 512), np.float32)),
        }
        m["rpack"], m["rmask"], m["rln"], m["rk_initS"] = rwkv_host(core, inp)
        for nm_ in ("rwkv_w_rkv", "rwkv_wA", "rwkv_wB", "rwkv_aA", "rwkv_aB", "rwkv_gA", "rwkv_gB", "rwkv_w_o"):
            m[nm_] = inp[nm_]
        m["mlstm_w_in"] = inp["mlstm_w_in"]
        m["mlstm_w_gate"] = inp["mlstm_w_gate"]
        m["mlstm_w_o"] = inp["mlstm_w_o"]
        m["mpack"] = mlstm_pack(core, inp)
        m["ml_initC"], m["ml_initm"] = mlstm_init(core, inp)
        if has_ffn:
            m["ffn_w_in"] = ffn_in_sel
            m["ffn_w_out"] = ffn_out_sel
        maps.append(m)
    return maps


FULL_CFG = {"layers": [0, 1, 2, 3], "phases": ("ffn1", "mix", "ffn2")}


def kernel(**inputs):
    inp = {k_: np.asarray(v) for k_, v in inputs.items()}
    cfg = FULL_CFG
    nc = build_program(cfg)
    maps = filter_maps(nc, make_in_maps(inp, cfg))
    res = run_bass_kernel_spmd(nc, maps, core_ids=list(range(NCORES)))
    R = res.results
    B, S_, DB, DS = 32, 256, 4, 1024
    y_prompt = np.zeros((B, S_, D), np.float32)
    y_sample = np.zeros((DB, DS, D), np.float32)
    new_k = np.zeros((B, 2, S_, 4, 128), np.float32)
    new_v = np.zeros((B, 2, S_, 4, 128), np.float32)
    new_C = np.zeros((B, 1, 2, 8, 128, 256), np.float32)
    new_n = np.zeros((B, 1, 2, 8, 128), np.float32)
    new_m = np.zeros((B, 1, 2, 8), np.float32)
    new_S = np.zeros((B, 1, 2, 32, 64, 64), np.float32)
    for core in range(NCORES):
        r = R[core]
        y = np.asarray(r["yT"]).T
        segs = core_segments(core)
        for g in range(NSEG):
            kind, idx = segs[g]
            rows = y[g * SEG:(g + 1) * SEG]
            if kind == "lat":
                y_sample[idx, g * SEG:(g + 1) * SEG] = rows
            else:
                y_prompt[idx] = rows
                for slot in range(2):
                    new_k[idx, slot] = np.asarray(r["newk"])[slot, g * SEG:(g + 1) * SEG].reshape(S_, 4, 128)
                    new_v[idx, slot] = np.asarray(r["newv"])[slot, g * SEG:(g + 1) * SEG].reshape(S_, 4, 128)
                Ck = np.asarray(r["ml_stC"])[g].reshape(2, 128, 8, 257)
                new_C[idx, 0] = np.transpose(Ck[..., :256], (0, 2, 1, 3))
                new_n[idx, 0] = np.transpose(Ck[..., 256], (0, 2, 1))
                new_m[idx, 0] = np.asarray(r["ml_stm"])[g][:, 0, :]
                Sk = np.asarray(r["rk_stS"])[g].reshape(2, 2, 64, 16, 64)
                new_S[idx, 0] = np.transpose(Sk, (0, 3, 1, 4, 2)).reshape(2, 32, 64, 64)
    return (y_prompt, y_sample, new_k, new_v, new_C, new_n, new_m, new_S)
```

```python
import contextlib
import numpy as np
import concourse.bass as bass
import concourse.mybir as mybir
from concourse.bass_utils import run_bass_kernel_spmd

F32 = mybir.dt.float32
BF16 = mybir.dt.bfloat16
AF = mybir.ActivationFunctionType
ALU = mybir.AluOpType
AX = mybir.AxisListType

D = 2048
KC = 16
NTOK = 1536
NSEG = 6
SEG = 256
NGRP = 3
TT = 512
DEPTH = 4
DFF = 5632
NMOD = 9
EPS = 1e-6
NCORES = 8
NAP = 147 + 2 * NTOK
NMP = 801 + D
NRP = 13 * KC + 64 + 1 + 128


_UNIQ = [0]


def _sbuf(nc, name, shape, dt):
    _UNIQ[0] += 1
    return nc.sbuf_tensor(f"{name}_u{_UNIQ[0]}", shape, dt)


class Buf:
    __slots__ = ("w", "r", "name", "excl", "mw")

    def __init__(self, name="", excl=False):
        self.w = None
        self.r = {}
        self.mw = {}
        self.name = name
        self.excl = excl


class Eng:
    def __init__(self, k, name, handle, is_pe=False):
        self.k = k
        self.name = name
        self.h = handle
        self.is_pe = is_pe
        self.sem = None
        self.count = 0
        self.waited = {}
        self.nsem = 0

    def cur_sem(self):
        if self.sem is None or self.count >= 30000:
            self.sem = self.k.new_sem(f"{self.name}{self.nsem}")
            self.nsem += 1
            self.count = 0
        return self.sem


class K:
    def __init__(self, nc, es):
        self.nc = nc
        self.es = es
        self.semcount = 0
        self.pe = Eng(self, "pe", nc.tensor, is_pe=True)
        self.act = Eng(self, "act", nc.scalar)
        self.dve = Eng(self, "dve", nc.vector)
        self.pool = Eng(self, "pool", nc.gpsimd)
        self.sp = Eng(self, "sp", nc.sync)
        self.dma_sems = []
        self.dma_rr = 0
        self.sync_same_engine = True
        self.nosync_engines = set()
        self.n_inst = 0

    def new_sem(self, name):
        self.semcount += 1
        return self.es.enter_context(self.nc.semaphore(f"s_{name}_{self.semcount}"))

    def sb(self, name, shape, dt):
        return self.es.enter_context(_sbuf(self.nc, name, list(shape), dt))

    def _wait_deps(self, eng, reads, writes, appends=()):
        deps = {}
        for b in appends:
            if b.w is not None:
                s, v = b.w
                if deps.get(s, 0) < v:
                    deps[s] = v
            for s, v in b.r.items():
                if deps.get(s, 0) < v:
                    deps[s] = v
        for b in reads:
            if b.w is not None:
                s, v = b.w
                if deps.get(s, 0) < v:
                    deps[s] = v
            for s, v in b.mw.items():
                if deps.get(s, 0) < v:
                    deps[s] = v
            if b.excl:
                for s, v in b.r.items():
                    if deps.get(s, 0) < v:
                        deps[s] = v
        for b in writes:
            if b.w is not None:
                s, v = b.w
                if deps.get(s, 0) < v:
                    deps[s] = v
            for s, v in b.r.items():
                if deps.get(s, 0) < v:
                    deps[s] = v
            for s, v in b.mw.items():
                if deps.get(s, 0) < v:
                    deps[s] = v
        for s, v in deps.items():
            if eng.is_pe and s is eng.sem:
                continue
            if (not self.sync_same_engine) and s is eng.sem:
                continue
            if s is eng.sem and eng.name in self.nosync_engines:
                continue
            if eng.waited.get(s, 0) < v:
                eng.h.wait_ge(s, v)
                eng.waited[s] = v

    def _mark(self, tok, reads, writes, appends=()):
        s, v = tok
        for b in reads:
            if b.r.get(s, 0) < v:
                b.r[s] = v
        for b in writes:
            b.w = tok
            b.r = {}
            b.mw = {}
        for b in appends:
            if b.mw.get(s, 0) < v:
                b.mw[s] = v

    def op(self, eng, fn, reads=(), writes=()):
        self._wait_deps(eng, reads, writes)
        sem = eng.cur_sem()
        ins = fn()
        ins.then_inc(sem, 1)
        eng.count += 1
        self.n_inst += 1
        self._mark((sem, eng.count), reads, writes)

    def dma(self, eng, out, in_, reads=(), writes=(), appends=(), **kw):
        if len(self.dma_sems) < 24:
            self.dma_sems.append([self.new_sem(f"dma{len(self.dma_sems)}"), 0])
            ent = self.dma_sems[-1]
        else:
            ent = self.dma_sems[self.dma_rr % len(self.dma_sems)]
            self.dma_rr += 1
        sem, cnt = ent
        if cnt >= 1800:
            ent[0] = sem = self.new_sem("dmax")
            ent[1] = cnt = 0
        self._wait_deps(eng, reads, writes, appends)
        if cnt > 0 and eng.waited.get(sem, 0) < cnt * 16:
            eng.h.wait_ge(sem, cnt * 16)
            eng.waited[sem] = cnt * 16
        eng.h.dma_start(out=out, in_=in_, **kw).then_inc(sem, 16)
        ent[1] = cnt + 1
        self.n_inst += 1
        self._mark((sem, (cnt + 1) * 16), reads, writes, appends)

    def wait_all(self, eng, bufs):
        self._wait_deps(eng, bufs, ())


def pack_layout():
    lay = {}
    off = 0

    def add(name, width):
        nonlocal off
        lay[name] = (off, width)
        off += width

    add("cond", KC * NGRP)
    add("modb", DEPTH * NMOD * KC)
    add("normg", DEPTH * 3 * KC)
    lay["_total"] = off
    return lay


def core_segments(core):
    if core < 4:
        return [("lat", core)] * 4 + [("ctx", 2 * core), ("ctx", 2 * core + 1)]
    base = 8 + (core - 4) * 6
    return [("ctx", base + j) for j in range(6)]


def fm(vec):
    v = np.asarray(vec, np.float32)
    lead = v.shape[:-1]
    v = v.reshape(lead + (KC, 128))
    v = np.moveaxis(v, -1, 0)
    return np.ascontiguousarray(v)


def host_pack(core, inp):
    lay = pack_layout()
    P = np.zeros((128, lay["_total"]), np.float32)

    def put(name, arr):
        o, w = lay[name]
        a = np.asarray(arr, np.float32).reshape(128, -1)
        assert a.shape[1] == w, (name, a.shape, w)
        P[:, o:o + w] = a

    segs = core_segments(core)
    conds = []
    for g in range(NGRP):
        kind, idx = segs[2 * g]
        conds.append(inp["c"][idx] if kind == "lat" else inp["c_ctx"])
    cond = fm(np.stack(conds, 0))
    put("cond", np.transpose(cond, (0, 2, 1)))
    put("modb", fm(inp["mod_b"].reshape(DEPTH, NMOD, D)))
    put("normg", fm(inp["norm_g"]))
    return P


class Prog:
    def __init__(self, cfg):
        self.cfg = cfg
        self.lay = pack_layout()

    def build(self):
        cfg = self.cfg
        nc = bass.Bass("TRN2", target_bir_lowering=False)
        self.nc = nc
        es = contextlib.ExitStack()
        with es:
            k = K(nc, es)
            self.k = k
            k.nosync_engines = set(cfg.get("nosync", ()))
            self.declare_io()
            self.alloc_common()
            self.load_common()
            self.adaln_all()
            for l in cfg["layers"]:
                self.layer(l)
            self.finish()
        return nc

    def declare_io(self):
        nc = self.nc
        NL = len(self.cfg["layers"])
        self.lidx = {l: i for i, l in enumerate(self.cfg["layers"])}
        self.in_shapes = {
            "xT": [D, NTOK], "pack": [128, self.lay["_total"]], "mod_w": [NL, D, NMOD * D],
            "ffn_w_in": [NL, 2, D, 2 * DFF], "ffn_w_out": [NL, 2, DFF, D],
            "attn_w_qkv": [2, D, 3072], "attn_w_o": [2, D, D], "apack": [2, 128, NAP],
            "mlstm_w_in": [1, D, 6144], "mlstm_w_gate": [1, D, 32], "mlstm_w_o": [1, D, D], "mpack": [1, 128, NMP],
            "ml_initC": [2, 128, 8 * 257], "ml_initm": [2, 128, 8],
            "rpack": [128, NRP], "rmask": [128, 2 * NTOK], "rln": [128, 2 * D], "rk_initS": [2, 128, 1024],
            "rwkv_w_rkv": [1, 3, D, D], "rwkv_wA": [1, 2, D, 96], "rwkv_wB": [1, 2, 96, D], "rwkv_aA": [1, 2, D, 96],
            "rwkv_aB": [1, 2, 96, D], "rwkv_gA": [1, D, 256], "rwkv_gB": [1, 256, D], "rwkv_w_o": [1, D, D],
            "amask": [128, 14 * 128], "permT": [128, 128], "cache_k": [2, 256, 512], "cache_v": [2, 256, 512],
        }
        self.out_shapes = {"rk_stS": [6, 2, 128, 1024], "ml_stC": [6, 2, 128, 8 * 257], "ml_stm": [6, 2, 128, 8], "yT": [D, NTOK], "newk": [2, NTOK, 512], "newv": [2, NTOK, 512]}
        self.inputs = {}
        self.outputs = {}

    def I(self, name):
        if name not in self.inputs:
            self.inputs[name] = self.nc.dram_tensor(name, list(self.in_shapes[name]), F32, kind="ExternalInput").ap()
        return self.inputs[name]

    def dbg_out(self, name, shape):
        if name not in self.outputs:
            self.outputs[name] = self.nc.dram_tensor(name, list(shape), BF16, kind="ExternalOutput").ap()
        return self.outputs[name]

    def O(self, name):
        if name not in self.outputs:
            self.outputs[name] = self.nc.dram_tensor(name, list(self.out_shapes[name]), F32, kind="ExternalOutput").ap()
        return self.outputs[name]

    def alloc_common(self):
        k = self.k
        self.x_sb = k.sb("x_sb", [128, KC, NTOK], F32)
        self.xb = [[Buf(f"x{c}_{t}") for t in range(NGRP)] for c in range(KC)]
        self.pack = None
        self.pack_b = Buf("pack")
        self.modT = k.sb("modT", [128, DEPTH, NMOD, KC, NGRP], F32)
        self.modT_b = Buf("modT")
        self.ones_bf = k.sb("ones_bf", [128, 128], BF16)
        self.ones_b = Buf("ones")
        self.ps = [self.k.es.enter_context(self.nc.psum_tensor(f"ps{i}", [128, 512], F32)) for i in range(8)]
        self.psb = [Buf(f"ps{i}", excl=True) for i in range(8)]
        self.ps_rr = 0
        self.ps_reserved = set()

    def next_ps(self):
        while True:
            i = self.ps_rr % 8
            self.ps_rr += 1
            if i not in self.ps_reserved:
                return self.ps[i], self.psb[i]

    def reserve_ps(self):
        pt, pb = self.next_ps()
        i = self.ps.index(pt)
        self.ps_reserved.add(i)
        return pt, pb

    def release_ps(self, pt):
        self.ps_reserved.discard(self.ps.index(pt))

    def pk(self, name):
        o, w = self.lay[name]
        return self.pack[:, o:o + w]

    def load_common(self):
        k = self.k
        nc = self.nc
        xv = self.I("xT").rearrange("(c p) t -> p c t", p=128)
        for c in range(KC):
            k.dma(k.sp, self.x_sb[:, c, :], xv[:, c, :], writes=self.xb[c])
        k.op(k.dve, lambda: nc.vector.memset(self.ones_bf[:], 1.0), writes=[self.ones_b])

    def adaln_all(self):
        k = self.k
        nc = self.nc
        cfg = self.cfg
        with contextlib.ExitStack() as es2:
            self.pack = es2.enter_context(_sbuf(nc, "pack_sb", [128, self.lay["_total"]], F32))
            k.dma(k.sp, self.pack[:], self.I("pack")[:, :], writes=[self.pack_b])
            sc = es2.enter_context(_sbuf(nc, "ada_sc", [128, KC, NGRP], BF16))
            sc_b = Buf("sc")
            wts = [es2.enter_context(_sbuf(nc, f"ada_w{i}", [128, KC, 512], BF16)) for i in range(3)]
            wts_b = [Buf(f"adaw{i}") for i in range(3)]
            condv = self.pk("cond").rearrange("p (c g) -> p c g", g=NGRP)
            k.op(k.act, lambda: nc.scalar.activation(out=sc[:], in_=condv, func=AF.Silu),
                 reads=[self.pack_b], writes=[sc_b])
            modb = self.pk("modb").rearrange("p (l j c) -> p l j c", l=DEPTH, j=NMOD)
            NCT = NMOD * D // 512
            jobs = [(l, ct) for l in cfg["layers"] for ct in range(NCT)]

            def load(i):
                l, ct = jobs[i]
                src = self.I("mod_w")[self.lidx[l]].rearrange("(kc p) n -> p kc n", p=128)[:, :, ct * 512:(ct + 1) * 512]
                k.dma(k.pool, wts[i % 3][:], src, writes=[wts_b[i % 3]])

            for i in range(min(2, len(jobs))):
                load(i)
            for i, (l, ct) in enumerate(jobs):
                if i + 2 < len(jobs):
                    load(i + 2)
                w = wts[i % 3]
                wb = wts_b[i % 3]
                pst, psb = self.next_ps()
                for q in range(4):
                    for kc in range(KC):
                        k.op(k.pe, lambda q=q, kc=kc: nc.tensor.matmul(
                            pst[:, q * NGRP:(q + 1) * NGRP], w[:, kc, q * 128:(q + 1) * 128], sc[:, kc, :],
                            start=(kc == 0), stop=(kc == KC - 1)),
                            reads=[wb, sc_b], writes=[psb])
                j = (ct * 4) // KC
                c0 = (ct * 4) % KC
                k.op(k.dve, lambda l=l, j=j, c0=c0: nc.vector.tensor_tensor(
                    out=self.modT[:, l, j, c0:c0 + 4, :],
                    in0=pst[:, 0:4 * NGRP].rearrange("p (q g) -> p q g", g=NGRP),
                    in1=modb[:, l, j, c0:c0 + 4].unsqueeze(2).to_broadcast([128, 4, NGRP]),
                    op=ALU.add),
                    reads=[psb, self.pack_b], writes=[self.modT_b])
            ng = self.pk("normg").rearrange("p (l s c) -> p l s c", l=DEPTH, s=3)
            for l in cfg["layers"]:
                for s in range(3):
                    k.op(k.dve, lambda l=l, s=s: nc.vector.scalar_tensor_tensor(
                        out=self.modT[:, l, 3 * s + 1, :, :], in0=self.modT[:, l, 3 * s + 1, :, :], scalar=1.0,
                        in1=ng[:, l, s, :].unsqueeze(2).to_broadcast([128, KC, NGRP]),
                        op0=ALU.add, op1=ALU.mult),
                        reads=[self.pack_b, self.modT_b], writes=[self.modT_b])
                    if s != 1:
                        k.op(k.dve, lambda l=l, s=s: nc.vector.tensor_scalar(
                            out=self.modT[:, l, 3 * s + 2, :, :], in0=self.modT[:, l, 3 * s + 2, :, :],
                            scalar1=0.5, scalar2=None, op0=ALU.mult),
                            reads=[self.modT_b], writes=[self.modT_b])
            self.phase_barrier()

    def compute_h(self, l, s, h, hb, sq, sqb):
        k = self.k
        nc = self.nc
        for t in range(NGRP):
            ts = slice(t * TT, (t + 1) * TT)
            pst, psb = self.next_ps()
            for c in range(KC):
                i = c % 2
                k.op(k.act, lambda c=c, i=i: nc.scalar.activation(out=sq[i], in_=self.x_sb[:, c, ts], func=AF.Square),
                     reads=[self.xb[c][t]], writes=[sqb[i]])
                k.op(k.pe, lambda c=c, i=i: nc.tensor.matmul(pst[:], self.ones_bf[:], sq[i],
                                                             start=(c == 0), stop=(c == KC - 1)),
                     reads=[sqb[i], self.ones_b], writes=[psb])
            k.op(k.dve, lambda: nc.vector.tensor_scalar(out=pst[:], in0=pst[:], scalar1=1.0 / D, scalar2=EPS,
                                                        op0=ALU.mult, op1=ALU.add),
                 reads=[psb], writes=[psb])
            k.op(k.act, lambda: nc.scalar.activation(out=pst[:], in_=pst[:], func=AF.Sqrt), reads=[psb], writes=[psb])
            k.op(k.dve, lambda: nc.vector.reciprocal(out=pst[:], in_=pst[:]), reads=[psb], writes=[psb])
            for c in range(KC):
                pt, ptb = self.next_ps()
                if pt is pst:
                    pt, ptb = self.next_ps()
                k.op(k.dve, lambda c=c, pt=pt: nc.vector.tensor_tensor(out=pt[:], in0=self.x_sb[:, c, ts], in1=pst[:],
                                                                       op=ALU.mult),
                     reads=[self.xb[c][t], psb], writes=[ptb])
                k.op(k.act, lambda c=c, pt=pt: nc.scalar.activation(
                    out=h[:, c, ts], in_=pt[:], func=AF.Identity,
                    scale=self.modT[:, l, 3 * s + 1, c, t:t + 1], bias=self.modT[:, l, 3 * s, c, t:t + 1]),
                    reads=[ptb, self.modT_b], writes=[hb[t]])

    def ffn(self, l, j):
        k = self.k
        nc = self.nc
        s = 0 if j == 0 else 2
        FG = 256
        NFG = DFF // FG
        with contextlib.ExitStack() as es2:
            sbt = lambda name, shape, dt: es2.enter_context(_sbuf(nc, name, list(shape), dt))
            h = sbt("ffn_h", [128, KC, NTOK], BF16)
            hb = [Buf(f"h{t}") for t in range(NGRP)]
            tmpn = [sbt(f"ffn_tmpn{i}", [128, TT], BF16) for i in range(2)]
            tmpnb = [Buf() for _ in range(2)]
            wg = [sbt(f"ffn_wg{i}", [128, KC, FG], BF16) for i in range(2)]
            wu = [sbt(f"ffn_wu{i}", [128, KC, FG], BF16) for i in range(2)]
            wo = [sbt(f"ffn_wo{i}", [128, FG // 128, D], BF16) for i in range(2)]
            wgb = [Buf() for _ in range(2)]
            wub = [Buf() for _ in range(2)]
            wob = [Buf() for _ in range(2)]
            gsb = [sbt(f"ffn_g{i}", [128, FG // 128, TT], BF16) for i in range(2)]
            gsbb = [Buf() for _ in range(2)]
            sq = [gsb[i][:, 0, :] for i in range(2)]
            sqb = gsbb
            self.phase_barrier()

            w_in_v = self.I("ffn_w_in")[self.lidx[l], j].rearrange("(kc p) n -> p kc n", p=128)
            w_out_v = self.I("ffn_w_out")[self.lidx[l], j].rearrange("(fc p) n -> p fc n", p=128)

            def load(fg):
                i = fg % 2
                k.dma(k.pool, wg[i][:], w_in_v[:, :, fg * FG:(fg + 1) * FG], writes=[wgb[i]])
                k.dma(k.pool, wu[i][:], w_in_v[:, :, DFF + fg * FG:DFF + (fg + 1) * FG], writes=[wub[i]])
                k.dma(k.pool, wo[i][:], w_out_v[:, fg * (FG // 128):(fg + 1) * (FG // 128), :], writes=[wob[i]])

            load(0)
            self.compute_h(l, s, h, hb, sq, sqb)
            it = 0
            for fg in range(NFG):
                if fg + 1 < NFG:
                    load(fg + 1)
                i = fg % 2
                for t in range(NGRP):
                    ts = slice(t * TT, (t + 1) * TT)
                    gi = it % 2
                    it += 1
                    for hf in range(FG // 128):
                        pg, pgb = self.next_ps()
                        pu, pub = self.next_ps()
                        for kc in range(KC):
                            k.op(k.pe, lambda kc=kc, hf=hf, pg=pg: nc.tensor.matmul(
                                pg[:], wg[i][:, kc, hf * 128:(hf + 1) * 128], h[:, kc, ts],
                                start=(kc == 0), stop=(kc == KC - 1)),
                                reads=[wgb[i], hb[t]], writes=[pgb])
                        for kc in range(KC):
                            k.op(k.pe, lambda kc=kc, hf=hf, pu=pu: nc.tensor.matmul(
                                pu[:], wu[i][:, kc, hf * 128:(hf + 1) * 128], h[:, kc, ts],
                                start=(kc == 0), stop=(kc == KC - 1)),
                                reads=[wub[i], hb[t]], writes=[pub])
                        si = hf % 2
                        k.op(k.act, lambda pg=pg, si=si: nc.scalar.activation(out=tmpn[si][:], in_=pg[:], func=AF.Silu),
                             reads=[pgb], writes=[tmpnb[si]])
                        k.op(k.dve, lambda pu=pu, si=si, hf=hf, gi=gi: nc.vector.tensor_tensor(
                            out=gsb[gi][:, hf, :], in0=tmpn[si][:], in1=pu[:], op=ALU.mult),
                            reads=[tmpnb[si], pub], writes=[gsbb[gi]])
                    for dc in range(KC):
                        py, pyb = self.next_ps()
                        for hf in range(FG // 128):
                            k.op(k.pe, lambda hf=hf, dc=dc, py=py, gi=gi: nc.tensor.matmul(
                                py[:], wo[i][:, hf, dc * 128:(dc + 1) * 128], gsb[gi][:, hf, :],
                                start=(hf == 0), stop=(hf == FG // 128 - 1)),
                                reads=[wob[i], gsbb[gi]], writes=[pyb])
                        k.op(k.dve, lambda dc=dc, py=py, t=t, ts=ts: nc.vector.scalar_tensor_tensor(
                            out=self.x_sb[:, dc, ts], in0=py[:], scalar=self.modT[:, l, 3 * s + 2, dc, t:t + 1],
                            in1=self.x_sb[:, dc, ts], op0=ALU.mult, op1=ALU.add),
                            reads=[pyb, self.modT_b, self.xb[dc][t]], writes=[self.xb[dc][t]])
            self.phase_barrier()

    def phase_barrier(self):
        k = self.k
        toks = []
        for e in (k.pe, k.act, k.dve, k.pool, k.sp):
            if e.sem is not None and e.count > 0:
                toks.append((e.sem, e.count))
        for ent in k.dma_sems:
            if ent[1] > 0:
                toks.append((ent[0], ent[1] * 16))
        for e in (k.pe, k.act, k.dve, k.pool, k.sp):
            for s, v in toks:
                if s is e.sem:
                    continue
                if e.waited.get(s, 0) < v:
                    e.h.wait_ge(s, v)
                    e.waited[s] = v


    def linear_fm(self, h, hb, wview, col0, ncols, evac, tag):
        k, nc = self.k, self.nc
        TC = 256
        with contextlib.ExitStack() as es2:
            wt = [es2.enter_context(_sbuf(nc, f"lf_{tag}_w{i}", [128, KC, TC], BF16)) for i in range(2)]
            wtb = [Buf() for _ in range(2)]
            ntile = ncols // TC

            def load(ti):
                k.dma(k.pool, wt[ti % 2][:], wview[:, :, col0 + ti * TC:col0 + (ti + 1) * TC], writes=[wtb[ti % 2]])

            load(0)
            for ti in range(ntile):
                if ti + 1 < ntile:
                    load(ti + 1)
                w, wb = wt[ti % 2], wtb[ti % 2]
                for sub in range(TC // 128):
                    oc = ti * (TC // 128) + sub
                    for t in range(NGRP):
                        ts = slice(t * TT, (t + 1) * TT)
                        ps, psb = self.next_ps()
                        for kc in range(KC):
                            k.op(k.pe, lambda kc=kc: nc.tensor.matmul(ps[:], w[:, kc, sub * 128:(sub + 1) * 128], h[:, kc, ts],
                                                                      start=(kc == 0), stop=(kc == KC - 1)),
                                 reads=[wb, hb[t]], writes=[psb])
                        evac(oc, t, ps, psb)
            self.scope_end(wtb)

    def linear_tm(self, h, hb, wview, col0, ncols, evac, tag):
        k, nc = self.k, self.nc
        TC = 512
        with contextlib.ExitStack() as es2:
            wt = [es2.enter_context(_sbuf(nc, f"lt_{tag}_w{i}", [128, KC, TC], BF16)) for i in range(2)]
            wtb = [Buf() for _ in range(2)]
            ntile = ncols // TC

            def load(ti):
                k.dma(k.pool, wt[ti % 2][:], wview[:, :, col0 + ti * TC:col0 + (ti + 1) * TC], writes=[wtb[ti % 2]])

            load(0)
            for ti in range(ntile):
                if ti + 1 < ntile:
                    load(ti + 1)
                w, wb = wt[ti % 2], wtb[ti % 2]
                for blk in range(NTOK // 128):
                    ps, psb = self.next_ps()
                    for kc in range(KC):
                        k.op(k.pe, lambda kc=kc: nc.tensor.matmul(ps[:], h[:, kc, blk * 128:(blk + 1) * 128], w[:, kc, :],
                                                                  start=(kc == 0), stop=(kc == KC - 1)),
                             reads=[wb, hb[blk // 4]], writes=[psb])
                    evac(blk, ti, ps, psb)
            self.scope_end(wtb)

    def scope_end(self, bufs):
        k = self.k
        for e in (k.pe, k.act, k.dve, k.pool, k.sp):
            k._wait_deps(e, (), bufs)

    def attn(self, l):
        k = self.k
        nc = self.nc
        slot = l // 3
        NH, NKV = 16, 4
        qs = nc.dram_tensor(f"qs{l}", [NH, 128, NTOK], BF16).ap()
        ks = nc.dram_tensor(f"ks{l}", [NKV, 128, NTOK], BF16).ap()
        vs = nc.dram_tensor(f"vs{l}", [12, 128, 512], BF16).ap()
        qs_b, ks_b, vs_b = Buf("qs"), Buf("ks"), Buf("vs")
        outb = Buf("attn_out")
        with contextlib.ExitStack() as es1:
            sbt1 = lambda name, shape, dt: es1.enter_context(_sbuf(nc, name, list(shape), dt))
            self.phase_barrier()
            apk = sbt1("apack_sb", [128, NAP], F32)
            apk_b = Buf("apk")
            k.dma(k.sp, apk[:], self.I("apack")[slot], writes=[apk_b])
            qng = apk[:, 0:1]
            kng = apk[:, 1:2]
            sink = apk[:, 2:18]
            cflag = apk[:, 18:19]
            ident = apk[:, 19:147]
            cos = apk[:, 147:147 + NTOK]
            sin = apk[:, 147 + NTOK:147 + 2 * NTOK]
            with contextlib.ExitStack() as es2:
                sbt = lambda name, shape, dt: es2.enter_context(_sbuf(nc, name, list(shape), dt))
                h = sbt("at_h", [128, KC, NTOK], BF16)
                hb = [Buf(f"h{t}") for t in range(NGRP)]
                sq = [sbt(f"at_sq{i}", [128, TT], BF16) for i in range(2)]
                sqb = [Buf() for _ in range(2)]
                PT = sbt("at_PT", [128, 128], BF16)
                PT_b = Buf()
                k.dma(k.pool, PT[:], self.I("permT")[:, :], writes=[PT_b])
                wq = [sbt(f"at_wq{i}", [128, KC, 256], BF16) for i in range(2)]
                wqb = [Buf() for _ in range(2)]
                rawg = [sbt(f"at_rawg{i}", [128, TT], F32) for i in range(2)]
                rawgb = [Buf() for _ in range(2)]
                qnf = [sbt(f"at_qnf{i}", [128, TT], F32) for i in range(2)]
                qnfb = [Buf() for _ in range(2)]
                qnb = [sbt(f"at_qnb{i}", [128, TT], BF16) for i in range(2)]
                qnbb = [Buf() for _ in range(2)]
                t1 = [sbt(f"at_t1{i}", [128, TT], F32) for i in range(2)]
                t1b = [Buf() for _ in range(2)]
                qrb = [sbt(f"at_qrb{i}", [128, TT], BF16) for i in range(2)]
                qrbb = [Buf() for _ in range(2)]
                kout = [sbt(f"at_kout{i}", [128, 4, 128], F32) for i in range(2)]
                koutb = [Buf() for _ in range(2)]
                vf = [sbt(f"at_vf{i}", [128, 256], F32) for i in range(2)]
                vfb = [Buf() for _ in range(2)]
                vb = [sbt(f"at_vb{i}", [128, 256], BF16) for i in range(2)]
                vbb = [Buf() for _ in range(2)]
                wv_ = self.I("attn_w_qkv")[slot].rearrange("(kc p) n -> p kc n", p=128)

                def load(ct):
                    k.dma(k.pool, wq[ct % 2][:], wv_[:, :, ct * 256:(ct + 1) * 256], writes=[wqb[ct % 2]])

                load(0)
                self.compute_h(l, 1, h, hb, [t_[:] for t_ in sq], sqb)
                it = 0
                ct_list = self.cfg.get("ct_list", list(range(12)))
                for ct in ct_list:
                    if ct + 1 < 12 and (ct + 1) in ct_list:
                        load(ct + 1)
                    w = wq[ct % 2]
                    wb = wqb[ct % 2]
                    if ct < 10:
                        for hh in range(2):
                            head = ct * 2 + hh
                            is_k = head >= 16
                            gain = kng if is_k else qng
                            for t in range(NGRP):
                                ts = slice(t * TT, (t + 1) * TT)
                                i = it % 2
                                it += 1
                                praw, prawb = self.next_ps()
                                for kc in range(KC):
                                    k.op(k.pe, lambda kc=kc: nc.tensor.matmul(
                                        praw[:], w[:, kc, hh * 128:(hh + 1) * 128], h[:, kc, ts],
                                        start=(kc == 0), stop=(kc == KC - 1)), reads=[wb, hb[t]], writes=[prawb])
                                k.op(k.act, lambda: nc.scalar.activation(out=sq[i][:], in_=praw[:], func=AF.Square),
                                     reads=[prawb], writes=[sqb[i]])
                                k.op(k.act, lambda: nc.scalar.activation(out=rawg[i][:], in_=praw[:], func=AF.Identity, scale=gain),
                                     reads=[prawb, apk_b], writes=[rawgb[i]])
                                pss, pssb = self.next_ps()
                                k.op(k.pe, lambda: nc.tensor.matmul(pss[:], self.ones_bf[:], sq[i][:], start=True, stop=True),
                                     reads=[sqb[i], self.ones_b], writes=[pssb])
                                k.op(k.dve, lambda: nc.vector.tensor_scalar(out=pss[:], in0=pss[:], scalar1=1.0 / 128, scalar2=EPS,
                                                                            op0=ALU.mult, op1=ALU.add), reads=[pssb], writes=[pssb])
                                k.op(k.act, lambda: nc.scalar.activation(out=pss[:], in_=pss[:], func=AF.Sqrt), reads=[pssb], writes=[pssb])
                                k.op(k.dve, lambda: nc.vector.reciprocal(out=pss[:], in_=pss[:]), reads=[pssb], writes=[pssb])
                                k.op(k.dve, lambda: nc.vector.tensor_tensor(out=qnf[i][:], in0=rawg[i][:], in1=pss[:], op=ALU.mult),
                                     reads=[rawgb[i], pssb], writes=[qnfb[i]])
                                k.op(k.act, lambda: nc.scalar.copy(out=qnb[i][:], in_=qnf[i][:]), reads=[qnfb[i]], writes=[qnbb[i]])
                                pp, ppb = self.next_ps()
                                k.op(k.pe, lambda: nc.tensor.matmul(pp[:], PT[:], qnb[i][:], start=True, stop=True),
                                     reads=[PT_b, qnbb[i]], writes=[ppb])
                                k.op(k.dve, lambda: nc.vector.tensor_tensor(out=t1[i][:], in0=qnf[i][:], in1=cos[:, ts], op=ALU.mult),
                                     reads=[qnfb[i], apk_b], writes=[t1b[i]])
                                k.op(k.dve, lambda: nc.vector.tensor_tensor(out=qnf[i][:], in0=pp[:], in1=sin[:, ts], op=ALU.mult),
                                     reads=[ppb, apk_b, qnbb[i]], writes=[qnfb[i]])
                                if not is_k:
                                    k.op(k.dve, lambda: nc.vector.tensor_tensor(out=qrb[i][:], in0=t1[i][:], in1=qnf[i][:], op=ALU.add),
                                         reads=[t1b[i], qnfb[i]], writes=[qrbb[i]])
                                    if not self.cfg.get("no_scratch"):
                                        k.dma(k.sp, qs[head, :, ts], qrb[i][:], reads=[qrbb[i]], writes=[qs_b])
                                else:
                                    kvh = head - 16
                                    k.op(k.dve, lambda: nc.vector.tensor_tensor(out=t1[i][:], in0=t1[i][:], in1=qnf[i][:], op=ALU.add),
                                         reads=[t1b[i], qnfb[i]], writes=[t1b[i]])
                                    k.op(k.act, lambda: nc.scalar.copy(out=qrb[i][:], in_=t1[i][:]), reads=[t1b[i]], writes=[qrbb[i]])
                                    if not self.cfg.get("no_scratch"):
                                        k.dma(k.sp, ks[kvh, :, ts], qrb[i][:], reads=[qrbb[i]], writes=[ks_b])
                                    if self.cfg.get("no_tr"):
                                        continue
                                    ptr, ptrb = self.next_ps()
                                    for b4 in range(4):
                                        k.op(k.pe, lambda b4=b4: nc.tensor.transpose(
                                            ptr[:, b4 * 128:(b4 + 1) * 128], t1[i][:, b4 * 128:(b4 + 1) * 128], ident),
                                            reads=[t1b[i], apk_b], writes=[ptrb])
                                    k.op(k.act, lambda: nc.scalar.copy(out=kout[i][:].rearrange("p b d -> p (b d)"), in_=ptr[:]),
                                         reads=[ptrb], writes=[koutb[i]])
                                    dst = self.O("newk")[slot, t * TT:(t + 1) * TT, kvh * 128:(kvh + 1) * 128].rearrange(
                                        "(b p) d -> p b d", p=128)
                                    k.dma(k.sp, dst, kout[i][:], reads=[koutb[i]], writes=[outb])
                    else:
                        c0 = (ct - 10) * 256
                        for blk in range(12):
                            i = it % 2
                            it += 1
                            pv, pvb = self.next_ps()
                            for kc in range(KC):
                                k.op(k.pe, lambda kc=kc: nc.tensor.matmul(
                                    pv[:, 0:256], h[:, kc, blk * 128:(blk + 1) * 128], w[:, kc, :],
                                    start=(kc == 0), stop=(kc == KC - 1)), reads=[wb, hb[blk // 4]], writes=[pvb])
                            k.op(k.act, lambda: nc.scalar.copy(out=vf[i][:], in_=pv[:, 0:256]), reads=[pvb], writes=[vfb[i]])
                            k.op(k.dve, lambda: nc.vector.tensor_copy(out=vb[i][:], in_=pv[:, 0:256]), reads=[pvb], writes=[vbb[i]])
                            if not self.cfg.get("no_newv"):
                                k.dma(k.sp, self.O("newv")[slot, blk * 128:(blk + 1) * 128, c0:c0 + 256], vf[i][:],
                                      reads=[vfb[i]], writes=[outb])
                            if not self.cfg.get("no_scratch"):
                                k.dma(k.sp, vs[blk, :, c0:c0 + 256], vb[i][:], reads=[vbb[i]], writes=[vs_b])
                self.phase_barrier()
            if self.cfg.get("attn_phase", "AB") == "A":
                return
            with contextlib.ExitStack() as es2:
                sbt = lambda name, shape, dt: es2.enter_context(_sbuf(nc, name, list(shape), dt))
                masks = sbt("at_masks", [128, 14, 128], BF16)
                masks_b = Buf()
                k.dma(k.pool, masks[:], self.I("amask").rearrange("p (m q) -> p m q", q=128), writes=[masks_b])
                ckf = sbt("at_ckf", [128, 2, 512], F32)
                ckf_b = Buf()
                k.dma(k.sp, ckf[:], self.I("cache_k")[slot].rearrange("(b p) d -> p b d", p=128), writes=[ckf_b])
                vc = sbt("at_vc", [128, 2, 512], BF16)
                vc_b = Buf()
                k.dma(k.pool, vc[:], self.I("cache_v")[slot].rearrange("(b p) d -> p b d", p=128), writes=[vc_b])
                kTc = sbt("at_kTc", [128, 4, 256], BF16)
                kTc_b = Buf()
                for cb in range(2):
                    ptr, ptrb = self.next_ps()
                    for kvh in range(4):
                        k.op(k.pe, lambda kvh=kvh: nc.tensor.transpose(
                            ptr[:, kvh * 128:(kvh + 1) * 128], ckf[:, cb, kvh * 128:(kvh + 1) * 128], ident),
                            reads=[ckf_b, apk_b], writes=[ptrb])
                    k.op(k.act, lambda: nc.scalar.copy(out=kTc[:, :, cb * 128:(cb + 1) * 128],
                                                       in_=ptr[:].rearrange("p (h s) -> p h s", s=128)),
                         reads=[ptrb], writes=[kTc_b])
                esink = sbt("at_esink", [128, 16], F32)
                esink_b = Buf()
                k.op(k.act, lambda: nc.scalar.activation(out=esink[:], in_=sink, func=AF.Exp), reads=[apk_b], writes=[esink_b])
                qg = [sbt(f"at_qg{i}", [128, 4, NTOK], BF16) for i in range(2)]
                kg = [sbt(f"at_kg{i}", [128, NTOK], BF16) for i in range(2)]
                vg = [sbt(f"at_vg{i}", [128, 12, 128], BF16) for i in range(2)]
                wo1 = sbt("at_wo", [128, 4, D], BF16)
                wo = [wo1, wo1]
                qgb = [Buf() for _ in range(2)]
                kgb = [Buf() for _ in range(2)]
                vgb = [Buf() for _ in range(2)]
                wob1 = Buf()
                wob = [wob1, wob1]
                og = sbt("at_og", [128, 4, NTOK], BF16)
                ogb = [Buf() for _ in range(NGRP)]
                ptile = [sbt(f"at_pt{i}", [128, 512], BF16) for i in range(3)]
                ptileb = [Buf() for _ in range(3)]
                dtmp = sbt("at_dtmp", [128, 512], F32)
                dtmp_b = Buf()
                wo_v = self.I("attn_w_o")[slot].rearrange("(hh p) n -> p hh n", p=128)

                def loadg(g):
                    i = g % 2
                    k.dma(k.sp, qg[i][:], qs[g * 4:(g + 1) * 4].rearrange("h p t -> p h t"), reads=[qs_b], writes=[qgb[i]])
                    k.dma(k.sp, kg[i][:], ks[g], reads=[ks_b], writes=[kgb[i]])
                    k.dma(k.sp, vg[i][:], vs[:, :, g * 128:(g + 1) * 128].rearrange("b p d -> p b d"), reads=[vs_b], writes=[vgb[i]])

                def loadwo(g):
                    k.dma(k.pool, wo1[:], wo_v[:, g * 4:(g + 1) * 4, :], writes=[wob1])

                loadg(0)
                scale = 128.0 ** -0.5
                pit = 0
                for g in range(4):
                    if g + 1 < 4:
                        loadg(g + 1)
                    loadwo(g)
                    i = g % 2
                    for qb in range(12):
                        kbs = []
                        if qb < 8:
                            for kb in (qb - 1, qb, qb + 1):
                                if 0 <= kb < 8:
                                    if kb == qb:
                                        midx = None
                                    elif kb == qb - 1:
                                        midx = 2 * (qb - 1)
                                    else:
                                        midx = 2 * qb + 1
                                    kbs.append((kg[i][:, kb * 128:(kb + 1) * 128], kgb[i], vg[i][:, kb, :], vgb[i], midx, False))
                            for cb in range(2):
                                kbs.append((kTc[:, g, cb * 128:(cb + 1) * 128], kTc_b, vc[:, cb, g * 128:(g + 1) * 128], vc_b, None, True))
                        else:
                            sb0 = 8 + 2 * ((qb - 8) // 2)
                            for kb in (sb0, sb0 + 1):
                                kbs.append((kg[i][:, kb * 128:(kb + 1) * 128], kgb[i], vg[i][:, kb, :], vgb[i], None, False))
                        pO, pOb = self.reserve_ps()
                        pD, pDb = self.reserve_ps()
                        for j, (kap, kbuf, vap, vbuf, midx, is_c) in enumerate(kbs):
                            pS, pSb = self.next_ps()
                            k.op(k.pe, lambda: nc.tensor.matmul(pS[:], kap, qg[i][:, :, qb * 128:(qb + 1) * 128], start=True, stop=True),
                                 reads=[kbuf, qgb[i]], writes=[pSb])
                            pi = pit % 3
                            pit += 1
                            k.op(k.act, lambda: nc.scalar.activation(out=ptile[pi][:], in_=pS[:], func=AF.Exp, scale=scale),
                                 reads=[pSb], writes=[ptileb[pi]])
                            if midx is not None:
                                k.op(k.dve, lambda: nc.vector.tensor_tensor(
                                    out=ptile[pi][:].rearrange("p (h q) -> p h q", q=128),
                                    in0=ptile[pi][:].rearrange("p (h q) -> p h q", q=128),
                                    in1=masks[:, midx, :].unsqueeze(1).to_broadcast([128, 4, 128]), op=ALU.mult),
                                    reads=[ptileb[pi], masks_b], writes=[ptileb[pi]])
                            if is_c:
                                k.op(k.dve, lambda: nc.vector.tensor_scalar(out=ptile[pi][:], in0=ptile[pi][:], scalar1=cflag, scalar2=None,
                                                                            op0=ALU.mult), reads=[ptileb[pi], apk_b], writes=[ptileb[pi]])
                            k.op(k.pe, lambda: nc.tensor.matmul(pO[:], vap, ptile[pi][:], start=(j == 0), stop=(j == len(kbs) - 1)),
                                 reads=[vbuf, ptileb[pi]], writes=[pOb])
                            k.op(k.pe, lambda: nc.tensor.matmul(pD[:], self.ones_bf[:], ptile[pi][:], start=(j == 0), stop=(j == len(kbs) - 1)),
                                 reads=[self.ones_b, ptileb[pi]], writes=[pDb])
                        k.op(k.dve, lambda: nc.vector.tensor_tensor(
                            out=dtmp[:].rearrange("p (h q) -> p h q", q=128), in0=pD[:].rearrange("p (h q) -> p h q", q=128),
                            in1=esink[:, g * 4:(g + 1) * 4].unsqueeze(2).to_broadcast([128, 4, 128]), op=ALU.add),
                            reads=[pDb, esink_b], writes=[dtmp_b])
                        k.op(k.dve, lambda: nc.vector.reciprocal(out=dtmp[:], in_=dtmp[:]), reads=[dtmp_b], writes=[dtmp_b])
                        k.op(k.dve, lambda: nc.vector.tensor_tensor(
                            out=og[:, :, qb * 128:(qb + 1) * 128], in0=pO[:].rearrange("p (h q) -> p h q", q=128),
                            in1=dtmp[:].rearrange("p (h q) -> p h q", q=128), op=ALU.mult),
                            reads=[pOb, dtmp_b], writes=[ogb[qb // 4]])
                        self.release_ps(pO)
                        self.release_ps(pD)
                    if self.cfg.get("dbg_attn"):
                            dbo = self.dbg_out("dbg_o", [16, 128, NTOK])
                            k.dma(k.sp, dbo[g * 4:(g + 1) * 4].rearrange("h p t -> p h t"), og[:], reads=ogb, writes=[outb])
                            dbq = self.dbg_out("dbg_q", [16, 128, NTOK])
                            k.dma(k.sp, dbq[g * 4:(g + 1) * 4].rearrange("h p t -> p h t"), qg[i][:], reads=[qgb[i]], writes=[outb])
                    for t in range(NGRP):
                        ts = slice(t * TT, (t + 1) * TT)
                        for dc in range(KC):
                            py, pyb = self.next_ps()
                            for hh in range(4):
                                k.op(k.pe, lambda hh=hh: nc.tensor.matmul(py[:], wo[i][:, hh, dc * 128:(dc + 1) * 128], og[:, hh, ts],
                                                                          start=(hh == 0), stop=(hh == 3)),
                                     reads=[wob[i], ogb[t]], writes=[pyb])
                            k.op(k.dve, lambda: nc.vector.scalar_tensor_tensor(
                                out=self.x_sb[:, dc, ts], in0=py[:], scalar=self.modT[:, l, 5, dc, t:t + 1],
                                in1=self.x_sb[:, dc, ts], op0=ALU.mult, op1=ALU.add),
                                reads=[pyb, self.modT_b, self.xb[dc][t]], writes=[self.xb[dc][t]])
                self.phase_barrier()


    def mlstm(self, l):
        k, nc = self.k, self.nc
        slot = l // 3
        NHm, DKm, DVm, NB = 8, 128, 256, NTOK // 128
        DV1 = DVm + 1
        qT_s = nc.dram_tensor(f"ml_qT{l}", [NHm, 128, NTOK], BF16).ap()
        kT_s = nc.dram_tensor(f"ml_kT{l}", [NHm, 128, NTOK], BF16).ap()
        ktm_s = nc.dram_tensor(f"ml_ktm{l}", [NB, 128, NHm * DKm], BF16).ap()
        vtm_s = nc.dram_tensor(f"ml_vtm{l}", [NB, 128, D], BF16).ap()
        og_s = nc.dram_tensor(f"ml_og{l}", [NB, 128, D], BF16).ap()
        hf_s = nc.dram_tensor(f"ml_hf{l}", [NB, 128, D], F32).ap()
        hsT_s = nc.dram_tensor(f"ml_hsT{l}", [KC, 128, NTOK], BF16).ap()
        scr_b = Buf("ml_scr")
        hf_b = Buf("ml_hf")
        hsT_b = Buf("ml_hsT")
        outb = Buf("ml_out")
        w_in_v = self.I("mlstm_w_in")[slot].rearrange("(kc p) n -> p kc n", p=128)
        with contextlib.ExitStack() as es1:
            sbt1 = lambda name, shape, dt: es1.enter_context(_sbuf(nc, name, list(shape), dt))
            self.phase_barrier()
            mpk = sbt1("ml_mpk", [128, NMP], F32)
            mpk_b = Buf()
            k.dma(k.sp, mpk[:], self.I("mpack")[slot], writes=[mpk_b])
            ident = mpk[:, 0:128]
            onesf = mpk[:, 128:256]
            V1_01, V2_01 = mpk[:, 256:384], mpk[:, 384:512]
            V1b, V2b = mpk[:, 512:640], mpk[:, 640:768]
            bgate = mpk[:, 768:800]
            flag = mpk[:, 800:801]
            onorm = mpk[:, 801:801 + D]
            gates = sbt1("ml_gates", [128, NB, 32], F32)
            gates_b = Buf()
            with contextlib.ExitStack() as es2:
                sbt = lambda name, shape, dt: es2.enter_context(_sbuf(nc, name, list(shape), dt))
                h = sbt("ml_h", [128, KC, NTOK], BF16)
                hb = [Buf(f"h{t}") for t in range(NGRP)]
                sq = [sbt(f"ml_sq{i}", [128, TT], BF16) for i in range(2)]
                sqb = [Buf() for _ in range(2)]
                ev = [sbt(f"ml_ev{i}", [128, TT], BF16) for i in range(3)]
                evb = [Buf() for _ in range(3)]
                wg = sbt("ml_wg", [128, KC, 32], BF16)
                wg_b = Buf()
                k.dma(k.pool, wg[:], self.I("mlstm_w_gate")[slot].rearrange("(kc p) n -> p kc n", p=128), writes=[wg_b])
                self.compute_h(l, 1, h, hb, [t_[:] for t_ in sq], sqb)
                cnt = [0]

                def nxt():
                    cnt[0] += 1
                    return cnt[0] % 3

                def ev_q(oc, t, ps, psb):
                    i = nxt()
                    k.op(k.act, lambda: nc.scalar.activation(out=ev[i][:], in_=ps[:], func=AF.Identity, scale=float(DKm) ** -0.5),
                         reads=[psb], writes=[evb[i]])
                    k.dma(k.sp, qT_s[oc, :, t * TT:(t + 1) * TT], ev[i][:], reads=[evb[i]], writes=[scr_b])

                def ev_k(oc, t, ps, psb):
                    i = nxt()
                    k.op(k.act, lambda: nc.scalar.copy(out=ev[i][:], in_=ps[:]), reads=[psb], writes=[evb[i]])
                    k.dma(k.sp, kT_s[oc, :, t * TT:(t + 1) * TT], ev[i][:], reads=[evb[i]], writes=[scr_b])

                def mk_tm(dst, func):
                    def f(blk, ci, ps, psb):
                        i = nxt()
                        if func is None:
                            k.op(k.dve, lambda: nc.vector.tensor_copy(out=ev[i][:], in_=ps[:]), reads=[psb], writes=[evb[i]])
                        else:
                            k.op(k.act, lambda: nc.scalar.activation(out=ev[i][:], in_=ps[:], func=func), reads=[psb], writes=[evb[i]])
                        k.dma(k.sp, dst[blk, :, ci * 512:(ci + 1) * 512], ev[i][:], reads=[evb[i]], writes=[scr_b])
                    return f

                self.linear_fm(h, hb, w_in_v, 0, 1024, ev_q, "q")
                self.linear_fm(h, hb, w_in_v, 1024, 1024, ev_k, "k")
                self.linear_tm(h, hb, w_in_v, 1024, 1024, mk_tm(ktm_s, None), "kt")
                self.linear_tm(h, hb, w_in_v, 2048, 2048, mk_tm(vtm_s, None), "v")
                self.linear_tm(h, hb, w_in_v, 4096, 2048, mk_tm(og_s, AF.Sigmoid), "og")
                for blk in range(NB):
                    ps, psb = self.next_ps()
                    for kc in range(KC):
                        k.op(k.pe, lambda kc=kc: nc.tensor.matmul(ps[:, 0:32], h[:, kc, blk * 128:(blk + 1) * 128], wg[:, kc, :],
                                                                  start=(kc == 0), stop=(kc == KC - 1)),
                             reads=[wg_b, hb[blk // 4]], writes=[psb])
                    k.op(k.dve, lambda: nc.vector.tensor_tensor(out=gates[:, blk, :], in0=ps[:, 0:32], in1=bgate, op=ALU.add),
                         reads=[psb, mpk_b], writes=[gates_b])
                self.phase_barrier()
            with contextlib.ExitStack() as es2:
                sbt = lambda name, shape, dt: es2.enter_context(_sbuf(nc, name, list(shape), dt))
                C = sbt("ml_C", [128, NHm, DV1], F32)
                Cb = sbt("ml_Cb", [128, NHm, DV1], BF16)
                mst = sbt("ml_mst", [128, NHm], F32)
                C_b, Cb_b, mst_b = Buf(), Buf(), Buf()
                qc = [sbt(f"ml_qc{i}", [128, NHm, 128], BF16) for i in range(2)]
                kc_ = [sbt(f"ml_kc{i}", [128, NHm, 128], BF16) for i in range(2)]
                ktc = [sbt(f"ml_ktc{i}", [128, NHm, 128], BF16) for i in range(2)]
                vx = [sbt(f"ml_vx{i}", [128, NHm, DV1], BF16) for i in range(2)]
                ldb = [[Buf() for _ in range(4)] for _ in range(2)]
                for i in range(2):
                    k.op(k.dve, lambda i=i: nc.vector.memset(vx[i][:, :, DVm:DV1], 1.0), writes=[ldb[i][3]])
                sm = {n_: sbt(f"ml_s_{n_}", [128, NHm], F32) for n_ in
                      ("e", "lsp", "b", "g", "gmax", "cm", "mx", "a", "em", "nmx", "mx2", "wgt", "dec", "tmp")}
                smb = {n_: Buf() for n_ in sm}
                diag = sbt("ml_diag", [128, 4, 128], F32)
                diag_b = Buf()
                dmat = sbt("ml_dmat", [128, 4, 128], F32)
                dmat_b = Buf()
                DmT = sbt("ml_DmT", [128, NHm, 128], F32)
                DmT_b = Buf()
                PT8 = sbt("ml_PT8", [128, NHm, 128], BF16)
                PT8_b = Buf()
                kw = sbt("ml_kw", [128, NHm, 128], BF16)
                kw_b = Buf()
                tmpn = [sbt(f"ml_tmpn{i}", [128, DV1], F32) for i in range(2)]
                tmpn_b = [Buf() for _ in range(2)]
                num = [sbt(f"ml_num{i}", [128, DV1], F32) for i in range(2)]
                num_b = [Buf() for _ in range(2)]
                dn = [sbt(f"ml_dn{i}", [128, 1], F32) for i in range(2)]
                dn_b = [Buf() for _ in range(2)]
                hout = sbt("ml_hout", [128, NHm, DVm], F32)
                hout_b = Buf()
                hfl = sbt("ml_hfl", [128, NHm, DVm], F32)
                hfl_b = Buf()
                ogl = sbt("ml_ogl", [128, D], BF16)
                ogl_b = Buf()
                ssum = sbt("ml_ssum", [128, NHm], F32)
                ssum_b = Buf()
                hsT = sbt("ml_hsTt", [128, KC, 128], BF16)
                hsT_tb = Buf()

                def load_chunk(blk, i):
                    sl = slice(blk * 128, (blk + 1) * 128)
                    k.dma(k.sp, qc[i][:], qT_s[:, :, sl].rearrange("h p t -> p h t"), reads=[scr_b], writes=[ldb[i][0]])
                    k.dma(k.sp, kc_[i][:], kT_s[:, :, sl].rearrange("h p t -> p h t"), reads=[scr_b], writes=[ldb[i][1]])
                    k.dma(k.sp, ktc[i][:], ktm_s[blk].rearrange("p (h d) -> p h d", d=128), reads=[scr_b], writes=[ldb[i][2]])
                    k.dma(k.sp, vx[i][:, :, 0:DVm], vtm_s[blk].rearrange("p (h d) -> p h d", d=DVm), reads=[scr_b], writes=[ldb[i][3]])

                def small(eng_fn, out_n, reads_n, extra_reads=()):
                    k.op(k.dve, eng_fn, reads=[smb[n_] for n_ in reads_n] + list(extra_reads), writes=[smb[out_n]])

                for dirn in range(2):
                    order = list(range(NB)) if dirn == 0 else list(range(NB - 1, -1, -1))
                    gi0 = dirn * 16
                    tri01 = V1_01 if dirn == 0 else V2_01
                    mb_st = V1b if dirn == 0 else V2b
                    mb_ts = V2b if dirn == 0 else V1b
                    load_chunk(order[0], 0)
                    for oi, blk in enumerate(order):
                        i = oi % 2
                        if oi + 1 < NB:
                            load_chunk(order[oi + 1], (oi + 1) % 2)
                        seg = blk // 2
                        first_in_seg = (blk % 2 == 0) if dirn == 0 else (blk % 2 == 1)
                        last_in_seg = not first_in_seg
                        if first_in_seg:
                            if seg >= 4:
                                k.op(k.dve, lambda: nc.vector.memset(C[:], 0.0), writes=[C_b])
                                k.op(k.dve, lambda: nc.vector.memset(mst[:], 0.0), writes=[mst_b])
                                k.op(k.dve, lambda: nc.vector.memset(Cb[:], 0.0), writes=[Cb_b])
                            elif (seg == 0 and dirn == 0) or (seg == 3 and dirn == 1):
                                k.dma(k.sp, C[:].rearrange("p h d -> p (h d)"), self.I("ml_initC")[dirn], writes=[C_b])
                                k.dma(k.sp, mst[:], self.I("ml_initm")[dirn], writes=[mst_b])
                                k.op(k.act, lambda: nc.scalar.copy(out=Cb[:], in_=C[:]), reads=[C_b], writes=[Cb_b])
                            else:
                                k.op(k.dve, lambda: nc.vector.tensor_scalar(out=C[:], in0=C[:], scalar1=flag, scalar2=None, op0=ALU.mult),
                                     reads=[C_b, mpk_b], writes=[C_b])
                                k.op(k.dve, lambda: nc.vector.tensor_scalar(out=mst[:], in0=mst[:], scalar1=flag, scalar2=None, op0=ALU.mult),
                                     reads=[mst_b, mpk_b], writes=[mst_b])
                                k.op(k.act, lambda: nc.scalar.copy(out=Cb[:], in_=C[:]), reads=[C_b], writes=[Cb_b])
                        gi = gates[:, blk, gi0:gi0 + 8]
                        gf = gates[:, blk, gi0 + 8:gi0 + 16]
                        k.op(k.act, lambda: nc.scalar.activation(out=sm["e"][:], in_=gf, func=AF.Exp, scale=-1.0),
                             reads=[gates_b], writes=[smb["e"]])
                        k.op(k.act, lambda: nc.scalar.activation(out=sm["lsp"][:], in_=sm["e"][:], func=AF.Ln, bias=1.0),
                             reads=[smb["e"]], writes=[smb["lsp"]])
                        pb, pbb = self.next_ps()
                        k.op(k.pe, lambda: nc.tensor.matmul(pb[:, 0:8], tri01, sm["lsp"][:], start=True, stop=True),
                             reads=[mpk_b, smb["lsp"]], writes=[pbb])
                        k.op(k.dve, lambda: nc.vector.tensor_tensor(out=sm["g"][:], in0=pb[:, 0:8], in1=gi, op=ALU.add),
                             reads=[pbb, gates_b], writes=[smb["g"]])
                        for hh in range(2):
                            hs_ = slice(hh * 4, (hh + 1) * 4)
                            k.op(k.dve, lambda: nc.vector.tensor_tensor(
                                out=diag[:], in0=ident.unsqueeze(1).to_broadcast([128, 4, 128]),
                                in1=sm["g"][:, hs_].unsqueeze(2).to_broadcast([128, 4, 128]), op=ALU.mult),
                                reads=[mpk_b, smb["g"]], writes=[diag_b])
                            pg, pgb = self.next_ps()
                            k.op(k.pe, lambda: nc.tensor.matmul(pg[:], onesf, diag[:].rearrange("p h s -> p (h s)"), start=True, stop=True),
                                 reads=[mpk_b, diag_b], writes=[pgb])
                            pg3 = pg[:].rearrange("p (h s) -> p h s", s=128)
                            k.op(k.dve, lambda: nc.vector.tensor_reduce(out=sm["gmax"][:, hs_], in_=pg3, axis=AX.X, op=ALU.max),
                                 reads=[pgb], writes=[smb["gmax"]])
                            k.op(k.dve, lambda: nc.vector.tensor_tensor(out=dmat[:], in0=pg3, in1=mb_ts.unsqueeze(1).to_broadcast([128, 4, 128]),
                                                                        op=ALU.add), reads=[pgb, mpk_b], writes=[dmat_b])
                            k.op(k.dve, lambda: nc.vector.tensor_reduce(out=sm["cm"][:, hs_], in_=dmat[:], axis=AX.X, op=ALU.max),
                                 reads=[dmat_b], writes=[smb["cm"]])
                        small(lambda: nc.vector.tensor_tensor(out=sm["mx"][:], in0=sm["cm"][:], in1=mst[:], op=ALU.max), "mx", ["cm"], [mst_b])
                        small(lambda: nc.vector.tensor_tensor(out=sm["tmp"][:], in0=mst[:], in1=sm["mx"][:], op=ALU.subtract), "tmp", ["mx"], [mst_b])
                        k.op(k.act, lambda: nc.scalar.activation(out=sm["a"][:], in_=sm["tmp"][:], func=AF.Exp), reads=[smb["tmp"]], writes=[smb["a"]])
                        small(lambda: nc.vector.tensor_tensor(out=sm["b"][:], in0=pb[:, 0:8], in1=sm["mx"][:], op=ALU.subtract), "b", ["mx"], [pbb])
                        k.op(k.act, lambda: nc.scalar.activation(out=sm["em"][:], in_=sm["b"][:], func=AF.Exp), reads=[smb["b"]], writes=[smb["em"]])
                        small(lambda: nc.vector.tensor_scalar(out=sm["nmx"][:], in0=sm["mx"][:], scalar1=-1.0, scalar2=None, op0=ALU.mult), "nmx", ["mx"])
                        small(lambda: nc.vector.tensor_tensor(out=sm["mx2"][:], in0=sm["gmax"][:], in1=mst[:], op=ALU.max), "mx2", ["gmax"], [mst_b])
                        small(lambda: nc.vector.tensor_tensor(out=sm["tmp"][:], in0=sm["g"][:], in1=sm["mx2"][:], op=ALU.subtract), "tmp", ["g", "mx2", "a"])
                        k.op(k.act, lambda: nc.scalar.activation(out=sm["wgt"][:], in_=sm["tmp"][:], func=AF.Exp), reads=[smb["tmp"]], writes=[smb["wgt"]])
                        small(lambda: nc.vector.tensor_tensor(out=sm["e"][:], in0=mst[:], in1=sm["mx2"][:], op=ALU.subtract), "e", ["mx2", "lsp"], [mst_b])
                        k.op(k.act, lambda: nc.scalar.activation(out=sm["dec"][:], in_=sm["e"][:], func=AF.Exp), reads=[smb["e"]], writes=[smb["dec"]])
                        for hh in range(2):
                            hs_ = slice(hh * 4, (hh + 1) * 4)
                            k.op(k.dve, lambda: nc.vector.tensor_tensor(
                                out=diag[:], in0=ident.unsqueeze(1).to_broadcast([128, 4, 128]),
                                in1=sm["nmx"][:, hs_].unsqueeze(2).to_broadcast([128, 4, 128]), op=ALU.mult),
                                reads=[mpk_b, smb["nmx"]], writes=[diag_b])
                            pu, pub = self.next_ps()
                            k.op(k.pe, lambda: nc.tensor.matmul(pu[:], onesf, diag[:].rearrange("p h s -> p (h s)"), start=True, stop=True),
                                 reads=[mpk_b, diag_b], writes=[pub])
                            pu3 = pu[:].rearrange("p (h t) -> p h t", t=128)
                            k.op(k.dve, lambda: nc.vector.tensor_tensor(out=dmat[:], in0=pu3,
                                                                        in1=sm["g"][:, hs_].unsqueeze(2).to_broadcast([128, 4, 128]), op=ALU.add),
                                 reads=[pub, smb["g"]], writes=[dmat_b])
                            k.op(k.dve, lambda: nc.vector.tensor_tensor(out=dmat[:], in0=dmat[:],
                                                                        in1=mb_st.unsqueeze(1).to_broadcast([128, 4, 128]), op=ALU.add),
                                 reads=[dmat_b, mpk_b], writes=[dmat_b])
                            k.op(k.act, lambda: nc.scalar.activation(out=DmT[:, hs_, :], in_=dmat[:], func=AF.Exp),
                                 reads=[dmat_b], writes=[DmT_b])
                            pss, pssb = self.next_ps()
                            for h4 in range(4):
                                hd_ = hh * 4 + h4
                                k.op(k.pe, lambda: nc.tensor.matmul(pss[:, h4 * 128:(h4 + 1) * 128], kc_[i][:, hd_, :], qc[i][:, hd_, :],
                                                                    start=True, stop=True),
                                     reads=[ldb[i][0], ldb[i][1]], writes=[pssb])
                            k.op(k.dve, lambda: nc.vector.tensor_tensor(out=PT8[:, hs_, :], in0=pss[:].rearrange("p (h t) -> p h t", t=128),
                                                                        in1=DmT[:, hs_, :], op=ALU.mult),
                                 reads=[pssb, DmT_b], writes=[PT8_b])
                        k.op(k.dve, lambda: nc.vector.tensor_tensor(out=kw[:], in0=ktc[i][:],
                                                                    in1=sm["wgt"][:].unsqueeze(2).to_broadcast([128, NHm, 128]), op=ALU.mult),
                             reads=[ldb[i][2], smb["wgt"]], writes=[kw_b])
                        for hd_ in range(NHm):
                            j = hd_ % 2
                            p1, p1b = self.next_ps()
                            k.op(k.pe, lambda: nc.tensor.matmul(p1[:, 0:DV1], PT8[:, hd_, :], vx[i][:, hd_, :], start=True, stop=True),
                                 reads=[PT8_b, ldb[i][3]], writes=[p1b])
                            p2, p2b = self.next_ps()
                            k.op(k.pe, lambda: nc.tensor.matmul(p2[:, 0:DV1], qc[i][:, hd_, :], Cb[:, hd_, :], start=True, stop=True),
                                 reads=[ldb[i][0], Cb_b], writes=[p2b])
                            k.op(k.act, lambda: nc.scalar.activation(out=tmpn[j][:], in_=p2[:, 0:DV1], func=AF.Identity,
                                                                     scale=sm["a"][:, hd_:hd_ + 1]),
                                 reads=[p2b, smb["a"]], writes=[tmpn_b[j]])
                            k.op(k.dve, lambda: nc.vector.tensor_tensor(out=num[j][:], in0=tmpn[j][:], in1=p1[:, 0:DV1], op=ALU.add),
                                 reads=[tmpn_b[j], p1b], writes=[num_b[j]])
                            k.op(k.dve, lambda: nc.vector.tensor_scalar(out=dn[j][:], in0=num[j][:, DVm:DV1], scalar1=-1.0,
                                                                        scalar2=None, op0=ALU.mult),
                                 reads=[num_b[j]], writes=[dn_b[j]])
                            k.op(k.dve, lambda: nc.vector.tensor_tensor(out=dn[j][:], in0=dn[j][:], in1=num[j][:, DVm:DV1], op=ALU.max),
                                 reads=[num_b[j], dn_b[j]], writes=[dn_b[j]])
                            k.op(k.dve, lambda: nc.vector.tensor_tensor(out=dn[j][:], in0=dn[j][:], in1=sm["em"][:, hd_:hd_ + 1], op=ALU.max),
                                 reads=[dn_b[j], smb["em"]], writes=[dn_b[j]])
                            k.op(k.dve, lambda: nc.vector.reciprocal(out=dn[j][:], in_=dn[j][:]), reads=[dn_b[j]], writes=[dn_b[j]])
                            k.op(k.dve, lambda: nc.vector.tensor_scalar(out=hout[:, hd_, :], in0=num[j][:, 0:DVm], scalar1=dn[j][:, 0:1],
                                                                        scalar2=None, op0=ALU.mult),
                                 reads=[num_b[j], dn_b[j]], writes=[hout_b])
                            p3, p3b = self.next_ps()
                            k.op(k.pe, lambda: nc.tensor.matmul(p3[:, 0:DV1], kw[:, hd_, :], vx[i][:, hd_, :], start=True, stop=True),
                                 reads=[kw_b, ldb[i][3]], writes=[p3b])
                            k.op(k.dve, lambda: nc.vector.scalar_tensor_tensor(out=C[:, hd_, :], in0=C[:, hd_, :], scalar=sm["dec"][:, hd_:hd_ + 1],
                                                                               in1=p3[:, 0:DV1], op0=ALU.mult, op1=ALU.add),
                                 reads=[C_b, smb["dec"], p3b], writes=[C_b])
                        k.op(k.act, lambda: nc.scalar.copy(out=Cb[:], in_=C[:]), reads=[C_b], writes=[Cb_b])
                        pbl, pblb = self.next_ps()
                        k.op(k.pe, lambda: nc.tensor.matmul(pbl[:, 0:8], onesf, sm["lsp"][:], start=True, stop=True),
                             reads=[mpk_b, smb["lsp"]], writes=[pblb])
                        k.op(k.dve, lambda: nc.vector.tensor_tensor(out=mst[:], in0=sm["mx2"][:], in1=pbl[:, 0:8], op=ALU.subtract),
                             reads=[smb["mx2"], pblb, smb["dec"], smb["a"], smb["mx"]], writes=[mst_b])
                        if last_in_seg:
                            k.dma(k.sp, self.O("ml_stC")[seg, dirn], C[:].rearrange("p h d -> p (h d)"), reads=[C_b], writes=[outb])
                            k.dma(k.sp, self.O("ml_stm")[seg, dirn], mst[:], reads=[mst_b], writes=[outb])
                        if dirn == 0:
                            k.dma(k.sp, hf_s[blk].rearrange("p (h d) -> p h d", d=DVm), hout[:], reads=[hout_b], writes=[hf_b])
                        else:
                            k.dma(k.sp, hfl[:], hf_s[blk].rearrange("p (h d) -> p h d", d=DVm), reads=[hf_b], writes=[hfl_b])
                            k.dma(k.sp, ogl[:], og_s[blk], reads=[scr_b], writes=[ogl_b])
                            k.op(k.dve, lambda: nc.vector.tensor_tensor(out=hout[:], in0=hout[:], in1=hfl[:], op=ALU.add),
                                 reads=[hout_b, hfl_b], writes=[hout_b])
                            k.op(k.act, lambda: nc.scalar.activation(out=hfl[:], in_=hout[:], func=AF.Square), reads=[hout_b], writes=[hfl_b])
                            k.op(k.dve, lambda: nc.vector.tensor_reduce(out=ssum[:], in_=hfl[:], axis=AX.X, op=ALU.add),
                                 reads=[hfl_b], writes=[ssum_b])
                            k.op(k.dve, lambda: nc.vector.tensor_scalar(out=ssum[:], in0=ssum[:], scalar1=1.0 / DVm, scalar2=EPS,
                                                                        op0=ALU.mult, op1=ALU.add), reads=[ssum_b], writes=[ssum_b])
                            k.op(k.act, lambda: nc.scalar.activation(out=ssum[:], in_=ssum[:], func=AF.Sqrt), reads=[ssum_b], writes=[ssum_b])
                            k.op(k.dve, lambda: nc.vector.reciprocal(out=ssum[:], in_=ssum[:]), reads=[ssum_b], writes=[ssum_b])
                            k.op(k.dve, lambda: nc.vector.tensor_tensor(out=hout[:], in0=hout[:],
                                                                        in1=ssum[:].unsqueeze(2).to_broadcast([128, NHm, DVm]), op=ALU.mult),
                                 reads=[hout_b, ssum_b], writes=[hout_b])
                            hflat = hout[:].rearrange("p h d -> p (h d)")
                            k.op(k.dve, lambda: nc.vector.tensor_tensor(out=hflat, in0=hflat, in1=onorm, op=ALU.mult),
                                 reads=[hout_b, mpk_b], writes=[hout_b])
                            k.op(k.dve, lambda: nc.vector.tensor_tensor(out=hflat, in0=hflat, in1=ogl[:], op=ALU.mult),
                                 reads=[hout_b, ogl_b], writes=[hout_b])
                            for q4 in range(4):
                                ptr, ptrb = self.next_ps()
                                for c4 in range(4):
                                    c = q4 * 4 + c4
                                    k.op(k.pe, lambda: nc.tensor.transpose(ptr[:, c4 * 128:(c4 + 1) * 128], hflat[:, c * 128:(c + 1) * 128], ident),
                                         reads=[hout_b, mpk_b], writes=[ptrb])
                                k.op(k.act, lambda: nc.scalar.copy(out=hsT[:, q4 * 4:(q4 + 1) * 4, :], in_=ptr[:].rearrange("p (c t) -> p c t", t=128)),
                                     reads=[ptrb], writes=[hsT_tb])
                            k.dma(k.sp, hsT_s[:, :, blk * 128:(blk + 1) * 128].rearrange("c p t -> p c t"), hsT[:], reads=[hsT_tb], writes=[hsT_b])
                self.phase_barrier()
            with contextlib.ExitStack() as es2:
                sbt = lambda name, shape, dt: es2.enter_context(_sbuf(nc, name, list(shape), dt))
                h2 = sbt("ml_h2", [128, KC, NTOK], BF16)
                h2b = [Buf() for _ in range(NGRP)]
                for t in range(NGRP):
                    k.dma(k.sp, h2[:, :, t * TT:(t + 1) * TT], hsT_s[:, :, t * TT:(t + 1) * TT].rearrange("c p t -> p c t"),
                          reads=[hsT_b], writes=[h2b[t]])
                wo_v = self.I("mlstm_w_o")[slot].rearrange("(kc p) n -> p kc n", p=128)

                def ev_o(oc, t, ps, psb):
                    ts = slice(t * TT, (t + 1) * TT)
                    k.op(k.dve, lambda: nc.vector.scalar_tensor_tensor(
                        out=self.x_sb[:, oc, ts], in0=ps[:], scalar=self.modT[:, l, 5, oc, t:t + 1],
                        in1=self.x_sb[:, oc, ts], op0=ALU.mult, op1=ALU.add),
                        reads=[psb, self.modT_b, self.xb[oc][t]], writes=[self.xb[oc][t]])

                self.linear_fm(h2, h2b, wo_v, 0, D, ev_o, "o")
                self.phase_barrier()


    def rwkv(self, l):
        k, nc = self.k, self.nc
        slot = l // 3
        TB, SB = 32, 2
        KZ = [nc.dram_tensor(f"rk_KZ{z}_{l}", [128, 5, KC, NTOK], F32).ap() for z in range(2)]
        VV = nc.dram_tensor(f"rk_VV{l}", [128, KC, NTOK], F32).ap()
        GG = nc.dram_tensor(f"rk_GG{l}", [128, KC, NTOK], F32).ap()
        VB = nc.dram_tensor(f"rk_VB{l}", [128, KC, NTOK], F32).ap()
        RAWK = nc.dram_tensor(f"rk_RAWK{l}", [128, KC, NTOK], F32).ap()
        AZ = [nc.dram_tensor(f"rk_AZ{z}_{l}", [128, KC, NTOK], F32).ap() for z in range(2)]
        YT = [nc.dram_tensor(f"rk_YT{z}_{l}", [NTOK, D], BF16).ap() for z in range(2)]
        hsT_s = nc.dram_tensor(f"rk_hsT{l}", [KC, 128, NTOK], BF16).ap()
        scr_b, yt_b, hsT_b, outb = Buf("rk_scr"), Buf("rk_yt"), Buf("rk_hsT"), Buf("rk_out")
        NFM = 13 * KC
        with contextlib.ExitStack() as es1:
            sbt1 = lambda name, shape, dt: es1.enter_context(_sbuf(nc, name, list(shape), dt))
            self.phase_barrier()
            rpk = sbt1("rk_rpk", [128, NRP], F32)
            rpk_b = Buf()
            k.dma(k.sp, rpk[:], self.I("rpack")[:, :], writes=[rpk_b])
            fmv = rpk[:, 0:NFM].rearrange("p (a c) -> p a c", c=KC)
            I2 = rpk[:, NFM:NFM + 64]
            flag = rpk[:, NFM + 64:NFM + 65]
            ident = rpk[:, NFM + 65:NFM + 193]
            BO = sbt1("rk_BO", [128, 128], BF16)
            hsel = sbt1("rk_hsel", [128, 2], BF16)
            cst_b = Buf()
            k.op(k.dve, lambda: nc.vector.memset(BO[:], 0.0), writes=[cst_b])
            k.op(k.dve, lambda: nc.vector.memset(BO[0:64, 0:64], 1.0), writes=[cst_b])
            k.op(k.dve, lambda: nc.vector.memset(BO[64:128, 64:128], 1.0), writes=[cst_b])
            k.op(k.dve, lambda: nc.vector.memset(hsel[:], 0.0), writes=[cst_b])
            k.op(k.dve, lambda: nc.vector.memset(hsel[0:64, 0:1], 1.0), writes=[cst_b])
            k.op(k.dve, lambda: nc.vector.memset(hsel[64:128, 1:2], 1.0), writes=[cst_b])
            with contextlib.ExitStack() as es2:
                sbt = lambda name, shape, dt: es2.enter_context(_sbuf(nc, name, list(shape), dt))
                hp_ = sbt("rk_h", [128, KC, NTOK + 2], BF16)
                hb = [Buf(f"h{t}") for t in range(NGRP)]
                k.op(k.dve, lambda: nc.vector.memset(hp_[:, :, 0:1], 0.0), writes=[hb[0]])
                k.op(k.dve, lambda: nc.vector.memset(hp_[:, :, NTOK + 1:NTOK + 2], 0.0), writes=[hb[2]])

                class Shift:
                    def __getitem__(self_, idx):
                        p, c, sl = idx
                        return hp_[p, c, slice(sl.start + 1, sl.stop + 1)]
                sq = [sbt(f"rk_sq{i}", [128, TT], BF16) for i in range(2)]
                sqb = [Buf() for _ in range(2)]
                self.compute_h(l, 1, Shift(), hb, [t_[:] for t_ in sq], sqb)
                msk = sbt("rk_msk", [128, 2, NTOK], BF16)
                msk_b = Buf()
                k.dma(k.pool, msk[:], self.I("rmask").rearrange("p (a t) -> p a t", a=2), writes=[msk_b])
                omm = sbt("rk_omm", [128, 6, KC], F32)
                hmu = sbt("rk_hmu", [128, 6, KC], F32)
                omm_b = Buf()
                k.op(k.dve, lambda: nc.vector.tensor_scalar(out=omm[:], in0=fmv[:, 0:6, :], scalar1=-1.0, scalar2=1.0, op0=ALU.mult, op1=ALU.add),
                     reads=[rpk_b], writes=[omm_b])
                k.op(k.dve, lambda: nc.vector.tensor_scalar(out=hmu[:], in0=fmv[:, 0:6, :], scalar1=0.5, scalar2=None, op0=ALU.mult),
                     reads=[rpk_b], writes=[omm_b])
                xs = sbt("rk_xs", [128, KC, TT], BF16)
                xs_b = Buf()
                tA = [sbt(f"rk_tA{i}", [128, TT], F32) for i in range(3)]
                tA_b = [Buf() for _ in range(3)]
                ev = [sbt(f"rk_ev{i}", [128, TT], F32) for i in range(3)]
                ev_b = [Buf() for _ in range(3)]
                mid = sbt("rk_mid", [128, 2, TT], BF16)
                mid_b = Buf()
                cnt = [0]

                def nxt():
                    cnt[0] += 1
                    return cnt[0] % 3

                def build_xs(p, t):
                    t0 = t * TT
                    for c in range(KC):
                        k.op(k.dve, lambda: nc.vector.tensor_tensor(out=tA[0][:], in0=hp_[:, c, t0:t0 + TT], in1=msk[:, 0, t0:t0 + TT], op=ALU.mult),
                             reads=[hb[t], hb[max(t - 1, 0)], msk_b], writes=[tA_b[0]])
                        k.op(k.dve, lambda: nc.vector.tensor_tensor(out=tA[1][:], in0=hp_[:, c, t0 + 2:t0 + TT + 2], in1=msk[:, 1, t0:t0 + TT], op=ALU.mult),
                             reads=[hb[t], hb[min(t + 1, 2)], msk_b], writes=[tA_b[1]])
                        k.op(k.dve, lambda: nc.vector.tensor_tensor(out=tA[0][:], in0=tA[0][:], in1=tA[1][:], op=ALU.add),
                             reads=[tA_b[0], tA_b[1]], writes=[tA_b[0]])
                        k.op(k.act, lambda: nc.scalar.activation(out=tA[2][:], in_=hp_[:, c, t0 + 1:t0 + TT + 1], func=AF.Identity, scale=omm[:, p, c:c + 1]),
                             reads=[hb[t], omm_b], writes=[tA_b[2]])
                        k.op(k.dve, lambda: nc.vector.scalar_tensor_tensor(out=xs[:, c, :], in0=tA[0][:], scalar=hmu[:, p, c:c + 1], in1=tA[2][:],
                                                                           op0=ALU.mult, op1=ALU.add),
                             reads=[tA_b[0], tA_b[2], omm_b], writes=[xs_b])

                def proj_tile(wview, col0, ncols, evac, tag):
                    TC = 256
                    with contextlib.ExitStack() as es3:
                        wt = [es3.enter_context(_sbuf(nc, f"rkw_{tag}{i}", [128, KC, TC], BF16)) for i in range(2)]
                        wtb = [Buf() for _ in range(2)]
                        ntile = ncols // TC

                        def load(ti):
                            k.dma(k.pool, wt[ti % 2][:], wview[:, :, col0 + ti * TC:col0 + (ti + 1) * TC], writes=[wtb[ti % 2]])
                        load(0)
                        for ti in range(ntile):
                            if ti + 1 < ntile:
                                load(ti + 1)
                            for sub in range(TC // 128):
                                oc = ti * (TC // 128) + sub
                                ps, psb = self.next_ps()
                                for kc in range(KC):
                                    k.op(k.pe, lambda kc=kc: nc.tensor.matmul(ps[:], wt[ti % 2][:, kc, sub * 128:(sub + 1) * 128], xs[:, kc, :],
                                                                              start=(kc == 0), stop=(kc == KC - 1)),
                                         reads=[wtb[ti % 2], xs_b], writes=[psb])
                                evac(oc, ps, psb)
                        self.scope_end(wtb)

                for p in range(3):
                    wv_ = self.I("rwkv_w_rkv")[slot, p].rearrange("(kc p) n -> p kc n", p=128)
                    for t in range(NGRP):
                        ts = slice(t * TT, (t + 1) * TT)
                        build_xs(p, t)

                        def ev_rkv(oc, ps, psb, p=p, ts=ts):
                            i = nxt()
                            k.op(k.act, lambda: nc.scalar.copy(out=ev[i][:], in_=ps[:]), reads=[psb], writes=[ev_b[i]])
                            if p == 0:
                                k.dma(k.sp, KZ[0][:, 4, oc, ts], ev[i][:], reads=[ev_b[i]], appends=[scr_b])
                                k.dma(k.sp, KZ[1][:, 4, oc, ts], ev[i][:], reads=[ev_b[i]], appends=[scr_b])
                            elif p == 1:
                                k.dma(k.sp, RAWK[:, oc, ts], ev[i][:], reads=[ev_b[i]], appends=[scr_b])
                            else:
                                k.dma(k.sp, VV[:, oc, ts], ev[i][:], reads=[ev_b[i]], appends=[scr_b])
                        proj_tile(wv_, 0, D, ev_rkv, f"p{p}")
                for p in (3, 4, 5):
                    with contextlib.ExitStack() as es3:
                        if p < 5:
                            nmA, nmB, R_ = ("rwkv_wA", "rwkv_wB", 96) if p == 3 else ("rwkv_aA", "rwkv_aB", 96)
                            dn_w = [es3.enter_context(_sbuf(nc, f"rk_lA{p}{z}", [128, KC, R_], BF16)) for z in range(2)]
                            up_w = [es3.enter_context(_sbuf(nc, f"rk_lB{p}{z}", [128, 1, D], BF16)) for z in range(2)]
                            lw_b = Buf()
                            for z in range(2):
                                k.dma(k.pool, dn_w[z][:], self.I(nmA)[slot, z].rearrange("(kc p) r -> p kc r", p=128), writes=[lw_b])
                                k.dma(k.pool, up_w[z][0:R_, 0, :], self.I(nmB)[slot, z], writes=[lw_b])
                            nz, nch = 2, 1
                        else:
                            R_ = 256
                            dn_w = [es3.enter_context(_sbuf(nc, "rk_lA5", [128, KC, R_], BF16))]
                            up_w = [es3.enter_context(_sbuf(nc, "rk_lB5", [128, 2, D], BF16))]
                            lw_b = Buf()
                            k.dma(k.pool, dn_w[0][:], self.I("rwkv_gA")[slot].rearrange("(kc p) r -> p kc r", p=128), writes=[lw_b])
                            k.dma(k.pool, up_w[0][:], self.I("rwkv_gB")[slot].rearrange("(c p) n -> p c n", p=128), writes=[lw_b])
                            nz, nch = 1, 2
                        for t in range(NGRP):
                            ts = slice(t * TT, (t + 1) * TT)
                            build_xs(p, t)
                            for z in range(nz):
                                rows = 96 if p < 5 else 128
                                for ch in range(nch):
                                    pd, pdb = self.next_ps()
                                    for kc in range(KC):
                                        k.op(k.pe, lambda kc=kc: nc.tensor.matmul(pd[0:rows, :], dn_w[z][:, kc, ch * 128:ch * 128 + rows], xs[:, kc, :],
                                                                                  start=(kc == 0), stop=(kc == KC - 1)),
                                             reads=[lw_b, xs_b], writes=[pdb])
                                    fn = AF.Tanh if p == 3 else (AF.Identity if p == 4 else AF.Sigmoid)
                                    k.op(k.act, lambda: nc.scalar.activation(out=mid[0:rows, ch, :], in_=pd[0:rows, :], func=fn),
                                         reads=[pdb], writes=[mid_b])
                                for oc in range(KC):
                                    pu, pub = self.next_ps()
                                    for ch in range(nch):
                                        k.op(k.pe, lambda ch=ch: nc.tensor.matmul(pu[:], up_w[z][0:rows, ch, oc * 128:(oc + 1) * 128], mid[0:rows, ch, :],
                                                                                  start=(ch == 0), stop=(ch == nch - 1)),
                                             reads=[lw_b, mid_b], writes=[pub])
                                    i = nxt()
                                    if p == 3:
                                        k.op(k.act, lambda: nc.scalar.activation(out=ev[i][:], in_=pu[:], func=AF.Sigmoid, bias=fmv[:, 6 + z, oc:oc + 1]),
                                             reads=[pub, rpk_b], writes=[ev_b[i]])
                                        k.op(k.act, lambda: nc.scalar.activation(out=ev[i][:], in_=ev[i][:], func=AF.Exp, scale=-float(np.exp(-0.5))),
                                             reads=[ev_b[i]], writes=[ev_b[i]])
                                        k.dma(k.sp, KZ[z][:, 1, oc, ts], ev[i][:], reads=[ev_b[i]], appends=[scr_b])
                                    elif p == 4:
                                        k.op(k.act, lambda: nc.scalar.activation(out=ev[i][:], in_=pu[:], func=AF.Sigmoid, bias=fmv[:, 8 + z, oc:oc + 1]),
                                             reads=[pub, rpk_b], writes=[ev_b[i]])
                                        k.dma(k.sp, AZ[z][:, oc, ts], ev[i][:], reads=[ev_b[i]], appends=[scr_b])
                                    else:
                                        k.op(k.act, lambda: nc.scalar.copy(out=ev[i][:], in_=pu[:]), reads=[pub], writes=[ev_b[i]])
                                        k.dma(k.sp, GG[:, oc, ts], ev[i][:], reads=[ev_b[i]], appends=[scr_b])
                        self.scope_end([lw_b])
                self.phase_barrier()
            with contextlib.ExitStack() as es2:
                sbt = lambda name, shape, dt: es2.enter_context(_sbuf(nc, name, list(shape), dt))
                names = ("k", "r", "v", "a0", "a1", "kq", "kk", "t", "kd0", "kd1", "b0", "b1", "vb")
                T2 = {n_: [sbt(f"rk2_{n_}{i}", [128, TT], F32) for i in range(2)] for n_ in names}
                T2b = {n_: [Buf() for _ in range(2)] for n_ in names}
                sqk = [sbt(f"rk2_sq{i}", [128, TT], BF16) for i in range(2)]
                sqk_b = [Buf() for _ in range(2)]
                it = 0
                for t in range(NGRP):
                    ts = slice(t * TT, (t + 1) * TT)
                    for c in range(KC):
                        i = it % 2
                        it += 1
                        X = {n_: T2[n_][i] for n_ in names}
                        B = {n_: T2b[n_][i] for n_ in names}
                        k.dma(k.sp, X["k"][:], RAWK[:, c, ts], reads=[scr_b], writes=[B["k"]])
                        k.dma(k.sp, X["r"][:], KZ[0][:, 4, c, ts], reads=[scr_b], writes=[B["r"]])
                        k.dma(k.sp, X["v"][:], VV[:, c, ts], reads=[scr_b], writes=[B["v"]])
                        k.dma(k.sp, X["a0"][:], AZ[0][:, c, ts], reads=[scr_b], writes=[B["a0"]])
                        k.dma(k.sp, X["a1"][:], AZ[1][:, c, ts], reads=[scr_b], writes=[B["a1"]])
                        k.op(k.act, lambda: nc.scalar.activation(out=X["kq"][:], in_=X["k"][:], func=AF.Identity, scale=fmv[:, 10, c:c + 1]),
                             reads=[B["k"], rpk_b], writes=[B["kq"]])
                        k.op(k.act, lambda: nc.scalar.activation(out=sqk[i][:], in_=X["kq"][:], func=AF.Square), reads=[B["kq"]], writes=[sqk_b[i]])
                        pn, pnb = self.next_ps()
                        k.op(k.pe, lambda: nc.tensor.matmul(pn[:], BO[:], sqk[i][:], start=True, stop=True), reads=[cst_b, sqk_b[i]], writes=[pnb])
                        k.op(k.act, lambda: nc.scalar.activation(out=pn[:], in_=pn[:], func=AF.Sqrt), reads=[pnb], writes=[pnb])
                        k.op(k.dve, lambda: nc.vector.tensor_scalar(out=pn[:], in0=pn[:], scalar1=1e-12, scalar2=None, op0=ALU.max), reads=[pnb], writes=[pnb])
                        k.op(k.dve, lambda: nc.vector.reciprocal(out=pn[:], in_=pn[:]), reads=[pnb], writes=[pnb])
                        k.op(k.dve, lambda: nc.vector.tensor_tensor(out=X["kk"][:], in0=X["kq"][:], in1=pn[:], op=ALU.mult),
                             reads=[B["kq"], pnb], writes=[B["kk"]])
                        k.dma(k.sp, KZ[0][:, 0, c, ts], X["kk"][:], reads=[B["kk"]], appends=[scr_b])
                        k.dma(k.sp, KZ[1][:, 0, c, ts], X["kk"][:], reads=[B["kk"]], appends=[scr_b])
                        for z in range(2):
                            az, kd, bz = X[f"a{z}"], X[f"kd{z}"], X[f"b{z}"]
                            k.op(k.dve, lambda: nc.vector.tensor_scalar(out=X["t"][:], in0=az[:], scalar1=-1.0, scalar2=fmv[:, 11, c:c + 1],
                                                                        op0=ALU.add, op1=ALU.mult), reads=[B[f"a{z}"], rpk_b], writes=[B["t"]])
                            k.op(k.dve, lambda: nc.vector.scalar_tensor_tensor(out=kd[:], in0=X["t"][:], scalar=1.0, in1=X["k"][:], op0=ALU.add, op1=ALU.mult),
                                 reads=[B["t"], B["k"]], writes=[B[f"kd{z}"]])
                            k.op(k.dve, lambda: nc.vector.tensor_tensor(out=bz[:], in0=X["kk"][:], in1=az[:], op=ALU.mult),
                                 reads=[B["kk"], B[f"a{z}"]], writes=[B[f"b{z}"]])
                            k.dma(k.sp, KZ[z][:, 3, c, ts], kd[:], reads=[B[f"kd{z}"]], appends=[scr_b])
                            k.dma(k.sp, KZ[z][:, 2, c, ts], bz[:], reads=[B[f"b{z}"]], appends=[scr_b])
                        k.op(k.dve, lambda: nc.vector.tensor_tensor(out=X["t"][:], in0=X["kd0"][:], in1=X["kd1"][:], op=ALU.add),
                             reads=[B["kd0"], B["kd1"]], writes=[B["t"]])
                        k.op(k.dve, lambda: nc.vector.scalar_tensor_tensor(out=sqk[i][:], in0=X["t"][:], scalar=fmv[:, 12, c:c + 1], in1=X["r"][:],
                                                                           op0=ALU.mult, op1=ALU.mult),
                             reads=[B["t"], B["r"], rpk_b], writes=[sqk_b[i]])
                        pbn, pbnb = self.next_ps()
                        k.op(k.pe, lambda: nc.tensor.matmul(pbn[:], BO[:], sqk[i][:], start=True, stop=True), reads=[cst_b, sqk_b[i]], writes=[pbnb])
                        k.op(k.dve, lambda: nc.vector.tensor_tensor(out=X["vb"][:], in0=X["v"][:], in1=pbn[:], op=ALU.mult),
                             reads=[B["v"], pbnb], writes=[B["vb"]])
                        k.dma(k.sp, VB[:, c, ts], X["vb"][:], reads=[B["vb"]], appends=[scr_b])
                self.phase_barrier()
            with contextlib.ExitStack() as es2:
                sbt = lambda name, shape, dt: es2.enter_context(_sbuf(nc, name, list(shape), dt))
                CH = []
                for z in range(2):
                    ch = dict(z=z)
                    ch["S"] = [sbt(f"rks_S{z}{i}", [128, KC, 64], F32) for i in range(2)]
                    ch["S_b"] = [Buf() for _ in range(2)]
                    ch["cur"] = 0
                    ch["tA1"] = sbt(f"rks_tA1{z}", [128, KC, 64], BF16)
                    ch["tA4"] = sbt(f"rks_tA4{z}", [128, KC, 64], BF16)
                    ch["vd"] = [sbt(f"rks_vd{z}{i}", [128, KC, 64], BF16) for i in range(2)]
                    ch["vd_b"] = [Buf() for _ in range(2)]
                    ch["psv"] = [None, None]
                    ch["tF2"] = sbt(f"rks_tF2{z}", [128, KC, 64], F32)
                    ch["tF3"] = ch["tF2"]
                    ch["kb"] = [sbt(f"rks_kb{z}{i}", [128, 5, KC, TB], F32) for i in range(2)]
                    ch["vb"] = [sbt(f"rks_vb{z}{i}", [128, KC, TB], F32) for i in range(2)]
                    ch["ys"] = sbt(f"rks_ys{z}", [2, SB, KC * 64], BF16)
                    for n_ in ("tA1", "tA4", "tF2", "ys"):
                        ch[n_ + "_b"] = Buf()
                    ch["tF3_b"] = ch["tF2_b"]
                    ch["kb_b"] = [Buf() for _ in range(2)]
                    ch["vb_b"] = [Buf() for _ in range(2)]
                    CH.append(ch)
                e_t4 = k.pool if self.cfg.get("rk_pool", True) else k.dve
                veng = lambda e: (nc.gpsimd if e is k.pool else nc.vector)

                def load_blk(ch, t0, i):
                    z = ch["z"]
                    k.dma(k.sp, ch["kb"][i][:], KZ[z][:, :, :, t0:t0 + TB], reads=[scr_b], writes=[ch["kb_b"][i]])
                    k.dma(k.sp, ch["vb"][i][:], VV[:, :, t0:t0 + TB], reads=[scr_b], writes=[ch["vb_b"][i]])

                def stage0(n, ch, i, j):
                    vi = n % 2
                    vcol = ch["vb"][i][:, :, j:j + 1].to_broadcast([128, KC, 64])
                    k.op(k.pool, lambda: nc.gpsimd.tensor_tensor(out=ch["vd"][vi][:], in0=I2.unsqueeze(1).to_broadcast([128, KC, 64]), in1=vcol,
                                                                 op=ALU.mult),
                         reads=[rpk_b, ch["vb_b"][i]], writes=[ch["vd_b"][vi]])

                def run_pass(T0, T1, is_L):
                    nblk = (T1 - T0) // TB
                    seq = []
                    for bi in range(nblk):
                        for jj in range(TB):
                            ent = []
                            for ch in CH:
                                z = ch["z"]
                                t0 = T0 + bi * TB if z == 0 else T1 - (bi + 1) * TB
                                j = jj if z == 0 else TB - 1 - jj
                                ent.append((ch, bi % 2, j, t0 + j))
                            seq.append((bi, jj, ent))
                    for ch in CH:
                        load_blk(ch, T0 if ch["z"] == 0 else T1 - TB, 0)
                    for (ch, i, j, t) in seq[0][2]:
                        stage0(0, ch, i, j)
                    for n, (bi, jj, ent) in enumerate(seq):
                        if jj == 0 and bi + 1 < nblk:
                            for ch in CH:
                                nt0 = T0 + (bi + 1) * TB if ch["z"] == 0 else T1 - (bi + 2) * TB
                                load_blk(ch, nt0, (bi + 1) % 2)
                        if n + 1 < len(seq):
                            for (ch, i, j, t) in seq[n + 1][2]:
                                stage0(n + 1, ch, i, j)
                        psa = {}
                        for (ch, i, j, t) in ent:
                            z = ch["z"]
                            Sa, Sab = ch["S"][ch["cur"]], ch["S_b"][ch["cur"]]
                            at_start = (t % SEG == 0) if z == 0 else (t % SEG == SEG - 1)
                            if at_start:
                                chain_start = (t == T0) if z == 0 else (t == T1 - 1)
                                if not is_L:
                                    k.op(k.dve, lambda: nc.vector.memset(Sa[:], 0.0), writes=[Sab])
                                elif chain_start:
                                    k.dma(k.sp, Sa[:].rearrange("p g v -> p (g v)"), self.I("rk_initS")[z], writes=[Sab])
                                else:
                                    k.op(k.dve, lambda: nc.vector.tensor_scalar(out=Sa[:], in0=Sa[:], scalar1=flag, scalar2=None, op0=ALU.mult),
                                         reads=[Sab, rpk_b], writes=[Sab])
                            kb, kbb = ch["kb"][i], ch["kb_b"][i]
                            k.op(k.dve, lambda: nc.vector.tensor_tensor(out=ch["tA1"][:], in0=Sa[:], in1=kb[:, 0, :, j:j + 1].to_broadcast([128, KC, 64]),
                                                                        op=ALU.mult), reads=[Sab, kbb], writes=[ch["tA1_b"]])
                            tAf = ch["tA1"][:].rearrange("p g v -> p (g v)")
                            pl = []
                            for q in range(2):
                                p_, pb_ = self.next_ps()
                                k.op(k.pe, lambda: nc.tensor.matmul(p_[:], BO[:], tAf[:, q * 512:(q + 1) * 512], start=True, stop=True),
                                     reads=[cst_b, ch["tA1_b"]], writes=[pb_])
                                pl.append((p_, pb_))
                            psa[z] = pl
                        for (ch, i, j, t) in ent:
                            vi = n % 2
                            vdf = ch["vd"][vi][:].rearrange("p g v -> p (g v)")
                            pl = []
                            for q in range(2):
                                p_, pb_ = self.next_ps()
                                k.op(k.pe, lambda: nc.tensor.matmul(p_[:], BO[:], vdf[:, q * 512:(q + 1) * 512], start=True, stop=True),
                                     reads=[cst_b, ch["vd_b"][vi]], writes=[pb_])
                                pl.append((p_, pb_))
                            ch["psv"][vi] = pl
                        for (ch, i, j, t) in ent:
                            kb, kbb = ch["kb"][i], ch["kb_b"][i]
                            for q in range(2):
                                p_, pb_ = ch["psv"][n % 2][q]
                                gs = slice(q * 8, (q + 1) * 8)
                                k.op(k.dve, lambda: nc.vector.tensor_tensor(out=ch["tF3"][:, gs, :], in0=p_[:].rearrange("p (g v) -> p g v", v=64),
                                                                            in1=kb[:, 3, gs, j:j + 1].to_broadcast([128, 8, 64]), op=ALU.mult),
                                     reads=[pb_, kbb], writes=[ch["tF3_b"]])
                        e_sw = k.pool if self.cfg.get("rk_sw_pool", True) else k.dve
                        for (ch, i, j, t) in ent:
                            kb, kbb = ch["kb"][i], ch["kb_b"][i]
                            cur = ch["cur"]
                            Sa, Sab, Sn, Snb = ch["S"][cur], ch["S_b"][cur], ch["S"][1 - cur], ch["S_b"][1 - cur]
                            k.op(e_sw, lambda: veng(e_sw).tensor_tensor(out=Sn[:], in0=Sa[:], in1=kb[:, 1, :, j:j + 1].to_broadcast([128, KC, 64]),
                                                                        op=ALU.mult), reads=[Sab, kbb], writes=[Snb])
                        for (ch, i, j, t) in ent:
                            cur = ch["cur"]
                            Sn, Snb = ch["S"][1 - cur], ch["S_b"][1 - cur]
                            k.op(k.dve, lambda: nc.vector.tensor_tensor(out=Sn[:], in0=Sn[:], in1=ch["tF3"][:], op=ALU.add),
                                 reads=[Snb, ch["tF3_b"]], writes=[Snb])
                        for (ch, i, j, t) in ent:
                            z = ch["z"]
                            kb, kbb = ch["kb"][i], ch["kb_b"][i]
                            for q in range(2):
                                p_, pb_ = psa[z][q]
                                gs = slice(q * 8, (q + 1) * 8)
                                k.op(k.dve, lambda: nc.vector.tensor_tensor(out=ch["tF2"][:, gs, :], in0=p_[:].rearrange("p (g v) -> p g v", v=64),
                                                                            in1=kb[:, 2, gs, j:j + 1].to_broadcast([128, 8, 64]), op=ALU.mult),
                                     reads=[pb_, kbb], writes=[ch["tF2_b"]])
                        for (ch, i, j, t) in ent:
                            cur = ch["cur"]
                            Sn, Snb = ch["S"][1 - cur], ch["S_b"][1 - cur]
                            k.op(k.dve, lambda: nc.vector.tensor_tensor(out=Sn[:], in0=Sn[:], in1=ch["tF2"][:], op=ALU.subtract),
                                 reads=[Snb, ch["tF2_b"]], writes=[Snb])
                            ch["cur"] = 1 - cur
                        for (ch, i, j, t) in ent:
                            z = ch["z"]
                            kb, kbb = ch["kb"][i], ch["kb_b"][i]
                            Sc, Scb = ch["S"][ch["cur"]], ch["S_b"][ch["cur"]]
                            k.op(e_t4, lambda: veng(e_t4).tensor_tensor(out=ch["tA4"][:], in0=Sc[:], in1=kb[:, 4, :, j:j + 1].to_broadcast([128, KC, 64]),
                                                                        op=ALU.mult), reads=[Scb, kbb], writes=[ch["tA4_b"]])
                            t4f = ch["tA4"][:].rearrange("p g v -> p (g v)")
                            sidx = (jj % SB) if z == 0 else (SB - 1 - (jj % SB))
                            for q in range(2):
                                p_, pb_ = self.next_ps()
                                k.op(k.pe, lambda: nc.tensor.matmul(p_[0:2, :], hsel[:], t4f[:, q * 512:(q + 1) * 512], start=True, stop=True),
                                     reads=[cst_b, ch["tA4_b"]], writes=[pb_])
                                k.op(k.act, lambda: nc.scalar.copy(out=ch["ys"][:, sidx, q * 512:(q + 1) * 512], in_=p_[0:2, :]),
                                     reads=[pb_], writes=[ch["ys_b"]])
                            if jj % SB == SB - 1:
                                tlo = t - (SB - 1) if z == 0 else t
                                dst = YT[z][tlo:tlo + SB, :].rearrange("t (g hp v) -> hp t g v", hp=2, v=64)
                                k.dma(k.sp, dst, ch["ys"][:].rearrange("p s (g v) -> p s g v", v=64), reads=[ch["ys_b"]], appends=[yt_b])
                            at_end = (t % SEG == SEG - 1) if z == 0 else (t % SEG == 0)
                            if at_end:
                                k.dma(k.sp, self.O("rk_stS")[t // SEG, z], Sc[:].rearrange("p g v -> p (g v)"), reads=[Scb], writes=[outb])

                run_pass(0, 1024, True)
                run_pass(1024, 1280, False)
                run_pass(1280, 1536, False)
                self.phase_barrier()
            with contextlib.ExitStack() as es2:
                sbt = lambda name, shape, dt: es2.enter_context(_sbuf(nc, name, list(shape), dt))
                rln = sbt("rk_rln", [128, 2, D], F32)
                rln_b = Buf()
                k.dma(k.sp, rln[:], self.I("rln").rearrange("p (a d) -> p a d", a=2), writes=[rln_b])
                yf = [sbt(f"rkc_yf{i}", [128, D], BF16) for i in range(2)]
                yb = [sbt(f"rkc_yb{i}", [128, D], BF16) for i in range(2)]
                yfb = [Buf() for _ in range(2)]
                ybb = [Buf() for _ in range(2)]
                ysum = sbt("rkc_ysum", [128, 32, 64], F32)
                ysq = sbt("rkc_ysq", [128, 32, 64], F32)
                ysum_b, ysq_b = Buf(), Buf()
                st1 = sbt("rkc_st1", [128, 32], F32)
                st2 = sbt("rkc_st2", [128, 32], F32)
                st1_b, st2_b = Buf(), Buf()
                yT = sbt("rkc_yT", [128, KC, 128], F32)
                yT_b = Buf()
                gl = [sbt(f"rkc_gl{i}", [128, KC, 128], F32) for i in range(2)]
                vl = [sbt(f"rkc_vl{i}", [128, KC, 128], F32) for i in range(2)]
                glb = [Buf() for _ in range(2)]
                vlb = [Buf() for _ in range(2)]
                zT = sbt("rkc_zT", [128, KC, 128], BF16)
                zT_b = Buf()

                def loadc(blk):
                    i = blk % 2
                    sl = slice(blk * 128, (blk + 1) * 128)
                    k.dma(k.sp, yf[i][:], YT[0][sl, :], reads=[yt_b], writes=[yfb[i]])
                    k.dma(k.sp, yb[i][:], YT[1][sl, :], reads=[yt_b], writes=[ybb[i]])
                    k.dma(k.sp, gl[i][:], GG[:, :, sl], reads=[scr_b], writes=[glb[i]])
                    k.dma(k.sp, vl[i][:], VB[:, :, sl], reads=[scr_b], writes=[vlb[i]])

                loadc(0)
                NBk = NTOK // 128
                for blk in range(NBk):
                    i = blk % 2
                    if blk + 1 < NBk:
                        loadc(blk + 1)
                    ysf = ysum[:].rearrange("p h v -> p (h v)")
                    k.op(k.dve, lambda: nc.vector.tensor_tensor(out=ysf, in0=yf[i][:], in1=yb[i][:], op=ALU.add),
                         reads=[yfb[i], ybb[i]], writes=[ysum_b])
                    k.op(k.dve, lambda: nc.vector.tensor_reduce(out=st1[:], in_=ysum[:], axis=AX.X, op=ALU.add), reads=[ysum_b], writes=[st1_b])
                    k.op(k.dve, lambda: nc.vector.tensor_scalar(out=st1[:], in0=st1[:], scalar1=1.0 / 64, scalar2=None, op0=ALU.mult),
                         reads=[st1_b], writes=[st1_b])
                    k.op(k.dve, lambda: nc.vector.tensor_tensor(out=ysum[:], in0=ysum[:], in1=st1[:].unsqueeze(2).to_broadcast([128, 32, 64]),
                                                                op=ALU.subtract), reads=[ysum_b, st1_b], writes=[ysum_b])
                    k.op(k.act, lambda: nc.scalar.activation(out=ysq[:], in_=ysum[:], func=AF.Square), reads=[ysum_b], writes=[ysq_b])
                    k.op(k.dve, lambda: nc.vector.tensor_reduce(out=st2[:], in_=ysq[:], axis=AX.X, op=ALU.add), reads=[ysq_b], writes=[st2_b])
                    k.op(k.dve, lambda: nc.vector.tensor_scalar(out=st2[:], in0=st2[:], scalar1=1.0 / 64, scalar2=64e-5, op0=ALU.mult, op1=ALU.add),
                         reads=[st2_b], writes=[st2_b])
                    k.op(k.act, lambda: nc.scalar.activation(out=st2[:], in_=st2[:], func=AF.Sqrt), reads=[st2_b], writes=[st2_b])
                    k.op(k.dve, lambda: nc.vector.reciprocal(out=st2[:], in_=st2[:]), reads=[st2_b], writes=[st2_b])
                    k.op(k.dve, lambda: nc.vector.tensor_tensor(out=ysum[:], in0=ysum[:], in1=st2[:].unsqueeze(2).to_broadcast([128, 32, 64]),
                                                                op=ALU.mult), reads=[ysum_b, st2_b], writes=[ysum_b])
                    k.op(k.dve, lambda: nc.vector.tensor_tensor(out=ysf, in0=ysf, in1=rln[:, 0, :], op=ALU.mult), reads=[ysum_b, rln_b], writes=[ysum_b])
                    k.op(k.dve, lambda: nc.vector.tensor_tensor(out=ysf, in0=ysf, in1=rln[:, 1, :], op=ALU.add), reads=[ysum_b, rln_b], writes=[ysum_b])
                    for q4 in range(4):
                        ptr, ptrb = self.next_ps()
                        for c4 in range(4):
                            c = q4 * 4 + c4
                            k.op(k.pe, lambda: nc.tensor.transpose(ptr[:, c4 * 128:(c4 + 1) * 128], ysf[:, c * 128:(c + 1) * 128], ident),
                                 reads=[ysum_b, rpk_b], writes=[ptrb])
                        k.op(k.act, lambda: nc.scalar.copy(out=yT[:, q4 * 4:(q4 + 1) * 4, :], in_=ptr[:].rearrange("p (c t) -> p c t", t=128)),
                             reads=[ptrb], writes=[yT_b])
                    k.op(k.dve, lambda: nc.vector.tensor_tensor(out=yT[:], in0=yT[:], in1=vl[i][:], op=ALU.add), reads=[yT_b, vlb[i]], writes=[yT_b])
                    k.op(k.dve, lambda: nc.vector.tensor_tensor(out=zT[:], in0=yT[:], in1=gl[i][:], op=ALU.mult), reads=[yT_b, glb[i]], writes=[zT_b])
                    k.dma(k.sp, hsT_s[:, :, blk * 128:(blk + 1) * 128].rearrange("c p t -> p c t"), zT[:], reads=[zT_b], appends=[hsT_b])
                self.phase_barrier()
            with contextlib.ExitStack() as es2:
                sbt = lambda name, shape, dt: es2.enter_context(_sbuf(nc, name, list(shape), dt))
                h2 = sbt("rk_h2", [128, KC, NTOK], BF16)
                h2b = [Buf() for _ in range(NGRP)]
                for t in range(NGRP):
                    k.dma(k.sp, h2[:, :, t * TT:(t + 1) * TT], hsT_s[:, :, t * TT:(t + 1) * TT].rearrange("c p t -> p c t"),
                          reads=[hsT_b], writes=[h2b[t]])
                wo_v = self.I("rwkv_w_o")[slot].rearrange("(kc p) n -> p kc n", p=128)

                def ev_o(oc, t, ps, psb):
                    ts = slice(t * TT, (t + 1) * TT)
                    k.op(k.dve, lambda: nc.vector.scalar_tensor_tensor(
                        out=self.x_sb[:, oc, ts], in0=ps[:], scalar=self.modT[:, l, 5, oc, t:t + 1],
                        in1=self.x_sb[:, oc, ts], op0=ALU.mult, op1=ALU.add),
                        reads=[psb, self.modT_b, self.xb[oc][t]], writes=[self.xb[oc][t]])

                self.linear_fm(h2, h2b, wo_v, 0, D, ev_o, "ro")
                self.phase_barrier()

    def layer(self, l):
        cfg = self.cfg
        ph = cfg.get("phases", ("ffn1", "mix", "ffn2"))
        if "ffn1" in ph:
            self.ffn(l, 0)
        if "mix" in ph:
            if l % 3 == 0:
                self.attn(l)
            elif l % 3 == 1:
                self.mlstm(l)
            else:
                self.rwkv(l)
        if "ffn2" in ph:
            self.ffn(l, 1)

    def finish(self):
        k = self.k
        yv = self.O("yT").rearrange("(c p) t -> p c t", p=128)
        outb = Buf("out")
        for c in range(KC):
            k.dma(k.sp, yv[:, c, :], self.x_sb[:, c, :], reads=self.xb[c], writes=[outb])
        self.phase_barrier()


def build_program(cfg):
    p = Prog(cfg)
    nc = p.build()
    nc._prog = p
    return nc


def filter_maps(nc, maps):
    names = set(nc._prog.inputs.keys())
    return [{k_: v for k_, v in m.items() if k_ in names} for m in maps]


def rope_tables(core):
    cos = np.ones((128, NTOK), np.float32)
    sin = np.zeros((128, NTOK), np.float32)
    if core < 4:
        t = np.arange(1024)
        row = (t // 64).astype(np.float32)
        col = (t % 64).astype(np.float32)
        inv = (10000.0 ** (-np.arange(32, dtype=np.float32) / 32)).astype(np.float32)
        for d in range(128):
            pos = row if d < 64 else col
            ang = pos * inv[d % 32]
            cos[d, :1024] = np.cos(ang)
            sin[d, :1024] = np.sin(ang)
    return cos, sin


def perm_T():
    PT = np.zeros((128, 128), np.float32)
    for m in range(128):
        if (m % 64) < 32:
            PT[m + 32, m] = -1.0
        else:
            PT[m - 32, m] = 1.0
    return PT


def attn_masks(core):
    M = np.zeros((128, 14, 128), np.float32)
    iq = np.arange(128)[None, :]
    is_ = np.arange(128)[:, None]
    for qb in range(8):
        if qb >= 1:
            if core < 4:
                M[:, 2 * (qb - 1), :] = (iq <= is_)
            else:
                M[:, 2 * (qb - 1), :] = 1.0 if (qb // 2 == (qb - 1) // 2) else 0.0
        if qb <= 6:
            if core < 4:
                M[:, 2 * qb + 1, :] = (is_ <= iq)
            else:
                M[:, 2 * qb + 1, :] = 1.0 if (qb // 2 == (qb + 1) // 2) else 0.0
    return M.reshape(128, 14 * 128)


def attn_pack(core, inp):
    A = np.zeros((2, 128, NAP), np.float32)
    cos, sin = rope_tables(core)
    for slot in range(2):
        A[slot, :, 0] = inp["attn_q_norm"][slot]
        A[slot, :, 1] = inp["attn_k_norm"][slot]
        A[slot, :, 2:18] = inp["attn_sink"][slot][None, :]
        A[slot, :, 18] = 1.0 if core < 4 else 0.0
        A[slot, :, 19:147] = np.eye(128, dtype=np.float32)
        A[slot, :, 147:147 + NTOK] = cos
        A[slot, :, 147 + NTOK:] = sin
    return A


def mlstm_pack(core, inp):
    M = np.zeros((1, 128, NMP), np.float32)
    p = np.arange(128)[:, None]
    f = np.arange(128)[None, :]
    M[0, :, 0:128] = np.eye(128, dtype=np.float32)
    M[0, :, 128:256] = 1.0
    M[0, :, 256:384] = (p <= f)
    M[0, :, 384:512] = (p >= f)
    M[0, :, 512:640] = np.where(p <= f, 0.0, -1e30)
    M[0, :, 640:768] = np.where(p >= f, 0.0, -1e30)
    M[0, :, 768:800] = inp["mlstm_b_gate"][0][None, :]
    M[0, :, 800] = 1.0 if core < 4 else 0.0
    M[0, :, 801:801 + D] = inp["mlstm_out_norm"][0][None, :]
    return M


def mlstm_init(core, inp):
    C0 = np.zeros((2, 128, 8, 257), np.float32)
    m0 = np.zeros((2, 128, 8), np.float32)
    if core < 4:
        C = inp["state_mlstm_C"][core, 0]
        n = inp["state_mlstm_n"][core, 0]
        m = inp["state_mlstm_m"][core, 0]
        C0[:, :, :, :256] = np.transpose(C, (0, 2, 1, 3))
        C0[:, :, :, 256] = np.transpose(n, (0, 2, 1))
        m0[:] = m[:, None, :]
    return C0.reshape(2, 128, 8 * 257), m0


def rwkv_host(core, inp):
    segs = core_segments(core)
    P = np.zeros((128, NRP), np.float32)
    NFM = 13 * KC
    vecs = [inp["rwkv_mu"][0][i] for i in range(6)] + [inp["rwkv_w0"][0][0], inp["rwkv_w0"][0][1], inp["rwkv_a0"][0][0], inp["rwkv_a0"][0][1],
                                                       inp["rwkv_k_k"][0], inp["rwkv_k_a"][0], inp["rwkv_r_k"][0].reshape(-1)]
    P[:, 0:NFM] = fm(np.stack(vecs, 0)).reshape(128, NFM)
    I2 = np.zeros((128, 64), np.float32)
    I2[np.arange(128), np.arange(128) % 64] = 1.0
    P[:, NFM:NFM + 64] = I2
    P[:, NFM + 64] = 1.0 if core < 4 else 0.0
    P[:, NFM + 65:NFM + 193] = np.eye(128, dtype=np.float32)
    seq_id = []
    for g in range(NSEG):
        kind, idx = segs[g]
        seq_id += [(0 if kind == "lat" else 1, idx)] * SEG
    pm = np.zeros(NTOK, np.float32)
    nm = np.zeros(NTOK, np.float32)
    for t in range(NTOK):
        if t > 0 and seq_id[t - 1] == seq_id[t]:
            pm[t] = 1.0
        if t < NTOK - 1 and seq_id[t + 1] == seq_id[t]:
            nm[t] = 1.0
    rmask = np.ascontiguousarray(np.broadcast_to(np.concatenate([pm, nm])[None, :], (128, 2 * NTOK)))
    rln = np.ascontiguousarray(np.broadcast_to(np.concatenate([inp["rwkv_ln_g"][0], inp["rwkv_ln_b"][0]])[None, :], (128, 2 * D)))
    initS = np.zeros((2, 128, 1024), np.float32)
    if core < 4:
        S0 = inp["state_rwkv"][core, 0]
        S0 = S0.reshape(2, 16, 2, 64, 64)
        initS = np.ascontiguousarray(np.transpose(S0, (0, 2, 4, 1, 3))).reshape(2, 128, 1024)
    return P, rmask, rln, initS


def make_in_maps(inp, cfg, cores=None):
    maps = []
    layers = list(cfg["layers"])
    ph = cfg.get("phases", ("ffn1", "mix", "ffn2"))
    full = (layers == [0, 1, 2, 3])
    mod_w_sel = inp["mod_w"] if full else np.ascontiguousarray(inp["mod_w"][layers])
    has_ffn = ("ffn1" in ph or "ffn2" in ph)
    if has_ffn:
        ffn_in_sel = inp["ffn_w_in"] if full else np.ascontiguousarray(inp["ffn_w_in"][layers])
        ffn_out_sel = inp["ffn_w_out"] if full else np.ascontiguousarray(inp["ffn_w_out"][layers])
    for core in (range(NCORES) if cores is None else cores):
        segs = core_segments(core)
        rows = []
        for g in range(NSEG):
            kind, idx = segs[g]
            if kind == "lat":
                rows.append(inp["x_sample"][idx, g * SEG:(g + 1) * SEG])
            else:
                rows.append(inp["x_prompt"][idx])
        xs = np.concatenate(rows, 0)
        m = {
            "xT": np.ascontiguousarray(xs.T),
            "pack": host_pack(core, inp),
            "mod_w": mod_w_sel,
            "attn_w_qkv": inp["attn_w_qkv"],
            "attn_w_o": inp["attn_w_o"],
            "apack": attn_pack(core, inp),
            "amask": attn_masks(core),
            "permT": perm_T(),
            "cache_k": (inp["cache_k"][core].reshape(2, 256, 512) if core < 4 else np.zeros((2, 256, 512), np.float32)),
            "cache_v": (inp["cache_v"][core].reshape(2, 256, 512) if core < 4 else np.zeros((2, 256, 512), np.float32)),
        }
        m["rpack"], m["rmask"], m["rln"], m["rk_initS"] = rwkv_host(core, inp)
        for nm_ in ("rwkv_w_rkv", "rwkv_wA", "rwkv_wB", "rwkv_aA", "rwkv_aB", "rwkv_gA", "rwkv_gB", "rwkv_w_o"):
            m[nm_] = inp[nm_]
        m["mlstm_w_in"] = inp["mlstm_w_in"]
        m["mlstm_w_gate"] = inp["mlstm_w_gate"]
        m["mlstm_w_o"] = inp["mlstm_w_o"]
        m["mpack"] = mlstm_pack(core, inp)
        m["ml_initC"], m["ml_initm"] = mlstm_init(core, inp)
        if has_ffn:
            m["ffn_w_in"] = ffn_in_sel
            m["ffn_w_out"] = ffn_out_sel
        maps.append(m)
    return maps


FULL_CFG = {"layers": [0, 1, 2, 3], "phases": ("ffn1", "mix", "ffn2")}


def kernel(**inputs):
    inp = {k_: np.asarray(v) for k_, v in inputs.items()}
    cfg = FULL_CFG
    nc = build_program(cfg)
    maps = filter_maps(nc, make_in_maps(inp, cfg))
    res = run_bass_kernel_spmd(nc, maps, core_ids=list(range(NCORES)))
    R = res.results
    B, S_, DB, DS = 32, 256, 4, 1024
    y_prompt = np.zeros((B, S_, D), np.float32)
    y_sample = np.zeros((DB, DS, D), np.float32)
    new_k = np.zeros((B, 2, S_, 4, 128), np.float32)
    new_v = np.zeros((B, 2, S_, 4, 128), np.float32)
    new_C = np.zeros((B, 1, 2, 8, 128, 256), np.float32)
    new_n = np.zeros((B, 1, 2, 8, 128), np.float32)
    new_m = np.zeros((B, 1, 2, 8), np.float32)
    new_S = np.zeros((B, 1, 2, 32, 64, 64), np.float32)
    for core in range(NCORES):
        r = R[core]
        y = np.asarray(r["yT"]).T
        segs = core_segments(core)
        for g in range(NSEG):
            kind, idx = segs[g]
            rows = y[g * SEG:(g + 1) * SEG]
            if kind == "lat":
                y_sample[idx, g * SEG:(g + 1) * SEG] = rows
            else:
                y_prompt[idx] = rows
                for slot in range(2):
                    new_k[idx, slot] = np.asarray(r["newk"])[slot, g * SEG:(g + 1) * SEG].reshape(S_, 4, 128)
                    new_v[idx, slot] = np.asarray(r["newv"])[slot, g * SEG:(g + 1) * SEG].reshape(S_, 4, 128)
                Ck = np.asarray(r["ml_stC"])[g].reshape(2, 128, 8, 257)
                new_C[idx, 0] = np.transpose(Ck[..., :256], (0, 2, 1, 3))
                new_n[idx, 0] = np.transpose(Ck[..., 256], (0, 2, 1))
                new_m[idx, 0] = np.asarray(r["ml_stm"])[g][:, 0, :]
                Sk = np.asarray(r["rk_stS"])[g].reshape(2, 2, 64, 16, 64)
                new_S[idx, 0] = np.transpose(Sk, (0, 3, 1, 4, 2)).reshape(2, 32, 64, 64)
    return (y_prompt, y_sample, new_k, new_v, new_C, new_n, new_m, new_S)
```

```python
import contextlib
import numpy as np
import concourse.bass as bass
import concourse.mybir as mybir
from concourse.bass_utils import run_bass_kernel_spmd

F32 = mybir.dt.float32
BF16 = mybir.dt.bfloat16
AF = mybir.ActivationFunctionType
ALU = mybir.AluOpType
AX = mybir.AxisListType

D = 2048
KC = 16
NTOK = 1536
NSEG = 6
SEG = 256
NGRP = 3
TT = 512
DEPTH = 4
DFF = 5632
NMOD = 9
EPS = 1e-6
NCORES = 8
NAP = 147 + 2 * NTOK
NMP = 801 + D
NRP = 13 * KC + 64 + 1 + 128


_UNIQ = [0]


def _sbuf(nc, name, shape, dt):
    _UNIQ[0] += 1
    return nc.sbuf_tensor(f"{name}_u{_UNIQ[0]}", shape, dt)


class Buf:
    __slots__ = ("w", "r", "name", "excl", "mw")

    def __init__(self, name="", excl=False):
        self.w = None
        self.r = {}
        self.mw = {}
        self.name = name
        self.excl = excl


class Eng:
    def __init__(self, k, name, handle, is_pe=False):
        self.k = k
        self.name = name
        self.h = handle
        self.is_pe = is_pe
        self.sem = None
        self.count = 0
        self.waited = {}
        self.nsem = 0

    def cur_sem(self):
        if self.sem is None or self.count >= 30000:
            self.sem = self.k.new_sem(f"{self.name}{self.nsem}")
            self.nsem += 1
            self.count = 0
        return self.sem


class K:
    def __init__(self, nc, es):
        self.nc = nc
        self.es = es
        self.semcount = 0
        self.pe = Eng(self, "pe", nc.tensor, is_pe=True)
        self.act = Eng(self, "act", nc.scalar)
        self.dve = Eng(self, "dve", nc.vector)
        self.pool = Eng(self, "pool", nc.gpsimd)
        self.sp = Eng(self, "sp", nc.sync)
        self.dma_sems = []
        self.dma_rr = 0
        self.sync_same_engine = True
        self.nosync_engines = set()
        self.n_inst = 0

    def new_sem(self, name):
        self.semcount += 1
        return self.es.enter_context(self.nc.semaphore(f"s_{name}_{self.semcount}"))

    def sb(self, name, shape, dt):
        return self.es.enter_context(_sbuf(self.nc, name, list(shape), dt))

    def _wait_deps(self, eng, reads, writes, appends=()):
        deps = {}
        for b in appends:
            if b.w is not None:
                s, v = b.w
                if deps.get(s, 0) < v:
                    deps[s] = v
            for s, v in b.r.items():
                if deps.get(s, 0) < v:
                    deps[s] = v
        for b in reads:
            if b.w is not None:
                s, v = b.w
                if deps.get(s, 0) < v:
                    deps[s] = v
            for s, v in b.mw.items():
                if deps.get(s, 0) < v:
                    deps[s] = v
            if b.excl:
                for s, v in b.r.items():
                    if deps.get(s, 0) < v:
                        deps[s] = v
        for b in writes:
            if b.w is not None:
                s, v = b.w
                if deps.get(s, 0) < v:
                    deps[s] = v
            for s, v in b.r.items():
                if deps.get(s, 0) < v:
                    deps[s] = v
            for s, v in b.mw.items():
                if deps.get(s, 0) < v:
                    deps[s] = v
        for s, v in deps.items():
            if eng.is_pe and s is eng.sem:
                continue
            if (not self.sync_same_engine) and s is eng.sem:
                continue
            if s is eng.sem and eng.name in self.nosync_engines:
                continue
            if eng.waited.get(s, 0) < v:
                eng.h.wait_ge(s, v)
                eng.waited[s] = v

    def _mark(self, tok, reads, writes, appends=()):
        s, v = tok
        for b in reads:
            if b.r.get(s, 0) < v:
                b.r[s] = v
        for b in writes:
            b.w = tok
            b.r = {}
            b.mw = {}
        for b in appends:
            if b.mw.get(s, 0) < v:
                b.mw[s] = v

    def op(self, eng, fn, reads=(), writes=()):
        self._wait_deps(eng, reads, writes)
        sem = eng.cur_sem()
        ins = fn()
        ins.then_inc(sem, 1)
        eng.count += 1
        self.n_inst += 1
        self._mark((sem, eng.count), reads, writes)

    def dma(self, eng, out, in_, reads=(), writes=(), appends=(), **kw):
        if len(self.dma_sems) < 24:
            self.dma_sems.append([self.new_sem(f"dma{len(self.dma_sems)}"), 0])
            ent = self.dma_sems[-1]
        else:
            ent = self.dma_sems[self.dma_rr % len(self.dma_sems)]
            self.dma_rr += 1
        sem, cnt = ent
        if cnt >= 1800:
            ent[0] = sem = self.new_sem("dmax")
            ent[1] = cnt = 0
        self._wait_deps(eng, reads, writes, appends)
        if cnt > 0 and eng.waited.get(sem, 0) < cnt * 16:
            eng.h.wait_ge(sem, cnt * 16)
            eng.waited[sem] = cnt * 16
        eng.h.dma_start(out=out, in_=in_, **kw).then_inc(sem, 16)
        ent[1] = cnt + 1
        self.n_inst += 1
        self._mark((sem, (cnt + 1) * 16), reads, writes, appends)

    def wait_all(self, eng, bufs):
        self._wait_deps(eng, bufs, ())


def pack_layout():
    lay = {}
    off = 0

    def add(name, width):
        nonlocal off
        lay[name] = (off, width)
        off += width

    add("cond", KC * NGRP)
    add("modb", DEPTH * NMOD * KC)
    add("normg", DEPTH * 3 * KC)
    lay["_total"] = off
    return lay


def core_segments(core):
    if core < 4:
        return [("lat", core)] * 4 + [("ctx", 2 * core), ("ctx", 2 * core + 1)]
    base = 8 + (core - 4) * 6
    return [("ctx", base + j) for j in range(6)]


def fm(vec):
    v = np.asarray(vec, np.float32)
    lead = v.shape[:-1]
    v = v.reshape(lead + (KC, 128))
    v = np.moveaxis(v, -1, 0)
    return np.ascontiguousarray(v)


def host_pack(core, inp):
    lay = pack_layout()
    P = np.zeros((128, lay["_total"]), np.float32)

    def put(name, arr):
        o, w = lay[name]
        a = np.asarray(arr, np.float32).reshape(128, -1)
        assert a.shape[1] == w, (name, a.shape, w)
        P[:, o:o + w] = a

    segs = core_segments(core)
    conds = []
    for g in range(NGRP):
        kind, idx = segs[2 * g]
        conds.append(inp["c"][idx] if kind == "lat" else inp["c_ctx"])
    cond = fm(np.stack(conds, 0))
    put("cond", np.transpose(cond, (0, 2, 1)))
    put("modb", fm(inp["mod_b"].reshape(DEPTH, NMOD, D)))
    put("normg", fm(inp["norm_g"]))
    return P


class Prog:
    def __init__(self, cfg):
        self.cfg = cfg
        self.lay = pack_layout()

    def build(self):
        cfg = self.cfg
        nc = bass.Bass("TRN2", target_bir_lowering=False)
        self.nc = nc
        es = contextlib.ExitStack()
        with es:
            k = K(nc, es)
            self.k = k
            k.nosync_engines = set(cfg.get("nosync", ()))
            self.declare_io()
            self.alloc_common()
            self.load_common()
            self.adaln_all()
            for l in cfg["layers"]:
                self.layer(l)
            self.finish()
        return nc

    def declare_io(self):
        nc = self.nc
        NL = len(self.cfg["layers"])
        self.lidx = {l: i for i, l in enumerate(self.cfg["layers"])}
        self.in_shapes = {
            "xT": [D, NTOK], "pack": [128, self.lay["_total"]], "mod_w": [NL, D, NMOD * D],
            "ffn_w_in": [NL, 2, D, 2 * DFF], "ffn_w_out": [NL, 2, DFF, D],
            "attn_w_qkv": [2, D, 3072], "attn_w_o": [2, D, D], "apack": [2, 128, NAP],
            "mlstm_w_in": [1, D, 6144], "mlstm_w_gate": [1, D, 32], "mlstm_w_o": [1, D, D], "mpack": [1, 128, NMP],
            "ml_initC": [2, 128, 8 * 257], "ml_initm": [2, 128, 8],
            "rpack": [128, NRP], "rmask": [128, 2 * NTOK], "rln": [128, 2 * D], "rk_initS": [2, 128, 1024],
            "rwkv_w_rkv": [1, 3, D, D], "rwkv_wA": [1, 2, D, 96], "rwkv_wB": [1, 2, 96, D], "rwkv_aA": [1, 2, D, 96],
            "rwkv_aB": [1, 2, 96, D], "rwkv_gA": [1, D, 256], "rwkv_gB": [1, 256, D], "rwkv_w_o": [1, D, D],
            "amask": [128, 14 * 128], "permT": [128, 128], "cache_k": [2, 256, 512], "cache_v": [2, 256, 512],
        }
        self.out_shapes = {"rk_stS": [6, 2, 128, 1024], "ml_stC": [6, 2, 128, 8 * 257], "ml_stm": [6, 2, 128, 8], "yT": [D, NTOK], "newk": [2, NTOK, 512], "newv": [2, NTOK, 512]}
        self.inputs = {}
        self.outputs = {}

    def I(self, name):
        if name not in self.inputs:
            self.inputs[name] = self.nc.dram_tensor(name, list(self.in_shapes[name]), F32, kind="ExternalInput").ap()
        return self.inputs[name]

    def dbg_out(self, name, shape):
        if name not in self.outputs:
            self.outputs[name] = self.nc.dram_tensor(name, list(shape), BF16, kind="ExternalOutput").ap()
        return self.outputs[name]

    def O(self, name):
        if name not in self.outputs:
            self.outputs[name] = self.nc.dram_tensor(name, list(self.out_shapes[name]), F32, kind="ExternalOutput").ap()
        return self.outputs[name]

    def alloc_common(self):
        k = self.k
        self.x_sb = k.sb("x_sb", [128, KC, NTOK], F32)
        self.xb = [[Buf(f"x{c}_{t}") for t in range(NGRP)] for c in range(KC)]
        self.pack = None
        self.pack_b = Buf("pack")
        self.modT = k.sb("modT", [128, DEPTH, NMOD, KC, NGRP], F32)
        self.modT_b = Buf("modT")
        self.ones_bf = k.sb("ones_bf", [128, 128], BF16)
        self.ones_b = Buf("ones")
        self.ps = [self.k.es.enter_context(self.nc.psum_tensor(f"ps{i}", [128, 512], F32)) for i in range(8)]
        self.psb = [Buf(f"ps{i}", excl=True) for i in range(8)]
        self.ps_rr = 0
        self.ps_reserved = set()

    def next_ps(self):
        while True:
            i = self.ps_rr % 8
            self.ps_rr += 1
            if i not in self.ps_reserved:
                return self.ps[i], self.psb[i]

    def reserve_ps(self):
        pt, pb = self.next_ps()
        i = self.ps.index(pt)
        self.ps_reserved.add(i)
        return pt, pb

    def release_ps(self, pt):
        self.ps_reserved.discard(self.ps.index(pt))

    def pk(self, name):
        o, w = self.lay[name]
        return self.pack[:, o:o + w]

    def load_common(self):
        k = self.k
        nc = self.nc
        xv = self.I("xT").rearrange("(c p) t -> p c t", p=128)
        for c in range(KC):
            k.dma(k.sp, self.x_sb[:, c, :], xv[:, c, :], writes=self.xb[c])
        k.op(k.dve, lambda: nc.vector.memset(self.ones_bf[:], 1.0), writes=[self.ones_b])

    def adaln_all(self):
        k = self.k
        nc = self.nc
        cfg = self.cfg
        with contextlib.ExitStack() as es2:
            self.pack = es2.enter_context(_sbuf(nc, "pack_sb", [128, self.lay["_total"]], F32))
            k.dma(k.sp, self.pack[:], self.I("pack")[:, :], writes=[self.pack_b])
            sc = es2.enter_context(_sbuf(nc, "ada_sc", [128, KC, NGRP], BF16))
            sc_b = Buf("sc")
            wts = [es2.enter_context(_sbuf(nc, f"ada_w{i}", [128, KC, 512], BF16)) for i in range(3)]
            wts_b = [Buf(f"adaw{i}") for i in range(3)]
            condv = self.pk("cond").rearrange("p (c g) -> p c g", g=NGRP)
            k.op(k.act, lambda: nc.scalar.activation(out=sc[:], in_=condv, func=AF.Silu),
                 reads=[self.pack_b], writes=[sc_b])
            modb = self.pk("modb").rearrange("p (l j c) -> p l j c", l=DEPTH, j=NMOD)
            NCT = NMOD * D // 512
            jobs = [(l, ct) for l in cfg["layers"] for ct in range(NCT)]

            def load(i):
                l, ct = jobs[i]
                src = self.I("mod_w")[self.lidx[l]].rearrange("(kc p) n -> p kc n", p=128)[:, :, ct * 512:(ct + 1) * 512]
                k.dma(k.pool, wts[i % 3][:], src, writes=[wts_b[i % 3]])

            for i in range(min(2, len(jobs))):
                load(i)
            for i, (l, ct) in enumerate(jobs):
                if i + 2 < len(jobs):
                    load(i + 2)
                w = wts[i % 3]
                wb = wts_b[i % 3]
                pst, psb = self.next_ps()
                for q in range(4):
                    for kc in range(KC):
                        k.op(k.pe, lambda q=q, kc=kc: nc.tensor.matmul(
                            pst[:, q * NGRP:(q + 1) * NGRP], w[:, kc, q * 128:(q + 1) * 128], sc[:, kc, :],
                            start=(kc == 0), stop=(kc == KC - 1)),
                            reads=[wb, sc_b], writes=[psb])
                j = (ct * 4) // KC
                c0 = (ct * 4) % KC
                k.op(k.dve, lambda l=l, j=j, c0=c0: nc.vector.tensor_tensor(
                    out=self.modT[:, l, j, c0:c0 + 4, :],
                    in0=pst[:, 0:4 * NGRP].rearrange("p (q g) -> p q g", g=NGRP),
                    in1=modb[:, l, j, c0:c0 + 4].unsqueeze(2).to_broadcast([128, 4, NGRP]),
                    op=ALU.add),
                    reads=[psb, self.pack_b], writes=[self.modT_b])
            ng = self.pk("normg").rearrange("p (l s c) -> p l s c", l=DEPTH, s=3)
            for l in cfg["layers"]:
                for s in range(3):
                    k.op(k.dve, lambda l=l, s=s: nc.vector.scalar_tensor_tensor(
                        out=self.modT[:, l, 3 * s + 1, :, :], in0=self.modT[:, l, 3 * s + 1, :, :], scalar=1.0,
                        in1=ng[:, l, s, :].unsqueeze(2).to_broadcast([128, KC, NGRP]),
                        op0=ALU.add, op1=ALU.mult),
                        reads=[self.pack_b, self.modT_b], writes=[self.modT_b])
                    if s != 1:
                        k.op(k.dve, lambda l=l, s=s: nc.vector.tensor_scalar(
                            out=self.modT[:, l, 3 * s + 2, :, :], in0=self.modT[:, l, 3 * s + 2, :, :],
                            scalar1=0.5, scalar2=None, op0=ALU.mult),
                            reads=[self.modT_b], writes=[self.modT_b])
            self.phase_barrier()

    def compute_h(self, l, s, h, hb, sq, sqb):
        k = self.k
        nc = self.nc
        for t in range(NGRP):
            ts = slice(t * TT, (t + 1) * TT)
            pst, psb = self.next_ps()
            for c in range(KC):
                i = c % 2
                k.op(k.act, lambda c=c, i=i: nc.scalar.activation(out=sq[i], in_=self.x_sb[:, c, ts], func=AF.Square),
                     reads=[self.xb[c][t]], writes=[sqb[i]])
                k.op(k.pe, lambda c=c, i=i: nc.tensor.matmul(pst[:], self.ones_bf[:], sq[i],
                                                             start=(c == 0), stop=(c == KC - 1)),
                     reads=[sqb[i], self.ones_b], writes=[psb])
            k.op(k.dve, lambda: nc.vector.tensor_scalar(out=pst[:], in0=pst[:], scalar1=1.0 / D, scalar2=EPS,
                                                        op0=ALU.mult, op1=ALU.add),
                 reads=[psb], writes=[psb])
            k.op(k.act, lambda: nc.scalar.activation(out=pst[:], in_=pst[:], func=AF.Sqrt), reads=[psb], writes=[psb])
            k.op(k.dve, lambda: nc.vector.reciprocal(out=pst[:], in_=pst[:]), reads=[psb], writes=[psb])
            for c in range(KC):
                pt, ptb = self.next_ps()
                if pt is pst:
                    pt, ptb = self.next_ps()
                k.op(k.dve, lambda c=c, pt=pt: nc.vector.tensor_tensor(out=pt[:], in0=self.x_sb[:, c, ts], in1=pst[:],
                                                                       op=ALU.mult),
                     reads=[self.xb[c][t], psb], writes=[ptb])
                k.op(k.act, lambda c=c, pt=pt: nc.scalar.activation(
                    out=h[:, c, ts], in_=pt[:], func=AF.Identity,
                    scale=self.modT[:, l, 3 * s + 1, c, t:t + 1], bias=self.modT[:, l, 3 * s, c, t:t + 1]),
                    reads=[ptb, self.modT_b], writes=[hb[t]])

    def ffn(self, l, j):
        k = self.k
        nc = self.nc
        s = 0 if j == 0 else 2
        FG = 256
        NFG = DFF // FG
        with contextlib.ExitStack() as es2:
            sbt = lambda name, shape, dt: es2.enter_context(_sbuf(nc, name, list(shape), dt))
            h = sbt("ffn_h", [128, KC, NTOK], BF16)
            hb = [Buf(f"h{t}") for t in range(NGRP)]
            tmpn = [sbt(f"ffn_tmpn{i}", [128, TT], BF16) for i in range(2)]
            tmpnb = [Buf() for _ in range(2)]
            wg = [sbt(f"ffn_wg{i}", [128, KC, FG], BF16) for i in range(2)]
            wu = [sbt(f"ffn_wu{i}", [128, KC, FG], BF16) for i in range(2)]
            wo = [sbt(f"ffn_wo{i}", [128, FG // 128, D], BF16) for i in range(2)]
            wgb = [Buf() for _ in range(2)]
            wub = [Buf() for _ in range(2)]
            wob = [Buf() for _ in range(2)]
            gsb = [sbt(f"ffn_g{i}", [128, FG // 128, TT], BF16) for i in range(2)]
            gsbb = [Buf() for _ in range(2)]
            sq = [gsb[i][:, 0, :] for i in range(2)]
            sqb = gsbb
            self.phase_barrier()

            w_in_v = self.I("ffn_w_in")[self.lidx[l], j].rearrange("(kc p) n -> p kc n", p=128)
            w_out_v = self.I("ffn_w_out")[self.lidx[l], j].rearrange("(fc p) n -> p fc n", p=128)

            def load(fg):
                i = fg % 2
                k.dma(k.pool, wg[i][:], w_in_v[:, :, fg * FG:(fg + 1) * FG], writes=[wgb[i]])
                k.dma(k.pool, wu[i][:], w_in_v[:, :, DFF + fg * FG:DFF + (fg + 1) * FG], writes=[wub[i]])
                k.dma(k.pool, wo[i][:], w_out_v[:, fg * (FG // 128):(fg + 1) * (FG // 128), :], writes=[wob[i]])

            load(0)
            self.compute_h(l, s, h, hb, sq, sqb)
            it = 0
            for fg in range(NFG):
                if fg + 1 < NFG:
                    load(fg + 1)
                i = fg % 2
                for t in range(NGRP):
                    ts = slice(t * TT, (t + 1) * TT)
                    gi = it % 2
                    it += 1
                    for hf in range(FG // 128):
                        pg, pgb = self.next_ps()
                        pu, pub = self.next_ps()
                        for kc in range(KC):
                            k.op(k.pe, lambda kc=kc, hf=hf, pg=pg: nc.tensor.matmul(
                                pg[:], wg[i][:, kc, hf * 128:(hf + 1) * 128], h[:, kc, ts],
                                start=(kc == 0), stop=(kc == KC - 1)),
                                reads=[wgb[i], hb[t]], writes=[pgb])
                        for kc in range(KC):
                            k.op(k.pe, lambda kc=kc, hf=hf, pu=pu: nc.tensor.matmul(
                                pu[:], wu[i][:, kc, hf * 128:(hf + 1) * 128], h[:, kc, ts],
                                start=(kc == 0), stop=(kc == KC - 1)),
                                reads=[wub[i], hb[t]], writes=[pub])
                        si = hf % 2
                        k.op(k.act, lambda pg=pg, si=si: nc.scalar.activation(out=tmpn[si][:], in_=pg[:], func=AF.Silu),
                             reads=[pgb], writes=[tmpnb[si]])
                        k.op(k.dve, lambda pu=pu, si=si, hf=hf, gi=gi: nc.vector.tensor_tensor(
                            out=gsb[gi][:, hf, :], in0=tmpn[si][:], in1=pu[:], op=ALU.mult),
                            reads=[tmpnb[si], pub], writes=[gsbb[gi]])
                    for dc in range(KC):
                        py, pyb = self.next_ps()
                        for hf in range(FG // 128):
                            k.op(k.pe, lambda hf=hf, dc=dc, py=py, gi=gi: nc.tensor.matmul(
                                py[:], wo[i][:, hf, dc * 128:(dc + 1) * 128], gsb[gi][:, hf, :],
                                start=(hf == 0), stop=(hf == FG // 128 - 1)),
                                reads=[wob[i], gsbb[gi]], writes=[pyb])
                        k.op(k.dve, lambda dc=dc, py=py, t=t, ts=ts: nc.vector.scalar_tensor_tensor(
                            out=self.x_sb[:, dc, ts], in0=py[:], scalar=self.modT[:, l, 3 * s + 2, dc, t:t + 1],
                            in1=self.x_sb[:, dc, ts], op0=ALU.mult, op1=ALU.add),
                            reads=[pyb, self.modT_b, self.xb[dc][t]], writes=[self.xb[dc][t]])
            self.phase_barrier()

    def phase_barrier(self):
        k = self.k
        toks = []
        for e in (k.pe, k.act, k.dve, k.pool, k.sp):
            if e.sem is not None and e.count > 0:
                toks.append((e.sem, e.count))
        for ent in k.dma_sems:
            if ent[1] > 0:
                toks.append((ent[0], ent[1] * 16))
        for e in (k.pe, k.act, k.dve, k.pool, k.sp):
            for s, v in toks:
                if s is e.sem:
                    continue
                if e.waited.get(s, 0) < v:
                    e.h.wait_ge(s, v)
                    e.waited[s] = v


    def linear_fm(self, h, hb, wview, col0, ncols, evac, tag):
        k, nc = self.k, self.nc
        TC = 256
        with contextlib.ExitStack() as es2:
            wt = [es2.enter_context(_sbuf(nc, f"lf_{tag}_w{i}", [128, KC, TC], BF16)) for i in range(2)]
            wtb = [Buf() for _ in range(2)]
            ntile = ncols // TC

            def load(ti):
                k.dma(k.pool, wt[ti % 2][:], wview[:, :, col0 + ti * TC:col0 + (ti + 1) * TC], writes=[wtb[ti % 2]])

            load(0)
            for ti in range(ntile):
                if ti + 1 < ntile:
                    load(ti + 1)
                w, wb = wt[ti % 2], wtb[ti % 2]
                for sub in range(TC // 128):
                    oc = ti * (TC // 128) + sub
                    for t in range(NGRP):
                        ts = slice(t * TT, (t + 1) * TT)
                        ps, psb = self.next_ps()
                        for kc in range(KC):
                            k.op(k.pe, lambda kc=kc: nc.tensor.matmul(ps[:], w[:, kc, sub * 128:(sub + 1) * 128], h[:, kc, ts],
                                                                      start=(kc == 0), stop=(kc == KC - 1)),
                                 reads=[wb, hb[t]], writes=[psb])
                        evac(oc, t, ps, psb)
            self.scope_end(wtb)

    def linear_tm(self, h, hb, wview, col0, ncols, evac, tag):
        k, nc = self.k, self.nc
        TC = 512
        with contextlib.ExitStack() as es2:
            wt = [es2.enter_context(_sbuf(nc, f"lt_{tag}_w{i}", [128, KC, TC], BF16)) for i in range(2)]
            wtb = [Buf() for _ in range(2)]
            ntile = ncols // TC

            def load(ti):
                k.dma(k.pool, wt[ti % 2][:], wview[:, :, col0 + ti * TC:col0 + (ti + 1) * TC], writes=[wtb[ti % 2]])

            load(0)
            for ti in range(ntile):
                if ti + 1 < ntile:
                    load(ti + 1)
                w, wb = wt[ti % 2], wtb[ti % 2]
                for blk in range(NTOK // 128):
                    ps, psb = self.next_ps()
                    for kc in range(KC):
                        k.op(k.pe, lambda kc=kc: nc.tensor.matmul(ps[:], h[:, kc, blk * 128:(blk + 1) * 128], w[:, kc, :],
                                                                  start=(kc == 0), stop=(kc == KC - 1)),
                             reads=[wb, hb[blk // 4]], writes=[psb])
                    evac(blk, ti, ps, psb)
            self.scope_end(wtb)

    def scope_end(self, bufs):
        k = self.k
        for e in (k.pe, k.act, k.dve, k.pool, k.sp):
            k._wait_deps(e, (), bufs)

    def attn(self, l):
        k = self.k
        nc = self.nc
        slot = l // 3
        NH, NKV = 16, 4
        qs = nc.dram_tensor(f"qs{l}", [NH, 128, NTOK], BF16).ap()
        ks = nc.dram_tensor(f"ks{l}", [NKV, 128, NTOK], BF16).ap()
        vs = nc.dram_tensor(f"vs{l}", [12, 128, 512], BF16).ap()
        qs_b, ks_b, vs_b = Buf("qs"), Buf("ks"), Buf("vs")
        outb = Buf("attn_out")
        with contextlib.ExitStack() as es1:
            sbt1 = lambda name, shape, dt: es1.enter_context(_sbuf(nc, name, list(shape), dt))
            self.phase_barrier()
            apk = sbt1("apack_sb", [128, NAP], F32)
            apk_b = Buf("apk")
            k.dma(k.sp, apk[:], self.I("apack")[slot], writes=[apk_b])
            qng = apk[:, 0:1]
            kng = apk[:, 1:2]
            sink = apk[:, 2:18]
            cflag = apk[:, 18:19]
            ident = apk[:, 19:147]
            cos = apk[:, 147:147 + NTOK]
            sin = apk[:, 147 + NTOK:147 + 2 * NTOK]
            with contextlib.ExitStack() as es2:
                sbt = lambda name, shape, dt: es2.enter_context(_sbuf(nc, name, list(shape), dt))
                h = sbt("at_h", [128, KC, NTOK], BF16)
                hb = [Buf(f"h{t}") for t in range(NGRP)]
                sq = [sbt(f"at_sq{i}", [128, TT], BF16) for i in range(2)]
                sqb = [Buf() for _ in range(2)]
                PT = sbt("at_PT", [128, 128], BF16)
                PT_b = Buf()
                k.dma(k.pool, PT[:], self.I("permT")[:, :], writes=[PT_b])
                wq = [sbt(f"at_wq{i}", [128, KC, 256], BF16) for i in range(2)]
                wqb = [Buf() for _ in range(2)]
                rawg = [sbt(f"at_rawg{i}", [128, TT], F32) for i in range(2)]
                rawgb = [Buf() for _ in range(2)]
                qnf = [sbt(f"at_qnf{i}", [128, TT], F32) for i in range(2)]
                qnfb = [Buf() for _ in range(2)]
                qnb = [sbt(f"at_qnb{i}", [128, TT], BF16) for i in range(2)]
                qnbb = [Buf() for _ in range(2)]
                t1 = [sbt(f"at_t1{i}", [128, TT], F32) for i in range(2)]
                t1b = [Buf() for _ in range(2)]
                qrb = [sbt(f"at_qrb{i}", [128, TT], BF16) for i in range(2)]
                qrbb = [Buf() for _ in range(2)]
                kout = [sbt(f"at_kout{i}", [128, 4, 128], F32) for i in range(2)]
                koutb = [Buf() for _ in range(2)]
                vf = [sbt(f"at_vf{i}", [128, 256], F32) for i in range(2)]
                vfb = [Buf() for _ in range(2)]
                vb = [sbt(f"at_vb{i}", [128, 256], BF16) for i in range(2)]
                vbb = [Buf() for _ in range(2)]
                wv_ = self.I("attn_w_qkv")[slot].rearrange("(kc p) n -> p kc n", p=128)

                def load(ct):
                    k.dma(k.pool, wq[ct % 2][:], wv_[:, :, ct * 256:(ct + 1) * 256], writes=[wqb[ct % 2]])

                load(0)
                self.compute_h(l, 1, h, hb, [t_[:] for t_ in sq], sqb)
                it = 0
                ct_list = self.cfg.get("ct_list", list(range(12)))
                for ct in ct_list:
                    if ct + 1 < 12 and (ct + 1) in ct_list:
                        load(ct + 1)
                    w = wq[ct % 2]
                    wb = wqb[ct % 2]
                    if ct < 10:
                        for hh in range(2):
                            head = ct * 2 + hh
                            is_k = head >= 16
                            gain = kng if is_k else qng
                            for t in range(NGRP):
                                ts = slice(t * TT, (t + 1) * TT)
                                i = it % 2
                                it += 1
                                praw, prawb = self.next_ps()
                                for kc in range(KC):
                                    k.op(k.pe, lambda kc=kc: nc.tensor.matmul(
                                        praw[:], w[:, kc, hh * 128:(hh + 1) * 128], h[:, kc, ts],
                                        start=(kc == 0), stop=(kc == KC - 1)), reads=[wb, hb[t]], writes=[prawb])
                                k.op(k.act, lambda: nc.scalar.activation(out=sq[i][:], in_=praw[:], func=AF.Square),
                                     reads=[prawb], writes=[sqb[i]])
                                k.op(k.act, lambda: nc.scalar.activation(out=rawg[i][:], in_=praw[:], func=AF.Identity, scale=gain),
                                     reads=[prawb, apk_b], writes=[rawgb[i]])
                                pss, pssb = self.next_ps()
                                k.op(k.pe, lambda: nc.tensor.matmul(pss[:], self.ones_bf[:], sq[i][:], start=True, stop=True),
                                     reads=[sqb[i], self.ones_b], writes=[pssb])
                                k.op(k.dve, lambda: nc.vector.tensor_scalar(out=pss[:], in0=pss[:], scalar1=1.0 / 128, scalar2=EPS,
                                                                            op0=ALU.mult, op1=ALU.add), reads=[pssb], writes=[pssb])
                                k.op(k.act, lambda: nc.scalar.activation(out=pss[:], in_=pss[:], func=AF.Sqrt), reads=[pssb], writes=[pssb])
                                k.op(k.dve, lambda: nc.vector.reciprocal(out=pss[:], in_=pss[:]), reads=[pssb], writes=[pssb])
                                k.op(k.dve, lambda: nc.vector.tensor_tensor(out=qnf[i][:], in0=rawg[i][:], in1=pss[:], op=ALU.mult),
                                     reads=[rawgb[i], pssb], writes=[qnfb[i]])
                                k.op(k.act, lambda: nc.scalar.copy(out=qnb[i][:], in_=qnf[i][:]), reads=[qnfb[i]], writes=[qnbb[i]])
                                pp, ppb = self.next_ps()
                                k.op(k.pe, lambda: nc.tensor.matmul(pp[:], PT[:], qnb[i][:], start=True, stop=True),
                                     reads=[PT_b, qnbb[i]], writes=[ppb])
                                k.op(k.dve, lambda: nc.vector.tensor_tensor(out=t1[i][:], in0=qnf[i][:], in1=cos[:, ts], op=ALU.mult),
                                     reads=[qnfb[i], apk_b], writes=[t1b[i]])
                                k.op(k.dve, lambda: nc.vector.tensor_tensor(out=qnf[i][:], in0=pp[:], in1=sin[:, ts], op=ALU.mult),
                                     reads=[ppb, apk_b, qnbb[i]], writes=[qnfb[i]])
                                if not is_k:
                                    k.op(k.dve, lambda: nc.vector.tensor_tensor(out=qrb[i][:], in0=t1[i][:], in1=qnf[i][:], op=ALU.add),
                                         reads=[t1b[i], qnfb[i]], writes=[qrbb[i]])
                                    if not self.cfg.get("no_scratch"):
                                        k.dma(k.sp, qs[head, :, ts], qrb[i][:], reads=[qrbb[i]], appends=[qs_b])
                                else:
                                    kvh = head - 16
                                    k.op(k.dve, lambda: nc.vector.tensor_tensor(out=t1[i][:], in0=t1[i][:], in1=qnf[i][:], op=ALU.add),
                                         reads=[t1b[i], qnfb[i]], writes=[t1b[i]])
                                    k.op(k.act, lambda: nc.scalar.copy(out=qrb[i][:], in_=t1[i][:]), reads=[t1b[i]], writes=[qrbb[i]])
                                    if not self.cfg.get("no_scratch"):
                                        k.dma(k.sp, ks[kvh, :, ts], qrb[i][:], reads=[qrbb[i]], appends=[ks_b])
                                    if self.cfg.get("no_tr"):
                                        continue
                                    ptr, ptrb = self.next_ps()
                                    for b4 in range(4):
                                        k.op(k.pe, lambda b4=b4: nc.tensor.transpose(
                                            ptr[:, b4 * 128:(b4 + 1) * 128], t1[i][:, b4 * 128:(b4 + 1) * 128], ident),
                                            reads=[t1b[i], apk_b], writes=[ptrb])
                                    k.op(k.act, lambda: nc.scalar.copy(out=kout[i][:].rearrange("p b d -> p (b d)"), in_=ptr[:]),
                                         reads=[ptrb], writes=[koutb[i]])
                                    dst = self.O("newk")[slot, t * TT:(t + 1) * TT, kvh * 128:(kvh + 1) * 128].rearrange(
                                        "(b p) d -> p b d", p=128)
                                    k.dma(k.sp, dst, kout[i][:], reads=[koutb[i]], appends=[outb])
                    else:
                        c0 = (ct - 10) * 256
                        for blk in range(12):
                            i = it % 2
                            it += 1
                            pv, pvb = self.next_ps()
                            for kc in range(KC):
                                k.op(k.pe, lambda kc=kc: nc.tensor.matmul(
                                    pv[:, 0:256], h[:, kc, blk * 128:(blk + 1) * 128], w[:, kc, :],
                                    start=(kc == 0), stop=(kc == KC - 1)), reads=[wb, hb[blk // 4]], writes=[pvb])
                            k.op(k.act, lambda: nc.scalar.copy(out=vf[i][:], in_=pv[:, 0:256]), reads=[pvb], writes=[vfb[i]])
                            k.op(k.dve, lambda: nc.vector.tensor_copy(out=vb[i][:], in_=pv[:, 0:256]), reads=[pvb], writes=[vbb[i]])
                            if not self.cfg.get("no_newv"):
                                k.dma(k.sp, self.O("newv")[slot, blk * 128:(blk + 1) * 128, c0:c0 + 256], vf[i][:],
                                      reads=[vfb[i]], appends=[outb])
                            if not self.cfg.get("no_scratch"):
                                k.dma(k.sp, vs[blk, :, c0:c0 + 256], vb[i][:], reads=[vbb[i]], appends=[vs_b])
                self.phase_barrier()
            if self.cfg.get("attn_phase", "AB") == "A":
                return
            with contextlib.ExitStack() as es2:
                sbt = lambda name, shape, dt: es2.enter_context(_sbuf(nc, name, list(shape), dt))
                masks = sbt("at_masks", [128, 14, 128], BF16)
                masks_b = Buf()
                k.dma(k.pool, masks[:], self.I("amask").rearrange("p (m q) -> p m q", q=128), writes=[masks_b])
                ckf = sbt("at_ckf", [128, 2, 512], F32)
                ckf_b = Buf()
                k.dma(k.sp, ckf[:], self.I("cache_k")[slot].rearrange("(b p) d -> p b d", p=128), writes=[ckf_b])
                vc = sbt("at_vc", [128, 2, 512], BF16)
                vc_b = Buf()
                k.dma(k.pool, vc[:], self.I("cache_v")[slot].rearrange("(b p) d -> p b d", p=128), writes=[vc_b])
                kTc = sbt("at_kTc", [128, 4, 256], BF16)
                kTc_b = Buf()
                for cb in range(2):
                    ptr, ptrb = self.next_ps()
                    for kvh in range(4):
                        k.op(k.pe, lambda kvh=kvh: nc.tensor.transpose(
                            ptr[:, kvh * 128:(kvh + 1) * 128], ckf[:, cb, kvh * 128:(kvh + 1) * 128], ident),
                            reads=[ckf_b, apk_b], writes=[ptrb])
                    k.op(k.act, lambda: nc.scalar.copy(out=kTc[:, :, cb * 128:(cb + 1) * 128],
                                                       in_=ptr[:].rearrange("p (h s) -> p h s", s=128)),
                         reads=[ptrb], writes=[kTc_b])
                esink = sbt("at_esink", [128, 16], F32)
                esink_b = Buf()
                k.op(k.act, lambda: nc.scalar.activation(out=esink[:], in_=sink, func=AF.Exp), reads=[apk_b], writes=[esink_b])
                qg = [sbt(f"at_qg{i}", [128, 4, NTOK], BF16) for i in range(2)]
                kg = [sbt(f"at_kg{i}", [128, NTOK], BF16) for i in range(2)]
                vg = [sbt(f"at_vg{i}", [128, 12, 128], BF16) for i in range(2)]
                wo1 = sbt("at_wo", [128, 4, D], BF16)
                wo = [wo1, wo1]
                qgb = [Buf() for _ in range(2)]
                kgb = [Buf() for _ in range(2)]
                vgb = [Buf() for _ in range(2)]
                wob1 = Buf()
                wob = [wob1, wob1]
                og = sbt("at_og", [128, 4, NTOK], BF16)
                ogb = [Buf() for _ in range(NGRP)]
                ptile = [sbt(f"at_pt{i}", [128, 512], BF16) for i in range(3)]
                ptileb = [Buf() for _ in range(3)]
                dtmp = sbt("at_dtmp", [128, 512], F32)
                dtmp_b = Buf()
                wo_v = self.I("attn_w_o")[slot].rearrange("(hh p) n -> p hh n", p=128)

                def loadg(g):
                    i = g % 2
                    k.dma(k.sp, qg[i][:], qs[g * 4:(g + 1) * 4].rearrange("h p t -> p h t"), reads=[qs_b], writes=[qgb[i]])
                    k.dma(k.sp, kg[i][:], ks[g], reads=[ks_b], writes=[kgb[i]])
                    k.dma(k.sp, vg[i][:], vs[:, :, g * 128:(g + 1) * 128].rearrange("b p d -> p b d"), reads=[vs_b], writes=[vgb[i]])

                def loadwo(g):
                    k.dma(k.pool, wo1[:], wo_v[:, g * 4:(g + 1) * 4, :], writes=[wob1])

                loadg(0)
                scale = 128.0 ** -0.5
                pit = 0
                for g in range(4):
                    if g + 1 < 4:
                        loadg(g + 1)
                    loadwo(g)
                    i = g % 2
                    for qb in range(12):
                        kbs = []
                        if qb < 8:
                            for kb in (qb - 1, qb, qb + 1):
                                if 0 <= kb < 8:
                                    if kb == qb:
                                        midx = None
                                    elif kb == qb - 1:
                                        midx = 2 * (qb - 1)
                                    else:
                                        midx = 2 * qb + 1
                                    kbs.append((kg[i][:, kb * 128:(kb + 1) * 128], kgb[i], vg[i][:, kb, :], vgb[i], midx, False))
                            for cb in range(2):
                                kbs.append((kTc[:, g, cb * 128:(cb + 1) * 128], kTc_b, vc[:, cb, g * 128:(g + 1) * 128], vc_b, None, True))
                        else:
                            sb0 = 8 + 2 * ((qb - 8) // 2)
                            for kb in (sb0, sb0 + 1):
                                kbs.append((kg[i][:, kb * 128:(kb + 1) * 128], kgb[i], vg[i][:, kb, :], vgb[i], None, False))
                        pO, pOb = self.reserve_ps()
                        pD, pDb = self.reserve_ps()
                        for j, (kap, kbuf, vap, vbuf, midx, is_c) in enumerate(kbs):
                            pS, pSb = self.next_ps()
                            k.op(k.pe, lambda: nc.tensor.matmul(pS[:], kap, qg[i][:, :, qb * 128:(qb + 1) * 128], start=True, stop=True),
                                 reads=[kbuf, qgb[i]], writes=[pSb])
                            pi = pit % 3
                            pit += 1
                            k.op(k.act, lambda: nc.scalar.activation(out=ptile[pi][:], in_=pS[:], func=AF.Exp, scale=scale),
                                 reads=[pSb], writes=[ptileb[pi]])
                            if midx is not None:
                                k.op(k.dve, lambda: nc.vector.tensor_tensor(
                                    out=ptile[pi][:].rearrange("p (h q) -> p h q", q=128),
                                    in0=ptile[pi][:].rearrange("p (h q) -> p h q", q=128),
                                    in1=masks[:, midx, :].unsqueeze(1).to_broadcast([128, 4, 128]), op=ALU.mult),
                                    reads=[ptileb[pi], masks_b], writes=[ptileb[pi]])
                            if is_c:
                                k.op(k.dve, lambda: nc.vector.tensor_scalar(out=ptile[pi][:], in0=ptile[pi][:], scalar1=cflag, scalar2=None,
                                                                            op0=ALU.mult), reads=[ptileb[pi], apk_b], writes=[ptileb[pi]])
                            k.op(k.pe, lambda: nc.tensor.matmul(pO[:], vap, ptile[pi][:], start=(j == 0), stop=(j == len(kbs) - 1)),
                                 reads=[vbuf, ptileb[pi]], writes=[pOb])
                            k.op(k.pe, lambda: nc.tensor.matmul(pD[:], self.ones_bf[:], ptile[pi][:], start=(j == 0), stop=(j == len(kbs) - 1)),
                                 reads=[self.ones_b, ptileb[pi]], writes=[pDb])
                        k.op(k.dve, lambda: nc.vector.tensor_tensor(
                            out=dtmp[:].rearrange("p (h q) -> p h q", q=128), in0=pD[:].rearrange("p (h q) -> p h q", q=128),
                            in1=esink[:, g * 4:(g + 1) * 4].unsqueeze(2).to_broadcast([128, 4, 128]), op=ALU.add),
                            reads=[pDb, esink_b], writes=[dtmp_b])
                        k.op(k.dve, lambda: nc.vector.reciprocal(out=dtmp[:], in_=dtmp[:]), reads=[dtmp_b], writes=[dtmp_b])
                        k.op(k.dve, lambda: nc.vector.tensor_tensor(
                            out=og[:, :, qb * 128:(qb + 1) * 128], in0=pO[:].rearrange("p (h q) -> p h q", q=128),
                            in1=dtmp[:].rearrange("p (h q) -> p h q", q=128), op=ALU.mult),
                            reads=[pOb, dtmp_b], writes=[ogb[qb // 4]])
                        self.release_ps(pO)
                        self.release_ps(pD)
                    if self.cfg.get("dbg_attn"):
                            dbo = self.dbg_out("dbg_o", [16, 128, NTOK])
                            k.dma(k.sp, dbo[g * 4:(g + 1) * 4].rearrange("h p t -> p h t"), og[:], reads=ogb, appends=[outb])
                            dbq = self.dbg_out("dbg_q", [16, 128, NTOK])
                            k.dma(k.sp, dbq[g * 4:(g + 1) * 4].rearrange("h p t -> p h t"), qg[i][:], reads=[qgb[i]], appends=[outb])
                    for t in range(NGRP):
                        ts = slice(t * TT, (t + 1) * TT)
                        for dc in range(KC):
                            py, pyb = self.next_ps()
                            for hh in range(4):
                                k.op(k.pe, lambda hh=hh: nc.tensor.matmul(py[:], wo[i][:, hh, dc * 128:(dc + 1) * 128], og[:, hh, ts],
                                                                          start=(hh == 0), stop=(hh == 3)),
                                     reads=[wob[i], ogb[t]], writes=[pyb])
                            k.op(k.dve, lambda: nc.vector.scalar_tensor_tensor(
                                out=self.x_sb[:, dc, ts], in0=py[:], scalar=self.modT[:, l, 5, dc, t:t + 1],
                                in1=self.x_sb[:, dc, ts], op0=ALU.mult, op1=ALU.add),
                                reads=[pyb, self.modT_b, self.xb[dc][t]], writes=[self.xb[dc][t]])
                self.phase_barrier()


    def mlstm(self, l):
        k, nc = self.k, self.nc
        slot = l // 3
        NHm, DKm, DVm, NB = 8, 128, 256, NTOK // 128
        DV1 = DVm + 1
        qT_s = nc.dram_tensor(f"ml_qT{l}", [NHm, 128, NTOK], BF16).ap()
        kT_s = nc.dram_tensor(f"ml_kT{l}", [NHm, 128, NTOK], BF16).ap()
        ktm_s = nc.dram_tensor(f"ml_ktm{l}", [NB, 128, NHm * DKm], BF16).ap()
        vtm_s = nc.dram_tensor(f"ml_vtm{l}", [NB, 128, D], BF16).ap()
        og_s = nc.dram_tensor(f"ml_og{l}", [NB, 128, D], BF16).ap()
        hf_s = nc.dram_tensor(f"ml_hf{l}", [NB, 128, D], F32).ap()
        hsT_s = nc.dram_tensor(f"ml_hsT{l}", [KC, 128, NTOK], BF16).ap()
        scr_b = Buf("ml_scr")
        hf_b = Buf("ml_hf")
        hsT_b = Buf("ml_hsT")
        outb = Buf("ml_out")
        w_in_v = self.I("mlstm_w_in")[slot].rearrange("(kc p) n -> p kc n", p=128)
        with contextlib.ExitStack() as es1:
            sbt1 = lambda name, shape, dt: es1.enter_context(_sbuf(nc, name, list(shape), dt))
            self.phase_barrier()
            mpk = sbt1("ml_mpk", [128, NMP], F32)
            mpk_b = Buf()
            k.dma(k.sp, mpk[:], self.I("mpack")[slot], writes=[mpk_b])
            ident = mpk[:, 0:128]
            onesf = mpk[:, 128:256]
            V1_01, V2_01 = mpk[:, 256:384], mpk[:, 384:512]
            V1b, V2b = mpk[:, 512:640], mpk[:, 640:768]
            bgate = mpk[:, 768:800]
            flag = mpk[:, 800:801]
            onorm = mpk[:, 801:801 + D]
            gates = sbt1("ml_gates", [128, NB, 32], F32)
            gates_b = Buf()
            with contextlib.ExitStack() as es2:
                sbt = lambda name, shape, dt: es2.enter_context(_sbuf(nc, name, list(shape), dt))
                h = sbt("ml_h", [128, KC, NTOK], BF16)
                hb = [Buf(f"h{t}") for t in range(NGRP)]
                sq = [sbt(f"ml_sq{i}", [128, TT], BF16) for i in range(2)]
                sqb = [Buf() for _ in range(2)]
                ev = [sbt(f"ml_ev{i}", [128, TT], BF16) for i in range(3)]
                evb = [Buf() for _ in range(3)]
                wg = sbt("ml_wg", [128, KC, 32], BF16)
                wg_b = Buf()
                k.dma(k.pool, wg[:], self.I("mlstm_w_gate")[slot].rearrange("(kc p) n -> p kc n", p=128), writes=[wg_b])
                self.compute_h(l, 1, h, hb, [t_[:] for t_ in sq], sqb)
                cnt = [0]

                def nxt():
                    cnt[0] += 1
                    return cnt[0] % 3

                def ev_q(oc, t, ps, psb):
                    i = nxt()
                    k.op(k.act, lambda: nc.scalar.activation(out=ev[i][:], in_=ps[:], func=AF.Identity, scale=float(DKm) ** -0.5),
                         reads=[psb], writes=[evb[i]])
                    k.dma(k.sp, qT_s[oc, :, t * TT:(t + 1) * TT], ev[i][:], reads=[evb[i]], appends=[scr_b])

                def ev_k(oc, t, ps, psb):
                    i = nxt()
                    k.op(k.act, lambda: nc.scalar.copy(out=ev[i][:], in_=ps[:]), reads=[psb], writes=[evb[i]])
                    k.dma(k.sp, kT_s[oc, :, t * TT:(t + 1) * TT], ev[i][:], reads=[evb[i]], appends=[scr_b])

                def mk_tm(dst, func):
                    def f(blk, ci, ps, psb):
                        i = nxt()
                        if func is None:
                            k.op(k.dve, lambda: nc.vector.tensor_copy(out=ev[i][:], in_=ps[:]), reads=[psb], writes=[evb[i]])
                        else:
                            k.op(k.act, lambda: nc.scalar.activation(out=ev[i][:], in_=ps[:], func=func), reads=[psb], writes=[evb[i]])
                        k.dma(k.sp, dst[blk, :, ci * 512:(ci + 1) * 512], ev[i][:], reads=[evb[i]], appends=[scr_b])
                    return f

                self.linear_fm(h, hb, w_in_v, 0, 1024, ev_q, "q")
                self.linear_fm(h, hb, w_in_v, 1024, 1024, ev_k, "k")
                self.linear_tm(h, hb, w_in_v, 1024, 1024, mk_tm(ktm_s, None), "kt")
                self.linear_tm(h, hb, w_in_v, 2048, 2048, mk_tm(vtm_s, None), "v")
                self.linear_tm(h, hb, w_in_v, 4096, 2048, mk_tm(og_s, AF.Sigmoid), "og")
                for blk in range(NB):
                    ps, psb = self.next_ps()
                    for kc in range(KC):
                        k.op(k.pe, lambda kc=kc: nc.tensor.matmul(ps[:, 0:32], h[:, kc, blk * 128:(blk + 1) * 128], wg[:, kc, :],
                                                                  start=(kc == 0), stop=(kc == KC - 1)),
                             reads=[wg_b, hb[blk // 4]], writes=[psb])
                    k.op(k.dve, lambda: nc.vector.tensor_tensor(out=gates[:, blk, :], in0=ps[:, 0:32], in1=bgate, op=ALU.add),
                         reads=[psb, mpk_b], writes=[gates_b])
                self.phase_barrier()
            with contextlib.ExitStack() as es2:
                sbt = lambda name, shape, dt: es2.enter_context(_sbuf(nc, name, list(shape), dt))
                C = sbt("ml_C", [128, NHm, DV1], F32)
                Cb = sbt("ml_Cb", [128, NHm, DV1], BF16)
                mst = sbt("ml_mst", [128, NHm], F32)
                C_b, Cb_b, mst_b = Buf(), Buf(), Buf()
                qc = [sbt(f"ml_qc{i}", [128, NHm, 128], BF16) for i in range(2)]
                kc_ = [sbt(f"ml_kc{i}", [128, NHm, 128], BF16) for i in range(2)]
                ktc = [sbt(f"ml_ktc{i}", [128, NHm, 128], BF16) for i in range(2)]
                vx = [sbt(f"ml_vx{i}", [128, NHm, DV1], BF16) for i in range(2)]
                ldb = [[Buf() for _ in range(4)] for _ in range(2)]
                for i in range(2):
                    k.op(k.dve, lambda i=i: nc.vector.memset(vx[i][:, :, DVm:DV1], 1.0), writes=[ldb[i][3]])
                sm = {n_: sbt(f"ml_s_{n_}", [128, NHm], F32) for n_ in
                      ("e", "lsp", "b", "g", "gmax", "cm", "mx", "a", "em", "nmx", "mx2", "wgt", "dec", "tmp")}
                smb = {n_: Buf() for n_ in sm}
                diag = sbt("ml_diag", [128, 4, 128], F32)
                diag_b = Buf()
                dmat = sbt("ml_dmat", [128, 4, 128], F32)
                dmat_b = Buf()
                DmT = sbt("ml_DmT", [128, NHm, 128], F32)
                DmT_b = Buf()
                PT8 = sbt("ml_PT8", [128, NHm, 128], BF16)
                PT8_b = Buf()
                kw = sbt("ml_kw", [128, NHm, 128], BF16)
                kw_b = Buf()
                tmpn = [sbt(f"ml_tmpn{i}", [128, DV1], F32) for i in range(2)]
                tmpn_b = [Buf() for _ in range(2)]
                num = [sbt(f"ml_num{i}", [128, DV1], F32) for i in range(2)]
                num_b = [Buf() for _ in range(2)]
                dn = [sbt(f"ml_dn{i}", [128, 1], F32) for i in range(2)]
                dn_b = [Buf() for _ in range(2)]
                hout = sbt("ml_hout", [128, NHm, DVm], F32)
                hout_b = Buf()
                hfl = sbt("ml_hfl", [128, NHm, DVm], F32)
                hfl_b = Buf()
                ogl = sbt("ml_ogl", [128, D], BF16)
                ogl_b = Buf()
                ssum = sbt("ml_ssum", [128, NHm], F32)
                ssum_b = Buf()
                hsT = sbt("ml_hsTt", [128, KC, 128], BF16)
                hsT_tb = Buf()

                def load_chunk(blk, i):
                    sl = slice(blk * 128, (blk + 1) * 128)
                    k.dma(k.sp, qc[i][:], qT_s[:, :, sl].rearrange("h p t -> p h t"), reads=[scr_b], writes=[ldb[i][0]])
                    k.dma(k.sp, kc_[i][:], kT_s[:, :, sl].rearrange("h p t -> p h t"), reads=[scr_b], writes=[ldb[i][1]])
                    k.dma(k.sp, ktc[i][:], ktm_s[blk].rearrange("p (h d) -> p h d", d=128), reads=[scr_b], writes=[ldb[i][2]])
                    k.dma(k.sp, vx[i][:, :, 0:DVm], vtm_s[blk].rearrange("p (h d) -> p h d", d=DVm), reads=[scr_b], writes=[ldb[i][3]])

                def small(eng_fn, out_n, reads_n, extra_reads=()):
                    k.op(k.dve, eng_fn, reads=[smb[n_] for n_ in reads_n] + list(extra_reads), writes=[smb[out_n]])

                for dirn in range(2):
                    order = list(range(NB)) if dirn == 0 else list(range(NB - 1, -1, -1))
                    gi0 = dirn * 16
                    tri01 = V1_01 if dirn == 0 else V2_01
                    mb_st = V1b if dirn == 0 else V2b
                    mb_ts = V2b if dirn == 0 else V1b
                    load_chunk(order[0], 0)
                    for oi, blk in enumerate(order):
                        i = oi % 2
                        if oi + 1 < NB:
                            load_chunk(order[oi + 1], (oi + 1) % 2)
                        seg = blk // 2
                        first_in_seg = (blk % 2 == 0) if dirn == 0 else (blk % 2 == 1)
                        last_in_seg = not first_in_seg
                        if first_in_seg:
                            if seg >= 4:
                                k.op(k.dve, lambda: nc.vector.memset(C[:], 0.0), writes=[C_b])
                                k.op(k.dve, lambda: nc.vector.memset(mst[:], 0.0), writes=[mst_b])
                                k.op(k.dve, lambda: nc.vector.memset(Cb[:], 0.0), writes=[Cb_b])
                            elif (seg == 0 and dirn == 0) or (seg == 3 and dirn == 1):
                                k.dma(k.sp, C[:].rearrange("p h d -> p (h d)"), self.I("ml_initC")[dirn], writes=[C_b])
                                k.dma(k.sp, mst[:], self.I("ml_initm")[dirn], writes=[mst_b])
                                k.op(k.act, lambda: nc.scalar.copy(out=Cb[:], in_=C[:]), reads=[C_b], writes=[Cb_b])
                            else:
                                k.op(k.dve, lambda: nc.vector.tensor_scalar(out=C[:], in0=C[:], scalar1=flag, scalar2=None, op0=ALU.mult),
                                     reads=[C_b, mpk_b], writes=[C_b])
                                k.op(k.dve, lambda: nc.vector.tensor_scalar(out=mst[:], in0=mst[:], scalar1=flag, scalar2=None, op0=ALU.mult),
                                     reads=[mst_b, mpk_b], writes=[mst_b])
                                k.op(k.act, lambda: nc.scalar.copy(out=Cb[:], in_=C[:]), reads=[C_b], writes=[Cb_b])
                        gi = gates[:, blk, gi0:gi0 + 8]
                        gf = gates[:, blk, gi0 + 8:gi0 + 16]
                        k.op(k.act, lambda: nc.scalar.activation(out=sm["e"][:], in_=gf, func=AF.Exp, scale=-1.0),
                             reads=[gates_b], writes=[smb["e"]])
                        k.op(k.act, lambda: nc.scalar.activation(out=sm["lsp"][:], in_=sm["e"][:], func=AF.Ln, bias=1.0),
                             reads=[smb["e"]], writes=[smb["lsp"]])
                        pb, pbb = self.next_ps()
                        k.op(k.pe, lambda: nc.tensor.matmul(pb[:, 0:8], tri01, sm["lsp"][:], start=True, stop=True),
                             reads=[mpk_b, smb["lsp"]], writes=[pbb])
                        k.op(k.dve, lambda: nc.vector.tensor_tensor(out=sm["g"][:], in0=pb[:, 0:8], in1=gi, op=ALU.add),
                             reads=[pbb, gates_b], writes=[smb["g"]])
                        for hh in range(2):
                            hs_ = slice(hh * 4, (hh + 1) * 4)
                            k.op(k.dve, lambda: nc.vector.tensor_tensor(
                                out=diag[:], in0=ident.unsqueeze(1).to_broadcast([128, 4, 128]),
                                in1=sm["g"][:, hs_].unsqueeze(2).to_broadcast([128, 4, 128]), op=ALU.mult),
                                reads=[mpk_b, smb["g"]], writes=[diag_b])
                            pg, pgb = self.next_ps()
                            k.op(k.pe, lambda: nc.tensor.matmul(pg[:], onesf, diag[:].rearrange("p h s -> p (h s)"), start=True, stop=True),
                                 reads=[mpk_b, diag_b], writes=[pgb])
                            pg3 = pg[:].rearrange("p (h s) -> p h s", s=128)
                            k.op(k.dve, lambda: nc.vector.tensor_reduce(out=sm["gmax"][:, hs_], in_=pg3, axis=AX.X, op=ALU.max),
                                 reads=[pgb], writes=[smb["gmax"]])
                            k.op(k.dve, lambda: nc.vector.tensor_tensor(out=dmat[:], in0=pg3, in1=mb_ts.unsqueeze(1).to_broadcast([128, 4, 128]),
                                                                        op=ALU.add), reads=[pgb, mpk_b], writes=[dmat_b])
                            k.op(k.dve, lambda: nc.vector.tensor_reduce(out=sm["cm"][:, hs_], in_=dmat[:], axis=AX.X, op=ALU.max),
                                 reads=[dmat_b], writes=[smb["cm"]])
                        small(lambda: nc.vector.tensor_tensor(out=sm["mx"][:], in0=sm["cm"][:], in1=mst[:], op=ALU.max), "mx", ["cm"], [mst_b])
                        small(lambda: nc.vector.tensor_tensor(out=sm["tmp"][:], in0=mst[:], in1=sm["mx"][:], op=ALU.subtract), "tmp", ["mx"], [mst_b])
                        k.op(k.act, lambda: nc.scalar.activation(out=sm["a"][:], in_=sm["tmp"][:], func=AF.Exp), reads=[smb["tmp"]], writes=[smb["a"]])
                        small(lambda: nc.vector.tensor_tensor(out=sm["b"][:], in0=pb[:, 0:8], in1=sm["mx"][:], op=ALU.subtract), "b", ["mx"], [pbb])
                        k.op(k.act, lambda: nc.scalar.activation(out=sm["em"][:], in_=sm["b"][:], func=AF.Exp), reads=[smb["b"]], writes=[smb["em"]])
                        small(lambda: nc.vector.tensor_scalar(out=sm["nmx"][:], in0=sm["mx"][:], scalar1=-1.0, scalar2=None, op0=ALU.mult), "nmx", ["mx"])
                        small(lambda: nc.vector.tensor_tensor(out=sm["mx2"][:], in0=sm["gmax"][:], in1=mst[:], op=ALU.max), "mx2", ["gmax"], [mst_b])
                        small(lambda: nc.vector.tensor_tensor(out=sm["tmp"][:], in0=sm["g"][:], in1=sm["mx2"][:], op=ALU.subtract), "tmp", ["g", "mx2", "a"])
                        k.op(k.act, lambda: nc.scalar.activation(out=sm["wgt"][:], in_=sm["tmp"][:], func=AF.Exp), reads=[smb["tmp"]], writes=[smb["wgt"]])
                        small(lambda: nc.vector.tensor_tensor(out=sm["e"][:], in0=mst[:], in1=sm["mx2"][:], op=ALU.subtract), "e", ["mx2", "lsp"], [mst_b])
                        k.op(k.act, lambda: nc.scalar.activation(out=sm["dec"][:], in_=sm["e"][:], func=AF.Exp), reads=[smb["e"]], writes=[smb["dec"]])
                        for hh in range(2):
                            hs_ = slice(hh * 4, (hh + 1) * 4)
                            k.op(k.dve, lambda: nc.vector.tensor_tensor(
                                out=diag[:], in0=ident.unsqueeze(1).to_broadcast([128, 4, 128]),
                                in1=sm["nmx"][:, hs_].unsqueeze(2).to_broadcast([128, 4, 128]), op=ALU.mult),
                                reads=[mpk_b, smb["nmx"]], writes=[diag_b])
                            pu, pub = self.next_ps()
                            k.op(k.pe, lambda: nc.tensor.matmul(pu[:], onesf, diag[:].rearrange("p h s -> p (h s)"), start=True, stop=True),
                                 reads=[mpk_b, diag_b], writes=[pub])
                            pu3 = pu[:].rearrange("p (h t) -> p h t", t=128)
                            k.op(k.dve, lambda: nc.vector.tensor_tensor(out=dmat[:], in0=pu3,
                                                                        in1=sm["g"][:, hs_].unsqueeze(2).to_broadcast([128, 4, 128]), op=ALU.add),
                                 reads=[pub, smb["g"]], writes=[dmat_b])
                            k.op(k.dve, lambda: nc.vector.tensor_tensor(out=dmat[:], in0=dmat[:],
                                                                        in1=mb_st.unsqueeze(1).to_broadcast([128, 4, 128]), op=ALU.add),
                                 reads=[dmat_b, mpk_b], writes=[dmat_b])
                            k.op(k.act, lambda: nc.scalar.activation(out=DmT[:, hs_, :], in_=dmat[:], func=AF.Exp),
                                 reads=[dmat_b], writes=[DmT_b])
                            pss, pssb = self.next_ps()
                            for h4 in range(4):
                                hd_ = hh * 4 + h4
                                k.op(k.pe, lambda: nc.tensor.matmul(pss[:, h4 * 128:(h4 + 1) * 128], kc_[i][:, hd_, :], qc[i][:, hd_, :],
                                                                    start=True, stop=True),
                                     reads=[ldb[i][0], ldb[i][1]], writes=[pssb])
                            k.op(k.dve, lambda: nc.vector.tensor_tensor(out=PT8[:, hs_, :], in0=pss[:].rearrange("p (h t) -> p h t", t=128),
                                                                        in1=DmT[:, hs_, :], op=ALU.mult),
                                 reads=[pssb, DmT_b], writes=[PT8_b])
                        k.op(k.dve, lambda: nc.vector.tensor_tensor(out=kw[:], in0=ktc[i][:],
                                                                    in1=sm["wgt"][:].unsqueeze(2).to_broadcast([128, NHm, 128]), op=ALU.mult),
                             reads=[ldb[i][2], smb["wgt"]], writes=[kw_b])
                        for hd_ in range(NHm):
                            j = hd_ % 2
                            p1, p1b = self.next_ps()
                            k.op(k.pe, lambda: nc.tensor.matmul(p1[:, 0:DV1], PT8[:, hd_, :], vx[i][:, hd_, :], start=True, stop=True),
                                 reads=[PT8_b, ldb[i][3]], writes=[p1b])
                            p2, p2b = self.next_ps()
                            k.op(k.pe, lambda: nc.tensor.matmul(p2[:, 0:DV1], qc[i][:, hd_, :], Cb[:, hd_, :], start=True, stop=True),
                                 reads=[ldb[i][0], Cb_b], writes=[p2b])
                            k.op(k.act, lambda: nc.scalar.activation(out=tmpn[j][:], in_=p2[:, 0:DV1], func=AF.Identity,
                                                                     scale=sm["a"][:, hd_:hd_ + 1]),
                                 reads=[p2b, smb["a"]], writes=[tmpn_b[j]])
                            k.op(k.dve, lambda: nc.vector.tensor_tensor(out=num[j][:], in0=tmpn[j][:], in1=p1[:, 0:DV1], op=ALU.add),
                                 reads=[tmpn_b[j], p1b], writes=[num_b[j]])
                            k.op(k.dve, lambda: nc.vector.tensor_scalar(out=dn[j][:], in0=num[j][:, DVm:DV1], scalar1=-1.0,
                                                                        scalar2=None, op0=ALU.mult),
                                 reads=[num_b[j]], writes=[dn_b[j]])
                            k.op(k.dve, lambda: nc.vector.tensor_tensor(out=dn[j][:], in0=dn[j][:], in1=num[j][:, DVm:DV1], op=ALU.max),
                                 reads=[num_b[j], dn_b[j]], writes=[dn_b[j]])
                            k.op(k.dve, lambda: nc.vector.tensor_tensor(out=dn[j][:], in0=dn[j][:], in1=sm["em"][:, hd_:hd_ + 1], op=ALU.max),
                                 reads=[dn_b[j], smb["em"]], writes=[dn_b[j]])
                            k.op(k.dve, lambda: nc.vector.reciprocal(out=dn[j][:], in_=dn[j][:]), reads=[dn_b[j]], writes=[dn_b[j]])
                            k.op(k.dve, lambda: nc.vector.tensor_scalar(out=hout[:, hd_, :], in0=num[j][:, 0:DVm], scalar1=dn[j][:, 0:1],
                                                                        scalar2=None, op0=ALU.mult),
                                 reads=[num_b[j], dn_b[j]], writes=[hout_b])
                            p3, p3b = self.next_ps()
                            k.op(k.pe, lambda: nc.tensor.matmul(p3[:, 0:DV1], kw[:, hd_, :], vx[i][:, hd_, :], start=True, stop=True),
                                 reads=[kw_b, ldb[i][3]], writes=[p3b])
                            k.op(k.dve, lambda: nc.vector.scalar_tensor_tensor(out=C[:, hd_, :], in0=C[:, hd_, :], scalar=sm["dec"][:, hd_:hd_ + 1],
                                                                               in1=p3[:, 0:DV1], op0=ALU.mult, op1=ALU.add),
                                 reads=[C_b, smb["dec"], p3b], writes=[C_b])
                        k.op(k.act, lambda: nc.scalar.copy(out=Cb[:], in_=C[:]), reads=[C_b], writes=[Cb_b])
                        pbl, pblb = self.next_ps()
                        k.op(k.pe, lambda: nc.tensor.matmul(pbl[:, 0:8], onesf, sm["lsp"][:], start=True, stop=True),
                             reads=[mpk_b, smb["lsp"]], writes=[pblb])
                        k.op(k.dve, lambda: nc.vector.tensor_tensor(out=mst[:], in0=sm["mx2"][:], in1=pbl[:, 0:8], op=ALU.subtract),
                             reads=[smb["mx2"], pblb, smb["dec"], smb["a"], smb["mx"]], writes=[mst_b])
                        if last_in_seg:
                            k.dma(k.sp, self.O("ml_stC")[seg, dirn], C[:].rearrange("p h d -> p (h d)"), reads=[C_b], appends=[outb])
                            k.dma(k.sp, self.O("ml_stm")[seg, dirn], mst[:], reads=[mst_b], appends=[outb])
                        if dirn == 0:
                            k.dma(k.sp, hf_s[blk].rearrange("p (h d) -> p h d", d=DVm), hout[:], reads=[hout_b], appends=[hf_b])
                        else:
                            k.dma(k.sp, hfl[:], hf_s[blk].rearrange("p (h d) -> p h d", d=DVm), reads=[hf_b], writes=[hfl_b])
                            k.dma(k.sp, ogl[:], og_s[blk], reads=[scr_b], writes=[ogl_b])
                            k.op(k.dve, lambda: nc.vector.tensor_tensor(out=hout[:], in0=hout[:], in1=hfl[:], op=ALU.add),
                                 reads=[hout_b, hfl_b], writes=[hout_b])
                            k.op(k.act, lambda: nc.scalar.activation(out=hfl[:], in_=hout[:], func=AF.Square), reads=[hout_b], writes=[hfl_b])
                            k.op(k.dve, lambda: nc.vector.tensor_reduce(out=ssum[:], in_=hfl[:], axis=AX.X, op=ALU.add),
                                 reads=[hfl_b], writes=[ssum_b])
                            k.op(k.dve, lambda: nc.vector.tensor_scalar(out=ssum[:], in0=ssum[:], scalar1=1.0 / DVm, scalar2=EPS,
                                                                        op0=ALU.mult, op1=ALU.add), reads=[ssum_b], writes=[ssum_b])
                            k.op(k.act, lambda: nc.scalar.activation(out=ssum[:], in_=ssum[:], func=AF.Sqrt), reads=[ssum_b], writes=[ssum_b])
                            k.op(k.dve, lambda: nc.vector.reciprocal(out=ssum[:], in_=ssum[:]), reads=[ssum_b], writes=[ssum_b])
                            k.op(k.dve, lambda: nc.vector.tensor_tensor(out=hout[:], in0=hout[:],
                                                                        in1=ssum[:].unsqueeze(2).to_broadcast([128, NHm, DVm]), op=ALU.mult),
                                 reads=[hout_b, ssum_b], writes=[hout_b])
                            hflat = hout[:].rearrange("p h d -> p (h d)")
                            k.op(k.dve, lambda: nc.vector.tensor_tensor(out=hflat, in0=hflat, in1=onorm, op=ALU.mult),
                                 reads=[hout_b, mpk_b], writes=[hout_b])
                            k.op(k.dve, lambda: nc.vector.tensor_tensor(out=hflat, in0=hflat, in1=ogl[:], op=ALU.mult),
                                 reads=[hout_b, ogl_b], writes=[hout_b])
                            for q4 in range(4):
                                ptr, ptrb = self.next_ps()
                                for c4 in range(4):
                                    c = q4 * 4 + c4
                                    k.op(k.pe, lambda: nc.tensor.transpose(ptr[:, c4 * 128:(c4 + 1) * 128], hflat[:, c * 128:(c + 1) * 128], ident),
                                         reads=[hout_b, mpk_b], writes=[ptrb])
                                k.op(k.act, lambda: nc.scalar.copy(out=hsT[:, q4 * 4:(q4 + 1) * 4, :], in_=ptr[:].rearrange("p (c t) -> p c t", t=128)),
                                     reads=[ptrb], writes=[hsT_tb])
                            k.dma(k.sp, hsT_s[:, :, blk * 128:(blk + 1) * 128].rearrange("c p t -> p c t"), hsT[:], reads=[hsT_tb], appends=[hsT_b])
                self.phase_barrier()
            with contextlib.ExitStack() as es2:
                sbt = lambda name, shape, dt: es2.enter_context(_sbuf(nc, name, list(shape), dt))
                h2 = sbt("ml_h2", [128, KC, NTOK], BF16)
                h2b = [Buf() for _ in range(NGRP)]
                for t in range(NGRP):
                    k.dma(k.sp, h2[:, :, t * TT:(t + 1) * TT], hsT_s[:, :, t * TT:(t + 1) * TT].rearrange("c p t -> p c t"),
                          reads=[hsT_b], writes=[h2b[t]])
                wo_v = self.I("mlstm_w_o")[slot].rearrange("(kc p) n -> p kc n", p=128)

                def ev_o(oc, t, ps, psb):
                    ts = slice(t * TT, (t + 1) * TT)
                    k.op(k.dve, lambda: nc.vector.scalar_tensor_tensor(
                        out=self.x_sb[:, oc, ts], in0=ps[:], scalar=self.modT[:, l, 5, oc, t:t + 1],
                        in1=self.x_sb[:, oc, ts], op0=ALU.mult, op1=ALU.add),
                        reads=[psb, self.modT_b, self.xb[oc][t]], writes=[self.xb[oc][t]])

                self.linear_fm(h2, h2b, wo_v, 0, D, ev_o, "o")
                self.phase_barrier()


    def rwkv(self, l):
        k, nc = self.k, self.nc
        slot = l // 3
        TB, SB = 32, 2
        KZ = [nc.dram_tensor(f"rk_KZ{z}_{l}", [128, 5, KC, NTOK], F32).ap() for z in range(2)]
        VV = nc.dram_tensor(f"rk_VV{l}", [128, KC, NTOK], F32).ap()
        GG = nc.dram_tensor(f"rk_GG{l}", [128, KC, NTOK], F32).ap()
        VB = nc.dram_tensor(f"rk_VB{l}", [128, KC, NTOK], F32).ap()
        RAWK = nc.dram_tensor(f"rk_RAWK{l}", [128, KC, NTOK], F32).ap()
        AZ = [nc.dram_tensor(f"rk_AZ{z}_{l}", [128, KC, NTOK], F32).ap() for z in range(2)]
        YT = [nc.dram_tensor(f"rk_YT{z}_{l}", [NTOK, D], BF16).ap() for z in range(2)]
        hsT_s = nc.dram_tensor(f"rk_hsT{l}", [KC, 128, NTOK], BF16).ap()
        scr_b, yt_b, hsT_b, outb = Buf("rk_scr"), Buf("rk_yt"), Buf("rk_hsT"), Buf("rk_out")
        NFM = 13 * KC
        with contextlib.ExitStack() as es1:
            sbt1 = lambda name, shape, dt: es1.enter_context(_sbuf(nc, name, list(shape), dt))
            self.phase_barrier()
            rpk = sbt1("rk_rpk", [128, NRP], F32)
            rpk_b = Buf()
            k.dma(k.sp, rpk[:], self.I("rpack")[:, :], writes=[rpk_b])
            fmv = rpk[:, 0:NFM].rearrange("p (a c) -> p a c", c=KC)
            I2 = rpk[:, NFM:NFM + 64]
            flag = rpk[:, NFM + 64:NFM + 65]
            ident = rpk[:, NFM + 65:NFM + 193]
            BO = sbt1("rk_BO", [128, 128], BF16)
            hsel = sbt1("rk_hsel", [128, 2], BF16)
            cst_b = Buf()
            k.op(k.dve, lambda: nc.vector.memset(BO[:], 0.0), writes=[cst_b])
            k.op(k.dve, lambda: nc.vector.memset(BO[0:64, 0:64], 1.0), writes=[cst_b])
            k.op(k.dve, lambda: nc.vector.memset(BO[64:128, 64:128], 1.0), writes=[cst_b])
            k.op(k.dve, lambda: nc.vector.memset(hsel[:], 0.0), writes=[cst_b])
            k.op(k.dve, lambda: nc.vector.memset(hsel[0:64, 0:1], 1.0), writes=[cst_b])
            k.op(k.dve, lambda: nc.vector.memset(hsel[64:128, 1:2], 1.0), writes=[cst_b])
            with contextlib.ExitStack() as es2:
                sbt = lambda name, shape, dt: es2.enter_context(_sbuf(nc, name, list(shape), dt))
                hp_ = sbt("rk_h", [128, KC, NTOK + 2], BF16)
                hb = [Buf(f"h{t}") for t in range(NGRP)]
                k.op(k.dve, lambda: nc.vector.memset(hp_[:, :, 0:1], 0.0), writes=[hb[0]])
                k.op(k.dve, lambda: nc.vector.memset(hp_[:, :, NTOK + 1:NTOK + 2], 0.0), writes=[hb[2]])

                class Shift:
                    def __getitem__(self_, idx):
                        p, c, sl = idx
                        return hp_[p, c, slice(sl.start + 1, sl.stop + 1)]
                sq = [sbt(f"rk_sq{i}", [128, TT], BF16) for i in range(2)]
                sqb = [Buf() for _ in range(2)]
                self.compute_h(l, 1, Shift(), hb, [t_[:] for t_ in sq], sqb)
                msk = sbt("rk_msk", [128, 2, NTOK], BF16)
                msk_b = Buf()
                k.dma(k.pool, msk[:], self.I("rmask").rearrange("p (a t) -> p a t", a=2), writes=[msk_b])
                omm = sbt("rk_omm", [128, 6, KC], F32)
                hmu = sbt("rk_hmu", [128, 6, KC], F32)
                omm_b = Buf()
                k.op(k.dve, lambda: nc.vector.tensor_scalar(out=omm[:], in0=fmv[:, 0:6, :], scalar1=-1.0, scalar2=1.0, op0=ALU.mult, op1=ALU.add),
                     reads=[rpk_b], writes=[omm_b])
                k.op(k.dve, lambda: nc.vector.tensor_scalar(out=hmu[:], in0=fmv[:, 0:6, :], scalar1=0.5, scalar2=None, op0=ALU.mult),
                     reads=[rpk_b], writes=[omm_b])
                xs = sbt("rk_xs", [128, KC, TT], BF16)
                xs_b = Buf()
                tA = [sbt(f"rk_tA{i}", [128, TT], F32) for i in range(3)]
                tA_b = [Buf() for _ in range(3)]
                ev = [sbt(f"rk_ev{i}", [128, TT], F32) for i in range(3)]
                ev_b = [Buf() for _ in range(3)]
                mid = sbt("rk_mid", [128, 2, TT], BF16)
                mid_b = Buf()
                cnt = [0]

                def nxt():
                    cnt[0] += 1
                    return cnt[0] % 3

                def build_xs(p, t):
                    t0 = t * TT
                    for c in range(KC):
                        k.op(k.dve, lambda: nc.vector.tensor_tensor(out=tA[0][:], in0=hp_[:, c, t0:t0 + TT], in1=msk[:, 0, t0:t0 + TT], op=ALU.mult),
                             reads=[hb[t], hb[max(t - 1, 0)], msk_b], writes=[tA_b[0]])
                        k.op(k.dve, lambda: nc.vector.tensor_tensor(out=tA[1][:], in0=hp_[:, c, t0 + 2:t0 + TT + 2], in1=msk[:, 1, t0:t0 + TT], op=ALU.mult),
                             reads=[hb[t], hb[min(t + 1, 2)], msk_b], writes=[tA_b[1]])
                        k.op(k.dve, lambda: nc.vector.tensor_tensor(out=tA[0][:], in0=tA[0][:], in1=tA[1][:], op=ALU.add),
                             reads=[tA_b[0], tA_b[1]], writes=[tA_b[0]])
                        k.op(k.act, lambda: nc.scalar.activation(out=tA[2][:], in_=hp_[:, c, t0 + 1:t0 + TT + 1], func=AF.Identity, scale=omm[:, p, c:c + 1]),
                             reads=[hb[t], omm_b], writes=[tA_b[2]])
                        k.op(k.dve, lambda: nc.vector.scalar_tensor_tensor(out=xs[:, c, :], in0=tA[0][:], scalar=hmu[:, p, c:c + 1], in1=tA[2][:],
                                                                           op0=ALU.mult, op1=ALU.add),
                             reads=[tA_b[0], tA_b[2], omm_b], writes=[xs_b])

                def proj_tile(wview, col0, ncols, evac, tag):
                    TC = 256
                    with contextlib.ExitStack() as es3:
                        wt = [es3.enter_context(_sbuf(nc, f"rkw_{tag}{i}", [128, KC, TC], BF16)) for i in range(2)]
                        wtb = [Buf() for _ in range(2)]
                        ntile = ncols // TC

                        def load(ti):
                            k.dma(k.pool, wt[ti % 2][:], wview[:, :, col0 + ti * TC:col0 + (ti + 1) * TC], writes=[wtb[ti % 2]])
                        load(0)
                        for ti in range(ntile):
                            if ti + 1 < ntile:
                                load(ti + 1)
                            for sub in range(TC // 128):
                                oc = ti * (TC // 128) + sub
                                ps, psb = self.next_ps()
                                for kc in range(KC):
                                    k.op(k.pe, lambda kc=kc: nc.tensor.matmul(ps[:], wt[ti % 2][:, kc, sub * 128:(sub + 1) * 128], xs[:, kc, :],
                                                                              start=(kc == 0), stop=(kc == KC - 1)),
                                         reads=[wtb[ti % 2], xs_b], writes=[psb])
                                evac(oc, ps, psb)
                        self.scope_end(wtb)

                for p in range(3):
                    wv_ = self.I("rwkv_w_rkv")[slot, p].rearrange("(kc p) n -> p kc n", p=128)
                    for t in range(NGRP):
                        ts = slice(t * TT, (t + 1) * TT)
                        build_xs(p, t)

                        def ev_rkv(oc, ps, psb, p=p, ts=ts):
                            i = nxt()
                            k.op(k.act, lambda: nc.scalar.copy(out=ev[i][:], in_=ps[:]), reads=[psb], writes=[ev_b[i]])
                            if p == 0:
                                k.dma(k.sp, KZ[0][:, 4, oc, ts], ev[i][:], reads=[ev_b[i]], appends=[scr_b])
                                k.dma(k.sp, KZ[1][:, 4, oc, ts], ev[i][:], reads=[ev_b[i]], appends=[scr_b])
                            elif p == 1:
                                k.dma(k.sp, RAWK[:, oc, ts], ev[i][:], reads=[ev_b[i]], appends=[scr_b])
                            else:
                                k.dma(k.sp, VV[:, oc, ts], ev[i][:], reads=[ev_b[i]], appends=[scr_b])
                        proj_tile(wv_, 0, D, ev_rkv, f"p{p}")
                for p in (3, 4, 5):
                    with contextlib.ExitStack() as es3:
                        if p < 5:
                            nmA, nmB, R_ = ("rwkv_wA", "rwkv_wB", 96) if p == 3 else ("rwkv_aA", "rwkv_aB", 96)
                            dn_w = [es3.enter_context(_sbuf(nc, f"rk_lA{p}{z}", [128, KC, R_], BF16)) for z in range(2)]
                            up_w = [es3.enter_context(_sbuf(nc, f"rk_lB{p}{z}", [128, 1, D], BF16)) for z in range(2)]
                            lw_b = Buf()
                            for z in range(2):
                                k.dma(k.pool, dn_w[z][:], self.I(nmA)[slot, z].rearrange("(kc p) r -> p kc r", p=128), writes=[lw_b])
                                k.dma(k.pool, up_w[z][0:R_, 0, :], self.I(nmB)[slot, z], writes=[lw_b])
                            nz, nch = 2, 1
                        else:
                            R_ = 256
                            dn_w = [es3.enter_context(_sbuf(nc, "rk_lA5", [128, KC, R_], BF16))]
                            up_w = [es3.enter_context(_sbuf(nc, "rk_lB5", [128, 2, D], BF16))]
                            lw_b = Buf()
                            k.dma(k.pool, dn_w[0][:], self.I("rwkv_gA")[slot].rearrange("(kc p) r -> p kc r", p=128), writes=[lw_b])
                            k.dma(k.pool, up_w[0][:], self.I("rwkv_gB")[slot].rearrange("(c p) n -> p c n", p=128), writes=[lw_b])
                            nz, nch = 1, 2
                        for t in range(NGRP):
                            ts = slice(t * TT, (t + 1) * TT)
                            build_xs(p, t)
                            for z in range(nz):
                                rows = 96 if p < 5 else 128
                                for ch in range(nch):
                                    pd, pdb = self.next_ps()
                                    for kc in range(KC):
                                        k.op(k.pe, lambda kc=kc: nc.tensor.matmul(pd[0:rows, :], dn_w[z][:, kc, ch * 128:ch * 128 + rows], xs[:, kc, :],
                                                                                  start=(kc == 0), stop=(kc == KC - 1)),
                                             reads=[lw_b, xs_b], writes=[pdb])
                                    fn = AF.Tanh if p == 3 else (AF.Identity if p == 4 else AF.Sigmoid)
                                    k.op(k.act, lambda: nc.scalar.activation(out=mid[0:rows, ch, :], in_=pd[0:rows, :], func=fn),
                                         reads=[pdb], writes=[mid_b])
                                for oc in range(KC):
                                    pu, pub = self.next_ps()
                                    for ch in range(nch):
                                        k.op(k.pe, lambda ch=ch: nc.tensor.matmul(pu[:], up_w[z][0:rows, ch, oc * 128:(oc + 1) * 128], mid[0:rows, ch, :],
                                                                                  start=(ch == 0), stop=(ch == nch - 1)),
                                             reads=[lw_b, mid_b], writes=[pub])
                                    i = nxt()
                                    if p == 3:
                                        k.op(k.act, lambda: nc.scalar.activation(out=ev[i][:], in_=pu[:], func=AF.Sigmoid, bias=fmv[:, 6 + z, oc:oc + 1]),
                                             reads=[pub, rpk_b], writes=[ev_b[i]])
                                        k.op(k.act, lambda: nc.scalar.activation(out=ev[i][:], in_=ev[i][:], func=AF.Exp, scale=-float(np.exp(-0.5))),
                                             reads=[ev_b[i]], writes=[ev_b[i]])
                                        k.dma(k.sp, KZ[z][:, 1, oc, ts], ev[i][:], reads=[ev_b[i]], appends=[scr_b])
                                    elif p == 4:
                                        k.op(k.act, lambda: nc.scalar.activation(out=ev[i][:], in_=pu[:], func=AF.Sigmoid, bias=fmv[:, 8 + z, oc:oc + 1]),
                                             reads=[pub, rpk_b], writes=[ev_b[i]])
                                        k.dma(k.sp, AZ[z][:, oc, ts], ev[i][:], reads=[ev_b[i]], appends=[scr_b])
                                    else:
                                        k.op(k.act, lambda: nc.scalar.copy(out=ev[i][:], in_=pu[:]), reads=[pub], writes=[ev_b[i]])
                                        k.dma(k.sp, GG[:, oc, ts], ev[i][:], reads=[ev_b[i]], appends=[scr_b])
                        self.scope_end([lw_b])
                self.phase_barrier()
            with contextlib.ExitStack() as es2:
                sbt = lambda name, shape, dt: es2.enter_context(_sbuf(nc, name, list(shape), dt))
                names = ("k", "r", "v", "a0", "a1", "kq", "kk", "t", "kd0", "kd1", "b0", "b1", "vb")
                T2 = {n_: [sbt(f"rk2_{n_}{i}", [128, TT], F32) for i in range(2)] for n_ in names}
                T2b = {n_: [Buf() for _ in range(2)] for n_ in names}
                sqk = [sbt(f"rk2_sq{i}", [128, TT], BF16) for i in range(2)]
                sqk_b = [Buf() for _ in range(2)]
                it = 0
                for t in range(NGRP):
                    ts = slice(t * TT, (t + 1) * TT)
                    for c in range(KC):
                        i = it % 2
                        it += 1
                        X = {n_: T2[n_][i] for n_ in names}
                        B = {n_: T2b[n_][i] for n_ in names}
                        k.dma(k.sp, X["k"][:], RAWK[:, c, ts], reads=[scr_b], writes=[B["k"]])
                        k.dma(k.sp, X["r"][:], KZ[0][:, 4, c, ts], reads=[scr_b], writes=[B["r"]])
                        k.dma(k.sp, X["v"][:], VV[:, c, ts], reads=[scr_b], writes=[B["v"]])
                        k.dma(k.sp, X["a0"][:], AZ[0][:, c, ts], reads=[scr_b], writes=[B["a0"]])
                        k.dma(k.sp, X["a1"][:], AZ[1][:, c, ts], reads=[scr_b], writes=[B["a1"]])
                        k.op(k.act, lambda: nc.scalar.activation(out=X["kq"][:], in_=X["k"][:], func=AF.Identity, scale=fmv[:, 10, c:c + 1]),
                             reads=[B["k"], rpk_b], writes=[B["kq"]])
                        k.op(k.act, lambda: nc.scalar.activation(out=sqk[i][:], in_=X["kq"][:], func=AF.Square), reads=[B["kq"]], writes=[sqk_b[i]])
                        pn, pnb = self.next_ps()
                        k.op(k.pe, lambda: nc.tensor.matmul(pn[:], BO[:], sqk[i][:], start=True, stop=True), reads=[cst_b, sqk_b[i]], writes=[pnb])
                        k.op(k.act, lambda: nc.scalar.activation(out=pn[:], in_=pn[:], func=AF.Sqrt), reads=[pnb], writes=[pnb])
                        k.op(k.dve, lambda: nc.vector.tensor_scalar(out=pn[:], in0=pn[:], scalar1=1e-12, scalar2=None, op0=ALU.max), reads=[pnb], writes=[pnb])
                        k.op(k.dve, lambda: nc.vector.reciprocal(out=pn[:], in_=pn[:]), reads=[pnb], writes=[pnb])
                        k.op(k.dve, lambda: nc.vector.tensor_tensor(out=X["kk"][:], in0=X["kq"][:], in1=pn[:], op=ALU.mult),
                             reads=[B["kq"], pnb], writes=[B["kk"]])
                        k.dma(k.sp, KZ[0][:, 0, c, ts], X["kk"][:], reads=[B["kk"]], appends=[scr_b])
                        k.dma(k.sp, KZ[1][:, 0, c, ts], X["kk"][:], reads=[B["kk"]], appends=[scr_b])
                        for z in range(2):
                            az, kd, bz = X[f"a{z}"], X[f"kd{z}"], X[f"b{z}"]
                            k.op(k.dve, lambda: nc.vector.tensor_scalar(out=X["t"][:], in0=az[:], scalar1=-1.0, scalar2=fmv[:, 11, c:c + 1],
                                                                        op0=ALU.add, op1=ALU.mult), reads=[B[f"a{z}"], rpk_b], writes=[B["t"]])
                            k.op(k.dve, lambda: nc.vector.scalar_tensor_tensor(out=kd[:], in0=X["t"][:], scalar=1.0, in1=X["k"][:], op0=ALU.add, op1=ALU.mult),
                                 reads=[B["t"], B["k"]], writes=[B[f"kd{z}"]])
                            k.op(k.dve, lambda: nc.vector.tensor_tensor(out=bz[:], in0=X["kk"][:], in1=az[:], op=ALU.mult),
                                 reads=[B["kk"], B[f"a{z}"]], writes=[B[f"b{z}"]])
                            k.dma(k.sp, KZ[z][:, 3, c, ts], kd[:], reads=[B[f"kd{z}"]], appends=[scr_b])
                            k.dma(k.sp, KZ[z][:, 2, c, ts], bz[:], reads=[B[f"b{z}"]], appends=[scr_b])
                        k.op(k.dve, lambda: nc.vector.tensor_tensor(out=X["t"][:], in0=X["kd0"][:], in1=X["kd1"][:], op=ALU.add),
                             reads=[B["kd0"], B["kd1"]], writes=[B["t"]])
                        k.op(k.dve, lambda: nc.vector.scalar_tensor_tensor(out=sqk[i][:], in0=X["t"][:], scalar=fmv[:, 12, c:c + 1], in1=X["r"][:],
                                                                           op0=ALU.mult, op1=ALU.mult),
                             reads=[B["t"], B["r"], rpk_b], writes=[sqk_b[i]])
                        pbn, pbnb = self.next_ps()
                        k.op(k.pe, lambda: nc.tensor.matmul(pbn[:], BO[:], sqk[i][:], start=True, stop=True), reads=[cst_b, sqk_b[i]], writes=[pbnb])
                        k.op(k.dve, lambda: nc.vector.tensor_tensor(out=X["vb"][:], in0=X["v"][:], in1=pbn[:], op=ALU.mult),
                             reads=[B["v"], pbnb], writes=[B["vb"]])
                        k.dma(k.sp, VB[:, c, ts], X["vb"][:], reads=[B["vb"]], appends=[scr_b])
                self.phase_barrier()
            with contextlib.ExitStack() as es2:
                sbt = lambda name, shape, dt: es2.enter_context(_sbuf(nc, name, list(shape), dt))
                CH = []
                for z in range(2):
                    ch = dict(z=z)
                    ch["S"] = [sbt(f"rks_S{z}{i}", [128, KC, 64], F32) for i in range(2)]
                    ch["S_b"] = [Buf() for _ in range(2)]
                    ch["cur"] = 0
                    ch["tA1"] = sbt(f"rks_tA1{z}", [128, KC, 64], BF16)
                    ch["tA4"] = sbt(f"rks_tA4{z}", [128, KC, 64], BF16)
                    ch["vd"] = [sbt(f"rks_vd{z}{i}", [128, KC, 64], BF16) for i in range(2)]
                    ch["vd_b"] = [Buf() for _ in range(2)]
                    ch["psv"] = [None, None]
                    ch["tF2"] = sbt(f"rks_tF2{z}", [128, KC, 64], F32)
                    ch["tF3"] = ch["tF2"]
                    ch["kb"] = [sbt(f"rks_kb{z}{i}", [128, 5, KC, TB], F32) for i in range(2)]
                    ch["vb"] = [sbt(f"rks_vb{z}{i}", [128, KC, TB], F32) for i in range(2)]
                    ch["ys"] = sbt(f"rks_ys{z}", [2, SB, KC * 64], BF16)
                    for n_ in ("tA1", "tA4", "tF2", "ys"):
                        ch[n_ + "_b"] = Buf()
                    ch["tF3_b"] = ch["tF2_b"]
                    ch["kb_b"] = [Buf() for _ in range(2)]
                    ch["vb_b"] = [Buf() for _ in range(2)]
                    CH.append(ch)
                e_t4 = k.pool if self.cfg.get("rk_pool", True) else k.dve
                veng = lambda e: (nc.gpsimd if e is k.pool else nc.vector)

                def load_blk(ch, t0, i):
                    z = ch["z"]
                    k.dma(k.sp, ch["kb"][i][:], KZ[z][:, :, :, t0:t0 + TB], reads=[scr_b], writes=[ch["kb_b"][i]])
                    k.dma(k.sp, ch["vb"][i][:], VV[:, :, t0:t0 + TB], reads=[scr_b], writes=[ch["vb_b"][i]])

                def stage0(n, ch, i, j):
                    vi = n % 2
                    vcol = ch["vb"][i][:, :, j:j + 1].to_broadcast([128, KC, 64])
                    k.op(k.pool, lambda: nc.gpsimd.tensor_tensor(out=ch["vd"][vi][:], in0=I2.unsqueeze(1).to_broadcast([128, KC, 64]), in1=vcol,
                                                                 op=ALU.mult),
                         reads=[rpk_b, ch["vb_b"][i]], writes=[ch["vd_b"][vi]])

                def run_pass(T0, T1, is_L):
                    nblk = (T1 - T0) // TB
                    seq = []
                    for bi in range(nblk):
                        for jj in range(TB):
                            ent = []
                            for ch in CH:
                                z = ch["z"]
                                t0 = T0 + bi * TB if z == 0 else T1 - (bi + 1) * TB
                                j = jj if z == 0 else TB - 1 - jj
                                ent.append((ch, bi % 2, j, t0 + j))
                            seq.append((bi, jj, ent))
                    for ch in CH:
                        load_blk(ch, T0 if ch["z"] == 0 else T1 - TB, 0)
                    for (ch, i, j, t) in seq[0][2]:
                        stage0(0, ch, i, j)
                    for n, (bi, jj, ent) in enumerate(seq):
                        if jj == 0 and bi + 1 < nblk:
                            for ch in CH:
                                nt0 = T0 + (bi + 1) * TB if ch["z"] == 0 else T1 - (bi + 2) * TB
                                load_blk(ch, nt0, (bi + 1) % 2)
                        if n + 1 < len(seq):
                            for (ch, i, j, t) in seq[n + 1][2]:
                                stage0(n + 1, ch, i, j)
                        psa = {}
                        for (ch, i, j, t) in ent:
                            z = ch["z"]
                            Sa, Sab = ch["S"][ch["cur"]], ch["S_b"][ch["cur"]]
                            at_start = (t % SEG == 0) if z == 0 else (t % SEG == SEG - 1)
                            if at_start:
                                chain_start = (t == T0) if z == 0 else (t == T1 - 1)
                                if not is_L:
                                    k.op(k.dve, lambda: nc.vector.memset(Sa[:], 0.0), writes=[Sab])
                                elif chain_start:
                                    k.dma(k.sp, Sa[:].rearrange("p g v -> p (g v)"), self.I("rk_initS")[z], writes=[Sab])
                                else:
                                    k.op(k.dve, lambda: nc.vector.tensor_scalar(out=Sa[:], in0=Sa[:], scalar1=flag, scalar2=None, op0=ALU.mult),
                                         reads=[Sab, rpk_b], writes=[Sab])
                            kb, kbb = ch["kb"][i], ch["kb_b"][i]
                            k.op(k.dve, lambda: nc.vector.tensor_tensor(out=ch["tA1"][:], in0=Sa[:], in1=kb[:, 0, :, j:j + 1].to_broadcast([128, KC, 64]),
                                                                        op=ALU.mult), reads=[Sab, kbb], writes=[ch["tA1_b"]])
                            tAf = ch["tA1"][:].rearrange("p g v -> p (g v)")
                            pl = []
                            for q in range(2):
                                p_, pb_ = self.next_ps()
                                k.op(k.pe, lambda: nc.tensor.matmul(p_[:], BO[:], tAf[:, q * 512:(q + 1) * 512], start=True, stop=True),
                                     reads=[cst_b, ch["tA1_b"]], writes=[pb_])
                                pl.append((p_, pb_))
                            psa[z] = pl
                        for (ch, i, j, t) in ent:
                            vi = n % 2
                            vdf = ch["vd"][vi][:].rearrange("p g v -> p (g v)")
                            pl = []
                            for q in range(2):
                                p_, pb_ = self.next_ps()
                                k.op(k.pe, lambda: nc.tensor.matmul(p_[:], BO[:], vdf[:, q * 512:(q + 1) * 512], start=True, stop=True),
                                     reads=[cst_b, ch["vd_b"][vi]], writes=[pb_])
                                pl.append((p_, pb_))
                            ch["psv"][vi] = pl
                        for (ch, i, j, t) in ent:
                            kb, kbb = ch["kb"][i], ch["kb_b"][i]
                            for q in range(2):
                                p_, pb_ = ch["psv"][n % 2][q]
                                gs = slice(q * 8, (q + 1) * 8)
                                k.op(k.dve, lambda: nc.vector.tensor_tensor(out=ch["tF3"][:, gs, :], in0=p_[:].rearrange("p (g v) -> p g v", v=64),
                                                                            in1=kb[:, 3, gs, j:j + 1].to_broadcast([128, 8, 64]), op=ALU.mult),
                                     reads=[pb_, kbb], writes=[ch["tF3_b"]])
                        e_sw = k.pool if self.cfg.get("rk_sw_pool", True) else k.dve
                        for (ch, i, j, t) in ent:
                            kb, kbb = ch["kb"][i], ch["kb_b"][i]
                            cur = ch["cur"]
                            Sa, Sab, Sn, Snb = ch["S"][cur], ch["S_b"][cur], ch["S"][1 - cur], ch["S_b"][1 - cur]
                            k.op(e_sw, lambda: veng(e_sw).tensor_tensor(out=Sn[:], in0=Sa[:], in1=kb[:, 1, :, j:j + 1].to_broadcast([128, KC, 64]),
                                                                        op=ALU.mult), reads=[Sab, kbb], writes=[Snb])
                        for (ch, i, j, t) in ent:
                            cur = ch["cur"]
                            Sn, Snb = ch["S"][1 - cur], ch["S_b"][1 - cur]
                            k.op(k.dve, lambda: nc.vector.tensor_tensor(out=Sn[:], in0=Sn[:], in1=ch["tF3"][:], op=ALU.add),
                                 reads=[Snb, ch["tF3_b"]], writes=[Snb])
                        for (ch, i, j, t) in ent:
                            z = ch["z"]
                            kb, kbb = ch["kb"][i], ch["kb_b"][i]
                            for q in range(2):
                                p_, pb_ = psa[z][q]
                                gs = slice(q * 8, (q + 1) * 8)
                                k.op(k.dve, lambda: nc.vector.tensor_tensor(out=ch["tF2"][:, gs, :], in0=p_[:].rearrange("p (g v) -> p g v", v=64),
                                                                            in1=kb[:, 2, gs, j:j + 1].to_broadcast([128, 8, 64]), op=ALU.mult),
                                     reads=[pb_, kbb], writes=[ch["tF2_b"]])
                        for (ch, i, j, t) in ent:
                            cur = ch["cur"]
                            Sn, Snb = ch["S"][1 - cur], ch["S_b"][1 - cur]
                            k.op(k.dve, lambda: nc.vector.tensor_tensor(out=Sn[:], in0=Sn[:], in1=ch["tF2"][:], op=ALU.subtract),
                                 reads=[Snb, ch["tF2_b"]], writes=[Snb])
                            ch["cur"] = 1 - cur
                        for (ch, i, j, t) in ent:
                            z = ch["z"]
                            kb, kbb = ch["kb"][i], ch["kb_b"][i]
                            Sc, Scb = ch["S"][ch["cur"]], ch["S_b"][ch["cur"]]
                            k.op(e_t4, lambda: veng(e_t4).tensor_tensor(out=ch["tA4"][:], in0=Sc[:], in1=kb[:, 4, :, j:j + 1].to_broadcast([128, KC, 64]),
                                                                        op=ALU.mult), reads=[Scb, kbb], writes=[ch["tA4_b"]])
                            t4f = ch["tA4"][:].rearrange("p g v -> p (g v)")
                            sidx = (jj % SB) if z == 0 else (SB - 1 - (jj % SB))
                            for q in range(2):
                                p_, pb_ = self.next_ps()
                                k.op(k.pe, lambda: nc.tensor.matmul(p_[0:2, :], hsel[:], t4f[:, q * 512:(q + 1) * 512], start=True, stop=True),
                                     reads=[cst_b, ch["tA4_b"]], writes=[pb_])
                                k.op(k.act, lambda: nc.scalar.copy(out=ch["ys"][:, sidx, q * 512:(q + 1) * 512], in_=p_[0:2, :]),
                                     reads=[pb_], writes=[ch["ys_b"]])
                            if jj % SB == SB - 1:
                                tlo = t - (SB - 1) if z == 0 else t
                                dst = YT[z][tlo:tlo + SB, :].rearrange("t (g hp v) -> hp t g v", hp=2, v=64)
                                k.dma(k.sp, dst, ch["ys"][:].rearrange("p s (g v) -> p s g v", v=64), reads=[ch["ys_b"]], appends=[yt_b])
                            at_end = (t % SEG == SEG - 1) if z == 0 else (t % SEG == 0)
                            if at_end:
                                k.dma(k.sp, self.O("rk_stS")[t // SEG, z], Sc[:].rearrange("p g v -> p (g v)"), reads=[Scb], appends=[outb])

                run_pass(0, 1024, True)
                run_pass(1024, 1280, False)
                run_pass(1280, 1536, False)
                self.phase_barrier()
            with contextlib.ExitStack() as es2:
                sbt = lambda name, shape, dt: es2.enter_context(_sbuf(nc, name, list(shape), dt))
                rln = sbt("rk_rln", [128, 2, D], F32)
                rln_b = Buf()
                k.dma(k.sp, rln[:], self.I("rln").rearrange("p (a d) -> p a d", a=2), writes=[rln_b])
                yf = [sbt(f"rkc_yf{i}", [128, D], BF16) for i in range(2)]
                yb = [sbt(f"rkc_yb{i}", [128, D], BF16) for i in range(2)]
                yfb = [Buf() for _ in range(2)]
                ybb = [Buf() for _ in range(2)]
                ysum = sbt("rkc_ysum", [128, 32, 64], F32)
                ysq = sbt("rkc_ysq", [128, 32, 64], F32)
                ysum_b, ysq_b = Buf(), Buf()
                st1 = sbt("rkc_st1", [128, 32], F32)
                st2 = sbt("rkc_st2", [128, 32], F32)
                st1_b, st2_b = Buf(), Buf()
                yT = sbt("rkc_yT", [128, KC, 128], F32)
                yT_b = Buf()
                gl = [sbt(f"rkc_gl{i}", [128, KC, 128], F32) for i in range(2)]
                vl = [sbt(f"rkc_vl{i}", [128, KC, 128], F32) for i in range(2)]
                glb = [Buf() for _ in range(2)]
                vlb = [Buf() for _ in range(2)]
                zT = sbt("rkc_zT", [128, KC, 128], BF16)
                zT_b = Buf()

                def loadc(blk):
                    i = blk % 2
                    sl = slice(blk * 128, (blk + 1) * 128)
                    k.dma(k.sp, yf[i][:], YT[0][sl, :], reads=[yt_b], writes=[yfb[i]])
                    k.dma(k.sp, yb[i][:], YT[1][sl, :], reads=[yt_b], writes=[ybb[i]])
                    k.dma(k.sp, gl[i][:], GG[:, :, sl], reads=[scr_b], writes=[glb[i]])
                    k.dma(k.sp, vl[i][:], VB[:, :, sl], reads=[scr_b], writes=[vlb[i]])

                loadc(0)
                NBk = NTOK // 128
                for blk in range(NBk):
                    i = blk % 2
                    if blk + 1 < NBk:
                        loadc(blk + 1)
                    ysf = ysum[:].rearrange("p h v -> p (h v)")
                    k.op(k.dve, lambda: nc.vector.tensor_tensor(out=ysf, in0=yf[i][:], in1=yb[i][:], op=ALU.add),
                         reads=[yfb[i], ybb[i]], writes=[ysum_b])
                    k.op(k.dve, lambda: nc.vector.tensor_reduce(out=st1[:], in_=ysum[:], axis=AX.X, op=ALU.add), reads=[ysum_b], writes=[st1_b])
                    k.op(k.dve, lambda: nc.vector.tensor_scalar(out=st1[:], in0=st1[:], scalar1=1.0 / 64, scalar2=None, op0=ALU.mult),
                         reads=[st1_b], writes=[st1_b])
                    k.op(k.dve, lambda: nc.vector.tensor_tensor(out=ysum[:], in0=ysum[:], in1=st1[:].unsqueeze(2).to_broadcast([128, 32, 64]),
                                                                op=ALU.subtract), reads=[ysum_b, st1_b], writes=[ysum_b])
                    k.op(k.act, lambda: nc.scalar.activation(out=ysq[:], in_=ysum[:], func=AF.Square), reads=[ysum_b], writes=[ysq_b])
                    k.op(k.dve, lambda: nc.vector.tensor_reduce(out=st2[:], in_=ysq[:], axis=AX.X, op=ALU.add), reads=[ysq_b], writes=[st2_b])
                    k.op(k.dve, lambda: nc.vector.tensor_scalar(out=st2[:], in0=st2[:], scalar1=1.0 / 64, scalar2=64e-5, op0=ALU.mult, op1=ALU.add),
                         reads=[st2_b], writes=[st2_b])
                    k.op(k.act, lambda: nc.scalar.activation(out=st2[:], in_=st2[:], func=AF.Sqrt), reads=[st2_b], writes=[st2_b])
                    k.op(k.dve, lambda: nc.vector.reciprocal(out=st2[:], in_=st2[:]), reads=[st2_b], writes=[st2_b])
                    k.op(k.dve, lambda: nc.vector.tensor_tensor(out=ysum[:], in0=ysum[:], in1=st2[:].unsqueeze(2).to_broadcast([128, 32, 64]),
                                                                op=ALU.mult), reads=[ysum_b, st2_b], writes=[ysum_b])
                    k.op(k.dve, lambda: nc.vector.tensor_tensor(out=ysf, in0=ysf, in1=rln[:, 0, :], op=ALU.mult), reads=[ysum_b, rln_b], writes=[ysum_b])
                    k.op(k.dve, lambda: nc.vector.tensor_tensor(out=ysf, in0=ysf, in1=rln[:, 1, :], op=ALU.add), reads=[ysum_b, rln_b], writes=[ysum_b])
                    for q4 in range(4):
                        ptr, ptrb = self.next_ps()
                        for c4 in range(4):
                            c = q4 * 4 + c4
                            k.op(k.pe, lambda: nc.tensor.transpose(ptr[:, c4 * 128:(c4 + 1) * 128], ysf[:, c * 128:(c + 1) * 128], ident),
                                 reads=[ysum_b, rpk_b], writes=[ptrb])
                        k.op(k.act, lambda: nc.scalar.copy(out=yT[:, q4 * 4:(q4 + 1) * 4, :], in_=ptr[:].rearrange("p (c t) -> p c t", t=128)),
                             reads=[ptrb], writes=[yT_b])
                    k.op(k.dve, lambda: nc.vector.tensor_tensor(out=yT[:], in0=yT[:], in1=vl[i][:], op=ALU.add), reads=[yT_b, vlb[i]], writes=[yT_b])
                    k.op(k.dve, lambda: nc.vector.tensor_tensor(out=zT[:], in0=yT[:], in1=gl[i][:], op=ALU.mult), reads=[yT_b, glb[i]], writes=[zT_b])
                    k.dma(k.sp, hsT_s[:, :, blk * 128:(blk + 1) * 128].rearrange("c p t -> p c t"), zT[:], reads=[zT_b], appends=[hsT_b])
                self.phase_barrier()
            with contextlib.ExitStack() as es2:
                sbt = lambda name, shape, dt: es2.enter_context(_sbuf(nc, name, list(shape), dt))
                h2 = sbt("rk_h2", [128, KC, NTOK], BF16)
                h2b = [Buf() for _ in range(NGRP)]
                for t in range(NGRP):
                    k.dma(k.sp, h2[:, :, t * TT:(t + 1) * TT], hsT_s[:, :, t * TT:(t + 1) * TT].rearrange("c p t -> p c t"),
                          reads=[hsT_b], writes=[h2b[t]])
                wo_v = self.I("rwkv_w_o")[slot].rearrange("(kc p) n -> p kc n", p=128)

                def ev_o(oc, t, ps, psb):
                    ts = slice(t * TT, (t + 1) * TT)
                    k.op(k.dve, lambda: nc.vector.scalar_tensor_tensor(
                        out=self.x_sb[:, oc, ts], in0=ps[:], scalar=self.modT[:, l, 5, oc, t:t + 1],
                        in1=self.x_sb[:, oc, ts], op0=ALU.mult, op1=ALU.add),
                        reads=[psb, self.modT_b, self.xb[oc][t]], writes=[self.xb[oc][t]])

                self.linear_fm(h2, h2b, wo_v, 0, D, ev_o, "ro")
                self.phase_barrier()

    def layer(self, l):
        cfg = self.cfg
        ph = cfg.get("phases", ("ffn1", "mix", "ffn2"))
        if "ffn1" in ph:
            self.ffn(l, 0)
        if "mix" in ph:
            if l % 3 == 0:
                self.attn(l)
            elif l % 3 == 1:
                self.mlstm(l)
            else:
                self.rwkv(l)
        if "ffn2" in ph:
            self.ffn(l, 1)

    def finish(self):
        k = self.k
        yv = self.O("yT").rearrange("(c p) t -> p c t", p=128)
        outb = Buf("out")
        for c in range(KC):
            k.dma(k.sp, yv[:, c, :], self.x_sb[:, c, :], reads=self.xb[c], appends=[outb])
        self.phase_barrier()


def build_program(cfg):
    p = Prog(cfg)
    nc = p.build()
    nc._prog = p
    return nc


def filter_maps(nc, maps):
    names = set(nc._prog.inputs.keys())
    return [{k_: v for k_, v in m.items() if k_ in names} for m in maps]


def rope_tables(core):
    cos = np.ones((128, NTOK), np.float32)
    sin = np.zeros((128, NTOK), np.float32)
    if core < 4:
        t = np.arange(1024)
        row = (t // 64).astype(np.float32)
        col = (t % 64).astype(np.float32)
        inv = (10000.0 ** (-np.arange(32, dtype=np.float32) / 32)).astype(np.float32)
        for d in range(128):
            pos = row if d < 64 else col
            ang = pos * inv[d % 32]
            cos[d, :1024] = np.cos(ang)
            sin[d, :1024] = np.sin(ang)
    return cos, sin


def perm_T():
    PT = np.zeros((128, 128), np.float32)
    for m in range(128):
        if (m % 64) < 32:
            PT[m + 32, m] = -1.0
        else:
            PT[m - 32, m] = 1.0
    return PT


def attn_masks(core):
    M = np.zeros((128, 14, 128), np.float32)
    iq = np.arange(128)[None, :]
    is_ = np.arange(128)[:, None]
    for qb in range(8):
        if qb >= 1:
            if core < 4:
                M[:, 2 * (qb - 1), :] = (iq <= is_)
            else:
                M[:, 2 * (qb - 1), :] = 1.0 if (qb // 2 == (qb - 1) // 2) else 0.0
        if qb <= 6:
            if core < 4:
                M[:, 2 * qb + 1, :] = (is_ <= iq)
            else:
                M[:, 2 * qb + 1, :] = 1.0 if (qb // 2 == (qb + 1) // 2) else 0.0
    return M.reshape(128, 14 * 128)


def attn_pack(core, inp):
    A = np.zeros((2, 128, NAP), np.float32)
    cos, sin = rope_tables(core)
    for slot in range(2):
        A[slot, :, 0] = inp["attn_q_norm"][slot]
        A[slot, :, 1] = inp["attn_k_norm"][slot]
        A[slot, :, 2:18] = inp["attn_sink"][slot][None, :]
        A[slot, :, 18] = 1.0 if core < 4 else 0.0
        A[slot, :, 19:147] = np.eye(128, dtype=np.float32)
        A[slot, :, 147:147 + NTOK] = cos
        A[slot, :, 147 + NTOK:] = sin
    return A


def mlstm_pack(core, inp):
    M = np.zeros((1, 128, NMP), np.float32)
    p = np.arange(128)[:, None]
    f = np.arange(128)[None, :]
    M[0, :, 0:128] = np.eye(128, dtype=np.float32)
    M[0, :, 128:256] = 1.0
    M[0, :, 256:384] = (p <= f)
    M[0, :, 384:512] = (p >= f)
    M[0, :, 512:640] = np.where(p <= f, 0.0, -1e30)
    M[0, :, 640:768] = np.where(p >= f, 0.0, -1e30)
    M[0, :, 768:800] = inp["mlstm_b_gate"][0][None, :]
    M[0, :, 800] = 1.0 if core < 4 else 0.0
    M[0, :, 801:801 + D] = inp["mlstm_out_norm"][0][None, :]
    return M


def mlstm_init(core, inp):
    C0 = np.zeros((2, 128, 8, 257), np.float32)
    m0 = np.zeros((2, 128, 8), np.float32)
    if core < 4:
        C = inp["state_mlstm_C"][core, 0]
        n = inp["state_mlstm_n"][core, 0]
        m = inp["state_mlstm_m"][core, 0]
        C0[:, :, :, :256] = np.transpose(C, (0, 2, 1, 3))
        C0[:, :, :, 256] = np.transpose(n, (0, 2, 1))
        m0[:] = m[:, None, :]
    return C0.reshape(2, 128, 8 * 257), m0


def rwkv_host(core, inp):
    segs = core_segments(core)
    P = np.zeros((128, NRP), np.float32)
    NFM = 13 * KC
    vecs = [inp["rwkv_mu"][0][i] for i in range(6)] + [inp["rwkv_w0"][0][0], inp["rwkv_w0"][0][1], inp["rwkv_a0"][0][0], inp["rwkv_a0"][0][1],
                                                       inp["rwkv_k_k"][0], inp["rwkv_k_a"][0], inp["rwkv_r_k"][0].reshape(-1)]
    P[:, 0:NFM] = fm(np.stack(vecs, 0)).reshape(128, NFM)
    I2 = np.zeros((128, 64), np.float32)
    I2[np.arange(128), np.arange(128) % 64] = 1.0
    P[:, NFM:NFM + 64] = I2
    P[:, NFM + 64] = 1.0 if core < 4 else 0.0
    P[:, NFM + 65:NFM + 193] = np.eye(128, dtype=np.float32)
    seq_id = []
    for g in range(NSEG):
        kind, idx = segs[g]
        seq_id += [(0 if kind == "lat" else 1, idx)] * SEG
    pm = np.zeros(NTOK, np.float32)
    nm = np.zeros(NTOK, np.float32)
    for t in range(NTOK):
        if t > 0 and seq_id[t - 1] == seq_id[t]:
            pm[t] = 1.0
        if t < NTOK - 1 and seq_id[t + 1] == seq_id[t]:
            nm[t] = 1.0
    rmask = np.ascontiguousarray(np.broadcast_to(np.concatenate([pm, nm])[None, :], (128, 2 * NTOK)))
    rln = np.ascontiguousarray(np.broadcast_to(np.concatenate([inp["rwkv_ln_g"][0], inp["rwkv_ln_b"][0]])[None, :], (128, 2 * D)))
    initS = np.zeros((2, 128, 1024), np.float32)
    if core < 4:
        S0 = inp["state_rwkv"][core, 0]
        S0 = S0.reshape(2, 16, 2, 64, 64)
        initS = np.ascontiguousarray(np.transpose(S0, (0, 2, 4, 1, 3))).reshape(2, 128, 1024)
    return P, rmask, rln, initS


def make_in_maps(inp, cfg, cores=None):
    maps = []
    layers = list(cfg["layers"])
    ph = cfg.get("phases", ("ffn1", "mix", "ffn2"))
    full = (layers == [0, 1, 2, 3])
    mod_w_sel = inp["mod_w"] if full else np.ascontiguousarray(inp["mod_w"][layers])
    has_ffn = ("ffn1" in ph or "ffn2" in ph)
    if has_ffn:
        ffn_in_sel = inp["ffn_w_in"] if full else np.ascontiguousarray(inp["ffn_w_in"][layers])
        ffn_out_sel = inp["ffn_w_out"] if full else np.ascontiguousarray(inp["ffn_w_out"][layers])
    for core in (range(NCORES) if cores is None else cores):
        segs = core_segments(core)
        rows = []
        for g in range(NSEG):
            kind, idx = segs[g]
            if kind == "lat":
                rows.append(inp["x_sample"][idx, g * SEG:(g + 1) * SEG])
            else:
                rows.append(inp["x_prompt"][idx])
        xs = np.concatenate(rows, 0)
        m = {
            "xT": np.ascontiguousarray(xs.T),
            "pack": host_pack(core, inp),
            "mod_w": mod_w_sel,
            "attn_w_qkv": inp["attn_w_qkv"],
            "attn_w_o": inp["attn_w_o"],
            "apack": attn_pack(core, inp),
            "amask": attn_masks(core),
            "permT": perm_T(),
            "cache_k": (inp["cache_k"][core].reshape(2, 256, 512) if core < 4 else np.zeros((2, 256, 512), np.float32)),
            "cache_v": (inp["cache_v"][core].reshape(2, 256, 512) if core < 4 else np.zeros((2, 256, 512), np.float32)),
        }
        m["rpack"], m["rmask"], m["rln"], m["rk_initS"] = rwkv_host(core, inp)
        for nm_ in ("rwkv_w_rkv", "rwkv_wA", "rwkv_wB", "rwkv_aA", "rwkv_aB", "rwkv_gA", "rwkv_gB", "rwkv_w_o"):
            m[nm_] = inp[nm_]
        m["mlstm_w_in"] = inp["mlstm_w_in"]
        m["mlstm_w_gate"] = inp["mlstm_w_gate"]
        m["mlstm_w_o"] = inp["mlstm_w_o"]
        m["mpack"] = mlstm_pack(core, inp)
        m["ml_initC"], m["ml_initm"] = mlstm_init(core, inp)
        if has_ffn:
            m["ffn_w_in"] = ffn_in_sel
            m["ffn_w_out"] = ffn_out_sel
        maps.append(m)
    return maps


FULL_CFG = {"layers": [0, 1, 2, 3], "phases": ("ffn1", "mix", "ffn2")}


def kernel(**inputs):
    inp = {k_: np.asarray(v) for k_, v in inputs.items()}
    cfg = FULL_CFG
    nc = build_program(cfg)
    maps = filter_maps(nc, make_in_maps(inp, cfg))
    res = run_bass_kernel_spmd(nc, maps, core_ids=list(range(NCORES)))
    R = res.results
    B, S_, DB, DS = 32, 256, 4, 1024
    y_prompt = np.zeros((B, S_, D), np.float32)
    y_sample = np.zeros((DB, DS, D), np.float32)
    new_k = np.zeros((B, 2, S_, 4, 128), np.float32)
    new_v = np.zeros((B, 2, S_, 4, 128), np.float32)
    new_C = np.zeros((B, 1, 2, 8, 128, 256), np.float32)
    new_n = np.zeros((B, 1, 2, 8, 128), np.float32)
    new_m = np.zeros((B, 1, 2, 8), np.float32)
    new_S = np.zeros((B, 1, 2, 32, 64, 64), np.float32)
    for core in range(NCORES):
        r = R[core]
        y = np.asarray(r["yT"]).T
        segs = core_segments(core)
        for g in range(NSEG):
            kind, idx = segs[g]
            rows = y[g * SEG:(g + 1) * SEG]
            if kind == "lat":
                y_sample[idx, g * SEG:(g + 1) * SEG] = rows
            else:
                y_prompt[idx] = rows
                for slot in range(2):
                    new_k[idx, slot] = np.asarray(r["newk"])[slot, g * SEG:(g + 1) * SEG].reshape(S_, 4, 128)
                    new_v[idx, slot] = np.asarray(r["newv"])[slot, g * SEG:(g + 1) * SEG].reshape(S_, 4, 128)
                Ck = np.asarray(r["ml_stC"])[g].reshape(2, 128, 8, 257)
                new_C[idx, 0] = np.transpose(Ck[..., :256], (0, 2, 1, 3))
                new_n[idx, 0] = np.transpose(Ck[..., 256], (0, 2, 1))
                new_m[idx, 0] = np.asarray(r["ml_stm"])[g][:, 0, :]
                Sk = np.asarray(r["rk_stS"])[g].reshape(2, 2, 64, 16, 64)
                new_S[idx, 0] = np.transpose(Sk, (0, 3, 1, 4, 2)).reshape(2, 32, 64, 64)
    return (y_prompt, y_sample, new_k, new_v, new_C, new_n, new_m, new_S)
```

```python
import contextlib
import numpy as np
import concourse.bass as bass
import concourse.mybir as mybir
from concourse.bass_utils import run_bass_kernel_spmd

F32 = mybir.dt.float32
BF16 = mybir.dt.bfloat16
AF = mybir.ActivationFunctionType
ALU = mybir.AluOpType
AX = mybir.AxisListType

D = 2048
KC = 16
NTOK = 1536
NSEG = 6
SEG = 256
NGRP = 3
TT = 512
DEPTH = 4
DFF = 5632
NMOD = 9
EPS = 1e-6
NCORES = 8
NAP = 147 + 2 * NTOK
NMP = 801 + D
NRP = 13 * KC + 64 + 1 + 128


_UNIQ = [0]


def _sbuf(nc, name, shape, dt):
    _UNIQ[0] += 1
    return nc.sbuf_tensor(f"{name}_u{_UNIQ[0]}", shape, dt)


class Buf:
    __slots__ = ("w", "r", "name", "excl", "mw")

    def __init__(self, name="", excl=False):
        self.w = None
        self.r = {}
        self.mw = {}
        self.name = name
        self.excl = excl


class Eng:
    def __init__(self, k, name, handle, is_pe=False):
        self.k = k
        self.name = name
        self.h = handle
        self.is_pe = is_pe
        self.sem = None
        self.count = 0
        self.waited = {}
        self.nsem = 0

    def cur_sem(self):
        if self.sem is None or self.count >= 30000:
            self.sem = self.k.new_sem(f"{self.name}{self.nsem}")
            self.nsem += 1
            self.count = 0
        return self.sem


class K:
    def __init__(self, nc, es):
        self.nc = nc
        self.es = es
        self.semcount = 0
        self.pe = Eng(self, "pe", nc.tensor, is_pe=True)
        self.act = Eng(self, "act", nc.scalar)
        self.dve = Eng(self, "dve", nc.vector)
        self.pool = Eng(self, "pool", nc.gpsimd)
        self.sp = Eng(self, "sp", nc.sync)
        self.dma_sems = []
        self.dma_rr = 0
        self.sync_same_engine = True
        self.nosync_engines = set()
        self.n_inst = 0

    def new_sem(self, name):
        self.semcount += 1
        return self.es.enter_context(self.nc.semaphore(f"s_{name}_{self.semcount}"))

    def sb(self, name, shape, dt):
        return self.es.enter_context(_sbuf(self.nc, name, list(shape), dt))

    def _wait_deps(self, eng, reads, writes, appends=()):
        deps = {}
        for b in appends:
            if b.w is not None:
                s, v = b.w
                if deps.get(s, 0) < v:
                    deps[s] = v
            for s, v in b.r.items():
                if deps.get(s, 0) < v:
                    deps[s] = v
        for b in reads:
            if b.w is not None:
                s, v = b.w
                if deps.get(s, 0) < v:
                    deps[s] = v
            for s, v in b.mw.items():
                if deps.get(s, 0) < v:
                    deps[s] = v
            if b.excl:
                for s, v in b.r.items():
                    if deps.get(s, 0) < v:
                        deps[s] = v
        for b in writes:
            if b.w is not None:
                s, v = b.w
                if deps.get(s, 0) < v:
                    deps[s] = v
            for s, v in b.r.items():
                if deps.get(s, 0) < v:
                    deps[s] = v
            for s, v in b.mw.items():
                if deps.get(s, 0) < v:
                    deps[s] = v
        for s, v in deps.items():
            if eng.is_pe and s is eng.sem:
                continue
            if (not self.sync_same_engine) and s is eng.sem:
                continue
            if s is eng.sem and eng.name in self.nosync_engines:
                continue
            if eng.waited.get(s, 0) < v:
                eng.h.wait_ge(s, v)
                eng.waited[s] = v

    def _mark(self, tok, reads, writes, appends=()):
        s, v = tok
        for b in reads:
            if b.r.get(s, 0) < v:
                b.r[s] = v
        for b in writes:
            b.w = tok
            b.r = {}
            b.mw = {}
        for b in appends:
            if b.mw.get(s, 0) < v:
                b.mw[s] = v

    def op(self, eng, fn, reads=(), writes=(), appends=()):
        self._wait_deps(eng, reads, writes, appends)
        sem = eng.cur_sem()
        ins = fn()
        ins.then_inc(sem, 1)
        eng.count += 1
        self.n_inst += 1
        self._mark((sem, eng.count), reads, writes, appends)

    def dma(self, eng, out, in_, reads=(), writes=(), appends=(), **kw):
        if len(self.dma_sems) < 24:
            self.dma_sems.append([self.new_sem(f"dma{len(self.dma_sems)}"), 0])
            ent = self.dma_sems[-1]
        else:
            ent = self.dma_sems[self.dma_rr % len(self.dma_sems)]
            self.dma_rr += 1
        sem, cnt = ent
        if cnt >= 1800:
            ent[0] = sem = self.new_sem("dmax")
            ent[1] = cnt = 0
        self._wait_deps(eng, reads, writes, appends)
        if cnt > 0 and eng.waited.get(sem, 0) < cnt * 16:
            eng.h.wait_ge(sem, cnt * 16)
            eng.waited[sem] = cnt * 16
        eng.h.dma_start(out=out, in_=in_, **kw).then_inc(sem, 16)
        ent[1] = cnt + 1
        self.n_inst += 1
        self._mark((sem, (cnt + 1) * 16), reads, writes, appends)

    def wait_all(self, eng, bufs):
        self._wait_deps(eng, bufs, ())


def pack_layout():
    lay = {}
    off = 0

    def add(name, width):
        nonlocal off
        lay[name] = (off, width)
        off += width

    add("cond", KC * NGRP)
    add("modb", DEPTH * NMOD * KC)
    add("normg", DEPTH * 3 * KC)
    lay["_total"] = off
    return lay


def core_segments(core):
    if core < 4:
        return [("lat", core)] * 4 + [("ctx", 2 * core), ("ctx", 2 * core + 1)]
    base = 8 + (core - 4) * 6
    return [("ctx", base + j) for j in range(6)]


def fm(vec):
    v = np.asarray(vec, np.float32)
    lead = v.shape[:-1]
    v = v.reshape(lead + (KC, 128))
    v = np.moveaxis(v, -1, 0)
    return np.ascontiguousarray(v)


def host_pack(core, inp):
    lay = pack_layout()
    P = np.zeros((128, lay["_total"]), np.float32)

    def put(name, arr):
        o, w = lay[name]
        a = np.asarray(arr, np.float32).reshape(128, -1)
        assert a.shape[1] == w, (name, a.shape, w)
        P[:, o:o + w] = a

    segs = core_segments(core)
    conds = []
    for g in range(NGRP):
        kind, idx = segs[2 * g]
        conds.append(inp["c"][idx] if kind == "lat" else inp["c_ctx"])
    cond = fm(np.stack(conds, 0))
    put("cond", np.transpose(cond, (0, 2, 1)))
    put("modb", fm(inp["mod_b"].reshape(DEPTH, NMOD, D)))
    put("normg", fm(inp["norm_g"]))
    return P


class Prog:
    def __init__(self, cfg):
        self.cfg = cfg
        self.lay = pack_layout()

    def build(self):
        cfg = self.cfg
        nc = bass.Bass("TRN2", target_bir_lowering=False)
        self.nc = nc
        es = contextlib.ExitStack()
        with es:
            k = K(nc, es)
            self.k = k
            k.nosync_engines = set(cfg.get("nosync", ()))
            self.declare_io()
            self.alloc_common()
            self.load_common()
            self.adaln_all()
            for l in cfg["layers"]:
                self.layer(l)
            self.finish()
        return nc

    def declare_io(self):
        nc = self.nc
        NL = len(self.cfg["layers"])
        self.lidx = {l: i for i, l in enumerate(self.cfg["layers"])}
        self.in_shapes = {
            "xT": [D, NTOK], "pack": [128, self.lay["_total"]], "mod_w": [NL, D, NMOD * D],
            "ffn_w_in": [NL, 2, D, 2 * DFF], "ffn_w_out": [NL, 2, DFF, D],
            "attn_w_qkv": [2, D, 3072], "attn_w_o": [2, D, D], "apack": [2, 128, NAP],
            "mlstm_w_in": [1, D, 6144], "mlstm_w_gate": [1, D, 32], "mlstm_w_o": [1, D, D], "mpack": [1, 128, NMP],
            "ml_initC": [2, 128, 8 * 257], "ml_initm": [2, 128, 8],
            "rpack": [128, NRP], "rmask": [128, 2 * NTOK], "rln": [128, 2 * D], "rk_initS": [2, 128, 1024], "rk_hselT": [2, 128],
            "rwkv_w_rkv": [1, 3, D, D], "rwkv_wA": [1, 2, D, 96], "rwkv_wB": [1, 2, 96, D], "rwkv_aA": [1, 2, D, 96],
            "rwkv_aB": [1, 2, 96, D], "rwkv_gA": [1, D, 256], "rwkv_gB": [1, 256, D], "rwkv_w_o": [1, D, D],
            "amask": [128, 14 * 128], "permT": [128, 128], "cache_k": [2, 256, 512], "cache_v": [2, 256, 512],
        }
        self.out_shapes = {"rk_stS": [6, 2, 128, 1024], "ml_stC": [6, 2, 128, 8 * 257], "ml_stm": [6, 2, 128, 8], "yT": [D, NTOK], "newk": [2, NTOK, 512], "newv": [2, NTOK, 512]}
        self.inputs = {}
        self.outputs = {}

    def I(self, name):
        if name not in self.inputs:
            self.inputs[name] = self.nc.dram_tensor(name, list(self.in_shapes[name]), F32, kind="ExternalInput").ap()
        return self.inputs[name]

    def dbg_out(self, name, shape):
        if name not in self.outputs:
            self.outputs[name] = self.nc.dram_tensor(name, list(shape), BF16, kind="ExternalOutput").ap()
        return self.outputs[name]

    def O(self, name):
        if name not in self.outputs:
            self.outputs[name] = self.nc.dram_tensor(name, list(self.out_shapes[name]), F32, kind="ExternalOutput").ap()
        return self.outputs[name]

    def alloc_common(self):
        k = self.k
        self.x_sb = k.sb("x_sb", [128, KC, NTOK], F32)
        self.xb = [[Buf(f"x{c}_{t}") for t in range(NGRP)] for c in range(KC)]
        self.pack = None
        self.pack_b = Buf("pack")
        self.modT = k.sb("modT", [128, DEPTH, NMOD, KC, NGRP], F32)
        self.modT_b = Buf("modT")
        self.ones_bf = k.sb("ones_bf", [128, 128], BF16)
        self.ones_b = Buf("ones")
        self.ps = [self.k.es.enter_context(self.nc.psum_tensor(f"ps{i}", [128, 512], F32)) for i in range(8)]
        self.psb = [Buf(f"ps{i}", excl=True) for i in range(8)]
        self.ps_rr = 0
        self.ps_reserved = set()

    def next_ps(self):
        while True:
            i = self.ps_rr % 8
            self.ps_rr += 1
            if i not in self.ps_reserved:
                return self.ps[i], self.psb[i]

    def reserve_ps(self):
        pt, pb = self.next_ps()
        i = self.ps.index(pt)
        self.ps_reserved.add(i)
        return pt, pb

    def release_ps(self, pt):
        self.ps_reserved.discard(self.ps.index(pt))

    def pk(self, name):
        o, w = self.lay[name]
        return self.pack[:, o:o + w]

    def load_common(self):
        k = self.k
        nc = self.nc
        xv = self.I("xT").rearrange("(c p) t -> p c t", p=128)
        for c in range(KC):
            k.dma(k.sp, self.x_sb[:, c, :], xv[:, c, :], writes=self.xb[c])
        k.op(k.dve, lambda: nc.vector.memset(self.ones_bf[:], 1.0), writes=[self.ones_b])

    def adaln_all(self):
        k = self.k
        nc = self.nc
        cfg = self.cfg
        with contextlib.ExitStack() as es2:
            self.pack = es2.enter_context(_sbuf(nc, "pack_sb", [128, self.lay["_total"]], F32))
            k.dma(k.sp, self.pack[:], self.I("pack")[:, :], writes=[self.pack_b])
            sc = es2.enter_context(_sbuf(nc, "ada_sc", [128, KC, NGRP], BF16))
            sc_b = Buf("sc")
            wts = [es2.enter_context(_sbuf(nc, f"ada_w{i}", [128, KC, 512], BF16)) for i in range(3)]
            wts_b = [Buf(f"adaw{i}") for i in range(3)]
            condv = self.pk("cond").rearrange("p (c g) -> p c g", g=NGRP)
            k.op(k.act, lambda: nc.scalar.activation(out=sc[:], in_=condv, func=AF.Silu),
                 reads=[self.pack_b], writes=[sc_b])
            modb = self.pk("modb").rearrange("p (l j c) -> p l j c", l=DEPTH, j=NMOD)
            NCT = NMOD * D // 512
            jobs = [(l, ct) for l in cfg["layers"] for ct in range(NCT)]

            def load(i):
                l, ct = jobs[i]
                src = self.I("mod_w")[self.lidx[l]].rearrange("(kc p) n -> p kc n", p=128)[:, :, ct * 512:(ct + 1) * 512]
                k.dma(k.pool, wts[i % 3][:], src, writes=[wts_b[i % 3]])

            for i in range(min(2, len(jobs))):
                load(i)
            for i, (l, ct) in enumerate(jobs):
                if i + 2 < len(jobs):
                    load(i + 2)
                w = wts[i % 3]
                wb = wts_b[i % 3]
                pst, psb = self.next_ps()
                for q in range(4):
                    for kc in range(KC):
                        k.op(k.pe, lambda q=q, kc=kc: nc.tensor.matmul(
                            pst[:, q * NGRP:(q + 1) * NGRP], w[:, kc, q * 128:(q + 1) * 128], sc[:, kc, :],
                            start=(kc == 0), stop=(kc == KC - 1)),
                            reads=[wb, sc_b], writes=[psb])
                j = (ct * 4) // KC
                c0 = (ct * 4) % KC
                k.op(k.dve, lambda l=l, j=j, c0=c0: nc.vector.tensor_tensor(
                    out=self.modT[:, l, j, c0:c0 + 4, :],
                    in0=pst[:, 0:4 * NGRP].rearrange("p (q g) -> p q g", g=NGRP),
                    in1=modb[:, l, j, c0:c0 + 4].unsqueeze(2).to_broadcast([128, 4, NGRP]),
                    op=ALU.add),
                    reads=[psb, self.pack_b], writes=[self.modT_b])
            ng = self.pk("normg").rearrange("p (l s c) -> p l s c", l=DEPTH, s=3)
            for l in cfg["layers"]:
                for s in range(3):
                    k.op(k.dve, lambda l=l, s=s: nc.vector.scalar_tensor_tensor(
                        out=self.modT[:, l, 3 * s + 1, :, :], in0=self.modT[:, l, 3 * s + 1, :, :], scalar=1.0,
                        in1=ng[:, l, s, :].unsqueeze(2).to_broadcast([128, KC, NGRP]),
                        op0=ALU.add, op1=ALU.mult),
                        reads=[self.pack_b, self.modT_b], writes=[self.modT_b])
                    if s != 1:
                        k.op(k.dve, lambda l=l, s=s: nc.vector.tensor_scalar(
                            out=self.modT[:, l, 3 * s + 2, :, :], in0=self.modT[:, l, 3 * s + 2, :, :],
                            scalar1=0.5, scalar2=None, op0=ALU.mult),
                            reads=[self.modT_b], writes=[self.modT_b])
            self.phase_barrier()

    def compute_h(self, l, s, h, hb, sq, sqb):
        k = self.k
        nc = self.nc
        for t in range(NGRP):
            ts = slice(t * TT, (t + 1) * TT)
            pst, psb = self.next_ps()
            for c in range(KC):
                i = c % 2
                k.op(k.act, lambda c=c, i=i: nc.scalar.activation(out=sq[i], in_=self.x_sb[:, c, ts], func=AF.Square),
                     reads=[self.xb[c][t]], writes=[sqb[i]])
                k.op(k.pe, lambda c=c, i=i: nc.tensor.matmul(pst[:], self.ones_bf[:], sq[i],
                                                             start=(c == 0), stop=(c == KC - 1)),
                     reads=[sqb[i], self.ones_b], writes=[psb])
            k.op(k.dve, lambda: nc.vector.tensor_scalar(out=pst[:], in0=pst[:], scalar1=1.0 / D, scalar2=EPS,
                                                        op0=ALU.mult, op1=ALU.add),
                 reads=[psb], writes=[psb])
            k.op(k.act, lambda: nc.scalar.activation(out=pst[:], in_=pst[:], func=AF.Sqrt), reads=[psb], writes=[psb])
            k.op(k.dve, lambda: nc.vector.reciprocal(out=pst[:], in_=pst[:]), reads=[psb], writes=[psb])
            for c in range(KC):
                pt, ptb = self.next_ps()
                if pt is pst:
                    pt, ptb = self.next_ps()
                k.op(k.dve, lambda c=c, pt=pt: nc.vector.tensor_tensor(out=pt[:], in0=self.x_sb[:, c, ts], in1=pst[:],
                                                                       op=ALU.mult),
                     reads=[self.xb[c][t], psb], writes=[ptb])
                k.op(k.act, lambda c=c, pt=pt: nc.scalar.activation(
                    out=h[:, c, ts], in_=pt[:], func=AF.Identity,
                    scale=self.modT[:, l, 3 * s + 1, c, t:t + 1], bias=self.modT[:, l, 3 * s, c, t:t + 1]),
                    reads=[ptb, self.modT_b], writes=[hb[t]])

    def ffn(self, l, j):
        k = self.k
        nc = self.nc
        s = 0 if j == 0 else 2
        FG = 256
        NFG = DFF // FG
        with contextlib.ExitStack() as es2:
            sbt = lambda name, shape, dt: es2.enter_context(_sbuf(nc, name, list(shape), dt))
            h = sbt("ffn_h", [128, KC, NTOK], BF16)
            hb = [Buf(f"h{t}") for t in range(NGRP)]
            tmpn = [sbt(f"ffn_tmpn{i}", [128, TT], BF16) for i in range(2)]
            tmpnb = [Buf() for _ in range(2)]
            wg = [sbt(f"ffn_wg{i}", [128, KC, FG], BF16) for i in range(2)]
            wu = [sbt(f"ffn_wu{i}", [128, KC, FG], BF16) for i in range(2)]
            wo = [sbt(f"ffn_wo{i}", [128, FG // 128, D], BF16) for i in range(2)]
            wgb = [Buf() for _ in range(2)]
            wub = [Buf() for _ in range(2)]
            wob = [Buf() for _ in range(2)]
            gsb = [sbt(f"ffn_g{i}", [128, FG // 128, TT], BF16) for i in range(2)]
            gsbb = [Buf() for _ in range(2)]
            sq = [gsb[i][:, 0, :] for i in range(2)]
            sqb = gsbb
            self.phase_barrier()

            w_in_v = self.I("ffn_w_in")[self.lidx[l], j].rearrange("(kc p) n -> p kc n", p=128)
            w_out_v = self.I("ffn_w_out")[self.lidx[l], j].rearrange("(fc p) n -> p fc n", p=128)

            def load(fg):
                i = fg % 2
                k.dma(k.pool, wg[i][:], w_in_v[:, :, fg * FG:(fg + 1) * FG], writes=[wgb[i]])
                k.dma(k.pool, wu[i][:], w_in_v[:, :, DFF + fg * FG:DFF + (fg + 1) * FG], writes=[wub[i]])
                k.dma(k.pool, wo[i][:], w_out_v[:, fg * (FG // 128):(fg + 1) * (FG // 128), :], writes=[wob[i]])

            load(0)
            self.compute_h(l, s, h, hb, sq, sqb)
            it = 0
            for fg in range(NFG):
                if fg + 1 < NFG:
                    load(fg + 1)
                i = fg % 2
                for t in range(NGRP):
                    ts = slice(t * TT, (t + 1) * TT)
                    gi = it % 2
                    it += 1
                    for hf in range(FG // 128):
                        pg, pgb = self.next_ps()
                        pu, pub = self.next_ps()
                        for kc in range(KC):
                            k.op(k.pe, lambda kc=kc, hf=hf, pg=pg: nc.tensor.matmul(
                                pg[:], wg[i][:, kc, hf * 128:(hf + 1) * 128], h[:, kc, ts],
                                start=(kc == 0), stop=(kc == KC - 1)),
                                reads=[wgb[i], hb[t]], writes=[pgb])
                        for kc in range(KC):
                            k.op(k.pe, lambda kc=kc, hf=hf, pu=pu: nc.tensor.matmul(
                                pu[:], wu[i][:, kc, hf * 128:(hf + 1) * 128], h[:, kc, ts],
                                start=(kc == 0), stop=(kc == KC - 1)),
                                reads=[wub[i], hb[t]], writes=[pub])
                        si = hf % 2
                        k.op(k.act, lambda pg=pg, si=si: nc.scalar.activation(out=tmpn[si][:], in_=pg[:], func=AF.Silu),
                             reads=[pgb], writes=[tmpnb[si]])
                        k.op(k.dve, lambda pu=pu, si=si, hf=hf, gi=gi: nc.vector.tensor_tensor(
                            out=gsb[gi][:, hf, :], in0=tmpn[si][:], in1=pu[:], op=ALU.mult),
                            reads=[tmpnb[si], pub], writes=[gsbb[gi]])
                    for dc in range(KC):
                        py, pyb = self.next_ps()
                        for hf in range(FG // 128):
                            k.op(k.pe, lambda hf=hf, dc=dc, py=py, gi=gi: nc.tensor.matmul(
                                py[:], wo[i][:, hf, dc * 128:(dc + 1) * 128], gsb[gi][:, hf, :],
                                start=(hf == 0), stop=(hf == FG // 128 - 1)),
                                reads=[wob[i], gsbb[gi]], writes=[pyb])
                        k.op(k.dve, lambda dc=dc, py=py, t=t, ts=ts: nc.vector.scalar_tensor_tensor(
                            out=self.x_sb[:, dc, ts], in0=py[:], scalar=self.modT[:, l, 3 * s + 2, dc, t:t + 1],
                            in1=self.x_sb[:, dc, ts], op0=ALU.mult, op1=ALU.add),
                            reads=[pyb, self.modT_b, self.xb[dc][t]], writes=[self.xb[dc][t]])
            self.phase_barrier()

    def phase_barrier(self):
        k = self.k
        toks = []
        for e in (k.pe, k.act, k.dve, k.pool, k.sp):
            if e.sem is not None and e.count > 0:
                toks.append((e.sem, e.count))
        for ent in k.dma_sems:
            if ent[1] > 0:
                toks.append((ent[0], ent[1] * 16))
        for e in (k.pe, k.act, k.dve, k.pool, k.sp):
            for s, v in toks:
                if s is e.sem:
                    continue
                if e.waited.get(s, 0) < v:
                    e.h.wait_ge(s, v)
                    e.waited[s] = v


    def linear_fm(self, h, hb, wview, col0, ncols, evac, tag):
        k, nc = self.k, self.nc
        TC = 256
        with contextlib.ExitStack() as es2:
            wt = [es2.enter_context(_sbuf(nc, f"lf_{tag}_w{i}", [128, KC, TC], BF16)) for i in range(2)]
            wtb = [Buf() for _ in range(2)]
            ntile = ncols // TC

            def load(ti):
                k.dma(k.pool, wt[ti % 2][:], wview[:, :, col0 + ti * TC:col0 + (ti + 1) * TC], writes=[wtb[ti % 2]])

            load(0)
            for ti in range(ntile):
                if ti + 1 < ntile:
                    load(ti + 1)
                w, wb = wt[ti % 2], wtb[ti % 2]
                for sub in range(TC // 128):
                    oc = ti * (TC // 128) + sub
                    for t in range(NGRP):
                        ts = slice(t * TT, (t + 1) * TT)
                        ps, psb = self.next_ps()
                        for kc in range(KC):
                            k.op(k.pe, lambda kc=kc: nc.tensor.matmul(ps[:], w[:, kc, sub * 128:(sub + 1) * 128], h[:, kc, ts],
                                                                      start=(kc == 0), stop=(kc == KC - 1)),
                                 reads=[wb, hb[t]], writes=[psb])
                        evac(oc, t, ps, psb)
            self.scope_end(wtb)

    def linear_tm(self, h, hb, wview, col0, ncols, evac, tag):
        k, nc = self.k, self.nc
        TC = 512
        with contextlib.ExitStack() as es2:
            wt = [es2.enter_context(_sbuf(nc, f"lt_{tag}_w{i}", [128, KC, TC], BF16)) for i in range(2)]
            wtb = [Buf() for _ in range(2)]
            ntile = ncols // TC

            def load(ti):
                k.dma(k.pool, wt[ti % 2][:], wview[:, :, col0 + ti * TC:col0 + (ti + 1) * TC], writes=[wtb[ti % 2]])

            load(0)
            for ti in range(ntile):
                if ti + 1 < ntile:
                    load(ti + 1)
                w, wb = wt[ti % 2], wtb[ti % 2]
                for blk in range(NTOK // 128):
                    ps, psb = self.next_ps()
                    for kc in range(KC):
                        k.op(k.pe, lambda kc=kc: nc.tensor.matmul(ps[:], h[:, kc, blk * 128:(blk + 1) * 128], w[:, kc, :],
                                                                  start=(kc == 0), stop=(kc == KC - 1)),
                             reads=[wb, hb[blk // 4]], writes=[psb])
                    evac(blk, ti, ps, psb)
            self.scope_end(wtb)

    def scope_end(self, bufs):
        k = self.k
        for e in (k.pe, k.act, k.dve, k.pool, k.sp):
            k._wait_deps(e, (), bufs)

    def attn(self, l):
        k = self.k
        nc = self.nc
        slot = l // 3
        NH, NKV = 16, 4
        qs = nc.dram_tensor(f"qs{l}", [NH, 128, NTOK], BF16).ap()
        ks = nc.dram_tensor(f"ks{l}", [NKV, 128, NTOK], BF16).ap()
        vs = nc.dram_tensor(f"vs{l}", [12, 128, 512], BF16).ap()
        qs_b, ks_b, vs_b = Buf("qs"), Buf("ks"), Buf("vs")
        outb = Buf("attn_out")
        with contextlib.ExitStack() as es1:
            sbt1 = lambda name, shape, dt: es1.enter_context(_sbuf(nc, name, list(shape), dt))
            self.phase_barrier()
            apk = sbt1("apack_sb", [128, NAP], F32)
            apk_b = Buf("apk")
            k.dma(k.sp, apk[:], self.I("apack")[slot], writes=[apk_b])
            qng = apk[:, 0:1]
            kng = apk[:, 1:2]
            sink = apk[:, 2:18]
            cflag = apk[:, 18:19]
            ident = apk[:, 19:147]
            cos = apk[:, 147:147 + NTOK]
            sin = apk[:, 147 + NTOK:147 + 2 * NTOK]
            with contextlib.ExitStack() as es2:
                sbt = lambda name, shape, dt: es2.enter_context(_sbuf(nc, name, list(shape), dt))
                h = sbt("at_h", [128, KC, NTOK], BF16)
                hb = [Buf(f"h{t}") for t in range(NGRP)]
                sq = [sbt(f"at_sq{i}", [128, TT], BF16) for i in range(2)]
                sqb = [Buf() for _ in range(2)]
                PT = sbt("at_PT", [128, 128], BF16)
                PT_b = Buf()
                k.dma(k.pool, PT[:], self.I("permT")[:, :], writes=[PT_b])
                wq = [sbt(f"at_wq{i}", [128, KC, 256], BF16) for i in range(2)]
                wqb = [Buf() for _ in range(2)]
                rawg = [sbt(f"at_rawg{i}", [128, TT], F32) for i in range(2)]
                rawgb = [Buf() for _ in range(2)]
                qnf = [sbt(f"at_qnf{i}", [128, TT], F32) for i in range(2)]
                qnfb = [Buf() for _ in range(2)]
                qnb = [sbt(f"at_qnb{i}", [128, TT], BF16) for i in range(2)]
                qnbb = [Buf() for _ in range(2)]
                t1 = [sbt(f"at_t1{i}", [128, TT], F32) for i in range(2)]
                t1b = [Buf() for _ in range(2)]
                qrb = [sbt(f"at_qrb{i}", [128, TT], BF16) for i in range(2)]
                qrbb = [Buf() for _ in range(2)]
                kout = [sbt(f"at_kout{i}", [128, 4, 128], F32) for i in range(2)]
                koutb = [Buf() for _ in range(2)]
                vf = [sbt(f"at_vf{i}", [128, 256], F32) for i in range(2)]
                vfb = [Buf() for _ in range(2)]
                vb = [sbt(f"at_vb{i}", [128, 256], BF16) for i in range(2)]
                vbb = [Buf() for _ in range(2)]
                wv_ = self.I("attn_w_qkv")[slot].rearrange("(kc p) n -> p kc n", p=128)

                def load(ct):
                    k.dma(k.pool, wq[ct % 2][:], wv_[:, :, ct * 256:(ct + 1) * 256], writes=[wqb[ct % 2]])

                load(0)
                self.compute_h(l, 1, h, hb, [t_[:] for t_ in sq], sqb)
                it = 0
                ct_list = self.cfg.get("ct_list", list(range(12)))
                for ct in ct_list:
                    if ct + 1 < 12 and (ct + 1) in ct_list:
                        load(ct + 1)
                    w = wq[ct % 2]
                    wb = wqb[ct % 2]
                    if ct < 10:
                        for hh in range(2):
                            head = ct * 2 + hh
                            is_k = head >= 16
                            gain = kng if is_k else qng
                            for t in range(NGRP):
                                ts = slice(t * TT, (t + 1) * TT)
                                i = it % 2
                                it += 1
                                praw, prawb = self.next_ps()
                                for kc in range(KC):
                                    k.op(k.pe, lambda kc=kc: nc.tensor.matmul(
                                        praw[:], w[:, kc, hh * 128:(hh + 1) * 128], h[:, kc, ts],
                                        start=(kc == 0), stop=(kc == KC - 1)), reads=[wb, hb[t]], writes=[prawb])
                                k.op(k.act, lambda: nc.scalar.activation(out=sq[i][:], in_=praw[:], func=AF.Square),
                                     reads=[prawb], writes=[sqb[i]])
                                k.op(k.act, lambda: nc.scalar.activation(out=rawg[i][:], in_=praw[:], func=AF.Identity, scale=gain),
                                     reads=[prawb, apk_b], writes=[rawgb[i]])
                                pss, pssb = self.next_ps()
                                k.op(k.pe, lambda: nc.tensor.matmul(pss[:], self.ones_bf[:], sq[i][:], start=True, stop=True),
                                     reads=[sqb[i], self.ones_b], writes=[pssb])
                                k.op(k.dve, lambda: nc.vector.tensor_scalar(out=pss[:], in0=pss[:], scalar1=1.0 / 128, scalar2=EPS,
                                                                            op0=ALU.mult, op1=ALU.add), reads=[pssb], writes=[pssb])
                                k.op(k.act, lambda: nc.scalar.activation(out=pss[:], in_=pss[:], func=AF.Sqrt), reads=[pssb], writes=[pssb])
                                k.op(k.dve, lambda: nc.vector.reciprocal(out=pss[:], in_=pss[:]), reads=[pssb], writes=[pssb])
                                k.op(k.dve, lambda: nc.vector.tensor_tensor(out=qnf[i][:], in0=rawg[i][:], in1=pss[:], op=ALU.mult),
                                     reads=[rawgb[i], pssb], writes=[qnfb[i]])
                                k.op(k.act, lambda: nc.scalar.copy(out=qnb[i][:], in_=qnf[i][:]), reads=[qnfb[i]], writes=[qnbb[i]])
                                pp, ppb = self.next_ps()
                                k.op(k.pe, lambda: nc.tensor.matmul(pp[:], PT[:], qnb[i][:], start=True, stop=True),
                                     reads=[PT_b, qnbb[i]], writes=[ppb])
                                k.op(k.dve, lambda: nc.vector.tensor_tensor(out=t1[i][:], in0=qnf[i][:], in1=cos[:, ts], op=ALU.mult),
                                     reads=[qnfb[i], apk_b], writes=[t1b[i]])
                                k.op(k.dve, lambda: nc.vector.tensor_tensor(out=qnf[i][:], in0=pp[:], in1=sin[:, ts], op=ALU.mult),
                                     reads=[ppb, apk_b, qnbb[i]], writes=[qnfb[i]])
                                if not is_k:
                                    k.op(k.dve, lambda: nc.vector.tensor_tensor(out=qrb[i][:], in0=t1[i][:], in1=qnf[i][:], op=ALU.add),
                                         reads=[t1b[i], qnfb[i]], writes=[qrbb[i]])
                                    if not self.cfg.get("no_scratch"):
                                        k.dma(k.sp, qs[head, :, ts], qrb[i][:], reads=[qrbb[i]], writes=[qs_b])
                                else:
                                    kvh = head - 16
                                    k.op(k.dve, lambda: nc.vector.tensor_tensor(out=t1[i][:], in0=t1[i][:], in1=qnf[i][:], op=ALU.add),
                                         reads=[t1b[i], qnfb[i]], writes=[t1b[i]])
                                    k.op(k.act, lambda: nc.scalar.copy(out=qrb[i][:], in_=t1[i][:]), reads=[t1b[i]], writes=[qrbb[i]])
                                    if not self.cfg.get("no_scratch"):
                                        k.dma(k.sp, ks[kvh, :, ts], qrb[i][:], reads=[qrbb[i]], writes=[ks_b])
                                    if self.cfg.get("no_tr"):
                                        continue
                                    ptr, ptrb = self.next_ps()
                                    for b4 in range(4):
                                        k.op(k.pe, lambda b4=b4: nc.tensor.transpose(
                                            ptr[:, b4 * 128:(b4 + 1) * 128], t1[i][:, b4 * 128:(b4 + 1) * 128], ident),
                                            reads=[t1b[i], apk_b], writes=[ptrb])
                                    k.op(k.act, lambda: nc.scalar.copy(out=kout[i][:].rearrange("p b d -> p (b d)"), in_=ptr[:]),
                                         reads=[ptrb], writes=[koutb[i]])
                                    dst = self.O("newk")[slot, t * TT:(t + 1) * TT, kvh * 128:(kvh + 1) * 128].rearrange(
                                        "(b p) d -> p b d", p=128)
                                    k.dma(k.sp, dst, kout[i][:], reads=[koutb[i]], writes=[outb])
                    else:
                        c0 = (ct - 10) * 256
                        for blk in range(12):
                            i = it % 2
                            it += 1
                            pv, pvb = self.next_ps()
                            for kc in range(KC):
                                k.op(k.pe, lambda kc=kc: nc.tensor.matmul(
                                    pv[:, 0:256], h[:, kc, blk * 128:(blk + 1) * 128], w[:, kc, :],
                                    start=(kc == 0), stop=(kc == KC - 1)), reads=[wb, hb[blk // 4]], writes=[pvb])
                            k.op(k.act, lambda: nc.scalar.copy(out=vf[i][:], in_=pv[:, 0:256]), reads=[pvb], writes=[vfb[i]])
                            k.op(k.dve, lambda: nc.vector.tensor_copy(out=vb[i][:], in_=pv[:, 0:256]), reads=[pvb], writes=[vbb[i]])
                            if not self.cfg.get("no_newv"):
                                k.dma(k.sp, self.O("newv")[slot, blk * 128:(blk + 1) * 128, c0:c0 + 256], vf[i][:],
                                      reads=[vfb[i]], writes=[outb])
                            if not self.cfg.get("no_scratch"):
                                k.dma(k.sp, vs[blk, :, c0:c0 + 256], vb[i][:], reads=[vbb[i]], writes=[vs_b])
                self.phase_barrier()
            if self.cfg.get("attn_phase", "AB") == "A":
                return
            with contextlib.ExitStack() as es2:
                sbt = lambda name, shape, dt: es2.enter_context(_sbuf(nc, name, list(shape), dt))
                masks = sbt("at_masks", [128, 14, 128], BF16)
                masks_b = Buf()
                k.dma(k.pool, masks[:], self.I("amask").rearrange("p (m q) -> p m q", q=128), writes=[masks_b])
                ckf = sbt("at_ckf", [128, 2, 512], F32)
                ckf_b = Buf()
                k.dma(k.sp, ckf[:], self.I("cache_k")[slot].rearrange("(b p) d -> p b d", p=128), writes=[ckf_b])
                vc = sbt("at_vc", [128, 2, 512], BF16)
                vc_b = Buf()
                k.dma(k.pool, vc[:], self.I("cache_v")[slot].rearrange("(b p) d -> p b d", p=128), writes=[vc_b])
                kTc = sbt("at_kTc", [128, 4, 256], BF16)
                kTc_b = Buf()
                for cb in range(2):
                    ptr, ptrb = self.next_ps()
                    for kvh in range(4):
                        k.op(k.pe, lambda kvh=kvh: nc.tensor.transpose(
                            ptr[:, kvh * 128:(kvh + 1) * 128], ckf[:, cb, kvh * 128:(kvh + 1) * 128], ident),
                            reads=[ckf_b, apk_b], writes=[ptrb])
                    k.op(k.act, lambda: nc.scalar.copy(out=kTc[:, :, cb * 128:(cb + 1) * 128],
                                                       in_=ptr[:].rearrange("p (h s) -> p h s", s=128)),
                         reads=[ptrb], writes=[kTc_b])
                esink = sbt("at_esink", [128, 16], F32)
                esink_b = Buf()
                k.op(k.act, lambda: nc.scalar.activation(out=esink[:], in_=sink, func=AF.Exp), reads=[apk_b], writes=[esink_b])
                qg = [sbt(f"at_qg{i}", [128, 4, NTOK], BF16) for i in range(2)]
                kg = [sbt(f"at_kg{i}", [128, NTOK], BF16) for i in range(2)]
                vg = [sbt(f"at_vg{i}", [128, 12, 128], BF16) for i in range(2)]
                wo1 = sbt("at_wo", [128, 4, D], BF16)
                wo = [wo1, wo1]
                qgb = [Buf() for _ in range(2)]
                kgb = [Buf() for _ in range(2)]
                vgb = [Buf() for _ in range(2)]
                wob1 = Buf()
                wob = [wob1, wob1]
                og = sbt("at_og", [128, 4, NTOK], BF16)
                ogb = [Buf() for _ in range(NGRP)]
                ptile = [sbt(f"at_pt{i}", [128, 512], BF16) for i in range(3)]
                ptileb = [Buf() for _ in range(3)]
                dtmp = sbt("at_dtmp", [128, 512], F32)
                dtmp_b = Buf()
                wo_v = self.I("attn_w_o")[slot].rearrange("(hh p) n -> p hh n", p=128)

                def loadg(g):
                    i = g % 2
                    k.dma(k.sp, qg[i][:], qs[g * 4:(g + 1) * 4].rearrange("h p t -> p h t"), reads=[qs_b], writes=[qgb[i]])
                    k.dma(k.sp, kg[i][:], ks[g], reads=[ks_b], writes=[kgb[i]])
                    k.dma(k.sp, vg[i][:], vs[:, :, g * 128:(g + 1) * 128].rearrange("b p d -> p b d"), reads=[vs_b], writes=[vgb[i]])

                def loadwo(g):
                    k.dma(k.pool, wo1[:], wo_v[:, g * 4:(g + 1) * 4, :], writes=[wob1])

                loadg(0)
                scale = 128.0 ** -0.5
                pit = 0
                for g in range(4):
                    if g + 1 < 4:
                        loadg(g + 1)
                    loadwo(g)
                    i = g % 2
                    for qb in range(12):
                        kbs = []
                        if qb < 8:
                            for kb in (qb - 1, qb, qb + 1):
                                if 0 <= kb < 8:
                                    if kb == qb:
                                        midx = None
                                    elif kb == qb - 1:
                                        midx = 2 * (qb - 1)
                                    else:
                                        midx = 2 * qb + 1
                                    kbs.append((kg[i][:, kb * 128:(kb + 1) * 128], kgb[i], vg[i][:, kb, :], vgb[i], midx, False))
                            for cb in range(2):
                                kbs.append((kTc[:, g, cb * 128:(cb + 1) * 128], kTc_b, vc[:, cb, g * 128:(g + 1) * 128], vc_b, None, True))
                        else:
                            sb0 = 8 + 2 * ((qb - 8) // 2)
                            for kb in (sb0, sb0 + 1):
                                kbs.append((kg[i][:, kb * 128:(kb + 1) * 128], kgb[i], vg[i][:, kb, :], vgb[i], None, False))
                        pO, pOb = self.reserve_ps()
                        pD, pDb = self.reserve_ps()
                        for j, (kap, kbuf, vap, vbuf, midx, is_c) in enumerate(kbs):
                            pS, pSb = self.next_ps()
                            k.op(k.pe, lambda: nc.tensor.matmul(pS[:], kap, qg[i][:, :, qb * 128:(qb + 1) * 128], start=True, stop=True),
                                 reads=[kbuf, qgb[i]], writes=[pSb])
                            pi = pit % 3
                            pit += 1
                            k.op(k.act, lambda: nc.scalar.activation(out=ptile[pi][:], in_=pS[:], func=AF.Exp, scale=scale),
                                 reads=[pSb], writes=[ptileb[pi]])
                            if midx is not None:
                                k.op(k.dve, lambda: nc.vector.tensor_tensor(
                                    out=ptile[pi][:].rearrange("p (h q) -> p h q", q=128),
                                    in0=ptile[pi][:].rearrange("p (h q) -> p h q", q=128),
                                    in1=masks[:, midx, :].unsqueeze(1).to_broadcast([128, 4, 128]), op=ALU.mult),
                                    reads=[ptileb[pi], masks_b], writes=[ptileb[pi]])
                            if is_c:
                                k.op(k.dve, lambda: nc.vector.tensor_scalar(out=ptile[pi][:], in0=ptile[pi][:], scalar1=cflag, scalar2=None,
                                                                            op0=ALU.mult), reads=[ptileb[pi], apk_b], writes=[ptileb[pi]])
                            k.op(k.pe, lambda: nc.tensor.matmul(pO[:], vap, ptile[pi][:], start=(j == 0), stop=(j == len(kbs) - 1)),
                                 reads=[vbuf, ptileb[pi]], writes=[pOb])
                            k.op(k.pe, lambda: nc.tensor.matmul(pD[:], self.ones_bf[:], ptile[pi][:], start=(j == 0), stop=(j == len(kbs) - 1)),
                                 reads=[self.ones_b, ptileb[pi]], writes=[pDb])
                        k.op(k.dve, lambda: nc.vector.tensor_tensor(
                            out=dtmp[:].rearrange("p (h q) -> p h q", q=128), in0=pD[:].rearrange("p (h q) -> p h q", q=128),
                            in1=esink[:, g * 4:(g + 1) * 4].unsqueeze(2).to_broadcast([128, 4, 128]), op=ALU.add),
                            reads=[pDb, esink_b], writes=[dtmp_b])
                        k.op(k.dve, lambda: nc.vector.reciprocal(out=dtmp[:], in_=dtmp[:]), reads=[dtmp_b], writes=[dtmp_b])
                        k.op(k.dve, lambda: nc.vector.tensor_tensor(
                            out=og[:, :, qb * 128:(qb + 1) * 128], in0=pO[:].rearrange("p (h q) -> p h q", q=128),
                            in1=dtmp[:].rearrange("p (h q) -> p h q", q=128), op=ALU.mult),
                            reads=[pOb, dtmp_b], writes=[ogb[qb // 4]])
                        self.release_ps(pO)
                        self.release_ps(pD)
                    if self.cfg.get("dbg_attn"):
                            dbo = self.dbg_out("dbg_o", [16, 128, NTOK])
                            k.dma(k.sp, dbo[g * 4:(g + 1) * 4].rearrange("h p t -> p h t"), og[:], reads=ogb, writes=[outb])
                            dbq = self.dbg_out("dbg_q", [16, 128, NTOK])
                            k.dma(k.sp, dbq[g * 4:(g + 1) * 4].rearrange("h p t -> p h t"), qg[i][:], reads=[qgb[i]], writes=[outb])
                    for t in range(NGRP):
                        ts = slice(t * TT, (t + 1) * TT)
                        for dc in range(KC):
                            py, pyb = self.next_ps()
                            for hh in range(4):
                                k.op(k.pe, lambda hh=hh: nc.tensor.matmul(py[:], wo[i][:, hh, dc * 128:(dc + 1) * 128], og[:, hh, ts],
                                                                          start=(hh == 0), stop=(hh == 3)),
                                     reads=[wob[i], ogb[t]], writes=[pyb])
                            k.op(k.dve, lambda: nc.vector.scalar_tensor_tensor(
                                out=self.x_sb[:, dc, ts], in0=py[:], scalar=self.modT[:, l, 5, dc, t:t + 1],
                                in1=self.x_sb[:, dc, ts], op0=ALU.mult, op1=ALU.add),
                                reads=[pyb, self.modT_b, self.xb[dc][t]], writes=[self.xb[dc][t]])
                self.phase_barrier()


    def mlstm(self, l):
        k, nc = self.k, self.nc
        slot = l // 3
        NHm, DKm, DVm, NB = 8, 128, 256, NTOK // 128
        DV1 = DVm + 1
        qT_s = nc.dram_tensor(f"ml_qT{l}", [NHm, 128, NTOK], BF16).ap()
        kT_s = nc.dram_tensor(f"ml_kT{l}", [NHm, 128, NTOK], BF16).ap()
        ktm_s = nc.dram_tensor(f"ml_ktm{l}", [NB, 128, NHm * DKm], BF16).ap()
        vtm_s = nc.dram_tensor(f"ml_vtm{l}", [NB, 128, D], BF16).ap()
        og_s = nc.dram_tensor(f"ml_og{l}", [NB, 128, D], BF16).ap()
        hf_s = nc.dram_tensor(f"ml_hf{l}", [NB, 128, D], F32).ap()
        hsT_s = nc.dram_tensor(f"ml_hsT{l}", [KC, 128, NTOK], BF16).ap()
        scr_b = Buf("ml_scr")
        hf_b = Buf("ml_hf")
        hsT_b = Buf("ml_hsT")
        outb = Buf("ml_out")
        w_in_v = self.I("mlstm_w_in")[slot].rearrange("(kc p) n -> p kc n", p=128)
        with contextlib.ExitStack() as es1:
            sbt1 = lambda name, shape, dt: es1.enter_context(_sbuf(nc, name, list(shape), dt))
            self.phase_barrier()
            mpk = sbt1("ml_mpk", [128, NMP], F32)
            mpk_b = Buf()
            k.dma(k.sp, mpk[:], self.I("mpack")[slot], writes=[mpk_b])
            ident = mpk[:, 0:128]
            onesf = mpk[:, 128:256]
            V1_01, V2_01 = mpk[:, 256:384], mpk[:, 384:512]
            V1b, V2b = mpk[:, 512:640], mpk[:, 640:768]
            bgate = mpk[:, 768:800]
            flag = mpk[:, 800:801]
            onorm = mpk[:, 801:801 + D]
            gates = sbt1("ml_gates", [128, NB, 32], F32)
            gates_b = Buf()
            with contextlib.ExitStack() as es2:
                sbt = lambda name, shape, dt: es2.enter_context(_sbuf(nc, name, list(shape), dt))
                h = sbt("ml_h", [128, KC, NTOK], BF16)
                hb = [Buf(f"h{t}") for t in range(NGRP)]
                sq = [sbt(f"ml_sq{i}", [128, TT], BF16) for i in range(2)]
                sqb = [Buf() for _ in range(2)]
                ev = [sbt(f"ml_ev{i}", [128, TT], BF16) for i in range(3)]
                evb = [Buf() for _ in range(3)]
                wg = sbt("ml_wg", [128, KC, 32], BF16)
                wg_b = Buf()
                k.dma(k.pool, wg[:], self.I("mlstm_w_gate")[slot].rearrange("(kc p) n -> p kc n", p=128), writes=[wg_b])
                self.compute_h(l, 1, h, hb, [t_[:] for t_ in sq], sqb)
                cnt = [0]

                def nxt():
                    cnt[0] += 1
                    return cnt[0] % 3

                def ev_q(oc, t, ps, psb):
                    i = nxt()
                    k.op(k.act, lambda: nc.scalar.activation(out=ev[i][:], in_=ps[:], func=AF.Identity, scale=float(DKm) ** -0.5),
                         reads=[psb], writes=[evb[i]])
                    k.dma(k.sp, qT_s[oc, :, t * TT:(t + 1) * TT], ev[i][:], reads=[evb[i]], writes=[scr_b])

                def ev_k(oc, t, ps, psb):
                    i = nxt()
                    k.op(k.act, lambda: nc.scalar.copy(out=ev[i][:], in_=ps[:]), reads=[psb], writes=[evb[i]])
                    k.dma(k.sp, kT_s[oc, :, t * TT:(t + 1) * TT], ev[i][:], reads=[evb[i]], writes=[scr_b])

                def mk_tm(dst, func):
                    def f(blk, ci, ps, psb):
                        i = nxt()
                        if func is None:
                            k.op(k.dve, lambda: nc.vector.tensor_copy(out=ev[i][:], in_=ps[:]), reads=[psb], writes=[evb[i]])
                        else:
                            k.op(k.act, lambda: nc.scalar.activation(out=ev[i][:], in_=ps[:], func=func), reads=[psb], writes=[evb[i]])
                        k.dma(k.sp, dst[blk, :, ci * 512:(ci + 1) * 512], ev[i][:], reads=[evb[i]], writes=[scr_b])
                    return f

                self.linear_fm(h, hb, w_in_v, 0, 1024, ev_q, "q")
                self.linear_fm(h, hb, w_in_v, 1024, 1024, ev_k, "k")
                self.linear_tm(h, hb, w_in_v, 1024, 1024, mk_tm(ktm_s, None), "kt")
                self.linear_tm(h, hb, w_in_v, 2048, 2048, mk_tm(vtm_s, None), "v")
                self.linear_tm(h, hb, w_in_v, 4096, 2048, mk_tm(og_s, AF.Sigmoid), "og")
                for blk in range(NB):
                    ps, psb = self.next_ps()
                    for kc in range(KC):
                        k.op(k.pe, lambda kc=kc: nc.tensor.matmul(ps[:, 0:32], h[:, kc, blk * 128:(blk + 1) * 128], wg[:, kc, :],
                                                                  start=(kc == 0), stop=(kc == KC - 1)),
                             reads=[wg_b, hb[blk // 4]], writes=[psb])
                    k.op(k.dve, lambda: nc.vector.tensor_tensor(out=gates[:, blk, :], in0=ps[:, 0:32], in1=bgate, op=ALU.add),
                         reads=[psb, mpk_b], writes=[gates_b])
                self.phase_barrier()
            with contextlib.ExitStack() as es2:
                sbt = lambda name, shape, dt: es2.enter_context(_sbuf(nc, name, list(shape), dt))
                C = sbt("ml_C", [128, NHm, DV1], F32)
                Cb = sbt("ml_Cb", [128, NHm, DV1], BF16)
                mst = sbt("ml_mst", [128, NHm], F32)
                C_b, Cb_b, mst_b = Buf(), Buf(), Buf()
                qc = [sbt(f"ml_qc{i}", [128, NHm, 128], BF16) for i in range(2)]
                kc_ = [sbt(f"ml_kc{i}", [128, NHm, 128], BF16) for i in range(2)]
                ktc = [sbt(f"ml_ktc{i}", [128, NHm, 128], BF16) for i in range(2)]
                vx = [sbt(f"ml_vx{i}", [128, NHm, DV1], BF16) for i in range(2)]
                ldb = [[Buf() for _ in range(4)] for _ in range(2)]
                for i in range(2):
                    k.op(k.dve, lambda i=i: nc.vector.memset(vx[i][:, :, DVm:DV1], 1.0), writes=[ldb[i][3]])
                sm = {n_: sbt(f"ml_s_{n_}", [128, NHm], F32) for n_ in
                      ("e", "lsp", "b", "g", "gmax", "cm", "mx", "a", "em", "nmx", "mx2", "wgt", "dec", "tmp")}
                smb = {n_: Buf() for n_ in sm}
                diag = sbt("ml_diag", [128, 4, 128], F32)
                diag_b = Buf()
                dmat = sbt("ml_dmat", [128, 4, 128], F32)
                dmat_b = Buf()
                DmT = sbt("ml_DmT", [128, NHm, 128], F32)
                DmT_b = Buf()
                PT8 = sbt("ml_PT8", [128, NHm, 128], BF16)
                PT8_b = Buf()
                kw = sbt("ml_kw", [128, NHm, 128], BF16)
                kw_b = Buf()
                tmpn = [sbt(f"ml_tmpn{i}", [128, DV1], F32) for i in range(2)]
                tmpn_b = [Buf() for _ in range(2)]
                num = [sbt(f"ml_num{i}", [128, DV1], F32) for i in range(2)]
                num_b = [Buf() for _ in range(2)]
                dn = [sbt(f"ml_dn{i}", [128, 1], F32) for i in range(2)]
                dn_b = [Buf() for _ in range(2)]
                hout = sbt("ml_hout", [128, NHm, DVm], F32)
                hout_b = Buf()
                hfl = sbt("ml_hfl", [128, NHm, DVm], F32)
                hfl_b = Buf()
                ogl = sbt("ml_ogl", [128, D], BF16)
                ogl_b = Buf()
                ssum = sbt("ml_ssum", [128, NHm], F32)
                ssum_b = Buf()
                hsT = sbt("ml_hsTt", [128, KC, 128], BF16)
                hsT_tb = Buf()

                def load_chunk(blk, i):
                    sl = slice(blk * 128, (blk + 1) * 128)
                    k.dma(k.sp, qc[i][:], qT_s[:, :, sl].rearrange("h p t -> p h t"), reads=[scr_b], writes=[ldb[i][0]])
                    k.dma(k.sp, kc_[i][:], kT_s[:, :, sl].rearrange("h p t -> p h t"), reads=[scr_b], writes=[ldb[i][1]])
                    k.dma(k.sp, ktc[i][:], ktm_s[blk].rearrange("p (h d) -> p h d", d=128), reads=[scr_b], writes=[ldb[i][2]])
                    k.dma(k.sp, vx[i][:, :, 0:DVm], vtm_s[blk].rearrange("p (h d) -> p h d", d=DVm), reads=[scr_b], writes=[ldb[i][3]])

                def small(eng_fn, out_n, reads_n, extra_reads=()):
                    k.op(k.dve, eng_fn, reads=[smb[n_] for n_ in reads_n] + list(extra_reads), writes=[smb[out_n]])

                for dirn in range(2):
                    order = list(range(NB)) if dirn == 0 else list(range(NB - 1, -1, -1))
                    gi0 = dirn * 16
                    tri01 = V1_01 if dirn == 0 else V2_01
                    mb_st = V1b if dirn == 0 else V2b
                    mb_ts = V2b if dirn == 0 else V1b
                    load_chunk(order[0], 0)
                    for oi, blk in enumerate(order):
                        i = oi % 2
                        if oi + 1 < NB:
                            load_chunk(order[oi + 1], (oi + 1) % 2)
                        seg = blk // 2
                        first_in_seg = (blk % 2 == 0) if dirn == 0 else (blk % 2 == 1)
                        last_in_seg = not first_in_seg
                        if first_in_seg:
                            if seg >= 4:
                                k.op(k.dve, lambda: nc.vector.memset(C[:], 0.0), writes=[C_b])
                                k.op(k.dve, lambda: nc.vector.memset(mst[:], 0.0), writes=[mst_b])
                                k.op(k.dve, lambda: nc.vector.memset(Cb[:], 0.0), writes=[Cb_b])
                            elif (seg == 0 and dirn == 0) or (seg == 3 and dirn == 1):
                                k.dma(k.sp, C[:].rearrange("p h d -> p (h d)"), self.I("ml_initC")[dirn], writes=[C_b])
                                k.dma(k.sp, mst[:], self.I("ml_initm")[dirn], writes=[mst_b])
                                k.op(k.act, lambda: nc.scalar.copy(out=Cb[:], in_=C[:]), reads=[C_b], writes=[Cb_b])
                            else:
                                k.op(k.dve, lambda: nc.vector.tensor_scalar(out=C[:], in0=C[:], scalar1=flag, scalar2=None, op0=ALU.mult),
                                     reads=[C_b, mpk_b], writes=[C_b])
                                k.op(k.dve, lambda: nc.vector.tensor_scalar(out=mst[:], in0=mst[:], scalar1=flag, scalar2=None, op0=ALU.mult),
                                     reads=[mst_b, mpk_b], writes=[mst_b])
                                k.op(k.act, lambda: nc.scalar.copy(out=Cb[:], in_=C[:]), reads=[C_b], writes=[Cb_b])
                        gi = gates[:, blk, gi0:gi0 + 8]
                        gf = gates[:, blk, gi0 + 8:gi0 + 16]
                        k.op(k.act, lambda: nc.scalar.activation(out=sm["e"][:], in_=gf, func=AF.Exp, scale=-1.0),
                             reads=[gates_b], writes=[smb["e"]])
                        k.op(k.act, lambda: nc.scalar.activation(out=sm["lsp"][:], in_=sm["e"][:], func=AF.Ln, bias=1.0),
                             reads=[smb["e"]], writes=[smb["lsp"]])
                        pb, pbb = self.next_ps()
                        k.op(k.pe, lambda: nc.tensor.matmul(pb[:, 0:8], tri01, sm["lsp"][:], start=True, stop=True),
                             reads=[mpk_b, smb["lsp"]], writes=[pbb])
                        k.op(k.dve, lambda: nc.vector.tensor_tensor(out=sm["g"][:], in0=pb[:, 0:8], in1=gi, op=ALU.add),
                             reads=[pbb, gates_b], writes=[smb["g"]])
                        for hh in range(2):
                            hs_ = slice(hh * 4, (hh + 1) * 4)
                            k.op(k.dve, lambda: nc.vector.tensor_tensor(
                                out=diag[:], in0=ident.unsqueeze(1).to_broadcast([128, 4, 128]),
                                in1=sm["g"][:, hs_].unsqueeze(2).to_broadcast([128, 4, 128]), op=ALU.mult),
                                reads=[mpk_b, smb["g"]], writes=[diag_b])
                            pg, pgb = self.next_ps()
                            k.op(k.pe, lambda: nc.tensor.matmul(pg[:], onesf, diag[:].rearrange("p h s -> p (h s)"), start=True, stop=True),
                                 reads=[mpk_b, diag_b], writes=[pgb])
                            pg3 = pg[:].rearrange("p (h s) -> p h s", s=128)
                            k.op(k.dve, lambda: nc.vector.tensor_reduce(out=sm["gmax"][:, hs_], in_=pg3, axis=AX.X, op=ALU.max),
                                 reads=[pgb], writes=[smb["gmax"]])
                            k.op(k.dve, lambda: nc.vector.tensor_tensor(out=dmat[:], in0=pg3, in1=mb_ts.unsqueeze(1).to_broadcast([128, 4, 128]),
                                                                        op=ALU.add), reads=[pgb, mpk_b], writes=[dmat_b])
                            k.op(k.dve, lambda: nc.vector.tensor_reduce(out=sm["cm"][:, hs_], in_=dmat[:], axis=AX.X, op=ALU.max),
                                 reads=[dmat_b], writes=[smb["cm"]])
                        small(lambda: nc.vector.tensor_tensor(out=sm["mx"][:], in0=sm["cm"][:], in1=mst[:], op=ALU.max), "mx", ["cm"], [mst_b])
                        small(lambda: nc.vector.tensor_tensor(out=sm["tmp"][:], in0=mst[:], in1=sm["mx"][:], op=ALU.subtract), "tmp", ["mx"], [mst_b])
                        k.op(k.act, lambda: nc.scalar.activation(out=sm["a"][:], in_=sm["tmp"][:], func=AF.Exp), reads=[smb["tmp"]], writes=[smb["a"]])
                        small(lambda: nc.vector.tensor_tensor(out=sm["b"][:], in0=pb[:, 0:8], in1=sm["mx"][:], op=ALU.subtract), "b", ["mx"], [pbb])
                        k.op(k.act, lambda: nc.scalar.activation(out=sm["em"][:], in_=sm["b"][:], func=AF.Exp), reads=[smb["b"]], writes=[smb["em"]])
                        small(lambda: nc.vector.tensor_scalar(out=sm["nmx"][:], in0=sm["mx"][:], scalar1=-1.0, scalar2=None, op0=ALU.mult), "nmx", ["mx"])
                        small(lambda: nc.vector.tensor_tensor(out=sm["mx2"][:], in0=sm["gmax"][:], in1=mst[:], op=ALU.max), "mx2", ["gmax"], [mst_b])
                        small(lambda: nc.vector.tensor_tensor(out=sm["tmp"][:], in0=sm["g"][:], in1=sm["mx2"][:], op=ALU.subtract), "tmp", ["g", "mx2", "a"])
                        k.op(k.act, lambda: nc.scalar.activation(out=sm["wgt"][:], in_=sm["tmp"][:], func=AF.Exp), reads=[smb["tmp"]], writes=[smb["wgt"]])
                        small(lambda: nc.vector.tensor_tensor(out=sm["e"][:], in0=mst[:], in1=sm["mx2"][:], op=ALU.subtract), "e", ["mx2", "lsp"], [mst_b])
                        k.op(k.act, lambda: nc.scalar.activation(out=sm["dec"][:], in_=sm["e"][:], func=AF.Exp), reads=[smb["e"]], writes=[smb["dec"]])
                        for hh in range(2):
                            hs_ = slice(hh * 4, (hh + 1) * 4)
                            k.op(k.dve, lambda: nc.vector.tensor_tensor(
                                out=diag[:], in0=ident.unsqueeze(1).to_broadcast([128, 4, 128]),
                                in1=sm["nmx"][:, hs_].unsqueeze(2).to_broadcast([128, 4, 128]), op=ALU.mult),
                                reads=[mpk_b, smb["nmx"]], writes=[diag_b])
                            pu, pub = self.next_ps()
                            k.op(k.pe, lambda: nc.tensor.matmul(pu[:], onesf, diag[:].rearrange("p h s -> p (h s)"), start=True, stop=True),
                                 reads=[mpk_b, diag_b], writes=[pub])
                            pu3 = pu[:].rearrange("p (h t) -> p h t", t=128)
                            k.op(k.dve, lambda: nc.vector.tensor_tensor(out=dmat[:], in0=pu3,
                                                                        in1=sm["g"][:, hs_].unsqueeze(2).to_broadcast([128, 4, 128]), op=ALU.add),
                                 reads=[pub, smb["g"]], writes=[dmat_b])
                            k.op(k.dve, lambda: nc.vector.tensor_tensor(out=dmat[:], in0=dmat[:],
                                                                        in1=mb_st.unsqueeze(1).to_broadcast([128, 4, 128]), op=ALU.add),
                                 reads=[dmat_b, mpk_b], writes=[dmat_b])
                            k.op(k.act, lambda: nc.scalar.activation(out=DmT[:, hs_, :], in_=dmat[:], func=AF.Exp),
                                 reads=[dmat_b], writes=[DmT_b])
                            pss, pssb = self.next_ps()
                            for h4 in range(4):
                                hd_ = hh * 4 + h4
                                k.op(k.pe, lambda: nc.tensor.matmul(pss[:, h4 * 128:(h4 + 1) * 128], kc_[i][:, hd_, :], qc[i][:, hd_, :],
                                                                    start=True, stop=True),
                                     reads=[ldb[i][0], ldb[i][1]], writes=[pssb])
                            k.op(k.dve, lambda: nc.vector.tensor_tensor(out=PT8[:, hs_, :], in0=pss[:].rearrange("p (h t) -> p h t", t=128),
                                                                        in1=DmT[:, hs_, :], op=ALU.mult),
                                 reads=[pssb, DmT_b], writes=[PT8_b])
                        k.op(k.dve, lambda: nc.vector.tensor_tensor(out=kw[:], in0=ktc[i][:],
                                                                    in1=sm["wgt"][:].unsqueeze(2).to_broadcast([128, NHm, 128]), op=ALU.mult),
                             reads=[ldb[i][2], smb["wgt"]], writes=[kw_b])
                        for hd_ in range(NHm):
                            j = hd_ % 2
                            p1, p1b = self.next_ps()
                            k.op(k.pe, lambda: nc.tensor.matmul(p1[:, 0:DV1], PT8[:, hd_, :], vx[i][:, hd_, :], start=True, stop=True),
                                 reads=[PT8_b, ldb[i][3]], writes=[p1b])
                            p2, p2b = self.next_ps()
                            k.op(k.pe, lambda: nc.tensor.matmul(p2[:, 0:DV1], qc[i][:, hd_, :], Cb[:, hd_, :], start=True, stop=True),
                                 reads=[ldb[i][0], Cb_b], writes=[p2b])
                            k.op(k.act, lambda: nc.scalar.activation(out=tmpn[j][:], in_=p2[:, 0:DV1], func=AF.Identity,
                                                                     scale=sm["a"][:, hd_:hd_ + 1]),
                                 reads=[p2b, smb["a"]], writes=[tmpn_b[j]])
                            k.op(k.dve, lambda: nc.vector.tensor_tensor(out=num[j][:], in0=tmpn[j][:], in1=p1[:, 0:DV1], op=ALU.add),
                                 reads=[tmpn_b[j], p1b], writes=[num_b[j]])
                            k.op(k.dve, lambda: nc.vector.tensor_scalar(out=dn[j][:], in0=num[j][:, DVm:DV1], scalar1=-1.0,
                                                                        scalar2=None, op0=ALU.mult),
                                 reads=[num_b[j]], writes=[dn_b[j]])
                            k.op(k.dve, lambda: nc.vector.tensor_tensor(out=dn[j][:], in0=dn[j][:], in1=num[j][:, DVm:DV1], op=ALU.max),
                                 reads=[num_b[j], dn_b[j]], writes=[dn_b[j]])
                            k.op(k.dve, lambda: nc.vector.tensor_tensor(out=dn[j][:], in0=dn[j][:], in1=sm["em"][:, hd_:hd_ + 1], op=ALU.max),
                                 reads=[dn_b[j], smb["em"]], writes=[dn_b[j]])
                            k.op(k.dve, lambda: nc.vector.reciprocal(out=dn[j][:], in_=dn[j][:]), reads=[dn_b[j]], writes=[dn_b[j]])
                            k.op(k.dve, lambda: nc.vector.tensor_scalar(out=hout[:, hd_, :], in0=num[j][:, 0:DVm], scalar1=dn[j][:, 0:1],
                                                                        scalar2=None, op0=ALU.mult),
                                 reads=[num_b[j], dn_b[j]], writes=[hout_b])
                            p3, p3b = self.next_ps()
                            k.op(k.pe, lambda: nc.tensor.matmul(p3[:, 0:DV1], kw[:, hd_, :], vx[i][:, hd_, :], start=True, stop=True),
                                 reads=[kw_b, ldb[i][3]], writes=[p3b])
                            k.op(k.dve, lambda: nc.vector.scalar_tensor_tensor(out=C[:, hd_, :], in0=C[:, hd_, :], scalar=sm["dec"][:, hd_:hd_ + 1],
                                                                               in1=p3[:, 0:DV1], op0=ALU.mult, op1=ALU.add),
                                 reads=[C_b, smb["dec"], p3b], writes=[C_b])
                        k.op(k.act, lambda: nc.scalar.copy(out=Cb[:], in_=C[:]), reads=[C_b], writes=[Cb_b])
                        pbl, pblb = self.next_ps()
                        k.op(k.pe, lambda: nc.tensor.matmul(pbl[:, 0:8], onesf, sm["lsp"][:], start=True, stop=True),
                             reads=[mpk_b, smb["lsp"]], writes=[pblb])
                        k.op(k.dve, lambda: nc.vector.tensor_tensor(out=mst[:], in0=sm["mx2"][:], in1=pbl[:, 0:8], op=ALU.subtract),
                             reads=[smb["mx2"], pblb, smb["dec"], smb["a"], smb["mx"]], writes=[mst_b])
                        if last_in_seg:
                            k.dma(k.sp, self.O("ml_stC")[seg, dirn], C[:].rearrange("p h d -> p (h d)"), reads=[C_b], writes=[outb])
                            k.dma(k.sp, self.O("ml_stm")[seg, dirn], mst[:], reads=[mst_b], writes=[outb])
                        if dirn == 0:
                            k.dma(k.sp, hf_s[blk].rearrange("p (h d) -> p h d", d=DVm), hout[:], reads=[hout_b], writes=[hf_b])
                        else:
                            k.dma(k.sp, hfl[:], hf_s[blk].rearrange("p (h d) -> p h d", d=DVm), reads=[hf_b], writes=[hfl_b])
                            k.dma(k.sp, ogl[:], og_s[blk], reads=[scr_b], writes=[ogl_b])
                            k.op(k.dve, lambda: nc.vector.tensor_tensor(out=hout[:], in0=hout[:], in1=hfl[:], op=ALU.add),
                                 reads=[hout_b, hfl_b], writes=[hout_b])
                            k.op(k.act, lambda: nc.scalar.activation(out=hfl[:], in_=hout[:], func=AF.Square), reads=[hout_b], writes=[hfl_b])
                            k.op(k.dve, lambda: nc.vector.tensor_reduce(out=ssum[:], in_=hfl[:], axis=AX.X, op=ALU.add),
                                 reads=[hfl_b], writes=[ssum_b])
                            k.op(k.dve, lambda: nc.vector.tensor_scalar(out=ssum[:], in0=ssum[:], scalar1=1.0 / DVm, scalar2=EPS,
                                                                        op0=ALU.mult, op1=ALU.add), reads=[ssum_b], writes=[ssum_b])
                            k.op(k.act, lambda: nc.scalar.activation(out=ssum[:], in_=ssum[:], func=AF.Sqrt), reads=[ssum_b], writes=[ssum_b])
                            k.op(k.dve, lambda: nc.vector.reciprocal(out=ssum[:], in_=ssum[:]), reads=[ssum_b], writes=[ssum_b])
                            k.op(k.dve, lambda: nc.vector.tensor_tensor(out=hout[:], in0=hout[:],
                                                                        in1=ssum[:].unsqueeze(2).to_broadcast([128, NHm, DVm]), op=ALU.mult),
                                 reads=[hout_b, ssum_b], writes=[hout_b])
                            hflat = hout[:].rearrange("p h d -> p (h d)")
                            k.op(k.dve, lambda: nc.vector.tensor_tensor(out=hflat, in0=hflat, in1=onorm, op=ALU.mult),
                                 reads=[hout_b, mpk_b], writes=[hout_b])
                            k.op(k.dve, lambda: nc.vector.tensor_tensor(out=hflat, in0=hflat, in1=ogl[:], op=ALU.mult),
                                 reads=[hout_b, ogl_b], writes=[hout_b])
                            for q4 in range(4):
                                ptr, ptrb = self.next_ps()
                                for c4 in range(4):
                                    c = q4 * 4 + c4
                                    k.op(k.pe, lambda: nc.tensor.transpose(ptr[:, c4 * 128:(c4 + 1) * 128], hflat[:, c * 128:(c + 1) * 128], ident),
                                         reads=[hout_b, mpk_b], writes=[ptrb])
                                k.op(k.act, lambda: nc.scalar.copy(out=hsT[:, q4 * 4:(q4 + 1) * 4, :], in_=ptr[:].rearrange("p (c t) -> p c t", t=128)),
                                     reads=[ptrb], writes=[hsT_tb])
                            k.dma(k.sp, hsT_s[:, :, blk * 128:(blk + 1) * 128].rearrange("c p t -> p c t"), hsT[:], reads=[hsT_tb], writes=[hsT_b])
                self.phase_barrier()
            with contextlib.ExitStack() as es2:
                sbt = lambda name, shape, dt: es2.enter_context(_sbuf(nc, name, list(shape), dt))
                h2 = sbt("ml_h2", [128, KC, NTOK], BF16)
                h2b = [Buf() for _ in range(NGRP)]
                for t in range(NGRP):
                    k.dma(k.sp, h2[:, :, t * TT:(t + 1) * TT], hsT_s[:, :, t * TT:(t + 1) * TT].rearrange("c p t -> p c t"),
                          reads=[hsT_b], writes=[h2b[t]])
                wo_v = self.I("mlstm_w_o")[slot].rearrange("(kc p) n -> p kc n", p=128)

                def ev_o(oc, t, ps, psb):
                    ts = slice(t * TT, (t + 1) * TT)
                    k.op(k.dve, lambda: nc.vector.scalar_tensor_tensor(
                        out=self.x_sb[:, oc, ts], in0=ps[:], scalar=self.modT[:, l, 5, oc, t:t + 1],
                        in1=self.x_sb[:, oc, ts], op0=ALU.mult, op1=ALU.add),
                        reads=[psb, self.modT_b, self.xb[oc][t]], writes=[self.xb[oc][t]])

                self.linear_fm(h2, h2b, wo_v, 0, D, ev_o, "o")
                self.phase_barrier()


    def rwkv(self, l):
        k, nc = self.k, self.nc
        slot = l // 3
        TB, SB = 32, 1
        KZ = [nc.dram_tensor(f"rk_KZ{z}_{l}", [128, 5, KC, NTOK], F32).ap() for z in range(2)]
        VV = nc.dram_tensor(f"rk_VV{l}", [128, KC, NTOK], F32).ap()
        GG = nc.dram_tensor(f"rk_GG{l}", [128, KC, NTOK], F32).ap()
        VB = nc.dram_tensor(f"rk_VB{l}", [128, KC, NTOK], F32).ap()
        RAWK = nc.dram_tensor(f"rk_RAWK{l}", [128, KC, NTOK], F32).ap()
        AZ = [nc.dram_tensor(f"rk_AZ{z}_{l}", [128, KC, NTOK], F32).ap() for z in range(2)]
        YT = [nc.dram_tensor(f"rk_YT{z}_{l}", [NTOK, D], BF16).ap() for z in range(2)]
        hsT_s = nc.dram_tensor(f"rk_hsT{l}", [KC, 128, NTOK], BF16).ap()
        VVT = nc.dram_tensor(f"rk_VVT{l}", [NTOK, D], BF16).ap()
        scr_b, yt_b, hsT_b, outb = Buf("rk_scr"), Buf("rk_yt"), Buf("rk_hsT"), Buf("rk_out")
        NFM = 13 * KC
        with contextlib.ExitStack() as es1:
            sbt1 = lambda name, shape, dt: es1.enter_context(_sbuf(nc, name, list(shape), dt))
            self.phase_barrier()
            rpk = sbt1("rk_rpk", [128, NRP], F32)
            rpk_b = Buf()
            k.dma(k.sp, rpk[:], self.I("rpack")[:, :], writes=[rpk_b])
            fmv = rpk[:, 0:NFM].rearrange("p (a c) -> p a c", c=KC)
            I2 = rpk[:, NFM:NFM + 64]
            flag = rpk[:, NFM + 64:NFM + 65]
            ident = rpk[:, NFM + 65:NFM + 193]
            BO = sbt1("rk_BO", [128, 128], BF16)
            hsel = sbt1("rk_hsel", [128, 2], BF16)
            cst_b = Buf()
            k.op(k.dve, lambda: nc.vector.memset(BO[:], 0.0), writes=[cst_b])
            k.op(k.dve, lambda: nc.vector.memset(BO[0:64, 0:64], 1.0), writes=[cst_b])
            k.op(k.dve, lambda: nc.vector.memset(BO[64:128, 64:128], 1.0), writes=[cst_b])
            k.op(k.dve, lambda: nc.vector.memset(hsel[:], 0.0), writes=[cst_b])
            k.op(k.dve, lambda: nc.vector.memset(hsel[0:64, 0:1], 1.0), writes=[cst_b])
            k.op(k.dve, lambda: nc.vector.memset(hsel[64:128, 1:2], 1.0), writes=[cst_b])

            with contextlib.ExitStack() as es2:
                sbt = lambda name, shape, dt: es2.enter_context(_sbuf(nc, name, list(shape), dt))
                hp_ = sbt("rk_h", [128, KC, NTOK + 2], BF16)
                hb = [Buf(f"h{t}") for t in range(NGRP)]
                k.op(k.dve, lambda: nc.vector.memset(hp_[:, :, 0:1], 0.0), writes=[hb[0]])
                k.op(k.dve, lambda: nc.vector.memset(hp_[:, :, NTOK + 1:NTOK + 2], 0.0), writes=[hb[2]])

                class Shift:
                    def __getitem__(self_, idx):
                        p, c, sl = idx
                        return hp_[p, c, slice(sl.start + 1, sl.stop + 1)]
                sq = [sbt(f"rk_sq{i}", [128, TT], BF16) for i in range(2)]
                sqb = [Buf() for _ in range(2)]
                self.compute_h(l, 1, Shift(), hb, [t_[:] for t_ in sq], sqb)
                msk = sbt("rk_msk", [128, 2, NTOK], BF16)
                msk_b = Buf()
                k.dma(k.pool, msk[:], self.I("rmask").rearrange("p (a t) -> p a t", a=2), writes=[msk_b])
                omm = sbt("rk_omm", [128, 6, KC], F32)
                hmu = sbt("rk_hmu", [128, 6, KC], F32)
                omm_b = Buf()
                k.op(k.dve, lambda: nc.vector.tensor_scalar(out=omm[:], in0=fmv[:, 0:6, :], scalar1=-1.0, scalar2=1.0, op0=ALU.mult, op1=ALU.add),
                     reads=[rpk_b], writes=[omm_b])
                k.op(k.dve, lambda: nc.vector.tensor_scalar(out=hmu[:], in0=fmv[:, 0:6, :], scalar1=0.5, scalar2=None, op0=ALU.mult),
                     reads=[rpk_b], writes=[omm_b])
                xs = sbt("rk_xs", [128, KC, TT], BF16)
                xs_b = Buf()
                tA = [sbt(f"rk_tA{i}", [128, TT], F32) for i in range(3)]
                tA_b = [Buf() for _ in range(3)]
                ev = [sbt(f"rk_ev{i}", [128, TT], F32) for i in range(3)]
                ev_b = [Buf() for _ in range(3)]
                mid = sbt("rk_mid", [128, 2, TT], BF16)
                mid_b = Buf()
                cnt = [0]

                def nxt():
                    cnt[0] += 1
                    return cnt[0] % 3

                def build_xs(p, t):
                    t0 = t * TT
                    for c in range(KC):
                        k.op(k.dve, lambda: nc.vector.tensor_tensor(out=tA[0][:], in0=hp_[:, c, t0:t0 + TT], in1=msk[:, 0, t0:t0 + TT], op=ALU.mult),
                             reads=[hb[t], hb[max(t - 1, 0)], msk_b], writes=[tA_b[0]])
                        k.op(k.dve, lambda: nc.vector.tensor_tensor(out=tA[1][:], in0=hp_[:, c, t0 + 2:t0 + TT + 2], in1=msk[:, 1, t0:t0 + TT], op=ALU.mult),
                             reads=[hb[t], hb[min(t + 1, 2)], msk_b], writes=[tA_b[1]])
                        k.op(k.dve, lambda: nc.vector.tensor_tensor(out=tA[0][:], in0=tA[0][:], in1=tA[1][:], op=ALU.add),
                             reads=[tA_b[0], tA_b[1]], writes=[tA_b[0]])
                        k.op(k.act, lambda: nc.scalar.activation(out=tA[2][:], in_=hp_[:, c, t0 + 1:t0 + TT + 1], func=AF.Identity, scale=omm[:, p, c:c + 1]),
                             reads=[hb[t], omm_b], writes=[tA_b[2]])
                        k.op(k.dve, lambda: nc.vector.scalar_tensor_tensor(out=xs[:, c, :], in0=tA[0][:], scalar=hmu[:, p, c:c + 1], in1=tA[2][:],
                                                                           op0=ALU.mult, op1=ALU.add),
                             reads=[tA_b[0], tA_b[2], omm_b], writes=[xs_b])

                def proj_tile(wview, col0, ncols, evac, tag):
                    TC = 256
                    with contextlib.ExitStack() as es3:
                        wt = [es3.enter_context(_sbuf(nc, f"rkw_{tag}{i}", [128, KC, TC], BF16)) for i in range(2)]
                        wtb = [Buf() for _ in range(2)]
                        ntile = ncols // TC

                        def load(ti):
                            k.dma(k.pool, wt[ti % 2][:], wview[:, :, col0 + ti * TC:col0 + (ti + 1) * TC], writes=[wtb[ti % 2]])
                        load(0)
                        for ti in range(ntile):
                            if ti + 1 < ntile:
                                load(ti + 1)
                            for sub in range(TC // 128):
                                oc = ti * (TC // 128) + sub
                                ps, psb = self.next_ps()
                                for kc in range(KC):
                                    k.op(k.pe, lambda kc=kc: nc.tensor.matmul(ps[:], wt[ti % 2][:, kc, sub * 128:(sub + 1) * 128], xs[:, kc, :],
                                                                              start=(kc == 0), stop=(kc == KC - 1)),
                                         reads=[wtb[ti % 2], xs_b], writes=[psb])
                                evac(oc, ps, psb)
                        self.scope_end(wtb)

                for p in range(3):
                    wv_ = self.I("rwkv_w_rkv")[slot, p].rearrange("(kc p) n -> p kc n", p=128)
                    for t in range(NGRP):
                        ts = slice(t * TT, (t + 1) * TT)
                        build_xs(p, t)

                        def ev_rkv(oc, ps, psb, p=p, ts=ts):
                            i = nxt()
                            k.op(k.act, lambda: nc.scalar.copy(out=ev[i][:], in_=ps[:]), reads=[psb], writes=[ev_b[i]])
                            if p == 0:
                                k.dma(k.sp, KZ[0][:, 4, oc, ts], ev[i][:], reads=[ev_b[i]], appends=[scr_b])
                                k.dma(k.sp, KZ[1][:, 4, oc, ts], ev[i][:], reads=[ev_b[i]], appends=[scr_b])
                            elif p == 1:
                                k.dma(k.sp, RAWK[:, oc, ts], ev[i][:], reads=[ev_b[i]], appends=[scr_b])
                            else:
                                k.dma(k.sp, VV[:, oc, ts], ev[i][:], reads=[ev_b[i]], appends=[scr_b])
                        proj_tile(wv_, 0, D, ev_rkv, f"p{p}")
                for p in (3, 4, 5):
                    with contextlib.ExitStack() as es3:
                        if p < 5:
                            nmA, nmB, R_ = ("rwkv_wA", "rwkv_wB", 96) if p == 3 else ("rwkv_aA", "rwkv_aB", 96)
                            dn_w = [es3.enter_context(_sbuf(nc, f"rk_lA{p}{z}", [128, KC, R_], BF16)) for z in range(2)]
                            up_w = [es3.enter_context(_sbuf(nc, f"rk_lB{p}{z}", [128, 1, D], BF16)) for z in range(2)]
                            lw_b = Buf()
                            for z in range(2):
                                k.dma(k.pool, dn_w[z][:], self.I(nmA)[slot, z].rearrange("(kc p) r -> p kc r", p=128), writes=[lw_b])
                                k.dma(k.pool, up_w[z][0:R_, 0, :], self.I(nmB)[slot, z], writes=[lw_b])
                            nz, nch = 2, 1
                        else:
                            R_ = 256
                            dn_w = [es3.enter_context(_sbuf(nc, "rk_lA5", [128, KC, R_], BF16))]
                            up_w = [es3.enter_context(_sbuf(nc, "rk_lB5", [128, 2, D], BF16))]
                            lw_b = Buf()
                            k.dma(k.pool, dn_w[0][:], self.I("rwkv_gA")[slot].rearrange("(kc p) r -> p kc r", p=128), writes=[lw_b])
                            k.dma(k.pool, up_w[0][:], self.I("rwkv_gB")[slot].rearrange("(c p) n -> p c n", p=128), writes=[lw_b])
                            nz, nch = 1, 2
                        for t in range(NGRP):
                            ts = slice(t * TT, (t + 1) * TT)
                            build_xs(p, t)
                            for z in range(nz):
                                rows = 96 if p < 5 else 128
                                for ch in range(nch):
                                    pd, pdb = self.next_ps()
                                    for kc in range(KC):
                                        k.op(k.pe, lambda kc=kc: nc.tensor.matmul(pd[0:rows, :], dn_w[z][:, kc, ch * 128:ch * 128 + rows], xs[:, kc, :],
                                                                                  start=(kc == 0), stop=(kc == KC - 1)),
                                             reads=[lw_b, xs_b], writes=[pdb])
                                    fn = AF.Tanh if p == 3 else (AF.Identity if p == 4 else AF.Sigmoid)
                                    k.op(k.act, lambda: nc.scalar.activation(out=mid[0:rows, ch, :], in_=pd[0:rows, :], func=fn),
                                         reads=[pdb], writes=[mid_b])
                                for oc in range(KC):
                                    pu, pub = self.next_ps()
                                    for ch in range(nch):
                                        k.op(k.pe, lambda ch=ch: nc.tensor.matmul(pu[:], up_w[z][0:rows, ch, oc * 128:(oc + 1) * 128], mid[0:rows, ch, :],
                                                                                  start=(ch == 0), stop=(ch == nch - 1)),
                                             reads=[lw_b, mid_b], writes=[pub])
                                    i = nxt()
                                    if p == 3:
                                        k.op(k.act, lambda: nc.scalar.activation(out=ev[i][:], in_=pu[:], func=AF.Sigmoid, bias=fmv[:, 6 + z, oc:oc + 1]),
                                             reads=[pub, rpk_b], writes=[ev_b[i]])
                                        k.op(k.act, lambda: nc.scalar.activation(out=ev[i][:], in_=ev[i][:], func=AF.Exp, scale=-float(np.exp(-0.5))),
                                             reads=[ev_b[i]], writes=[ev_b[i]])
                                        k.dma(k.sp, KZ[z][:, 1, oc, ts], ev[i][:], reads=[ev_b[i]], appends=[scr_b])
                                    elif p == 4:
                                        k.op(k.act, lambda: nc.scalar.activation(out=ev[i][:], in_=pu[:], func=AF.Sigmoid, bias=fmv[:, 8 + z, oc:oc + 1]),
                                             reads=[pub, rpk_b], writes=[ev_b[i]])
                                        k.dma(k.sp, AZ[z][:, oc, ts], ev[i][:], reads=[ev_b[i]], appends=[scr_b])
                                    else:
                                        k.op(k.act, lambda: nc.scalar.copy(out=ev[i][:], in_=pu[:]), reads=[pub], writes=[ev_b[i]])
                                        k.dma(k.sp, GG[:, oc, ts], ev[i][:], reads=[ev_b[i]], appends=[scr_b])
                        self.scope_end([lw_b])
                self.phase_barrier()
            with contextlib.ExitStack() as es2:
                sbt = lambda name, shape, dt: es2.enter_context(_sbuf(nc, name, list(shape), dt))
                names = ("k", "r", "v", "a0", "a1", "kq", "kk", "t", "kd0", "kd1", "b0", "b1", "vb")
                T2 = {n_: [sbt(f"rk2_{n_}{i}", [128, TT], F32) for i in range(2)] for n_ in names}
                T2b = {n_: [Buf() for _ in range(2)] for n_ in names}
                sqk = [sbt(f"rk2_sq{i}", [128, TT], BF16) for i in range(2)]
                sqk_b = [Buf() for _ in range(2)]
                vtt = [sbt(f"rk2_vtt{i}", [128, 4, 128], BF16) for i in range(2)]
                vtt_b = [Buf() for _ in range(2)]
                it = 0
                for t in range(NGRP):
                    ts = slice(t * TT, (t + 1) * TT)
                    for c in range(KC):
                        i = it % 2
                        it += 1
                        X = {n_: T2[n_][i] for n_ in names}
                        B = {n_: T2b[n_][i] for n_ in names}
                        k.dma(k.sp, X["k"][:], RAWK[:, c, ts], reads=[scr_b], writes=[B["k"]])
                        k.dma(k.sp, X["r"][:], KZ[0][:, 4, c, ts], reads=[scr_b], writes=[B["r"]])
                        k.dma(k.sp, X["v"][:], VV[:, c, ts], reads=[scr_b], writes=[B["v"]])
                        k.dma(k.sp, X["a0"][:], AZ[0][:, c, ts], reads=[scr_b], writes=[B["a0"]])
                        k.dma(k.sp, X["a1"][:], AZ[1][:, c, ts], reads=[scr_b], writes=[B["a1"]])
                        k.op(k.act, lambda: nc.scalar.activation(out=X["kq"][:], in_=X["k"][:], func=AF.Identity, scale=fmv[:, 10, c:c + 1]),
                             reads=[B["k"], rpk_b], writes=[B["kq"]])
                        k.op(k.act, lambda: nc.scalar.activation(out=sqk[i][:], in_=X["kq"][:], func=AF.Square), reads=[B["kq"]], writes=[sqk_b[i]])
                        pn, pnb = self.next_ps()
                        k.op(k.pe, lambda: nc.tensor.matmul(pn[:], BO[:], sqk[i][:], start=True, stop=True), reads=[cst_b, sqk_b[i]], writes=[pnb])
                        k.op(k.act, lambda: nc.scalar.activation(out=pn[:], in_=pn[:], func=AF.Sqrt), reads=[pnb], writes=[pnb])
                        k.op(k.dve, lambda: nc.vector.tensor_scalar(out=pn[:], in0=pn[:], scalar1=1e-12, scalar2=None, op0=ALU.max), reads=[pnb], writes=[pnb])
                        k.op(k.dve, lambda: nc.vector.reciprocal(out=pn[:], in_=pn[:]), reads=[pnb], writes=[pnb])
                        k.op(k.dve, lambda: nc.vector.tensor_tensor(out=X["kk"][:], in0=X["kq"][:], in1=pn[:], op=ALU.mult),
                             reads=[B["kq"], pnb], writes=[B["kk"]])
                        k.dma(k.sp, KZ[0][:, 0, c, ts], X["kk"][:], reads=[B["kk"]], appends=[scr_b])
                        k.dma(k.sp, KZ[1][:, 0, c, ts], X["kk"][:], reads=[B["kk"]], appends=[scr_b])
                        for z in range(2):
                            az, kd, bz = X[f"a{z}"], X[f"kd{z}"], X[f"b{z}"]
                            k.op(k.dve, lambda: nc.vector.tensor_scalar(out=X["t"][:], in0=az[:], scalar1=-1.0, scalar2=fmv[:, 11, c:c + 1],
                                                                        op0=ALU.add, op1=ALU.mult), reads=[B[f"a{z}"], rpk_b], writes=[B["t"]])
                            k.op(k.dve, lambda: nc.vector.scalar_tensor_tensor(out=kd[:], in0=X["t"][:], scalar=1.0, in1=X["k"][:], op0=ALU.add, op1=ALU.mult),
                                 reads=[B["t"], B["k"]], writes=[B[f"kd{z}"]])
                            k.op(k.dve, lambda: nc.vector.tensor_tensor(out=bz[:], in0=X["kk"][:], in1=az[:], op=ALU.mult),
                                 reads=[B["kk"], B[f"a{z}"]], writes=[B[f"b{z}"]])
                            k.dma(k.sp, KZ[z][:, 3, c, ts], kd[:], reads=[B[f"kd{z}"]], appends=[scr_b])
                            k.dma(k.sp, KZ[z][:, 2, c, ts], bz[:], reads=[B[f"b{z}"]], appends=[scr_b])
                        k.op(k.dve, lambda: nc.vector.tensor_tensor(out=X["t"][:], in0=X["kd0"][:], in1=X["kd1"][:], op=ALU.add),
                             reads=[B["kd0"], B["kd1"]], writes=[B["t"]])
                        k.op(k.dve, lambda: nc.vector.scalar_tensor_tensor(out=sqk[i][:], in0=X["t"][:], scalar=fmv[:, 12, c:c + 1], in1=X["r"][:],
                                                                           op0=ALU.mult, op1=ALU.mult),
                             reads=[B["t"], B["r"], rpk_b], writes=[sqk_b[i]])
                        pbn, pbnb = self.next_ps()
                        k.op(k.pe, lambda: nc.tensor.matmul(pbn[:], BO[:], sqk[i][:], start=True, stop=True), reads=[cst_b, sqk_b[i]], writes=[pbnb])
                        k.op(k.dve, lambda: nc.vector.tensor_tensor(out=X["vb"][:], in0=X["v"][:], in1=pbn[:], op=ALU.mult),
                             reads=[B["v"], pbnb], writes=[B["vb"]])
                        k.dma(k.sp, VB[:, c, ts], X["vb"][:], reads=[B["vb"]], appends=[scr_b])
                        ptr, ptrb = self.next_ps()
                        for b4 in range(4):
                            k.op(k.pe, lambda b4=b4: nc.tensor.transpose(ptr[:, b4 * 128:(b4 + 1) * 128], X["v"][:, b4 * 128:(b4 + 1) * 128], ident),
                                 reads=[B["v"], rpk_b], writes=[ptrb])
                        k.op(k.act, lambda: nc.scalar.copy(out=vtt[i][:].rearrange("p b d -> p (b d)"), in_=ptr[:]), reads=[ptrb], writes=[vtt_b[i]])
                        k.dma(k.sp, VVT[t * TT:(t + 1) * TT, c * 128:(c + 1) * 128].rearrange("(b p) d -> p b d", p=128), vtt[i][:],
                              reads=[vtt_b[i]], appends=[scr_b])
                self.phase_barrier()
            with contextlib.ExitStack() as es2:
                sbt = lambda name, shape, dt: es2.enter_context(_sbuf(nc, name, list(shape), dt))
                hselT = sbt("rk_hselT", [2, 128], BF16)
                cst2_b = Buf()
                k.dma(k.pool, hselT[:], self.I("rk_hselT")[:, :], writes=[cst2_b])
                hselF = sbt("rk_hselF", [128, 2], F32)
                k.op(k.dve, lambda: nc.vector.memset(hselF[:], 0.0), writes=[cst2_b])
                k.op(k.dve, lambda: nc.vector.memset(hselF[0:64, 0:1], 1.0), writes=[cst2_b])
                k.op(k.dve, lambda: nc.vector.memset(hselF[64:128, 1:2], 1.0), writes=[cst2_b])
                CH = []
                for z in range(2):
                    ch = dict(z=z)
                    ch["S"] = [sbt(f"rks_S{z}{i}", [128, KC, 64], F32) for i in range(2)]
                    ch["S_b"] = [Buf() for _ in range(2)]
                    ch["cur"] = 0
                    ch["tA1"] = sbt(f"rks_tA1{z}", [128, KC, 64], BF16)
                    ch["vrow"] = [sbt(f"rks_vrow{z}{i}", [2, KC, 64], BF16) for i in range(3)]
                    ch["vrow_b"] = [Buf() for _ in range(3)]
                    ch["psv"] = [None, None]
                    ch["tF2"] = sbt(f"rks_tF2{z}", [128, KC, 64], F32)
                    ch["tF3"] = ch["tF2"]
                    ch["R2"] = [sbt(f"rks_R2{z}{i}", [128, KC, 2], F32) for i in range(2)]
                    ch["R2_b"] = [Buf() for _ in range(2)]
                    ch["r2i"] = 0
                    ch["saB"] = sbt(f"rks_saB{z}", [128, KC, 64], BF16)
                    ch["vbB"] = sbt(f"rks_vbB{z}", [128, KC, 64], BF16)
                    ch["saB_b"] = Buf()
                    ch["vbB_b"] = Buf()
                    ch["kb"] = [sbt(f"rks_kb{z}{i}", [128, 5, KC, TB], F32) for i in range(2)]
                    ch["ys"] = sbt(f"rks_ys{z}", [2, SB, KC * 64], BF16)
                    for n_ in ("tA1", "tF2", "ys"):
                        ch[n_ + "_b"] = Buf()
                    ch["tF3_b"] = ch["tF2_b"]
                    ch["kb_b"] = [Buf() for _ in range(2)]
                    CH.append(ch)
                e_t4 = k.pool if self.cfg.get("rk_pool", True) else k.dve
                veng = lambda e: (nc.gpsimd if e is k.pool else nc.vector)

                def load_blk(ch, t0, i):
                    z = ch["z"]
                    k.dma(k.sp, ch["kb"][i][:], KZ[z][:, :, :, t0:t0 + TB], reads=[scr_b], writes=[ch["kb_b"][i]])

                def stage0(n, ch, i, j, t):
                    vi = n % 3
                    k.dma(k.sp, ch["vrow"][vi][:], VVT[t, :].rearrange("(g hp v) -> hp g v", hp=2, v=64), reads=[scr_b], writes=[ch["vrow_b"][vi]])

                deferred = [None]

                def run_pass(T0, T1, is_L):
                    nblk = (T1 - T0) // TB
                    seq = []
                    for bi in range(nblk):
                        for jj in range(TB):
                            ent = []
                            for ch in CH:
                                z = ch["z"]
                                t0 = T0 + bi * TB if z == 0 else T1 - (bi + 1) * TB
                                j = jj if z == 0 else TB - 1 - jj
                                ent.append((ch, bi % 2, j, t0 + j))
                            seq.append((bi, jj, ent))
                    for ch in CH:
                        load_blk(ch, T0 if ch["z"] == 0 else T1 - TB, 0)
                    for (ch, i, j, t) in seq[0][2]:
                        stage0(0, ch, i, j, t)
                    for n, (bi, jj, ent) in enumerate(seq):
                        if jj == 0 and bi + 1 < nblk:
                            if deferred[0] is not None:
                                deferred[0]()
                                deferred[0] = None
                            for ch in CH:
                                nt0 = T0 + (bi + 1) * TB if ch["z"] == 0 else T1 - (bi + 2) * TB
                                load_blk(ch, nt0, (bi + 1) % 2)
                        if n + 1 < len(seq):
                            for (ch, i, j, t) in seq[n + 1][2]:
                                stage0(n + 1, ch, i, j, t)
                        for (ch, i, j, t) in ent:
                            vi = n % 3
                            vrf = ch["vrow"][vi][:].rearrange("p g v -> p (g v)")
                            pl = []
                            for q in range(2):
                                p_, pb_ = self.next_ps()
                                k.op(k.pe, lambda: nc.tensor.matmul(p_[:], hselT[:], vrf[:, q * 512:(q + 1) * 512], start=True, stop=True),
                                     reads=[cst2_b, ch["vrow_b"][vi]], writes=[pb_])
                                k.op(k.act, lambda: nc.scalar.copy(out=ch["vbB"][:, q * 8:(q + 1) * 8, :], in_=p_[:].rearrange("p (g v) -> p g v", v=64)),
                                     reads=[pb_], appends=[ch["vbB_b"]])
                        psa = {}
                        for (ch, i, j, t) in ent:
                            z = ch["z"]
                            Sa, Sab = ch["S"][ch["cur"]], ch["S_b"][ch["cur"]]
                            at_start = (t % SEG == 0) if z == 0 else (t % SEG == SEG - 1)
                            if at_start:
                                chain_start = (t == T0) if z == 0 else (t == T1 - 1)
                                if not is_L:
                                    k.op(k.dve, lambda: nc.vector.memset(Sa[:], 0.0), writes=[Sab])
                                elif chain_start:
                                    k.dma(k.sp, Sa[:].rearrange("p g v -> p (g v)"), self.I("rk_initS")[z], writes=[Sab])
                                else:
                                    k.op(k.dve, lambda: nc.vector.tensor_scalar(out=Sa[:], in0=Sa[:], scalar1=flag, scalar2=None, op0=ALU.mult),
                                         reads=[Sab, rpk_b], writes=[Sab])
                            kb, kbb = ch["kb"][i], ch["kb_b"][i]
                            k.op(k.dve, lambda: nc.vector.tensor_tensor(out=ch["tA1"][:], in0=Sa[:], in1=kb[:, 0, :, j:j + 1].to_broadcast([128, KC, 64]),
                                                                        op=ALU.mult), reads=[Sab, kbb], writes=[ch["tA1_b"]])
                            tAf = ch["tA1"][:].rearrange("p g v -> p (g v)")
                            pl = []
                            for q in range(2):
                                p_, pb_ = self.next_ps()
                                k.op(k.pe, lambda: nc.tensor.matmul(p_[:], BO[:], tAf[:, q * 512:(q + 1) * 512], start=True, stop=True),
                                     reads=[cst_b, ch["tA1_b"]], writes=[pb_])
                                k.op(k.act, lambda: nc.scalar.copy(out=ch["saB"][:, q * 8:(q + 1) * 8, :], in_=p_[:].rearrange("p (g v) -> p g v", v=64)),
                                     reads=[pb_], appends=[ch["saB_b"]])
                                pl.append((p_, pb_))
                            psa[z] = pl
                        for (ch, i, j, t) in ent:
                            kb, kbb = ch["kb"][i], ch["kb_b"][i]
                            k.op(k.dve, lambda: nc.vector.tensor_tensor(out=ch["tF3"][:], in0=ch["vbB"][:],
                                                                        in1=kb[:, 3, :, j:j + 1].to_broadcast([128, KC, 64]), op=ALU.mult),
                                 reads=[ch["vbB_b"], kbb], writes=[ch["tF3_b"]])
                        e_sw = k.pool if self.cfg.get("rk_sw_pool", False) else k.dve
                        for (ch, i, j, t) in ent:
                            kb, kbb = ch["kb"][i], ch["kb_b"][i]
                            cur = ch["cur"]
                            Sa, Sab, Sn, Snb = ch["S"][cur], ch["S_b"][cur], ch["S"][1 - cur], ch["S_b"][1 - cur]
                            k.op(e_sw, lambda: veng(e_sw).tensor_tensor(out=Sn[:], in0=Sa[:], in1=kb[:, 1, :, j:j + 1].to_broadcast([128, KC, 64]),
                                                                        op=ALU.mult), reads=[Sab, kbb], writes=[Snb])
                        for (ch, i, j, t) in ent:
                            cur = ch["cur"]
                            Sn, Snb = ch["S"][1 - cur], ch["S_b"][1 - cur]
                            k.op(k.dve, lambda: nc.vector.tensor_tensor(out=Sn[:], in0=Sn[:], in1=ch["tF3"][:], op=ALU.add),
                                 reads=[Snb, ch["tF3_b"]], writes=[Snb])
                        for (ch, i, j, t) in ent:
                            z = ch["z"]
                            kb, kbb = ch["kb"][i], ch["kb_b"][i]
                            k.op(k.dve, lambda: nc.vector.tensor_tensor(out=ch["tF2"][:], in0=ch["saB"][:],
                                                                        in1=kb[:, 2, :, j:j + 1].to_broadcast([128, KC, 64]), op=ALU.mult),
                                 reads=[ch["saB_b"], kbb], writes=[ch["tF2_b"]])
                        for (ch, i, j, t) in ent:
                            cur = ch["cur"]
                            Sn, Snb = ch["S"][1 - cur], ch["S_b"][1 - cur]
                            k.op(k.dve, lambda: nc.vector.tensor_tensor(out=Sn[:], in0=Sn[:], in1=ch["tF2"][:], op=ALU.subtract),
                                 reads=[Snb, ch["tF2_b"]], writes=[Snb])
                            ch["cur"] = 1 - cur
                        snap = [(ch, i, j, t, ch["S"][ch["cur"]], ch["S_b"][ch["cur"]]) for (ch, i, j, t) in ent]

                        def emit_y(snap=snap, jj=jj):
                            for (ch, i, j, t, Sc, Scb) in snap:
                                z = ch["z"]
                                kb, kbb = ch["kb"][i], ch["kb_b"][i]
                                r2 = ch["R2"][ch["r2i"]]
                                r2b = ch["R2_b"][ch["r2i"]]
                                ch["r2i"] = 1 - ch["r2i"]
                                for hp_i in range(2):
                                    k.op(k.act, lambda: nc.scalar.activation(out=r2[:, :, hp_i], in_=kb[:, 4, :, j], func=AF.Identity,
                                                                             scale=hselF[:, hp_i:hp_i + 1]),
                                         reads=[kbb, cst2_b], appends=[r2b])
                                sidx = (jj % SB) if z == 0 else (SB - 1 - (jj % SB))
                                for q in range(2):
                                    p_, pb_ = self.next_ps()
                                    for g8 in range(8):
                                        g_ = q * 8 + g8
                                        k.op(k.pe, lambda: nc.tensor.matmul(p_[0:2, g8 * 64:(g8 + 1) * 64], r2[:, g_, :], Sc[:, g_, :], start=True, stop=True),
                                             reads=[r2b, Scb], writes=[pb_])
                                    k.op(k.act, lambda: nc.scalar.copy(out=ch["ys"][:, sidx, q * 512:(q + 1) * 512], in_=p_[0:2, :]),
                                         reads=[pb_], writes=[ch["ys_b"]])
                                if jj % SB == SB - 1:
                                    tlo = t - (SB - 1) if z == 0 else t
                                    dst = YT[z][tlo:tlo + SB, :].rearrange("t (g hp v) -> hp t g v", hp=2, v=64)
                                    k.dma(k.sp, dst, ch["ys"][:].rearrange("p s (g v) -> p s g v", v=64), reads=[ch["ys_b"]], appends=[yt_b])
                                at_end = (t % SEG == SEG - 1) if z == 0 else (t % SEG == 0)
                                if at_end:
                                    k.dma(k.sp, self.O("rk_stS")[t // SEG, z], Sc[:].rearrange("p g v -> p (g v)"), reads=[Scb], writes=[outb])

                        if deferred[0] is not None:
                            deferred[0]()
                        deferred[0] = emit_y
                    if deferred[0] is not None:
                        deferred[0]()
                        deferred[0] = None

                run_pass(0, 1024, True)
                run_pass(1024, 1280, False)
                run_pass(1280, 1536, False)
                self.phase_barrier()
            with contextlib.ExitStack() as es2:
                sbt = lambda name, shape, dt: es2.enter_context(_sbuf(nc, name, list(shape), dt))
                rln = sbt("rk_rln", [128, 2, D], F32)
                rln_b = Buf()
                k.dma(k.sp, rln[:], self.I("rln").rearrange("p (a d) -> p a d", a=2), writes=[rln_b])
                yf = [sbt(f"rkc_yf{i}", [128, D], BF16) for i in range(2)]
                yb = [sbt(f"rkc_yb{i}", [128, D], BF16) for i in range(2)]
                yfb = [Buf() for _ in range(2)]
                ybb = [Buf() for _ in range(2)]
                ysum = sbt("rkc_ysum", [128, 32, 64], F32)
                ysq = sbt("rkc_ysq", [128, 32, 64], F32)
                ysum_b, ysq_b = Buf(), Buf()
                st1 = sbt("rkc_st1", [128, 32], F32)
                st2 = sbt("rkc_st2", [128, 32], F32)
                st1_b, st2_b = Buf(), Buf()
                yT = sbt("rkc_yT", [128, KC, 128], F32)
                yT_b = Buf()
                gl = [sbt(f"rkc_gl{i}", [128, KC, 128], F32) for i in range(2)]
                vl = [sbt(f"rkc_vl{i}", [128, KC, 128], F32) for i in range(2)]
                glb = [Buf() for _ in range(2)]
                vlb = [Buf() for _ in range(2)]
                zT = sbt("rkc_zT", [128, KC, 128], BF16)
                zT_b = Buf()

                def loadc(blk):
                    i = blk % 2
                    sl = slice(blk * 128, (blk + 1) * 128)
                    k.dma(k.sp, yf[i][:], YT[0][sl, :], reads=[yt_b], writes=[yfb[i]])
                    k.dma(k.sp, yb[i][:], YT[1][sl, :], reads=[yt_b], writes=[ybb[i]])
                    k.dma(k.sp, gl[i][:], GG[:, :, sl], reads=[scr_b], writes=[glb[i]])
                    k.dma(k.sp, vl[i][:], VB[:, :, sl], reads=[scr_b], writes=[vlb[i]])

                loadc(0)
                NBk = NTOK // 128
                for blk in range(NBk):
                    i = blk % 2
                    if blk + 1 < NBk:
                        loadc(blk + 1)
                    ysf = ysum[:].rearrange("p h v -> p (h v)")
                    k.op(k.dve, lambda: nc.vector.tensor_tensor(out=ysf, in0=yf[i][:], in1=yb[i][:], op=ALU.add),
                         reads=[yfb[i], ybb[i]], writes=[ysum_b])
                    k.op(k.dve, lambda: nc.vector.tensor_reduce(out=st1[:], in_=ysum[:], axis=AX.X, op=ALU.add), reads=[ysum_b], writes=[st1_b])
                    k.op(k.dve, lambda: nc.vector.tensor_scalar(out=st1[:], in0=st1[:], scalar1=1.0 / 64, scalar2=None, op0=ALU.mult),
                         reads=[st1_b], writes=[st1_b])
                    k.op(k.dve, lambda: nc.vector.tensor_tensor(out=ysum[:], in0=ysum[:], in1=st1[:].unsqueeze(2).to_broadcast([128, 32, 64]),
                                                                op=ALU.subtract), reads=[ysum_b, st1_b], writes=[ysum_b])
                    k.op(k.act, lambda: nc.scalar.activation(out=ysq[:], in_=ysum[:], func=AF.Square), reads=[ysum_b], writes=[ysq_b])
                    k.op(k.dve, lambda: nc.vector.tensor_reduce(out=st2[:], in_=ysq[:], axis=AX.X, op=ALU.add), reads=[ysq_b], writes=[st2_b])
                    k.op(k.dve, lambda: nc.vector.tensor_scalar(out=st2[:], in0=st2[:], scalar1=1.0 / 64, scalar2=64e-5, op0=ALU.mult, op1=ALU.add),
                         reads=[st2_b], writes=[st2_b])
                    k.op(k.act, lambda: nc.scalar.activation(out=st2[:], in_=st2[:], func=AF.Sqrt), reads=[st2_b], writes=[st2_b])
                    k.op(k.dve, lambda: nc.vector.reciprocal(out=st2[:], in_=st2[:]), reads=[st2_b], writes=[st2_b])
                    k.op(k.dve, lambda: nc.vector.tensor_tensor(out=ysum[:], in0=ysum[:], in1=st2[:].unsqueeze(2).to_broadcast([128, 32, 64]),
                                                                op=ALU.mult), reads=[ysum_b, st2_b], writes=[ysum_b])
                    k.op(k.dve, lambda: nc.vector.tensor_tensor(out=ysf, in0=ysf, in1=rln[:, 0, :], op=ALU.mult), reads=[ysum_b, rln_b], writes=[ysum_b])
                    k.op(k.dve, lambda: nc.vector.tensor_tensor(out=ysf, in0=ysf, in1=rln[:, 1, :], op=ALU.add), reads=[ysum_b, rln_b], writes=[ysum_b])
                    for q4 in range(4):
                        ptr, ptrb = self.next_ps()
                        for c4 in range(4):
                            c = q4 * 4 + c4
                            k.op(k.pe, lambda: nc.tensor.transpose(ptr[:, c4 * 128:(c4 + 1) * 128], ysf[:, c * 128:(c + 1) * 128], ident),
                                 reads=[ysum_b, rpk_b], writes=[ptrb])
                        k.op(k.act, lambda: nc.scalar.copy(out=yT[:, q4 * 4:(q4 + 1) * 4, :], in_=ptr[:].rearrange("p (c t) -> p c t", t=128)),
                             reads=[ptrb], writes=[yT_b])
                    k.op(k.dve, lambda: nc.vector.tensor_tensor(out=yT[:], in0=yT[:], in1=vl[i][:], op=ALU.add), reads=[yT_b, vlb[i]], writes=[yT_b])
                    k.op(k.dve, lambda: nc.vector.tensor_tensor(out=zT[:], in0=yT[:], in1=gl[i][:], op=ALU.mult), reads=[yT_b, glb[i]], writes=[zT_b])
                    k.dma(k.sp, hsT_s[:, :, blk * 128:(blk + 1) * 128].rearrange("c p t -> p c t"), zT[:], reads=[zT_b], appends=[hsT_b])
                self.phase_barrier()
            with contextlib.ExitStack() as es2:
                sbt = lambda name, shape, dt: es2.enter_context(_sbuf(nc, name, list(shape), dt))
                h2 = sbt("rk_h2", [128, KC, NTOK], BF16)
                h2b = [Buf() for _ in range(NGRP)]
                for t in range(NGRP):
                    k.dma(k.sp, h2[:, :, t * TT:(t + 1) * TT], hsT_s[:, :, t * TT:(t + 1) * TT].rearrange("c p t -> p c t"),
                          reads=[hsT_b], writes=[h2b[t]])
                wo_v = self.I("rwkv_w_o")[slot].rearrange("(kc p) n -> p kc n", p=128)

                def ev_o(oc, t, ps, psb):
                    ts = slice(t * TT, (t + 1) * TT)
                    k.op(k.dve, lambda: nc.vector.scalar_tensor_tensor(
                        out=self.x_sb[:, oc, ts], in0=ps[:], scalar=self.modT[:, l, 5, oc, t:t + 1],
                        in1=self.x_sb[:, oc, ts], op0=ALU.mult, op1=ALU.add),
                        reads=[psb, self.modT_b, self.xb[oc][t]], writes=[self.xb[oc][t]])

                self.linear_fm(h2, h2b, wo_v, 0, D, ev_o, "ro")
                self.phase_barrier()

    def layer(self, l):
        cfg = self.cfg
        ph = cfg.get("phases", ("ffn1", "mix", "ffn2"))
        if "ffn1" in ph:
            self.ffn(l, 0)
        if "mix" in ph:
            if l % 3 == 0:
                self.attn(l)
            elif l % 3 == 1:
                self.mlstm(l)
            else:
                self.rwkv(l)
        if "ffn2" in ph:
            self.ffn(l, 1)

    def finish(self):
        k = self.k
        yv = self.O("yT").rearrange("(c p) t -> p c t", p=128)
        outb = Buf("out")
        for c in range(KC):
            k.dma(k.sp, yv[:, c, :], self.x_sb[:, c, :], reads=self.xb[c], writes=[outb])
        self.phase_barrier()


def build_program(cfg):
    p = Prog(cfg)
    nc = p.build()
    nc._prog = p
    return nc


def filter_maps(nc, maps):
    names = set(nc._prog.inputs.keys())
    return [{k_: v for k_, v in m.items() if k_ in names} for m in maps]


def rope_tables(core):
    cos = np.ones((128, NTOK), np.float32)
    sin = np.zeros((128, NTOK), np.float32)
    if core < 4:
        t = np.arange(1024)
        row = (t // 64).astype(np.float32)
        col = (t % 64).astype(np.float32)
        inv = (10000.0 ** (-np.arange(32, dtype=np.float32) / 32)).astype(np.float32)
        for d in range(128):
            pos = row if d < 64 else col
            ang = pos * inv[d % 32]
            cos[d, :1024] = np.cos(ang)
            sin[d, :1024] = np.sin(ang)
    return cos, sin


def perm_T():
    PT = np.zeros((128, 128), np.float32)
    for m in range(128):
        if (m % 64) < 32:
            PT[m + 32, m] = -1.0
        else:
            PT[m - 32, m] = 1.0
    return PT


def attn_masks(core):
    M = np.zeros((128, 14, 128), np.float32)
    iq = np.arange(128)[None, :]
    is_ = np.arange(128)[:, None]
    for qb in range(8):
        if qb >= 1:
            if core < 4:
                M[:, 2 * (qb - 1), :] = (iq <= is_)
            else:
                M[:, 2 * (qb - 1), :] = 1.0 if (qb // 2 == (qb - 1) // 2) else 0.0
        if qb <= 6:
            if core < 4:
                M[:, 2 * qb + 1, :] = (is_ <= iq)
            else:
                M[:, 2 * qb + 1, :] = 1.0 if (qb // 2 == (qb + 1) // 2) else 0.0
    return M.reshape(128, 14 * 128)


def attn_pack(core, inp):
    A = np.zeros((2, 128, NAP), np.float32)
    cos, sin = rope_tables(core)
    for slot in range(2):
        A[slot, :, 0] = inp["attn_q_norm"][slot]
        A[slot, :, 1] = inp["attn_k_norm"][slot]
        A[slot, :, 2:18] = inp["attn_sink"][slot][None, :]
        A[slot, :, 18] = 1.0 if core < 4 else 0.0
        A[slot, :, 19:147] = np.eye(128, dtype=np.float32)
        A[slot, :, 147:147 + NTOK] = cos
        A[slot, :, 147 + NTOK:] = sin
    return A


def mlstm_pack(core, inp):
    M = np.zeros((1, 128, NMP), np.float32)
    p = np.arange(128)[:, None]
    f = np.arange(128)[None, :]
    M[0, :, 0:128] = np.eye(128, dtype=np.float32)
    M[0, :, 128:256] = 1.0
    M[0, :, 256:384] = (p <= f)
    M[0, :, 384:512] = (p >= f)
    M[0, :, 512:640] = np.where(p <= f, 0.0, -1e30)
    M[0, :, 640:768] = np.where(p >= f, 0.0, -1e30)
    M[0, :, 768:800] = inp["mlstm_b_gate"][0][None, :]
    M[0, :, 800] = 1.0 if core < 4 else 0.0
    M[0, :, 801:801 + D] = inp["mlstm_out_norm"][0][None, :]
    return M


def mlstm_init(core, inp):
    C0 = np.zeros((2, 128, 8, 257), np.float32)
    m0 = np.zeros((2, 128, 8), np.float32)
    if core < 4:
        C = inp["state_mlstm_C"][core, 0]
        n = inp["state_mlstm_n"][core, 0]
        m = inp["state_mlstm_m"][core, 0]
        C0[:, :, :, :256] = np.transpose(C, (0, 2, 1, 3))
        C0[:, :, :, 256] = np.transpose(n, (0, 2, 1))
        m0[:] = m[:, None, :]
    return C0.reshape(2, 128, 8 * 257), m0


def rwkv_host(core, inp):
    segs = core_segments(core)
    P = np.zeros((128, NRP), np.float32)
    NFM = 13 * KC
    vecs = [inp["rwkv_mu"][0][i] for i in range(6)] + [inp["rwkv_w0"][0][0], inp["rwkv_w0"][0][1], inp["rwkv_a0"][0][0], inp["rwkv_a0"][0][1],
                                                       inp["rwkv_k_k"][0], inp["rwkv_k_a"][0], inp["rwkv_r_k"][0].reshape(-1)]
    P[:, 0:NFM] = fm(np.stack(vecs, 0)).reshape(128, NFM)
    I2 = np.zeros((128, 64), np.float32)
    I2[np.arange(128), np.arange(128) % 64] = 1.0
    P[:, NFM:NFM + 64] = I2
    P[:, NFM + 64] = 1.0 if core < 4 else 0.0
    P[:, NFM + 65:NFM + 193] = np.eye(128, dtype=np.float32)
    seq_id = []
    for g in range(NSEG):
        kind, idx = segs[g]
        seq_id += [(0 if kind == "lat" else 1, idx)] * SEG
    pm = np.zeros(NTOK, np.float32)
    nm = np.zeros(NTOK, np.float32)
    for t in range(NTOK):
        if t > 0 and seq_id[t - 1] == seq_id[t]:
            pm[t] = 1.0
        if t < NTOK - 1 and seq_id[t + 1] == seq_id[t]:
            nm[t] = 1.0
    rmask = np.ascontiguousarray(np.broadcast_to(np.concatenate([pm, nm])[None, :], (128, 2 * NTOK)))
    rln = np.ascontiguousarray(np.broadcast_to(np.concatenate([inp["rwkv_ln_g"][0], inp["rwkv_ln_b"][0]])[None, :], (128, 2 * D)))
    initS = np.zeros((2, 128, 1024), np.float32)
    if core < 4:
        S0 = inp["state_rwkv"][core, 0]
        S0 = S0.reshape(2, 16, 2, 64, 64)
        initS = np.ascontiguousarray(np.transpose(S0, (0, 2, 4, 1, 3))).reshape(2, 128, 1024)
    return P, rmask, rln, initS


def make_in_maps(inp, cfg, cores=None):
    maps = []
    layers = list(cfg["layers"])
    ph = cfg.get("phases", ("ffn1", "mix", "ffn2"))
    full = (layers == [0, 1, 2, 3])
    mod_w_sel = inp["mod_w"] if full else np.ascontiguousarray(inp["mod_w"][layers])
    has_ffn = ("ffn1" in ph or "ffn2" in ph)
    if has_ffn:
        ffn_in_sel = inp["ffn_w_in"] if full else np.ascontiguousarray(inp["ffn_w_in"][layers])
        ffn_out_sel = inp["ffn_w_out"] if full else np.ascontiguousarray(inp["ffn_w_out"][layers])
    for core in (range(NCORES) if cores is None else cores):
        segs = core_segments(core)
        rows = []
        for g in range(NSEG):
            kind, idx = segs[g]
            if kind == "lat":
                rows.append(inp["x_sample"][idx, g * SEG:(g + 1) * SEG])
            else:
                rows.append(inp["x_prompt"][idx])
        xs = np.concatenate(rows, 0)
        m = {
            "xT": np.ascontiguousarray(xs.T),
            "pack": host_pack(core, inp),
            "mod_w": mod_w_sel,
            "attn_w_qkv": inp["attn_w_qkv"],
            "attn_w_o": inp["attn_w_o"],
            "apack": attn_pack(core, inp),
            "amask": attn_masks(core),
            "permT": perm_T(),
            "cache_k": (inp["cache_k"][core].reshape(2, 256, 512) if core < 4 else np.zeros((2, 256, 512), np.float32)),
            "cache_v": (inp["cache_v"][core].reshape(2, 256, 512) if core < 4 else np.zeros((2, 256, 512), np.float32)),
        }
        m["rpack"], m["rmask"], m["rln"], m["rk_initS"] = rwkv_host(core, inp)
        hT = np.zeros((2, 128), np.float32)
        hT[0, :64] = 1.0
        hT[1, 64:] = 1.0
        m["rk_hselT"] = hT
        for nm_ in ("rwkv_w_rkv", "rwkv_wA", "rwkv_wB", "rwkv_aA", "rwkv_aB", "rwkv_gA", "rwkv_gB", "rwkv_w_o"):
            m[nm_] = inp[nm_]
        m["mlstm_w_in"] = inp["mlstm_w_in"]
        m["mlstm_w_gate"] = inp["mlstm_w_gate"]
        m["mlstm_w_o"] = inp["mlstm_w_o"]
        m["mpack"] = mlstm_pack(core, inp)
        m["ml_initC"], m["ml_initm"] = mlstm_init(core, inp)
        if has_ffn:
            m["ffn_w_in"] = ffn_in_sel
            m["ffn_w_out"] = ffn_out_sel
        maps.append(m)
    return maps


FULL_CFG = {"layers": [0, 1, 2, 3], "phases": ("ffn1", "mix", "ffn2")}


def kernel(**inputs):
    inp = {k_: np.asarray(v) for k_, v in inputs.items()}
    cfg = FULL_CFG
    nc = build_program(cfg)
    maps = filter_maps(nc, make_in_maps(inp, cfg))
    res = run_bass_kernel_spmd(nc, maps, core_ids=list(range(NCORES)))
    R = res.results
    B, S_, DB, DS = 32, 256, 4, 1024
    y_prompt = np.zeros((B, S_, D), np.float32)
    y_sample = np.zeros((DB, DS, D), np.float32)
    new_k = np.zeros((B, 2, S_, 4, 128), np.float32)
    new_v = np.zeros((B, 2, S_, 4, 128), np.float32)
    new_C = np.zeros((B, 1, 2, 8, 128, 256), np.float32)
    new_n = np.zeros((B, 1, 2, 8, 128), np.float32)
    new_m = np.zeros((B, 1, 2, 8), np.float32)
    new_S = np.zeros((B, 1, 2, 32, 64, 64), np.float32)
    for core in range(NCORES):
        r = R[core]
        y = np.asarray(r["yT"]).T
        segs = core_segments(core)
        for g in range(NSEG):
            kind, idx = segs[g]
            rows = y[g * SEG:(g + 1) * SEG]
            if kind == "lat":
                y_sample[idx, g * SEG:(g + 1) * SEG] = rows
            else:
                y_prompt[idx] = rows
                for slot in range(2):
                    new_k[idx, slot] = np.asarray(r["newk"])[slot, g * SEG:(g + 1) * SEG].reshape(S_, 4, 128)
                    new_v[idx, slot] = np.asarray(r["newv"])[slot, g * SEG:(g + 1) * SEG].reshape(S_, 4, 128)
                Ck = np.asarray(r["ml_stC"])[g].reshape(2, 128, 8, 257)
                new_C[idx, 0] = np.transpose(Ck[..., :256], (0, 2, 1, 3))
                new_n[idx, 0] = np.transpose(Ck[..., 256], (0, 2, 1))
                new_m[idx, 0] = np.asarray(r["ml_stm"])[g][:, 0, :]
                Sk = np.asarray(r["rk_stS"])[g].reshape(2, 2, 64, 16, 64)
                new_S[idx, 0] = np.transpose(Sk, (0, 3, 1, 4, 2)).reshape(2, 32, 64, 64)
    return (y_prompt, y_sample, new_k, new_v, new_C, new_n, new_m, new_S)
```
